# Optimizing a Trainium2 kernel written in Bass

```python
import math
import jax, jax.numpy as jnp
from jax import lax
import numpy as np


D_MODEL = 1024
BATCH = 2
SEQ = 16384
DEPTH = 4

HEAD_DIM = 64
BLOCK = 128
DIFF_HEADS = 6
DIFF_QK_DIM = HEAD_DIM // 2
DIFF_V_DIM = HEAD_DIM
DIFF_WIDTH = DIFF_HEADS * DIFF_V_DIM
GMLP_GROUPS = 4
GMLP_GROUP_DIM = 64
GMLP_WIDTH = GMLP_GROUPS * GMLP_GROUP_DIM
CHUNK = 128
SWA_HEADS = 6
SWA_KV_HEADS = 2
SWA_GROUP = SWA_HEADS // SWA_KV_HEADS
SWA_WIDTH = SWA_HEADS * HEAD_DIM
WINDOW = 128
MIX_WIDTH = DIFF_WIDTH + GMLP_WIDTH + SWA_WIDTH
IN_SIZES = (DIFF_HEADS * 2 * DIFF_QK_DIM, DIFF_HEADS * 2 * DIFF_QK_DIM, DIFF_WIDTH,
            GMLP_WIDTH, GMLP_WIDTH,
            SWA_WIDTH, SWA_KV_HEADS * HEAD_DIM, SWA_KV_HEADS * HEAD_DIM)
IN_WIDTH = 384 + 384 + 384 + 256 + 256 + 384 + 128 + 128
D_FF = ((-(-8 * D_MODEL // 3) + 255) // 256) * 256
PLE_DIM = 256
N_ATTN_HEADS = DIFF_HEADS + SWA_HEADS
ALIBI_MAX_EXP = 8.0
EPS = 1e-6
NEG_INF = -1e30

kernel_name = 'hybrid_parallel_group_encoder_block'


def rmsnorm(x, g):
    x32 = x.astype(jnp.float32)
    y = x32 * lax.rsqrt(jnp.mean(x32 * x32, axis=-1, keepdims=True) + EPS)
    return (y * g.astype(jnp.float32)).astype(x.dtype)


def alibi_slopes():
    k = jnp.arange(1, N_ATTN_HEADS + 1, dtype=jnp.float32)
    s = jnp.exp2(-ALIBI_MAX_EXP * k / N_ATTN_HEADS)
    return s[SWA_HEADS:], s[:SWA_HEADS]


def diff_attention(q, k, v, lam, lam_init, g_sub, slopes):
    b, s_len = q.shape[0], q.shape[1]
    nb = s_len // BLOCK
    scale = DIFF_QK_DIM ** -0.5
    key_pos = jnp.arange(s_len)
    qb = q.reshape(b, nb, BLOCK, DIFF_HEADS, 2, DIFF_QK_DIM).transpose(1, 0, 2, 3, 4, 5)

    def one_block(args):
        q_blk, j = args
        sc = jnp.einsum('bqhcd,bkhcd->bhcqk', q_blk, k,
                        preferred_element_type=jnp.float32) * scale
        q_pos = j * BLOCK + jnp.arange(BLOCK)
        dist = jnp.abs(q_pos[:, None] - key_pos[None, :]).astype(jnp.float32)
        sc = sc - slopes[None, :, None, None, None] * dist
        pr = jax.nn.softmax(sc, axis=-1)
        attn = pr[:, :, 0] - lam * pr[:, :, 1]
        return jnp.einsum('bhqk,bkhd->bqhd', attn.astype(v.dtype), v)

    o = lax.map(one_block, (qb, jnp.arange(nb)))
    o = o.transpose(1, 0, 2, 3, 4).reshape(b, s_len, DIFF_HEADS, DIFF_V_DIM)
    o = rmsnorm(o, g_sub) * (1.0 - lam_init)
    return o.reshape(b, s_len, DIFF_WIDTH)


def spatial_gating(u, v, ln_g, ln_b, w_s, b_s):
    b, s_len = u.shape[0], u.shape[1]
    v32 = v.astype(jnp.float32)
    mu = jnp.mean(v32, axis=-1, keepdims=True)
    var = jnp.mean(jnp.square(v32 - mu), axis=-1, keepdims=True)
    vn = ((v32 - mu) * lax.rsqrt(var + EPS) * ln_g.astype(jnp.float32)
          + ln_b.astype(jnp.float32)).astype(v.dtype)
    nc = s_len // CHUNK
    vc = vn.reshape(b, nc, CHUNK, GMLP_GROUPS, GMLP_GROUP_DIM)
    mixed = jnp.einsum('gts,bnsgd->bntgd', w_s, vc) + b_s.T[:, :, None]
    return u * mixed.reshape(b, s_len, GMLP_WIDTH)


def window_gqa(q, k, v, sinks, slopes):
    b, s_len = q.shape[0], q.shape[1]
    nb = s_len // BLOCK
    k = k.reshape(b, s_len, SWA_KV_HEADS, HEAD_DIM)
    v = v.reshape(b, s_len, SWA_KV_HEADS, HEAD_DIM)
    pad = ((0, 0), (BLOCK, BLOCK), (0, 0), (0, 0))
    kp = jnp.pad(k, pad).reshape(b, nb + 2, BLOCK, SWA_KV_HEADS, HEAD_DIM)
    vp = jnp.pad(v, pad).reshape(b, nb + 2, BLOCK, SWA_KV_HEADS, HEAD_DIM)
    kb = jnp.concatenate([kp[:, :-2], kp[:, 1:-1], kp[:, 2:]], axis=2)
    vb = jnp.concatenate([vp[:, :-2], vp[:, 1:-1], vp[:, 2:]], axis=2)
    qb = q.reshape(b, nb, BLOCK, SWA_KV_HEADS, SWA_GROUP, HEAD_DIM)
    sc = jnp.einsum('bnqkgd,bnckd->bnkgqc', qb, kb,
                    preferred_element_type=jnp.float32) * (HEAD_DIM ** -0.5)
    rel = jnp.arange(3 * BLOCK)[None, :] - BLOCK - jnp.arange(BLOCK)[:, None]
    dist = jnp.abs(rel)
    key_pos = (jnp.arange(nb)[:, None] - 1) * BLOCK + jnp.arange(3 * BLOCK)[None, :]
    valid = (dist <= WINDOW)[None] & ((key_pos >= 0) & (key_pos < s_len))[:, None, :]
    sl = slopes.reshape(SWA_KV_HEADS, SWA_GROUP)
    sc = sc - sl[:, :, None, None] * dist.astype(jnp.float32)
    sc = jnp.where(valid[None, :, None, None], sc, NEG_INF)
    sink = sinks.astype(jnp.float32).reshape(SWA_KV_HEADS, SWA_GROUP)[:, :, None, None]
    m = jnp.maximum(jnp.max(sc, axis=-1, keepdims=True), sink)
    e = jnp.exp(sc - m)
    pr = e / (jnp.sum(e, axis=-1, keepdims=True) + jnp.exp(sink - m))
    o = jnp.einsum('bnkgqc,bnckd->bnqkgd', pr.astype(v.dtype), vb)
    return o.reshape(b, s_len, SWA_WIDTH)


def setup_inputs(seed: int = 0) -> dict:
    key = jax.random.key(seed)
    ks = jax.random.split(key, 24)

    def nrm(k, shape, scale):
        return jax.random.normal(k, shape, jnp.float32) * scale

    def gain(k, shape):
        return 1.0 + 0.05 * jax.random.normal(k, shape, jnp.float32)

    L = DEPTH
    return {
        'x': nrm(ks[0], (BATCH, SEQ, D_MODEL), 1.0),
        'p': nrm(ks[1], (DEPTH, BATCH, SEQ, PLE_DIM), 1.0),
        'g_pre_mix': gain(ks[2], (L, D_MODEL)),
        'w_in': nrm(ks[3], (L, D_MODEL, IN_WIDTH), D_MODEL ** -0.5),
        'lam_q1': nrm(ks[4], (L, DIFF_QK_DIM), 0.1),
        'lam_k1': nrm(ks[5], (L, DIFF_QK_DIM), 0.1),
        'lam_q2': nrm(ks[6], (L, DIFF_QK_DIM), 0.1),
        'lam_k2': nrm(ks[7], (L, DIFF_QK_DIM), 0.1),
        'g_diff_sub': gain(ks[8], (L, DIFF_V_DIM)),
        'gmlp_ln_g': gain(ks[9], (L, GMLP_WIDTH)),
        'gmlp_ln_b': nrm(ks[10], (L, GMLP_WIDTH), 0.02),
        'w_spatial': nrm(ks[11], (L, GMLP_GROUPS, CHUNK, CHUNK), CHUNK ** -0.5),
        'b_spatial': gain(ks[12], (L, GMLP_GROUPS, CHUNK)),
        'swa_sinks': nrm(ks[13], (L, SWA_HEADS), 0.5),
        'w_out': nrm(ks[14], (L, MIX_WIDTH, D_MODEL), MIX_WIDTH ** -0.5),
        'g_post_mix': gain(ks[15], (L, D_MODEL)),
        'g_pre_ffn': gain(ks[16], (L, D_MODEL)),
        'w_ffn_in': nrm(ks[17], (L, D_MODEL, 2 * D_FF), D_MODEL ** -0.5),
        'w_ffn_out': nrm(ks[18], (L, D_FF, D_MODEL), D_FF ** -0.5),
        'g_post_ffn': gain(ks[19], (L, D_MODEL)),
        'w_ple_up': nrm(ks[20], (L, PLE_DIM, D_MODEL), PLE_DIM ** -0.5),
        'w_ple_gate': nrm(ks[21], (L, D_MODEL, D_MODEL), D_MODEL ** -0.5),
        'g_ple_gate': gain(ks[22], (L, D_MODEL)),
        'g_ple_post': gain(ks[23], (L, D_MODEL)),
    }


def reference(x, p, g_pre_mix, w_in, lam_q1, lam_k1, lam_q2, lam_k2, g_diff_sub,
              gmlp_ln_g, gmlp_ln_b, w_spatial, b_spatial, swa_sinks, w_out, g_post_mix,
              g_pre_ffn, w_ffn_in, w_ffn_out, g_post_ffn, w_ple_up, w_ple_gate,
              g_ple_gate, g_ple_post):
    b, s_len = x.shape[0], x.shape[1]
    diff_slopes, swa_slopes = alibi_slopes()
    split_at = np.cumsum(IN_SIZES)[:-1].tolist()
    for l in range(DEPTH):
        h = rmsnorm(x, g_pre_mix[l])
        proj = h @ w_in[l]
        a_q, a_k, a_v, b_u, b_v, c_q, c_k, c_v = jnp.split(proj, split_at, axis=-1)
        lam_init = 0.8 - 0.6 * math.exp(-0.3 * l)
        f32 = jnp.float32
        lam = (jnp.exp(jnp.sum(lam_q1[l].astype(f32) * lam_k1[l].astype(f32)))
               - jnp.exp(jnp.sum(lam_q2[l].astype(f32) * lam_k2[l].astype(f32)))
               + lam_init)
        y_a = diff_attention(
            a_q.reshape(b, s_len, DIFF_HEADS, 2, DIFF_QK_DIM),
            a_k.reshape(b, s_len, DIFF_HEADS, 2, DIFF_QK_DIM),
            a_v.reshape(b, s_len, DIFF_HEADS, DIFF_V_DIM),
            lam, lam_init, g_diff_sub[l], diff_slopes)
        y_b = spatial_gating(b_u, b_v, gmlp_ln_g[l], gmlp_ln_b[l], w_spatial[l], b_spatial[l])
        y_c = window_gqa(c_q, c_k, c_v, swa_sinks[l], swa_slopes)
        y = jnp.concatenate([y_a, y_b, y_c], axis=-1) @ w_out[l]
        x = x + rmsnorm(y, g_post_mix[l])
        h = rmsnorm(x, g_pre_ffn[l])
        gate, up = jnp.split(h @ w_ffn_in[l], 2, axis=-1)
        f = (jax.nn.silu(gate) * up) @ w_ffn_out[l]
        x = x + rmsnorm(f, g_post_ffn[l])
        e = p[l] @ w_ple_up[l]
        g = jax.nn.sigmoid(rmsnorm(x, g_ple_gate[l]) @ w_ple_gate[l])
        x = x + rmsnorm(e * g, g_ple_post[l])
    return x
```

```python
import numpy as np
import ml_dtypes
from contextlib import ExitStack
import concourse.bass as bass
import concourse.mybir as mybir
from concourse.bass_utils import run_bass_kernel_spmd

F32 = mybir.dt.float32
BF16 = mybir.dt.bfloat16
AF = mybir.ActivationFunctionType
ALU = mybir.AluOpType
AX = mybir.AxisListType
NPBF = ml_dtypes.bfloat16

D = 1024
SEQ = 16384
NB = 2
L = 4
NTOK = 4096
NQ = 8
DFF = 2816
EPS = 1e-6
SC_A = 32 ** -0.5
SC_W = 64 ** -0.5
_k = np.arange(1, 13, dtype=np.float64)
_sl = np.exp2(-8.0 * _k / 12.0)
SL_DIFF = _sl[6:]
SL_WIN = _sl[:6]
CUT = 30.0
BIG = 30000.0
LA = 2
NSR = 3
NPR = 4


def lam_init(l):
    return 0.8 - 0.6 * float(np.exp(-0.3 * l))


class Dummy:
    def __getattr__(self, n):
        return self

    def __call__(self, *a, **k):
        return self

    def __getitem__(self, k):
        return self

    def __enter__(self):
        return self

    def __exit__(self, *a):
        return False


class Sync:
    def __init__(self):
        self.dry = True
        self.total = {}
        self.count = {}
        self.seq = {}
        self.rpos = {}
        self.handles = {}
        self.scope = None

    def _k(self, key, glob):
        return key if glob else (self.scope, key)

    def sig(self, ins, sem, key, dma=False, glob=False):
        inc = 16 if dma else 1
        k = (sem, self._k(key, glob))
        if self.dry:
            assert k not in self.count, k
            t = self.total.get(sem, 0) + inc
            self.total[sem] = t
            self.count[k] = t
            self.seq.setdefault(sem, []).append((k, t))
        else:
            i = self.rpos.get(sem, 0)
            kk, t = self.seq[sem][i]
            assert kk == k, (kk, k)
            self.rpos[sem] = i + 1
            ins.then_inc(self.handles[sem], inc)

    def wait(self, e, sem, key, glob=False):
        if not self.dry:
            e.wait_ge(self.handles[sem], self.count[(sem, self._k(key, glob))])

    def fence(self, e, ins, sem):
        if self.dry:
            t = self.total.get(sem, 0) + 1
            self.total[sem] = t
            self.seq.setdefault(sem, []).append((None, t))
        else:
            i = self.rpos.get(sem, 0)
            kk, t = self.seq[sem][i]
            assert kk is None, kk
            self.rpos[sem] = i + 1
            ins.then_inc(self.handles[sem], 1)
            e.wait_ge(self.handles[sem], t)


def band_tiles(h, i):
    out = []
    for j in range(128):
        if j < 16 * i:
            dmin = 2048 * i - (128 * j + 127)
        elif j >= 16 * (i + 1):
            dmin = 128 * j - (2048 * i + 2047)
        else:
            dmin = 0
        if SL_DIFF[h] * dmin <= CUT:
            out.append(j)
    return out


def j2idx(j):
    T = j // 4
    sub = j % 4
    r = T % 4
    ii = T // 4
    return r, 4 * ii + sub


_CIDX = None


def cidx_map():
    global _CIDX
    if _CIDX is None:
        m = {}
        n = 1
        for h in range(6):
            for i in range(NQ):
                for j in band_tiles(h, i):
                    if j < 16 * i or j >= 16 * (i + 1):
                        m[(h, i, j)] = n
                        n += 1
        _CIDX = (m, n)
    return _CIDX


def split_bf(v):
    hi = np.float32(np.asarray(v, np.float32).astype(NPBF).astype(np.float32))
    lo = np.float32(np.asarray(np.float32(v) - hi, np.float32).astype(NPBF).astype(np.float32))
    return hi, lo


def host_tables(c):
    m, n = cidx_map()
    ct = np.zeros((n,), np.float32)
    for (h, i, j), col in m.items():
        qc = 2048 * i + 512 * c + 256
        ct[col] = -SL_DIFF[h] * abs(128 * j - qc)
    ctab = np.ascontiguousarray(np.broadcast_to(ct[None, :], (128, n))).astype(np.float32)
    ki = np.arange(128)[:, None, None]
    qi = np.arange(512)[None, None, :]
    tg = np.arange(16)[None, :, None]
    dn = np.abs(128 * tg + ki - (512 * c + qi)).astype(np.float32)
    koff = np.array([-128] + [128 * t for t in range(16)] + [2048])[None, :, None]
    dw = np.abs(koff + ki - (512 * c + qi)).astype(np.float32)
    dw = np.where(dw <= 128.0, dw, BIG).astype(np.float32)
    return {"ctab": ctab, "dn": np.ascontiguousarray(dn), "dw": np.ascontiguousarray(dw)}


def host_consts():
    kaug = np.zeros((6, 4, 128), np.float32)
    qaug = np.zeros((6, 2, 4, NTOK), np.float32)
    qip = (np.arange(NTOK) % 512 - 256).astype(np.float32)
    for h in range(6):
        sp = SL_DIFF[h] / SC_A
        hi, lo = split_bf(sp)
        kaug[h, 0, :] = -hi
        kaug[h, 1, :] = -lo
        kaug[h, 2, :] = np.arange(128)
        kaug[h, 3, :] = np.arange(128)
        qaug[h, 0, 0, :] = qip
        qaug[h, 0, 1, :] = qip
        qaug[h, 0, 2, :] = hi
        qaug[h, 0, 3, :] = lo
        qaug[h, 1] = -qaug[h, 0]
    kaug = np.tile(kaug, (1, 1, 128))
    return {
        "kaug": kaug.astype(NPBF),
        "qaug": qaug.astype(NPBF),
        "identb": np.eye(128, dtype=np.float32).astype(NPBF),
        "identf": np.eye(128, dtype=np.float32),
    }


class Prog:
    def __init__(self):
        self.nc = bass.Bass("TRN2", target_bir_lowering=False)
        self.S = Sync()
        self.dram = {}
        self.phases = []

    def din(self, name, shape, dt):
        t = self.nc.dram_tensor(name, list(shape), dt, kind="ExternalInput").ap()
        self.dram[name] = t
        return t

    def dout(self, name, shape, dt):
        t = self.nc.dram_tensor(name, list(shape), dt, kind="ExternalOutput").ap()
        self.dram[name] = t
        return t

    def dint(self, name, shape, dt):
        t = self.nc.dram_tensor(name, list(shape), dt, kind="Internal").ap()
        self.dram[name] = t
        return t

    def build(self):
        S = self.S
        nph = len(self.phases)
        engs = ["sp", "act", "pe", "dve", "pool"]
        S.dry = True
        for pi, ph in enumerate(self.phases):
            S.scope = pi
            ph.reset()
            ph.bind(Dummy())
            for en in engs:
                e = Dummy()
                self._barrier_pre(en, e, pi)
                getattr(ph, en)(e)
                self._barrier_post(en, e, pi)
        S.dry = False
        nc = self.nc
        with ExitStack() as es:
            for sem in S.seq:
                S.handles[sem] = es.enter_context(nc.semaphore(sem))
            self.bar_tile = es.enter_context(nc.sbuf_tensor("bar_tile", [128, 8], F32))
            for pi, ph in enumerate(self.phases):
                S.scope = pi
                ph.reset()
                with ExitStack() as pes:
                    bufs = ph.alloc(nc, pes)
                    ph.bind(bufs)
                    with nc.Block() as block:
                        def mk(en, ph=ph, pi=pi):
                            def f(e):
                                S.scope = pi
                                self._barrier_pre(en, e, pi)
                                getattr(ph, en)(e)
                                self._barrier_post(en, e, pi)
                            return f
                        block.sync(mk("sp"))
                        block.scalar(mk("act"))
                        block.tensor(mk("pe"))
                        block.vector(mk("dve"))
                        block.gpsimd(mk("pool"))
        for sem in S.seq:
            assert S.rpos.get(sem, 0) == len(S.seq[sem]), sem
        return nc

    def _barrier_pre(self, en, e, pi):
        if pi == 0 or en == "pool":
            return
        self.S.wait(e, "bar", ("end", pi - 1), glob=True)

    def _barrier_post(self, en, e, pi):
        if en != "pool":
            return
        S = self.S
        if S.dry:
            S.sig(None, "bar", ("end", pi), glob=True)
        else:
            ins = e.memset(self.bar_tile[:, :], 0.0)
            S.sig(ins, "bar", ("end", pi), glob=True)


class Phase:
    def __init__(self, prog, tag):
        self.P = prog
        self.S = prog.S
        self.tag = tag

    def reset(self):
        pass

    def bind(self, bufs):
        self.b = bufs

    def alloc(self, nc, es):
        return Dummy()

    def sp(self, e):
        pass

    def act(self, e):
        pass

    def pe(self, e):
        pass

    def dve(self, e):
        pass

    def pool(self, e):
        pass

    def sem(self, n):
        return n


class Bufs:
    pass


_UID = [0]


def _uname(name):
    _UID[0] += 1
    return "%s_%d" % (name, _UID[0])


def sb(nc, es, name, shape, dt):
    return es.enter_context(nc.sbuf_tensor(_uname(name), list(shape), dt))


def ps(nc, es, name, shape, dt):
    return es.enter_context(nc.psum_tensor(_uname(name), list(shape), dt))


NCH = 16
CHE = 262144


def wview(w, c0, c1):
    return w[:, c0:c1].rearrange("(kc p) c -> p kc c", p=128)


class NormT:
    def __init__(self, ph, name):
        self.ph = ph
        self.S = ph.S
        self.n = ph.sem(name)

    def dve(self, e, t, xt, gbc, hb, junk, ss, rstd, wait_x, wait_hb_free):
        S = self.S
        n = self.n
        wait_x()
        wait_hb_free()
        for s in range(4):
            ins = e.tensor_tensor(out=junk[:, :], in0=xt[:, s, :], in1=xt[:, s, :], op=ALU.mult)
            S.fence(e, ins, n + "_f")
            ins = e.tensor_reduce(out=ss[:, s:s + 1], in_=junk[:, :], axis=AX.X, op=ALU.add)
            S.fence(e, ins, n + "_f")
        ins = e.tensor_scalar(out=rstd[:, 0:4], in0=ss[:, 0:4], scalar1=1.0 / D, scalar2=EPS, op0=ALU.mult, op1=ALU.add)
        S.sig(ins, n + "_v", t)
        S.wait(e, n + "_sq", t)
        ins = e.reciprocal(out=rstd[:, 0:4], in_=rstd[:, 0:4])
        S.fence(e, ins, n + "_f")
        for s in range(4):
            ins = e.scalar_tensor_tensor(out=hb[:, s, :], in0=xt[:, s, :], scalar=rstd[:, s:s + 1], in1=gbc[:, :],
                                         op0=ALU.mult, op1=ALU.mult)
            S.sig(ins, n + "_hb", (t, s))

    def pe(self, e, t, hb, pst, identb):
        S = self.S
        n = self.n
        for kc in range(8):
            g = t * 8 + kc
            if g >= 2:
                S.wait(e, n + "_ev", g - 2)
            for s in range(4):
                if kc == 0:
                    S.wait(e, n + "_hb", (t, s))
                ins = e.transpose(pst[:, g % 2, s * 128:(s + 1) * 128], hb[:, s, kc * 128:(kc + 1) * 128], identb[:, :])
            S.sig(ins, n + "_tr", g)

    def act(self, e, t, pst, hT, wait_hT_free, rstd=None):
        S = self.S
        n = self.n
        S.wait(e, n + "_v", t)
        ins = e.activation(out=rstd[:, 0:4], in_=rstd[:, 0:4], func=AF.Sqrt)
        S.sig(ins, n + "_sq", t)
        wait_hT_free()
        for kc in range(8):
            g = t * 8 + kc
            S.wait(e, n + "_tr", g)
            ins = e.activation(out=hT[:, kc, :], in_=pst[:, g % 2, 0:512], func=AF.Copy)
            S.sig(ins, n + "_ev", g)

    def wait_hT(self, e, t):
        self.S.wait(e, self.n + "_ev", t * 8 + 7)

    def wait_tr_done(self, e, t):
        self.S.wait(e, self.n + "_tr", t * 8 + 7)


class P1(Phase):
    FM = [(0, 384), (384, 768), (2048, 2176)]
    TM = [(768, 1152), (2176, 2304)]

    def __init__(self, prog, tag, x, w_in, g_pre, identb, Qs, E):
        super().__init__(prog, tag)
        self.x, self.w_in, self.g_pre, self.identb_d = x, w_in, g_pre, identb
        self.Qs, self.E = Qs, E
        self.EK = E[0:6, :].rearrange("h (a n) -> (h a) n", n=NTOK)
        self.ECK = E[12:14, :].rearrange("h (a n) -> (h a) n", n=NTOK)
        self.nt = NormT(self, "nt")

    def alloc(self, nc, es):
        b = Bufs()
        b.wfm = sb(nc, es, "p1_wfm", [128, 8, 896], BF16)
        b.wtm = sb(nc, es, "p1_wtm", [128, 8, 512], BF16)
        b.gbc = sb(nc, es, "p1_gbc", [128, D], F32)
        b.identb = sb(nc, es, "p1_id", [128, 128], BF16)
        b.xt = sb(nc, es, "p1_xt", [128, 4, D], F32)
        b.hb = sb(nc, es, "p1_hb", [128, 4, D], BF16)
        b.junk = sb(nc, es, "p1_junk", [128, D], F32)
        b.ss = sb(nc, es, "p1_ss", [128, 4], F32)
        b.rstd = sb(nc, es, "p1_rstd", [128, 4], F32)
        b.hT = sb(nc, es, "p1_hT", [128, 8, 512], BF16)
        b.fm = sb(nc, es, "p1_fm", [128, 2, 7, 512], BF16)
        b.vt = sb(nc, es, "p1_vt", [128, 2, 4, 512], BF16)
        b.pst = ps(nc, es, "p1_pst", [128, 2, 1024], BF16)
        b.pm = ps(nc, es, "p1_pm", [128, 4, 512], F32)
        return b

    def sp(self, e):
        S, b = self.S, self.b
        ld = self.sem("ld")
        ins = e.dma_start(out=b.identb[:, :], in_=self.identb_d[:, :])
        S.sig(ins, ld, "id", dma=True)
        ins = e.dma_start(out=b.gbc[:, :], in_=self.g_pre.partition_broadcast(128))
        S.sig(ins, ld, "g", dma=True)
        for t in range(NQ):
            if t >= 1:
                S.wait(e, self.nt.n + "_hb", (t - 1, 3))
            ins = e.dma_start(out=b.xt[:, :, :], in_=self.x[t * 512:(t + 1) * 512, :].rearrange("(s p) d -> p s d", p=128))
            S.sig(ins, self.sem("ldx"), t, dma=True)

    def pool(self, e):
        S, b = self.S, self.b
        lw = self.sem("lw")
        c = 0
        for (c0, c1) in self.FM:
            ins = e.dma_start(out=b.wfm[:, :, c:c + (c1 - c0)], in_=wview(self.w_in, c0, c1))
            S.sig(ins, lw, ("fm", c0), dma=True)
            c += c1 - c0
        c = 0
        for (c0, c1) in self.TM:
            ins = e.dma_start(out=b.wtm[:, :, c:c + (c1 - c0)], in_=wview(self.w_in, c0, c1))
            S.sig(ins, lw, ("tm", c0), dma=True)
            c += c1 - c0
        for t in range(NQ):
            sl = t % 2
            st = self.sem("st%d" % sl)
            cs = slice(t * 512, (t + 1) * 512)
            S.wait(e, self.sem("ev"), ("fm", t, 6))
            ins = e.dma_start(out=self.Qs[:, cs].rearrange("(b p) n -> p b n", p=128), in_=b.fm[:, sl, 0:3, :])
            S.sig(ins, st, ("q", t), dma=True)
            ins = e.dma_start(out=self.EK[:, cs].rearrange("(b p) n -> p b n", p=128), in_=b.fm[:, sl, 3:6, :])
            S.sig(ins, st, ("k", t), dma=True)
            ins = e.dma_start(out=self.ECK[:, cs], in_=b.fm[:, sl, 6, :])
            S.sig(ins, st, ("ck", t), dma=True)
            S.wait(e, self.sem("ev"), ("tm", t, 3))
            for hh in range(8):
                ch = 6 + hh if hh < 6 else 14 + (hh - 6)
                dst = self.E[ch, :].rearrange("(p t d) -> p t d", p=128, t=32)
                ins = e.dma_start(out=dst[:, 4 * t:4 * t + 4, :], in_=b.vt[:, sl, :, hh * 64:(hh + 1) * 64])
                S.sig(ins, st, ("cv" if hh == 7 else ("v", hh), t), dma=True)
        S.wait(e, self.sem("st0"), ("cv", NQ - 2))
        S.wait(e, self.sem("st1"), ("cv", NQ - 1))

    def dve(self, e):
        S, b = self.S, self.b
        S.wait(e, self.sem("ld"), "g")
        for t in range(NQ):
            self.nt.dve(e, t, b.xt, b.gbc, b.hb, b.junk, b.ss, b.rstd,
                        wait_x=lambda: S.wait(e, self.sem("ldx"), t),
                        wait_hb_free=lambda: (self.nt.wait_tr_done(e, t - 1) if t >= 1 else None))

    def pe(self, e):
        S, b = self.S, self.b
        S.wait(e, self.sem("ld"), "g")
        S.wait(e, self.sem("lw"), ("tm", self.TM[-1][0]))
        n = 0
        for t in range(NQ):
            self.nt.pe(e, t, b.hb, b.pst, b.identb)
            self.nt.wait_hT(e, t)
            for blk in range(7):
                if n >= 4:
                    S.wait(e, self.sem("ev"), self.evkeys[n - 4])
                for kc in range(8):
                    ins = e.matmul(b.pm[:, n % 4, :], lhsT=b.wfm[:, kc, blk * 128:(blk + 1) * 128], rhs=b.hT[:, kc, :],
                                   start=(kc == 0), stop=(kc == 7))
                S.sig(ins, self.sem("mm"), ("fm", t, blk))
                self.evkeys.append(("fm", t, blk))
                n += 1
            for s in range(4):
                if n >= 4:
                    S.wait(e, self.sem("ev"), self.evkeys[n - 4])
                for kc in range(8):
                    ins = e.matmul(b.pm[:, n % 4, :], lhsT=b.hT[:, kc, s * 128:(s + 1) * 128], rhs=b.wtm[:, kc, :],
                                   start=(kc == 0), stop=(kc == 7))
                S.sig(ins, self.sem("mm"), ("tm", t, s))
                self.evkeys.append(("tm", t, s))
                n += 1

    def reset(self):
        self.evkeys = []

    def act(self, e):
        S, b = self.S, self.b
        n = 0
        for t in range(NQ):
            sl = t % 2
            self.nt.act(e, t, b.pst, b.hT,
                        wait_hT_free=lambda: (S.wait(e, self.sem("mm"), ("tm", t - 1, 3)) if t >= 1 else None),
                        rstd=b.rstd)
            for blk in range(7):
                S.wait(e, self.sem("mm"), ("fm", t, blk))
                if t >= 2 and blk == 0:
                    S.wait(e, self.sem("st%d" % sl), ("cv", t - 2))
                ins = e.activation(out=b.fm[:, sl, blk, :], in_=b.pm[:, n % 4, :], func=AF.Copy)
                S.sig(ins, self.sem("ev"), ("fm", t, blk))
                n += 1
            for s in range(4):
                S.wait(e, self.sem("mm"), ("tm", t, s))
                if t >= 2 and s == 0:
                    S.wait(e, self.sem("st%d" % sl), ("cv", t - 2))
                ins = e.activation(out=b.vt[:, sl, s, :], in_=b.pm[:, n % 4, :], func=AF.Copy)
                S.sig(ins, self.sem("ev"), ("tm", t, s))
                n += 1


class AttnPipe(Phase):
    NSR = NSR

    def reset(self):
        self.gu = {"pe": 0, "act": 0, "dve": 0}
        self.prev_tp = {"pe": None}

    def pe_group(self, e, g):
        S, b = self.S, self.b
        U = len(g["units"])
        g0 = self.gu["pe"]
        for t in range(U + LA):
            if t < U:
                u = g["units"][t]
                gu = g0 + t
                if gu >= self.NSR:
                    S.wait(e, self.sem("p"), gu - self.NSR)
                if t == 0 and g.get("wait_pe") is not None:
                    g["wait_pe"](e)
                ins = e.matmul(b.sring[:, gu % self.NSR, :], lhsT=u["lhsT"], rhs=u["rhs"], start=True, stop=True)
                S.sig(ins, self.sem("qk"), gu)
            if t >= LA:
                v = t - LA
                u = g["units"][v]
                gv = g0 + v
                S.wait(e, self.sem("p"), gv)
                if u["first"] and g["acc_prev"] is not None:
                    S.wait(e, self.sem("ev"), (g["acc_prev"], u["acc"]))
                ins = e.matmul(b.oacc[:, g["par"] * 2 + u["acc"], :], lhsT=u["vl"], rhs=b.pring[:, gv % NPR, :],
                               start=u["first"], stop=u["last"])
                S.sig(ins, self.sem("pv"), gv)
        self.gu["pe"] = g0 + U

    def pe_group_pairs(self, e, g):
        S, b = self.S, self.b
        LAP = 2
        U = len(g["units"])
        NP_ = U // 2
        g0 = self.gu["pe"]
        for t in range(NP_ + LAP):
            if t >= LAP:
                S.wait(e, self.sem("p"), g0 + 2 * (t - LAP) + 1)
            elif g0 + 2 * (t - LAP) + 1 >= 0:
                S.wait(e, self.sem("p"), g0 + 2 * (t - LAP) + 1)
            if t == 0 and g.get("wait_pe") is not None:
                g["wait_pe"](e)
            if t < NP_:
                for c in (0, 1):
                    u = g["units"][2 * t + c]
                    gu = g0 + 2 * t + c
                    ins = e.matmul(b.sring[:, gu % self.NSR, :], lhsT=u["lhsT"], rhs=u["rhs"], start=True, stop=True)
                    S.sig(ins, self.sem("qk"), gu)
            if t >= LAP:
                for c in (0, 1):
                    v = 2 * (t - LAP) + c
                    u = g["units"][v]
                    gv = g0 + v
                    if u["first"] and g["acc_prev"] is not None:
                        S.wait(e, self.sem("ev"), (g["acc_prev"], u["acc"]))
                    ins = e.matmul(b.oacc[:, g["par"] * 2 + u["acc"], :], lhsT=u["vl"], rhs=b.pring[:, gv % NPR, :],
                                   start=u["first"], stop=u["last"])
                    S.sig(ins, self.sem("pv"), gv)
        self.gu["pe"] = g0 + U

    def pe_post(self, e, g):
        S, b = self.S, self.b
        for c in (0, 1):
            S.wait(e, self.sem("ev"), (g["gid"], c))
            if self.prev_tp["pe"] is not None:
                S.wait(e, self.sem("tpc"), self.prev_tp["pe"])
            for s in range(4):
                ins = e.transpose(b.tps[:, s, 0:65], b.oT[0:65, c, s * 128:(s + 1) * 128], b.identf[0:65, 0:65])
            S.sig(ins, self.sem("tp"), (g["gid"], c))
            self.prev_tp["pe"] = (g["gid"], c)

    def act_group(self, e, g):
        S, b = self.S, self.b
        g0 = self.gu["act"]
        for t, u in enumerate(g["units"]):
            gu = g0 + t
            if u["kind"] == "G":
                S.wait(e, self.sem("gb"), gu)
            else:
                S.wait(e, self.sem("qk"), gu)
            if gu >= NPR:
                S.wait(e, self.sem("pv"), gu - NPR)
            ins = e.activation(out=b.pring[:, gu % NPR, :], in_=b.sring[:, gu % self.NSR, :], func=AF.Exp,
                               bias=u["bias"], scale=u["scale"])
            S.sig(ins, self.sem("p"), gu)
        self.gu["act"] = g0 + len(g["units"])

    def dve_group(self, e, g, consume):
        S, b = self.S, self.b
        g0 = self.gu["dve"]
        last = {}
        for t, u in enumerate(g["units"]):
            gu = g0 + t
            last[u["acc"]] = gu
            if u["kind"] == "G":
                S.wait(e, self.sem("qk"), gu)
                ins = e.scalar_tensor_tensor(out=b.sring[:, gu % self.NSR, :], in0=u["dD"], scalar=u["dcoef"],
                                             in1=b.sring[:, gu % self.NSR, :], op0=ALU.mult, op1=ALU.add)
                S.sig(ins, self.sem("gb"), gu)
        self.gu["dve"] = g0 + len(g["units"])
        for c in (0, 1):
            S.wait(e, self.sem("pv"), last[c])
            if g["gid"] >= 1:
                S.wait(e, self.sem("tp"), (g["gid"] - 1, c))
            ins = e.tensor_copy(out=b.oT[0:65, c, :], in_=b.oacc[0:65, g["par"] * 2 + c, :])
            S.sig(ins, self.sem("ev"), (g["gid"], c))
        for c in (0, 1):
            S.wait(e, self.sem("tp"), (g["gid"], c))
            consume(e, g, c)


class P2a(AttnPipe):
    NSR = 4

    def __init__(self, prog, tag, G, Qs, ctab, dn, kaug, qaug, identf, lamv, lamc, gsub, ya_d):
        super().__init__(prog, tag)
        self.G, self.Qs, self.ctab_d, self.dn_d = G, Qs, ctab, dn
        self.kaug, self.qaug, self.identf_d, self.lamv, self.lamc_d, self.gsub_d, self.ya_d = kaug, qaug, identf, lamv, lamc, gsub, ya_d
        self.ncol = cidx_map()[1]

    def alloc(self, nc, es):
        b = Bufs()
        b.KT = sb(nc, es, "a_KT", [100, 2, SEQ], BF16)
        b.V = sb(nc, es, "a_V", [128, 2, 129, 65], BF16)
        b.QA = sb(nc, es, "a_QA", [100, 2, NTOK], BF16)
        b.QB = sb(nc, es, "a_QB", [100, 2, NTOK], BF16)
        b.ctab = sb(nc, es, "a_ctab", [128, self.ncol], F32)
        b.dn = sb(nc, es, "a_dn", [128, 16, 512], F32)
        b.identf = sb(nc, es, "a_idf", [128, 128], F32)
        b.pring = sb(nc, es, "a_pr", [128, NPR, 512], BF16)
        b.oT = sb(nc, es, "a_oT", [65, 2, 512], F32)
        b.lv = sb(nc, es, "a_lv", [128, 4, 32], F32)
        b.lt = sb(nc, es, "a_lt", [128, 8], F32)
        b.lamc = sb(nc, es, "a_lamc", [128, 2], F32)
        b.gsub = sb(nc, es, "a_gsub", [128, 64], F32)
        b.r = sb(nc, es, "a_r", [128, 8], F32)
        b.t0 = sb(nc, es, "a_t0", [128, 4, 64], F32)
        b.y = sb(nc, es, "a_y", [128, 4, 64], F32)
        b.sq = sb(nc, es, "a_sq", [128, 4, 64], F32)
        b.sst = sb(nc, es, "a_sst", [128, 32, 6], F32)
        b.ya = sb(nc, es, "a_ya", [128, 32, 384], BF16)
        b.sring = ps(nc, es, "a_sr", [128, 4, 512], F32)
        b.oacc = ps(nc, es, "a_oacc", [128, 2, 512], F32)
        b.tps = ps(nc, es, "a_tps", [128, 4, 128], F32)
        return b

    def groups(self):
        b = self.b
        cm, _ = cidx_map()
        out = []
        gid = 0
        for h in range(6):
            sl = h % 2
            sp = float(SL_DIFF[h] / SC_A)
            for i in range(NQ):
                units = []
                tl = band_tiles(h, i)
                qs = slice(512 * i, 512 * i + 512)
                for n, j in enumerate(tl):
                    r, lt = j2idx(j)
                    ks = slice(r * 4096 + lt * 128, r * 4096 + lt * 128 + 128)
                    for comp in (0, 1):
                        base = 64 * comp
                        vi = (r * 32 + lt) * 65
                        u = {"acc": comp, "first": n == 0, "last": n == len(tl) - 1, "scale": SC_A,
                             "vl": b.V[:, sl, :, :].rearrange("p t d -> p (t d)")[:, vi:vi + 128]}
                        if j < 16 * i or j >= 16 * (i + 1):
                            Q = b.QA if j < 16 * i else b.QB
                            u["kind"] = "F"
                            u["lhsT"] = b.KT[base:base + 36, sl, ks]
                            u["rhs"] = Q[base:base + 36, sl, qs]
                            u["bias"] = b.ctab[:, cm[(h, i, j)]:cm[(h, i, j)] + 1]
                        else:
                            u["kind"] = "G"
                            u["lhsT"] = b.KT[base:base + 32, sl, ks]
                            u["rhs"] = b.QA[base:base + 32, sl, qs]
                            u["bias"] = b.ctab[:, 0:1]
                            u["dD"] = b.dn[:, j - 16 * i, :]
                            u["dcoef"] = -sp
                        units.append(u)
                out.append({"gid": gid, "par": 0, "units": units, "h": h, "i": i,
                            "acc_prev": gid - 1 if gid >= 1 else None})
                gid += 1
        return out

    def sp(self, e):
        S, b = self.S, self.b
        ld = self.sem("ldc")
        lst = [(b.ctab[:, :], self.ctab_d[:, :]), (b.dn[:, :, :], self.dn_d[:, :, :]),
               (b.identf[:, :], self.identf_d[:, :]), (b.lamc[:, :], self.lamc_d[:, :]),
               (b.gsub[:, :], self.gsub_d.partition_broadcast(128))]
        for n, (dst, src) in enumerate(lst):
            ins = e.dma_start(out=dst, in_=src)
            S.sig(ins, ld, ("c", n), dma=True)
        for k in range(4):
            ins = e.dma_start(out=b.lv[:, k, :], in_=self.lamv[k, :].partition_broadcast(128))
            S.sig(ins, ld, ("lv", k), dma=True)
        for h in range(6):
            sl = h % 2
            sem = self.sem("ld%d" % sl)
            if h >= 2:
                S.wait(e, self.sem("hd"), h - 2)
            for comp in (0, 1):
                rows = slice(h * 64 + comp * 32, h * 64 + comp * 32 + 32)
                base = 64 * comp
                ins = e.dma_start(out=b.KT[base:base + 32, sl, :].rearrange("p (r n) -> p r n", r=4),
                                  in_=self.G[h, :, :].rearrange("r (a n) -> a r n", n=NTOK)[comp * 32:(comp + 1) * 32])
                S.sig(ins, sem, ("k", h, comp), dma=True)
                ins = e.dma_start(out=b.KT[base + 32:base + 36, sl, :], in_=self.kaug[h, :, :])
                S.sig(ins, sem, ("ka", h, comp), dma=True)
                for var, Q in ((0, b.QA), (1, b.QB)):
                    ins = e.dma_start(out=Q[base:base + 32, sl, :], in_=self.Qs[rows, :])
                    S.sig(ins, sem, ("q", h, comp, var), dma=True)
                    ins = e.dma_start(out=Q[base + 32:base + 36, sl, :], in_=self.qaug[h, var, :, :])
                    S.sig(ins, sem, ("qa", h, comp, var), dma=True)
            for r in range(4):
                ins = e.dma_start(out=b.V[:, sl, r * 32:(r + 1) * 32, 0:64],
                                  in_=self.G[6 + h, r, :].rearrange("(p t d) -> p t d", p=128, t=32))
                S.sig(ins, sem, ("v", h, r), dma=True)

    def pe(self, e):
        S, b = self.S, self.b
        S.wait(e, self.sem("ldc"), ("lv", 3))
        S.wait(e, self.sem("ones"), 0)
        prev = None
        for g in self.groups():
            if g["i"] == 0:
                h = g["h"]
                g["wait_pe"] = lambda e, h=h: S.wait(e, self.sem("ld%d" % (h % 2)), ("v", h, 3))
            if prev is not None:
                self.pe_post(e, prev)
            self.pe_group_pairs(e, g)
            prev = g
        self.pe_post(e, prev)

    def act(self, e):
        S, b = self.S, self.b
        S.wait(e, self.sem("ldc"), ("lv", 3))
        S.wait(e, self.sem("lm"), "dots")
        ins = e.activation(out=b.lt[:, 2:4], in_=b.lt[:, 0:2], func=AF.Exp)
        S.sig(ins, self.sem("lma"), "exp")
        for g in self.groups():
            self.act_group(e, g)
        S.wait(e, self.sem("fin"), "v")
        ins = e.activation(out=b.sst[:, 0:4 * NQ, :], in_=b.sst[:, 0:4 * NQ, :], func=AF.Sqrt)
        S.sig(ins, self.sem("lma"), "sqrt")

    def consume(self, e, g, c):
        S, b = self.S, self.b
        f = self.sem("f")
        h, i = g["h"], g["i"]
        ins = e.reciprocal(out=b.r[:, 4 * c:4 * c + 4], in_=b.tps[:, :, 64])
        S.fence(e, ins, f)
        if c == 0:
            for s in range(4):
                ins = e.tensor_scalar(out=b.t0[:, s, :], in0=b.tps[:, s, 0:64], scalar1=b.r[:, s:s + 1], scalar2=None,
                                      op0=ALU.mult)
            S.sig(ins, self.sem("tpc"), (g["gid"], 0))
        else:
            ins = e.tensor_scalar(out=b.r[:, 4:8], in0=b.r[:, 4:8], scalar1=b.lt[:, 4:5], scalar2=None, op0=ALU.mult)
            S.fence(e, ins, f)
            for s in range(4):
                ins = e.scalar_tensor_tensor(out=b.y[:, s, :], in0=b.tps[:, s, 0:64], scalar=b.r[:, 4 + s:5 + s],
                                             in1=b.t0[:, s, :], op0=ALU.mult, op1=ALU.add)
            S.sig(ins, self.sem("tpc"), (g["gid"], 1))
            S.wait(e, self.sem("tpc"), (g["gid"], 1))
            ins = e.tensor_tensor(out=b.sq[:, :, :], in0=b.y[:, :, :], in1=b.y[:, :, :], op=ALU.mult)
            S.fence(e, ins, f)
            ins = e.tensor_reduce(out=b.sst[:, 4 * i:4 * i + 4, h], in_=b.sq[:, :, :], axis=AX.X, op=ALU.add)
            ins = e.tensor_copy(out=b.ya[:, 4 * i:4 * i + 4, h * 64:(h + 1) * 64], in_=b.y[:, :, :])
            S.fence(e, ins, f)
            if i == NQ - 1:
                S.sig(e.tensor_copy(out=b.lt[:, 7:8], in_=b.lt[:, 4:5]), self.sem("hd"), h)

    def dve(self, e):
        S, b = self.S, self.b
        f = self.sem("f")
        for sl in (0, 1):
            ins = e.memset(b.V[:, sl, 128, :], 0.0)
            ins = e.memset(b.V[:, sl, 0:128, 64:65], 1.0)
        S.sig(ins, self.sem("ones"), 0)
        S.wait(e, self.sem("ldc"), ("lv", 3))
        ins = e.tensor_tensor(out=b.lv[:, 0, :], in0=b.lv[:, 0, :], in1=b.lv[:, 1, :], op=ALU.mult)
        ins = e.tensor_tensor(out=b.lv[:, 2, :], in0=b.lv[:, 2, :], in1=b.lv[:, 3, :], op=ALU.mult)
        S.fence(e, ins, f)
        ins = e.tensor_reduce(out=b.lt[:, 0:1], in_=b.lv[:, 0, :], axis=AX.X, op=ALU.add)
        ins = e.tensor_reduce(out=b.lt[:, 1:2], in_=b.lv[:, 2, :], axis=AX.X, op=ALU.add)
        S.sig(ins, self.sem("lm"), "dots")
        S.wait(e, self.sem("lma"), "exp")
        ins = e.tensor_tensor(out=b.lt[:, 5:6], in0=b.lt[:, 3:4], in1=b.lt[:, 2:3], op=ALU.subtract)
        S.fence(e, ins, f)
        ins = e.tensor_tensor(out=b.lt[:, 4:5], in0=b.lt[:, 5:6], in1=b.lamc[:, 0:1], op=ALU.subtract)
        ins = e.tensor_scalar(out=b.gsub[:, :], in0=b.gsub[:, :], scalar1=b.lamc[:, 1:2], scalar2=None, op0=ALU.mult)
        S.fence(e, ins, f)
        for g in self.groups():
            self.dve_group(e, g, self.consume)
        ins = e.tensor_scalar(out=b.sst[:, 0:4 * NQ, :], in0=b.sst[:, 0:4 * NQ, :], scalar1=1.0 / 64, scalar2=EPS,
                              op0=ALU.mult, op1=ALU.add)
        S.sig(ins, self.sem("fin"), "v")
        S.wait(e, self.sem("lma"), "sqrt")
        ins = e.reciprocal(out=b.sst[:, 0:4 * NQ, :], in_=b.sst[:, 0:4 * NQ, :])
        S.fence(e, ins, f)
        for lt in range(4 * NQ):
            for h in range(6):
                ins = e.scalar_tensor_tensor(out=b.ya[:, lt, h * 64:(h + 1) * 64], in0=b.ya[:, lt, h * 64:(h + 1) * 64],
                                             scalar=b.sst[:, lt, h:h + 1], in1=b.gsub[:, :], op0=ALU.mult, op1=ALU.mult)
        S.sig(ins, self.sem("fin"), "ya")

    def pool(self, e):
        S, b = self.S, self.b
        S.wait(e, self.sem("fin"), "ya")
        ins = e.dma_start(out=self.ya_d[:, 0:4 * NQ, :], in_=b.ya[:, 0:4 * NQ, :])
        S.sig(ins, self.sem("st"), 0, dma=True)
        S.wait(e, self.sem("st"), 0)


class P2b(AttnPipe):
    def __init__(self, prog, tag, x, w_in, g_pre, G, ya_d, ln_g, ln_b, w_sp, b_sp, sinks, w_out, g_post,
                 dw, identb, identf, x1):
        super().__init__(prog, tag)
        self.x, self.w_in, self.g_pre, self.G, self.ya_d = x, w_in, g_pre, G, ya_d
        self.ln_g, self.ln_b, self.w_sp, self.b_sp, self.sinks, self.w_out, self.g_post = ln_g, ln_b, w_sp, b_sp, sinks, w_out, g_post
        self.dw_d, self.identb_d, self.identf_d, self.x1 = dw, identb, identf, x1
        self.nt = NormT(self, "nt")

    def reset(self):
        super().reset()
        self.m = {"pe": 0, "act": 0, "dve": 0}

    def alloc(self, nc, es):
        b = Bufs()
        b.wuv = sb(nc, es, "b_wuv", [128, 8, 512], BF16)
        b.wcq = sb(nc, es, "b_wcq", [128, 8, 3, 128], BF16)
        b.wout = sb(nc, es, "b_wout", [128, 8, D], BF16)
        b.dw = sb(nc, es, "b_dw", [128, 18, 512], F32)
        b.gbc = sb(nc, es, "b_gbc", [128, D], F32)
        b.gpost = sb(nc, es, "b_gpost", [128, D], F32)
        b.lng = sb(nc, es, "b_lng", [128, 256], F32)
        b.lnb = sb(nc, es, "b_lnb", [128, 256], F32)
        b.identb = sb(nc, es, "b_idb", [128, 128], BF16)
        b.identf = sb(nc, es, "b_idf", [128, 128], F32)
        b.wsp = sb(nc, es, "b_wsp", [128, 4, 128], F32)
        b.wsT = sb(nc, es, "b_wsT", [128, 4, 128], BF16)
        b.bsp = sb(nc, es, "b_bsp", [4, 128], F32)
        b.bsT = sb(nc, es, "b_bsT", [128, 4], F32)
        b.esink = sb(nc, es, "b_esink", [128, 6], F32)
        b.zero = sb(nc, es, "b_zero", [128, 1], F32)
        b.xt = sb(nc, es, "b_xt", [128, 2, 4, D], F32)
        b.hb = sb(nc, es, "b_hb", [128, 4, D], BF16)
        b.junk = sb(nc, es, "b_junk", [128, D], F32)
        b.ss = sb(nc, es, "b_ss", [128, 4], F32)
        b.rstd = sb(nc, es, "b_rstd", [128, 4], F32)
        b.hT = sb(nc, es, "b_hT", [128, 8, 512], BF16)
        b.cq = sb(nc, es, "b_cq", [128, 3, 512], BF16)
        b.ckx = sb(nc, es, "b_ckx", [128, 18 * 128], BF16)
        b.cv = sb(nc, es, "b_cv", [128, 19, 2, 65], BF16)
        b.pring = sb(nc, es, "b_pr", [128, NPR, 512], BF16)
        b.oT = sb(nc, es, "b_oT", [65, 2, 512], F32)
        b.r = sb(nc, es, "b_r", [128, 8], F32)
        b.y = sb(nc, es, "b_y", [128, 4, D], BF16)
        b.u = sb(nc, es, "b_u", [128, 4, 256], F32)
        b.vt = sb(nc, es, "b_vt", [128, 4, 256], F32)
        b.vnb = sb(nc, es, "b_vnb", [128, 4, 256], BF16)
        b.st = sb(nc, es, "b_st", [128, 16], F32)
        b.osb = sb(nc, es, "b_osb", [128, 4, D], F32)
        b.os2 = sb(nc, es, "b_os2", [128, 8], F32)
        b.pst = ps(nc, es, "b_pst", [128, 2, 1024], BF16)
        b.sring = ps(nc, es, "b_sr", [128, NSR, 512], F32)
        b.oacc = ps(nc, es, "b_oacc", [128, 2, 512], F32)
        b.tps = ps(nc, es, "b_tps", [128, 4, 128], F32)
        return b

    def kts(self, i):
        return [kt for kt in range(18) if not (kt == 0 and i == 0) and not (kt == 17 and i == 7)]

    def groups(self, i):
        b = self.b
        out = []
        kts = self.kts(i)
        for g in range(3):
            units = []
            for n, kt in enumerate(kts):
                for kv in (0, 1):
                    head = kv * 3 + g
                    units.append({"acc": kv, "first": n == 0, "last": n == len(kts) - 1, "scale": SC_W, "kind": "G",
                                  "lhsT": b.ckx[kv * 64:(kv + 1) * 64, kt * 128:(kt + 1) * 128],
                                  "rhs": b.cq[kv * 64:(kv + 1) * 64, g, :], "bias": b.zero[:, 0:1],
                                  "dD": b.dw[:, kt, :], "dcoef": -float(SL_WIN[head] / SC_W),
                                  "vl": b.cv[:, :, :, :].rearrange("p t k d -> p (t k d)")[:, (kt * 2 + kv) * 65:(kt * 2 + kv) * 65 + 128]})
            gid = 3 * i + g
            out.append({"gid": gid, "par": 0, "units": units, "g": g, "i": i, "acc_prev": gid - 1 if gid >= 1 else None})
        return out

    MSEM = {"cq": "mA", "uv": "mD", "sp": "mD", "op": "mA", "ws": "mA"}

    def misc_begin(self, e, kind):
        S = self.S
        m = self.m["pe"]
        self.mk.append(kind)
        if m >= 3:
            S.wait(e, self.sem(self.MSEM[self.mk[m - 3]]), ("rel", m - 3))
        self.m["pe"] = m + 1
        return self.b.sring[:, m % self.NSR, :], m

    def sp(self, e):
        S, b = self.S, self.b
        ld = self.sem("ldc")
        lst = [(b.dw[:, :, :], self.dw_d[:, :, :]), (b.identb[:, :], self.identb_d[:, :]), (b.identf[:, :], self.identf_d[:, :]),
               (b.gbc[:, :], self.g_pre.partition_broadcast(128)), (b.gpost[:, :], self.g_post.partition_broadcast(128)),
               (b.lng[:, :], self.ln_g.partition_broadcast(128)), (b.lnb[:, :], self.ln_b.partition_broadcast(128)),
               (b.esink[:, :], self.sinks.partition_broadcast(128)),
               (b.wsp[:, :, :], self.w_sp.rearrange("g t s -> t g s")), (b.bsp[:, :], self.b_sp[:, :])]
        for n, (dst, src) in enumerate(lst):
            ins = e.dma_start(out=dst, in_=src)
            S.sig(ins, ld, ("c", n), dma=True)
        for i in range(NQ):
            sl = i % 2
            if i >= 2:
                S.wait(e, self.sem("stx%d" % sl), i - 2)
            ins = e.dma_start(out=b.xt[:, sl, :, :], in_=self.x[i * 512:(i + 1) * 512, :].rearrange("(s p) d -> p s d", p=128))
            S.sig(ins, self.sem("ldx%d" % sl), i, dma=True)
            if i >= 1:
                S.wait(e, self.sem("tpc"), (3 * (i - 1) + 2, 1))
            lk = self.sem("ldk")
            for kt in self.kts(i):
                if 1 <= kt <= 16 and (kt - 1) % 4 != 0:
                    continue
                if kt == 0:
                    r, lt, n = j2idx(16 * i - 1) + (1,)
                elif kt == 17:
                    r, lt, n = j2idx(16 * (i + 1)) + (1,)
                else:
                    r, lt, n = (kt - 1) // 4, 4 * i, 4
                for kv in (0, 1):
                    ins = e.dma_start(out=b.ckx[kv * 64:(kv + 1) * 64, kt * 128:(kt + n) * 128],
                                      in_=self.G[12 + kv, r, :].rearrange("(a n) -> a n", n=NTOK)[:, lt * 128:(lt + n) * 128])
                    S.sig(ins, lk, ("k", i, kt, kv), dma=True)
                for kv in (0, 1):
                    ins = e.dma_start(out=b.cv[:, kt:kt + n, kv, 0:64],
                                      in_=self.G[14 + kv, r, :].rearrange("(p t d) -> p t d", p=128, t=32)[:, lt:lt + n, :])
                    S.sig(ins, lk, ("v", i, kt) if kv == 1 else ("v0", i, kt), dma=True)
            if i >= 1:
                S.wait(e, self.sem("ytr"), i - 1)
            ins = e.dma_start(out=b.y[:, :, 0:384], in_=self.ya_d[:, 4 * i:4 * i + 4, :])
            S.sig(ins, self.sem("ldy"), i, dma=True)

    def pool(self, e):
        S, b = self.S, self.b
        lw = self.sem("lw")
        ins = e.dma_start(out=b.wuv[:, :, :], in_=wview(self.w_in, 1152, 1664))
        S.sig(ins, lw, "uv", dma=True)
        for g in range(3):
            for kv in (0, 1):
                c0 = 1664 + (kv * 3 + g) * 64
                ins = e.dma_start(out=b.wcq[:, :, g, kv * 64:(kv + 1) * 64], in_=wview(self.w_in, c0, c0 + 64))
                S.sig(ins, lw, ("cq", g, kv), dma=True)
        ins = e.dma_start(out=b.wout[:, :, :], in_=wview(self.w_out, 0, D))
        S.sig(ins, lw, "out", dma=True)
        for i in range(NQ):
            sl = i % 2
            S.wait(e, self.sem("res"), i)
            ins = e.dma_start(out=self.x1[i * 512:(i + 1) * 512, :].rearrange("(s p) d -> p s d", p=128), in_=b.xt[:, sl, :, :])
            S.sig(ins, self.sem("stx%d" % sl), i, dma=True)
        for i in (NQ - 2, NQ - 1):
            if i >= 0:
                S.wait(e, self.sem("stx%d" % (i % 2)), i)

    def pe(self, e):
        S, b = self.S, self.b
        self.mk = []
        S.wait(e, self.sem("ldc"), ("c", 9))
        S.wait(e, self.sem("lw"), "out")
        for g4 in range(4):
            if g4 >= 1:
                S.wait(e, self.sem("mA"), ("ws", g4 - 1))
            ins = e.transpose(b.tps[:, 0, :], b.wsp[:, g4, :], b.identf[:, :])
            S.sig(ins, self.sem("wst"), g4)
        S.wait(e, self.sem("mA"), ("ws", 3))
        ins = e.transpose(b.tps[:, 1, 0:4], b.bsp[0:4, :], b.identf[0:4, 0:4])
        S.sig(ins, self.sem("wst"), 4)
        S.wait(e, self.sem("ones"), 0)
        for i in range(NQ):
            self.nt.pe(e, i, b.hb, b.pst, b.identb)
            self.nt.wait_hT(e, i)
            for g in range(3):
                bank, m = self.misc_begin(e, "cq")
                for kc in range(8):
                    ins = e.matmul(bank, lhsT=b.wcq[:, kc, g, :], rhs=b.hT[:, kc, :], start=(kc == 0), stop=(kc == 7))
                S.sig(ins, self.sem("mm"), m)
            for s in range(4):
                bank, m = self.misc_begin(e, "uv")
                for kc in range(8):
                    ins = e.matmul(bank, lhsT=b.hT[:, kc, s * 128:(s + 1) * 128], rhs=b.wuv[:, kc, :], start=(kc == 0), stop=(kc == 7))
                S.sig(ins, self.sem("mm"), m)
            for s in range(4):
                bank, m = self.misc_begin(e, "sp")
                S.wait(e, self.sem("vn"), (i, s))
                for g4 in range(4):
                    ins = e.matmul(bank[:, g4 * 64:(g4 + 1) * 64], lhsT=b.wsT[:, g4, :], rhs=b.vnb[:, s, g4 * 64:(g4 + 1) * 64],
                                   start=True, stop=True)
                S.sig(ins, self.sem("mm"), m)
            mlast = self.m["pe"] - 1
            for gi, g in enumerate(self.groups(i)):
                if gi == 0:
                    def w0(e, mlast=mlast, i=i):
                        for mm_ in range(max(0, mlast - 2), mlast + 1):
                            S.wait(e, self.sem(self.MSEM[self.mk[mm_]]), ("rel", mm_))
                        S.wait(e, self.sem("ldk"), ("v", i, self.kts(i)[-1] if self.kts(i)[-1] == 17 else 13))
                        S.wait(e, self.sem("mA"), ("rel", mlast - 8))
                    g["wait_pe"] = w0
                self.pe_group(e, g)
                self.pe_post(e, g)
            S.wait(e, self.sem("ldy"), i)
            S.wait(e, self.sem("gate"), (i, 3))
            S.wait(e, self.sem("tpc"), (3 * i + 2, 1))
            for kc in range(8):
                gq = (NQ + i) * 8 + kc
                S.wait(e, self.nt.n + "_ev", gq - 2 if kc >= 2 else i * 8 + 6 + kc)
                for s in range(4):
                    ins = e.transpose(b.pst[:, gq % 2, s * 128:(s + 1) * 128], b.y[:, s, kc * 128:(kc + 1) * 128], b.identb[:, :])
                S.sig(ins, self.nt.n + "_tr", gq)
            S.wait(e, self.nt.n + "_ev", (NQ + i) * 8 + 7)
            S.wait(e, self.sem("p"), self.gu["pe"] - 1)
            for s in range(4):
                for half in range(2):
                    bank, m = self.misc_begin(e, "op")
                    for kc in range(8):
                        ins = e.matmul(bank, lhsT=b.hT[:, kc, s * 128:(s + 1) * 128], rhs=b.wout[:, kc, half * 512:(half + 1) * 512],
                                       start=(kc == 0), stop=(kc == 7))
                    S.sig(ins, self.sem("mm"), m)
            S.sig(e.transpose(b.pst[:, 0, 0:128], b.identb[:, :], b.identb[:, :]), self.sem("ytr"), i)

    def act(self, e):
        S, b = self.S, self.b
        S.wait(e, self.sem("ldc"), ("c", 9))
        ins = e.activation(out=b.esink[:, :], in_=b.esink[:, :], func=AF.Exp)
        S.sig(ins, self.sem("es"), 0)
        for g4 in range(4):
            S.wait(e, self.sem("wst"), g4)
            ins = e.activation(out=b.wsT[:, g4, :], in_=b.tps[:, 0, :], func=AF.Copy)
            S.sig(ins, self.sem("mA"), ("ws", g4))
        S.wait(e, self.sem("wst"), 4)
        ins = e.activation(out=b.bsT[:, :], in_=b.tps[:, 1, 0:4], func=AF.Copy)
        S.sig(ins, self.sem("es"), 1)
        m = 0
        for i in range(NQ):
            self.nt.act(e, i, b.pst, b.hT,
                        wait_hT_free=lambda: (S.wait(e, self.sem("mm"), self.m_last_op) if i >= 1 else None), rstd=b.rstd)
            for g in range(3):
                S.wait(e, self.sem("mm"), m)
                if i >= 1 and g == 0:
                    S.wait(e, self.sem("pv"), self.gu["act"] - 1)
                ins = e.activation(out=b.cq[:, g, :], in_=b.sring[:, m % self.NSR, :], func=AF.Copy)
                S.sig(ins, self.sem("mA"), ("rel", m))
                m += 1
            m += 8
            S.wait(e, self.sem("lnv"), i)
            ins = e.activation(out=b.st[:, 8:12], in_=b.st[:, 8:12], func=AF.Sqrt)
            S.sig(ins, self.sem("lnq"), i)
            for g in self.groups(i):
                self.act_group(e, g)
            for kc in range(8):
                gq = (NQ + i) * 8 + kc
                S.wait(e, self.nt.n + "_tr", gq)
                ins = e.activation(out=b.hT[:, kc, :], in_=b.pst[:, gq % 2, 0:512], func=AF.Copy)
                S.sig(ins, self.nt.n + "_ev", gq)
            for s in range(4):
                for half in range(2):
                    S.wait(e, self.sem("mm"), m)
                    if s == 0 and half == 0 and i >= 1:
                        S.wait(e, self.sem("res"), i - 1)
                    ins = e.activation(out=b.osb[:, s, half * 512:(half + 1) * 512], in_=b.sring[:, m % self.NSR, :], func=AF.Copy)
                    S.sig(ins, self.sem("mA"), ("rel", m))
                    self.m_last_op = m
                    m += 1
            S.wait(e, self.sem("onv"), i)
            ins = e.activation(out=b.os2[:, 4:8], in_=b.os2[:, 4:8], func=AF.Sqrt)
            S.sig(ins, self.sem("onq"), i)

    def consume(self, e, g, c):
        S, b = self.S, self.b
        f = self.sem("f")
        head = c * 3 + g["g"]
        ins = e.tensor_scalar(out=b.r[:, 0:4], in0=b.tps[:, :, 64], scalar1=b.esink[:, head:head + 1], scalar2=None, op0=ALU.add)
        S.fence(e, ins, f)
        ins = e.reciprocal(out=b.r[:, 0:4], in_=b.r[:, 0:4])
        S.fence(e, ins, f)
        for s in range(4):
            ins = e.tensor_scalar(out=b.y[:, s, 640 + head * 64:640 + (head + 1) * 64], in0=b.tps[:, s, 0:64],
                                  scalar1=b.r[:, s:s + 1], scalar2=None, op0=ALU.mult)
        S.sig(ins, self.sem("tpc"), (g["gid"], c))

    def dve(self, e):
        S, b = self.S, self.b
        f = self.sem("f")
        ins = e.memset(b.cv[:, 18, :, :], 0.0)
        ins = e.memset(b.cv[:, 0:18, :, 64:65], 1.0)
        ins = e.memset(b.zero[:, :], 0.0)
        S.sig(ins, self.sem("ones"), 0)
        S.wait(e, self.sem("ldc"), ("c", 9))
        S.wait(e, self.sem("es"), 1)
        m = 0
        for i in range(NQ):
            sl = i % 2
            xt = b.xt[:, sl, :, :]
            self.nt.dve(e, i, xt, b.gbc, b.hb, b.junk, b.ss, b.rstd,
                        wait_x=lambda: S.wait(e, self.sem("ldx%d" % sl), i),
                        wait_hb_free=lambda: (self.nt.wait_tr_done(e, i - 1) if i >= 1 else None))
            m += 3
            for s in range(4):
                S.wait(e, self.sem("mm"), m)
                if i >= 1 and s == 0:
                    S.wait(e, self.sem("gate"), (i - 1, 3))
                ins = e.tensor_copy(out=b.u[:, s, :], in_=b.sring[:, m % self.NSR, 0:256])
                ins = e.tensor_copy(out=b.vt[:, s, :], in_=b.sring[:, m % self.NSR, 256:512])
                S.sig(ins, self.sem("mD"), ("rel", m))
                S.wait(e, self.sem("mD"), ("rel", m))
                ins = e.tensor_reduce(out=b.st[:, s:s + 1], in_=b.vt[:, s, :], axis=AX.X, op=ALU.add)
                ins = e.tensor_tensor(out=b.junk[:, 0:256], in0=b.vt[:, s, :], in1=b.vt[:, s, :], op=ALU.mult)
                S.fence(e, ins, f)
                ins = e.tensor_reduce(out=b.st[:, 4 + s:5 + s], in_=b.junk[:, 0:256], axis=AX.X, op=ALU.add)
                S.fence(e, ins, f)
                m += 1
            ins = e.tensor_scalar(out=b.st[:, 0:4], in0=b.st[:, 0:4], scalar1=1.0 / 256, scalar2=None, op0=ALU.mult)
            S.fence(e, ins, f)
            ins = e.tensor_tensor(out=b.st[:, 12:16], in0=b.st[:, 0:4], in1=b.st[:, 0:4], op=ALU.mult)
            S.fence(e, ins, f)
            ins = e.scalar_tensor_tensor(out=b.st[:, 8:12], in0=b.st[:, 4:8], scalar=1.0 / 256, in1=b.st[:, 12:16],
                                         op0=ALU.mult, op1=ALU.subtract)
            S.fence(e, ins, f)
            ins = e.tensor_scalar(out=b.st[:, 8:12], in0=b.st[:, 8:12], scalar1=EPS, scalar2=None, op0=ALU.add)
            S.sig(ins, self.sem("lnv"), i)
            S.wait(e, self.sem("lnq"), i)
            ins = e.reciprocal(out=b.st[:, 8:12], in_=b.st[:, 8:12])
            S.fence(e, ins, f)
            for s in range(4):
                ins = e.tensor_scalar(out=b.vt[:, s, :], in0=b.vt[:, s, :], scalar1=b.st[:, s:s + 1], scalar2=b.st[:, 8 + s:9 + s],
                                      op0=ALU.subtract, op1=ALU.mult)
                S.fence(e, ins, f)
                ins = e.tensor_tensor(out=b.vt[:, s, :], in0=b.vt[:, s, :], in1=b.lng[:, :], op=ALU.mult)
                S.fence(e, ins, f)
                ins = e.tensor_tensor(out=b.vnb[:, s, :], in0=b.vt[:, s, :], in1=b.lnb[:, :], op=ALU.add)
                S.sig(ins, self.sem("vn"), (i, s))
            for s in range(4):
                S.wait(e, self.sem("mm"), m)
                if i >= 1 and s == 0:
                    S.wait(e, self.sem("ytr"), i - 1)
                for g4 in range(4):
                    ins = e.scalar_tensor_tensor(out=b.y[:, s, 384 + g4 * 64:384 + (g4 + 1) * 64],
                                                 in0=b.sring[:, m % self.NSR, g4 * 64:(g4 + 1) * 64], scalar=b.bsT[:, g4:g4 + 1],
                                                 in1=b.u[:, s, g4 * 64:(g4 + 1) * 64], op0=ALU.add, op1=ALU.mult)
                S.sig(ins, self.sem("mD"), ("rel", m))
                S.wait(e, self.sem("mD"), ("rel", m))
                S.sig(e.tensor_copy(out=b.r[:, 4:5], in_=b.zero[:, 0:1]), self.sem("gate"), (i, s))
                m += 1
            for g in self.groups(i):
                self.dve_group(e, g, self.consume)
            for s in range(4):
                S.wait(e, self.sem("mA"), ("rel", m + 1))
                ins = e.tensor_tensor(out=b.junk[:, :], in0=b.osb[:, s, :], in1=b.osb[:, s, :], op=ALU.mult)
                S.fence(e, ins, f)
                ins = e.tensor_reduce(out=b.os2[:, s:s + 1], in_=b.junk[:, :], axis=AX.X, op=ALU.add)
                S.fence(e, ins, f)
                m += 2
            ins = e.tensor_scalar(out=b.os2[:, 4:8], in0=b.os2[:, 0:4], scalar1=1.0 / D, scalar2=EPS, op0=ALU.mult, op1=ALU.add)
            S.sig(ins, self.sem("onv"), i)
            S.wait(e, self.sem("onq"), i)
            ins = e.reciprocal(out=b.os2[:, 4:8], in_=b.os2[:, 4:8])
            S.fence(e, ins, f)
            for s in range(4):
                ins = e.tensor_tensor(out=b.osb[:, s, :], in0=b.osb[:, s, :], in1=b.gpost[:, :], op=ALU.mult)
                S.fence(e, ins, f)
                ins = e.scalar_tensor_tensor(out=xt[:, s, :], in0=b.osb[:, s, :], scalar=b.os2[:, 4 + s:5 + s], in1=xt[:, s, :],
                                             op0=ALU.mult, op1=ALU.add)
            S.sig(ins, self.sem("res"), i)
            S.wait(e, self.sem("res"), i)


NMR = 6


class TokPhase(Phase):
    def reset(self):
        self.m = 0
        self.mk = []

    def ring(self, e, rel_sem):
        S = self.S
        m = self.m
        self.mk.append(rel_sem)
        if m >= NMR:
            S.wait(e, self.sem(self.mk[m - NMR]), ("rel", m - NMR))
        self.m = m + 1
        return self.b.ring[:, m % NMR, :], m

    def dve_postnorm(self, e, i, s, xt):
        S, b = self.S, self.b
        f = self.sem("f")
        o = b.osb[:, s % 2, :]
        ins = e.tensor_tensor(out=b.junk[:, :], in0=o, in1=o, op=ALU.mult)
        S.fence(e, ins, f)
        ins = e.tensor_reduce(out=b.os2[:, 0:1], in_=b.junk[:, :], axis=AX.X, op=ALU.add)
        S.fence(e, ins, f)
        ins = e.tensor_scalar(out=b.os2[:, 1:2], in0=b.os2[:, 0:1], scalar1=1.0 / D, scalar2=EPS, op0=ALU.mult, op1=ALU.add)
        S.sig(ins, self.sem("onv"), (i, s))
        S.wait(e, self.sem("onq"), (i, s))
        ins = e.reciprocal(out=b.os2[:, 1:2], in_=b.os2[:, 1:2])
        ins2 = e.tensor_tensor(out=o, in0=o, in1=b.gpost[:, :], op=ALU.mult)
        S.fence(e, ins2, f)
        ins = e.scalar_tensor_tensor(out=xt[:, s, :], in0=o, scalar=b.os2[:, 1:2], in1=xt[:, s, :], op0=ALU.mult, op1=ALU.add)
        S.sig(ins, self.sem("res"), (i, s))
        S.wait(e, self.sem("res"), (i, s))

    def act_postnorm(self, e, i, s):
        S, b = self.S, self.b
        S.wait(e, self.sem("onv"), (i, s))
        ins = e.activation(out=b.os2[:, 1:2], in_=b.os2[:, 1:2], func=AF.Sqrt)
        S.sig(ins, self.sem("onq"), (i, s))

    def pool_store(self, e, xo):
        S, b = self.S, self.b
        for i in range(NQ):
            S.wait(e, self.sem("res"), (i, 3))
            ins = e.dma_start(out=xo[i * 512:(i + 1) * 512, :].rearrange("(s p) d -> p s d", p=128), in_=b.xt[:, :, :])
            S.sig(ins, self.sem("stx"), i, dma=True)
        S.wait(e, self.sem("stx"), NQ - 1)

    def sp_loadx(self, e, i, xin):
        S, b = self.S, self.b
        if i >= 1:
            S.wait(e, self.sem("stx"), i - 1)
        ins = e.dma_start(out=b.xt[:, :, :], in_=xin[i * 512:(i + 1) * 512, :].rearrange("(s p) d -> p s d", p=128))
        S.sig(ins, self.sem("ldx"), i, dma=True)


class P3a(TokPhase):
    NBLK = DFF // 128

    def __init__(self, prog, tag, x1, w_fi, w_fo, g_pre, g_post, identb, x2):
        super().__init__(prog, tag)
        self.x1, self.w_fi, self.w_fo, self.g_pre, self.g_post, self.identb_d, self.x2 = x1, w_fi, w_fo, g_pre, g_post, identb, x2
        self.nt = NormT(self, "nt")

    def alloc(self, nc, es):
        b = Bufs()
        b.wfi = sb(nc, es, "f_wfi", [128, 8, 2 * DFF], BF16)
        b.wfo = sb(nc, es, "f_wfo", [128, self.NBLK, D], BF16)
        b.gbc = sb(nc, es, "f_gbc", [128, D], F32)
        b.gpost = sb(nc, es, "f_gpost", [128, D], F32)
        b.identb = sb(nc, es, "f_idb", [128, 128], BF16)
        b.xt = sb(nc, es, "f_xt", [128, 4, D], F32)
        b.hb = sb(nc, es, "f_hb", [128, 4, D], BF16)
        b.ss = sb(nc, es, "f_ss", [128, 4], F32)
        b.rstd = sb(nc, es, "f_rstd", [128, 4], F32)
        b.hT = sb(nc, es, "f_hT", [128, 8, 512], BF16)
        b.actT = sb(nc, es, "f_actT", [128, self.NBLK, 512], BF16)
        b.sg = sb(nc, es, "f_sg", [128, 2, 512], F32)
        b.junk = b.sg[:, :, :].rearrange("p a b -> p (a b)")
        b.osb = sb(nc, es, "f_osb", [128, 2, D], F32)
        b.os2 = sb(nc, es, "f_os2", [128, 4], F32)
        b.pst = ps(nc, es, "f_pst", [128, 2, 1024], BF16)
        b.ring = ps(nc, es, "f_ring", [128, NMR, 512], F32)
        return b

    def sp(self, e):
        S, b = self.S, self.b
        ld = self.sem("ldc")
        for n, (dst, src) in enumerate([(b.identb[:, :], self.identb_d[:, :]), (b.gbc[:, :], self.g_pre.partition_broadcast(128)),
                                        (b.gpost[:, :], self.g_post.partition_broadcast(128))]):
            ins = e.dma_start(out=dst, in_=src)
            S.sig(ins, ld, ("c", n), dma=True)
        for i in range(NQ):
            self.sp_loadx(e, i, self.x1)

    def pool(self, e):
        S, b = self.S, self.b
        lw = self.sem("lw")
        nch = 8
        w = 2 * DFF // nch
        for k in range(nch):
            ins = e.dma_start(out=b.wfi[:, :, k * w:(k + 1) * w], in_=wview(self.w_fi, k * w, (k + 1) * w))
            S.sig(ins, lw, ("fi", k), dma=True)
        for k in range(2):
            ins = e.dma_start(out=b.wfo[:, :, k * 512:(k + 1) * 512], in_=wview(self.w_fo, k * 512, (k + 1) * 512))
            S.sig(ins, lw, ("fo", k), dma=True)
        self.pool_store(e, self.x2)

    def pe(self, e):
        S, b = self.S, self.b
        S.wait(e, self.sem("ldc"), ("c", 2))
        S.wait(e, self.sem("lw"), ("fo", 1))
        for i in range(NQ):
            self.nt.pe(e, i, b.hb, b.pst, b.identb)
            self.nt.wait_hT(e, i)
            for blk in range(self.NBLK):
                for part, rel in ((0, "mA"), (1, "mD")):
                    bank, m = self.ring(e, rel)
                    c0 = part * DFF + blk * 128
                    for kc in range(8):
                        ins = e.matmul(bank, lhsT=b.wfi[:, kc, c0:c0 + 128], rhs=b.hT[:, kc, :], start=(kc == 0), stop=(kc == 7))
                    S.sig(ins, self.sem("mm"), m)
            S.wait(e, self.sem("mD"), ("rel", self.m - 1))
            for s in range(4):
                for half in range(2):
                    bank, m = self.ring(e, "mA")
                    for blk in range(self.NBLK):
                        ins = e.matmul(bank, lhsT=b.actT[:, blk, s * 128:(s + 1) * 128], rhs=b.wfo[:, blk, half * 512:(half + 1) * 512],
                                       start=(blk == 0), stop=(blk == self.NBLK - 1))
                    S.sig(ins, self.sem("mm"), m)

    def act(self, e):
        S, b = self.S, self.b
        m = 0
        nb = 0
        for i in range(NQ):
            self.nt.act(e, i, b.pst, b.hT,
                        wait_hT_free=lambda: (S.wait(e, self.sem("mm"), m - 9) if i >= 1 else None), rstd=b.rstd)
            for blk in range(self.NBLK):
                S.wait(e, self.sem("mm"), m)
                if nb >= 2:
                    S.wait(e, self.sem("mD"), ("rel", self.sgrel[nb - 2]))
                ins = e.activation(out=b.sg[:, nb % 2, :], in_=b.ring[:, m % NMR, :], func=AF.Silu)
                S.sig(ins, self.sem("mA"), ("rel", m))
                self.sgrel.append(m + 1)
                nb += 1
                m += 2
            for s in range(4):
                for half in range(2):
                    S.wait(e, self.sem("mm"), m)
                    if half == 0 and (i, s) >= (0, 2):
                        ps_ = (i, s - 2) if s >= 2 else (i - 1, s + 2)
                        S.wait(e, self.sem("res"), ps_)
                    ins = e.activation(out=b.osb[:, s % 2, half * 512:(half + 1) * 512], in_=b.ring[:, m % NMR, :], func=AF.Copy)
                    S.sig(ins, self.sem("mA"), ("rel", m))
                    m += 1
                self.act_postnorm(e, i, s)

    def reset(self):
        super().reset()
        self.sgrel = []

    def dve(self, e):
        S, b = self.S, self.b
        S.wait(e, self.sem("ldc"), ("c", 2))
        m = 0
        nb = 0
        for i in range(NQ):
            self.nt.dve(e, i, b.xt, b.gbc, b.hb, b.junk, b.ss, b.rstd,
                        wait_x=lambda: S.wait(e, self.sem("ldx"), i),
                        wait_hb_free=lambda: (self.nt.wait_tr_done(e, i - 1) if i >= 1 else None))
            for blk in range(self.NBLK):
                S.wait(e, self.sem("mm"), m + 1)
                S.wait(e, self.sem("mA"), ("rel", m))
                if i >= 1 and blk == 0:
                    S.wait(e, self.sem("mm"), m - 1)
                ins = e.tensor_tensor(out=b.actT[:, blk, :], in0=b.sg[:, nb % 2, :], in1=b.ring[:, (m + 1) % NMR, :], op=ALU.mult)
                S.sig(ins, self.sem("mD"), ("rel", m + 1))
                nb += 1
                m += 2
            for s in range(4):
                S.wait(e, self.sem("mA"), ("rel", m + 1))
                self.dve_postnorm(e, i, s, b.xt)
                m += 2


class P3b(TokPhase):
    def __init__(self, prog, tag, x2, p, w_up, w_gate, g_gate, g_post, identb, x3):
        super().__init__(prog, tag)
        self.x2, self.p, self.w_up, self.w_gate, self.g_gate, self.g_post, self.identb_d, self.x3 = x2, p, w_up, w_gate, g_gate, g_post, identb, x3
        self.nt = NormT(self, "nt")

    def alloc(self, nc, es):
        b = Bufs()
        b.wg = sb(nc, es, "e_wg", [128, 8, D], BF16)
        b.wu = sb(nc, es, "e_wu", [128, 2, D], BF16)
        b.gbc = sb(nc, es, "e_gbc", [128, D], F32)
        b.gpost = sb(nc, es, "e_gpost", [128, D], F32)
        b.identb = sb(nc, es, "e_idb", [128, 128], BF16)
        b.xt = sb(nc, es, "e_xt", [128, 4, D], F32)
        b.pt = sb(nc, es, "e_pt", [128, 4, 256], F32)
        b.pb = sb(nc, es, "e_pb", [128, 4, 256], BF16)
        b.pT = sb(nc, es, "e_pT", [128, 2, 512], BF16)
        b.hb = sb(nc, es, "e_hb", [128, 4, D], BF16)
        b.junk = sb(nc, es, "e_junk", [128, D], F32)
        b.ss = sb(nc, es, "e_ss", [128, 4], F32)
        b.rstd = sb(nc, es, "e_rstd", [128, 4], F32)
        b.hT = sb(nc, es, "e_hT", [128, 8, 512], BF16)
        b.sgm = sb(nc, es, "e_sgm", [128, 2, D], F32)
        b.osb = sb(nc, es, "e_osb", [128, 2, D], F32)
        b.os2 = sb(nc, es, "e_os2", [128, 4], F32)
        b.pst = ps(nc, es, "e_pst", [128, 2, 1024], BF16)
        b.ring = ps(nc, es, "e_ring", [128, NMR, 512], F32)
        return b

    def sp(self, e):
        S, b = self.S, self.b
        ld = self.sem("ldc")
        for n, (dst, src) in enumerate([(b.identb[:, :], self.identb_d[:, :]), (b.gbc[:, :], self.g_gate.partition_broadcast(128)),
                                        (b.gpost[:, :], self.g_post.partition_broadcast(128))]):
            ins = e.dma_start(out=dst, in_=src)
            S.sig(ins, ld, ("c", n), dma=True)
        for i in range(NQ):
            self.sp_loadx(e, i, self.x2)
            if i >= 1:
                S.wait(e, self.sem("pb"), i - 1)
            ins = e.dma_start(out=b.pt[:, :, :], in_=self.p[i * 512:(i + 1) * 512, :].rearrange("(s p) d -> p s d", p=128))
            S.sig(ins, self.sem("ldp"), i, dma=True)

    def pool(self, e):
        S, b = self.S, self.b
        lw = self.sem("lw")
        for k in range(2):
            ins = e.dma_start(out=b.wg[:, :, k * 512:(k + 1) * 512], in_=wview(self.w_gate, k * 512, (k + 1) * 512))
            S.sig(ins, lw, ("g", k), dma=True)
        ins = e.dma_start(out=b.wu[:, :, :], in_=wview(self.w_up, 0, D))
        S.sig(ins, lw, "u", dma=True)
        self.pool_store(e, self.x3)

    def pe(self, e):
        S, b = self.S, self.b
        S.wait(e, self.sem("ldc"), ("c", 2))
        S.wait(e, self.sem("lw"), "u")
        for i in range(NQ):
            self.nt.pe(e, i, b.hb, b.pst, b.identb)
            S.wait(e, self.sem("pb"), i)
            for kc in range(2):
                gq = (NQ + i) * 8 + kc
                S.wait(e, self.nt.n + "_ev", i * 8 + 6 + kc)
                for s in range(4):
                    ins = e.transpose(b.pst[:, gq % 2, s * 128:(s + 1) * 128], b.pb[:, s, kc * 128:(kc + 1) * 128], b.identb[:, :])
                S.sig(ins, self.nt.n + "_tr", gq)
            S.wait(e, self.nt.n + "_ev", (NQ + i) * 8 + 1)
            for s in range(4):
                for half in range(2):
                    bank, m = self.ring(e, "mD")
                    for kc in range(2):
                        ins = e.matmul(bank, lhsT=b.pT[:, kc, s * 128:(s + 1) * 128], rhs=b.wu[:, kc, half * 512:(half + 1) * 512],
                                       start=(kc == 0), stop=(kc == 1))
                    S.sig(ins, self.sem("mm"), m)
                for half in range(2):
                    bank, m = self.ring(e, "mA")
                    for kc in range(8):
                        ins = e.matmul(bank, lhsT=b.hT[:, kc, s * 128:(s + 1) * 128], rhs=b.wg[:, kc, half * 512:(half + 1) * 512],
                                       start=(kc == 0), stop=(kc == 7))
                    S.sig(ins, self.sem("mm"), m)

    def act(self, e):
        S, b = self.S, self.b
        m = 0
        for i in range(NQ):
            self.nt.act(e, i, b.pst, b.hT,
                        wait_hT_free=lambda: (S.wait(e, self.sem("mm"), m - 1) if i >= 1 else None), rstd=b.rstd)
            for kc in range(2):
                gq = (NQ + i) * 8 + kc
                S.wait(e, self.nt.n + "_tr", gq)
                if i >= 1 and kc == 0:
                    S.wait(e, self.sem("mm"), m - 3)
                ins = e.activation(out=b.pT[:, kc, :], in_=b.pst[:, gq % 2, 0:512], func=AF.Copy)
                S.sig(ins, self.nt.n + "_ev", gq)
            for s in range(4):
                for half in range(2):
                    mz = m + 2 + half
                    S.wait(e, self.sem("mm"), mz)
                    if half == 0 and (i, s) >= (0, 2):
                        ps_ = (i, s - 2) if s >= 2 else (i - 1, s + 2)
                        S.wait(e, self.sem("eg"), ps_)
                    ins = e.activation(out=b.sgm[:, s % 2, half * 512:(half + 1) * 512], in_=b.ring[:, mz % NMR, :], func=AF.Sigmoid)
                    S.sig(ins, self.sem("mA"), ("rel", mz))
                m += 4
                self.act_postnorm(e, i, s)

    def dve(self, e):
        S, b = self.S, self.b
        S.wait(e, self.sem("ldc"), ("c", 2))
        m = 0
        for i in range(NQ):
            self.nt.dve(e, i, b.xt, b.gbc, b.hb, b.junk, b.ss, b.rstd,
                        wait_x=lambda: S.wait(e, self.sem("ldx"), i),
                        wait_hb_free=lambda: (self.nt.wait_tr_done(e, i - 1) if i >= 1 else None))
            S.wait(e, self.sem("ldp"), i)
            if i >= 1:
                S.wait(e, self.nt.n + "_tr", (NQ + i - 1) * 8 + 1)
            ins = e.tensor_copy(out=b.pb[:, :, :], in_=b.pt[:, :, :])
            S.sig(ins, self.sem("pb"), i)
            for s in range(4):
                for half in range(2):
                    S.wait(e, self.sem("mm"), m + half)
                    S.wait(e, self.sem("mA"), ("rel", m + 2 + half))
                    ins = e.tensor_tensor(out=b.osb[:, s % 2, half * 512:(half + 1) * 512], in0=b.sgm[:, s % 2, half * 512:(half + 1) * 512],
                                          in1=b.ring[:, (m + half) % NMR, :], op=ALU.mult)
                    S.sig(ins, self.sem("mD"), ("rel", m + half))
                S.wait(e, self.sem("mD"), ("rel", m + 1))
                S.sig(e.memset(b.os2[:, 2:3], 0.0), self.sem("eg"), (i, s))
                m += 4
                self.dve_postnorm(e, i, s, b.xt)


class AG(Phase):
    def __init__(self, prog, tag, E, G):
        super().__init__(prog, tag)
        self.E, self.G = E, G

    def pool(self, e):
        S = self.S
        for k in range(NCH):
            ins = e.collective_compute("AllGather", ALU.bypass, replica_groups=[[0, 1, 2, 3], [4, 5, 6, 7]],
                                       ins=[self.E[k, :].rearrange("(p c) -> p c", c=2048)],
                                       outs=[self.G[k, :, :].rearrange("r (p c) -> (r p) c", c=2048)])
            S.sig(ins, "cc", k)
        S.wait(e, "cc", NCH - 1)


def _local_tokens(c):
    ii = np.arange(8)[:, None]
    t = np.arange(512)[None, :]
    return (2048 * ii + 512 * c + t).reshape(-1)


WKEYS = [("g_pre_mix", [D]), ("w_in", [D, 2304]), ("g_diff_sub", [64]), ("gmlp_ln_g", [256]), ("gmlp_ln_b", [256]),
         ("w_spatial", [4, 128, 128]), ("b_spatial", [4, 128]), ("swa_sinks", [6]), ("w_out", [D, D]), ("g_post_mix", [D]),
         ("g_pre_ffn", [D]), ("w_ffn_in", [D, 2 * DFF]), ("w_ffn_out", [DFF, D]), ("g_post_ffn", [D]),
         ("w_ple_up", [256, D]), ("w_ple_gate", [D, D]), ("g_ple_gate", [D]), ("g_ple_post", [D])]


def build_fused(nl=L):
    P = Prog()
    ncol = cidx_map()[1]
    x = P.din("x", [NTOK, D], F32)
    pp = P.din("p", [L, NTOK, 256], F32)
    ctab = P.din("ctab", [128, ncol], F32)
    dn = P.din("dn", [128, 16, 512], F32)
    dw = P.din("dw", [128, 18, 512], F32)
    kaug = P.din("kaug", [6, 4, SEQ], BF16)
    qaug = P.din("qaug", [6, 2, 4, NTOK], BF16)
    identb = P.din("identb", [128, 128], BF16)
    identf = P.din("identf", [128, 128], F32)
    lamv = P.din("lamv", [L, 4, 32], F32)
    lamc = P.din("lamc", [L, 128, 2], F32)
    W = {k: P.din(k, [L] + shp, F32) for k, shp in WKEYS}
    E = P.dint("E", [NCH, CHE], BF16)
    G = P.dint("G", [NCH, 4, CHE], BF16)
    Qs = P.dint("Qs", [384, NTOK], BF16)
    ya_d = P.dint("ya_d", [128, 32, 384], BF16)
    x1 = P.dint("x1", [NTOK, D], F32)
    x2 = P.dint("x2", [NTOK, D], F32)
    x3 = P.dint("x3", [NTOK, D], F32)
    xo = P.dout("xo", [NTOK, D], F32)
    for l in range(nl):
        xin = x if l == 0 else x3
        xout = xo if l == nl - 1 else x3
        P.phases.append(P1(P, "p1", xin, W["w_in"][l], W["g_pre_mix"][l], identb, Qs, E))
        P.phases.append(AG(P, "ag", E, G))
        P.phases.append(P2a(P, "a", G, Qs, ctab, dn, kaug, qaug, identf, lamv[l], lamc[l], W["g_diff_sub"][l], ya_d))
        P.phases.append(P2b(P, "b", xin, W["w_in"][l], W["g_pre_mix"][l], G, ya_d, W["gmlp_ln_g"][l], W["gmlp_ln_b"][l],
                            W["w_spatial"][l], W["b_spatial"][l], W["swa_sinks"][l], W["w_out"][l], W["g_post_mix"][l],
                            dw, identb, identf, x1))
        P.phases.append(P3a(P, "f", x1, W["w_ffn_in"][l], W["w_ffn_out"][l], W["g_pre_ffn"][l], W["g_post_ffn"][l], identb, x2))
        P.phases.append(P3b(P, "e", x2, pp[l], W["w_ple_up"][l], W["w_ple_gate"][l], W["g_ple_gate"][l], W["g_ple_post"][l],
                            identb, xout))
    return P.build()


_PROGS = {}


def kernel(**inputs):
    if "F" not in _PROGS:
        _PROGS["F"] = build_fused()
    consts = host_consts()
    tabs = [host_tables(c) for c in range(4)]
    x = np.asarray(inputs["x"], np.float32)
    p = np.asarray(inputs["p"], np.float32)
    lamv = np.stack([np.stack([np.asarray(inputs[k][l], np.float32) for k in ("lam_q1", "lam_k1", "lam_q2", "lam_k2")])
                     for l in range(L)])
    lamc = np.stack([np.tile(np.array([[lam_init(l), 1.0 - lam_init(l)]], np.float32), (128, 1)) for l in range(L)])
    shared = {k: np.ascontiguousarray(np.asarray(inputs[k], np.float32)) for k, _ in WKEYS}
    shared.update(kaug=consts["kaug"], qaug=consts["qaug"], identb=consts["identb"], identf=consts["identf"],
                  lamv=lamv, lamc=lamc)
    cores = list(range(8))
    in_maps = []
    for core in cores:
        bb, c = core // 4, core % 4
        tok = _local_tokens(c)
        m = dict(shared)
        m["x"] = np.ascontiguousarray(x[bb][tok])
        m["p"] = np.ascontiguousarray(p[:, bb][:, tok])
        m["ctab"], m["dn"], m["dw"] = tabs[c]["ctab"], tabs[c]["dn"], tabs[c]["dw"]
        in_maps.append(m)
    res = run_bass_kernel_spmd(_PROGS["F"], in_maps, core_ids=cores)
    out = np.empty((NB, SEQ, D), np.float32)
    for core in cores:
        out[core // 4][_local_tokens(core % 4)] = np.asarray(res.results[core]["xo"], np.float32)
    return out
```

```python
import numpy as np
import ml_dtypes
from contextlib import ExitStack
import concourse.bass as bass
import concourse.mybir as mybir
from concourse.bass_utils import run_bass_kernel_spmd

F32 = mybir.dt.float32
BF16 = mybir.dt.bfloat16
AF = mybir.ActivationFunctionType
ALU = mybir.AluOpType
AX = mybir.AxisListType
NPBF = ml_dtypes.bfloat16

D = 1024
SEQ = 16384
NB = 2
L = 4
NTOK = 4096
NQ = 8
DFF = 2816
EPS = 1e-6
SC_A = 32 ** -0.5
SC_W = 64 ** -0.5
_k = np.arange(1, 13, dtype=np.float64)
_sl = np.exp2(-8.0 * _k / 12.0)
SL_DIFF = _sl[6:]
SL_WIN = _sl[:6]
CUT = 30.0
BIG = 30000.0
LA = 2
NSR = 3
NPR = 4


def lam_init(l):
    return 0.8 - 0.6 * float(np.exp(-0.3 * l))


class Dummy:
    def __getattr__(self, n):
        return self

    def __call__(self, *a, **k):
        return self

    def __getitem__(self, k):
        return self

    def __enter__(self):
        return self

    def __exit__(self, *a):
        return False


class Sync:
    def __init__(self):
        self.dry = True
        self.total = {}
        self.count = {}
        self.seq = {}
        self.rpos = {}
        self.handles = {}
        self.scope = None

    def _k(self, key, glob):
        return key if glob else (self.scope, key)

    def sig(self, ins, sem, key, dma=False, glob=False):
        inc = 16 if dma else 1
        k = (sem, self._k(key, glob))
        if self.dry:
            assert k not in self.count, k
            t = self.total.get(sem, 0) + inc
            self.total[sem] = t
            self.count[k] = t
            self.seq.setdefault(sem, []).append((k, t))
        else:
            i = self.rpos.get(sem, 0)
            kk, t = self.seq[sem][i]
            assert kk == k, (kk, k)
            self.rpos[sem] = i + 1
            ins.then_inc(self.handles[sem], inc)

    def wait(self, e, sem, key, glob=False):
        if not self.dry:
            e.wait_ge(self.handles[sem], self.count[(sem, self._k(key, glob))])

    def fence(self, e, ins, sem):
        if self.dry:
            t = self.total.get(sem, 0) + 1
            self.total[sem] = t
            self.seq.setdefault(sem, []).append((None, t))
        else:
            i = self.rpos.get(sem, 0)
            kk, t = self.seq[sem][i]
            assert kk is None, kk
            self.rpos[sem] = i + 1
            ins.then_inc(self.handles[sem], 1)
            e.wait_ge(self.handles[sem], t)


def band_tiles(h, i):
    out = []
    for j in range(128):
        if j < 16 * i:
            dmin = 2048 * i - (128 * j + 127)
        elif j >= 16 * (i + 1):
            dmin = 128 * j - (2048 * i + 2047)
        else:
            dmin = 0
        if SL_DIFF[h] * dmin <= CUT:
            out.append(j)
    return out


def j2idx(j):
    T = j // 4
    sub = j % 4
    r = T % 4
    ii = T // 4
    return r, 4 * ii + sub


_CIDX = None


def cidx_map():
    global _CIDX
    if _CIDX is None:
        m = {}
        n = 1
        for h in range(6):
            for i in range(NQ):
                for j in band_tiles(h, i):
                    if j < 16 * i or j >= 16 * (i + 1):
                        m[(h, i, j)] = n
                        n += 1
        _CIDX = (m, n)
    return _CIDX


def split_bf(v):
    hi = np.float32(np.asarray(v, np.float32).astype(NPBF).astype(np.float32))
    lo = np.float32(np.asarray(np.float32(v) - hi, np.float32).astype(NPBF).astype(np.float32))
    return hi, lo


def host_tables(c):
    m, n = cidx_map()
    ct = np.zeros((n,), np.float32)
    for (h, i, j), col in m.items():
        qc = 2048 * i + 512 * c + 256
        ct[col] = -SL_DIFF[h] * abs(128 * j - qc)
    ctab = np.ascontiguousarray(np.broadcast_to(ct[None, :], (128, n))).astype(np.float32)
    ki = np.arange(128)[:, None, None]
    qi = np.arange(512)[None, None, :]
    tg = np.arange(16)[None, :, None]
    dn = np.abs(128 * tg + ki - (512 * c + qi)).astype(np.float32)
    koff = np.array([-128] + [128 * t for t in range(16)] + [2048])[None, :, None]
    dw = np.abs(koff + ki - (512 * c + qi)).astype(np.float32)
    dw = np.where(dw <= 128.0, dw, BIG).astype(np.float32)
    return {"ctab": ctab, "dn": np.ascontiguousarray(dn), "dw": np.ascontiguousarray(dw)}


def host_consts():
    kaug = np.zeros((6, 4, 128), np.float32)
    qaug = np.zeros((6, 2, 4, NTOK), np.float32)
    qip = (np.arange(NTOK) % 512 - 256).astype(np.float32)
    for h in range(6):
        sp = SL_DIFF[h] / SC_A
        hi, lo = split_bf(sp)
        kaug[h, 0, :] = -hi
        kaug[h, 1, :] = -lo
        kaug[h, 2, :] = np.arange(128)
        kaug[h, 3, :] = np.arange(128)
        qaug[h, 0, 0, :] = qip
        qaug[h, 0, 1, :] = qip
        qaug[h, 0, 2, :] = hi
        qaug[h, 0, 3, :] = lo
        qaug[h, 1] = -qaug[h, 0]
    kaug = np.tile(kaug, (1, 1, 128))
    return {
        "kaug": kaug.astype(NPBF),
        "qaug": qaug.astype(NPBF),
        "identb": np.eye(128, dtype=np.float32).astype(NPBF),
        "identf": np.eye(128, dtype=np.float32),
    }


class Prog:
    def __init__(self):
        self.nc = bass.Bass("TRN2", target_bir_lowering=False)
        self.S = Sync()
        self.dram = {}
        self.phases = []

    def din(self, name, shape, dt):
        t = self.nc.dram_tensor(name, list(shape), dt, kind="ExternalInput").ap()
        self.dram[name] = t
        return t

    def dout(self, name, shape, dt):
        t = self.nc.dram_tensor(name, list(shape), dt, kind="ExternalOutput").ap()
        self.dram[name] = t
        return t

    def dint(self, name, shape, dt):
        t = self.nc.dram_tensor(name, list(shape), dt, kind="Internal").ap()
        self.dram[name] = t
        return t

    def build(self):
        S = self.S
        nph = len(self.phases)
        engs = ["sp", "act", "pe", "dve", "pool"]
        S.dry = True
        for pi, ph in enumerate(self.phases):
            S.scope = pi
            ph.reset()
            ph.bind(Dummy())
            for en in engs:
                e = Dummy()
                self._barrier_pre(en, e, pi)
                getattr(ph, en)(e)
                self._barrier_post(en, e, pi)
        S.dry = False
        nc = self.nc
        with ExitStack() as es:
            for sem in S.seq:
                S.handles[sem] = es.enter_context(nc.semaphore(sem))
            self.bar_tile = es.enter_context(nc.sbuf_tensor("bar_tile", [128, 8], F32))
            for pi, ph in enumerate(self.phases):
                S.scope = pi
                ph.reset()
                with ExitStack() as pes:
                    bufs = ph.alloc(nc, pes)
                    ph.bind(bufs)
                    with nc.Block() as block:
                        def mk(en, ph=ph, pi=pi):
                            def f(e):
                                S.scope = pi
                                self._barrier_pre(en, e, pi)
                                getattr(ph, en)(e)
                                self._barrier_post(en, e, pi)
                            return f
                        block.sync(mk("sp"))
                        block.scalar(mk("act"))
                        block.tensor(mk("pe"))
                        block.vector(mk("dve"))
                        block.gpsimd(mk("pool"))
        for sem in S.seq:
            assert S.rpos.get(sem, 0) == len(S.seq[sem]), sem
        return nc

    def _barrier_pre(self, en, e, pi):
        if pi == 0 or en == "pool":
            return
        self.S.wait(e, "bar", ("end", pi - 1), glob=True)

    def _barrier_post(self, en, e, pi):
        if en != "pool":
            return
        S = self.S
        if S.dry:
            S.sig(None, "bar", ("end", pi), glob=True)
        else:
            ins = e.memset(self.bar_tile[:, :], 0.0)
            S.sig(ins, "bar", ("end", pi), glob=True)


class Phase:
    def __init__(self, prog, tag):
        self.P = prog
        self.S = prog.S
        self.tag = tag

    def reset(self):
        pass

    def bind(self, bufs):
        self.b = bufs

    def alloc(self, nc, es):
        return Dummy()

    def sp(self, e):
        pass

    def act(self, e):
        pass

    def pe(self, e):
        pass

    def dve(self, e):
        pass

    def pool(self, e):
        pass

    def sem(self, n):
        return n


class Bufs:
    pass


_UID = [0]


def _uname(name):
    _UID[0] += 1
    return "%s_%d" % (name, _UID[0])


def sb(nc, es, name, shape, dt):
    return es.enter_context(nc.sbuf_tensor(_uname(name), list(shape), dt))


def ps(nc, es, name, shape, dt):
    return es.enter_context(nc.psum_tensor(_uname(name), list(shape), dt))


NCH = 16
CHE = 262144


def wview(w, c0, c1):
    return w[:, c0:c1].rearrange("(kc p) c -> p kc c", p=128)


class NormT:
    def __init__(self, ph, name):
        self.ph = ph
        self.S = ph.S
        self.n = ph.sem(name)

    def dve(self, e, t, xt, gbc, hb, junk, ss, rstd, wait_x, wait_hb_free):
        S = self.S
        n = self.n
        wait_x()
        wait_hb_free()
        for s in range(4):
            ins = e.tensor_tensor(out=junk[:, :], in0=xt[:, s, :], in1=xt[:, s, :], op=ALU.mult)
            S.fence(e, ins, n + "_f")
            ins = e.tensor_reduce(out=ss[:, s:s + 1], in_=junk[:, :], axis=AX.X, op=ALU.add)
            S.fence(e, ins, n + "_f")
        ins = e.tensor_scalar(out=rstd[:, 0:4], in0=ss[:, 0:4], scalar1=1.0 / D, scalar2=EPS, op0=ALU.mult, op1=ALU.add)
        S.sig(ins, n + "_v", t)
        S.wait(e, n + "_sq", t)
        ins = e.reciprocal(out=rstd[:, 0:4], in_=rstd[:, 0:4])
        S.fence(e, ins, n + "_f")
        for s in range(4):
            ins = e.scalar_tensor_tensor(out=hb[:, s, :], in0=xt[:, s, :], scalar=rstd[:, s:s + 1], in1=gbc[:, :],
                                         op0=ALU.mult, op1=ALU.mult)
            S.sig(ins, n + "_hb", (t, s))

    def pe(self, e, t, hb, pst, identb):
        S = self.S
        n = self.n
        for kc in range(8):
            g = t * 8 + kc
            if g >= 2:
                S.wait(e, n + "_ev", g - 2)
            for s in range(4):
                if kc == 0:
                    S.wait(e, n + "_hb", (t, s))
                ins = e.transpose(pst[:, g % 2, s * 128:(s + 1) * 128], hb[:, s, kc * 128:(kc + 1) * 128], identb[:, :])
            S.sig(ins, n + "_tr", g)

    def act(self, e, t, pst, hT, wait_hT_free, rstd=None):
        S = self.S
        n = self.n
        S.wait(e, n + "_v", t)
        ins = e.activation(out=rstd[:, 0:4], in_=rstd[:, 0:4], func=AF.Sqrt)
        S.sig(ins, n + "_sq", t)
        wait_hT_free()
        for kc in range(8):
            g = t * 8 + kc
            S.wait(e, n + "_tr", g)
            ins = e.activation(out=hT[:, kc, :], in_=pst[:, g % 2, 0:512], func=AF.Copy)
            S.sig(ins, n + "_ev", g)

    def wait_hT(self, e, t):
        self.S.wait(e, self.n + "_ev", t * 8 + 7)

    def wait_tr_done(self, e, t):
        self.S.wait(e, self.n + "_tr", t * 8 + 7)


class P1(Phase):
    FM = [(0, 384), (384, 768), (2048, 2176)]
    TM = [(768, 1152), (2176, 2304)]

    def __init__(self, prog, tag, x, w_in, g_pre, identb, Qs, E):
        super().__init__(prog, tag)
        self.x, self.w_in, self.g_pre, self.identb_d = x, w_in, g_pre, identb
        self.Qs, self.E = Qs, E
        self.EK = E[0:6, :].rearrange("h (a n) -> (h a) n", n=NTOK)
        self.ECK = E[12:14, :].rearrange("h (a n) -> (h a) n", n=NTOK)
        self.nt = NormT(self, "nt")

    def alloc(self, nc, es):
        b = Bufs()
        b.wfm = sb(nc, es, "p1_wfm", [128, 8, 896], BF16)
        b.wtm = sb(nc, es, "p1_wtm", [128, 8, 512], BF16)
        b.gbc = sb(nc, es, "p1_gbc", [128, D], F32)
        b.identb = sb(nc, es, "p1_id", [128, 128], BF16)
        b.xt = sb(nc, es, "p1_xt", [128, 4, D], F32)
        b.hb = sb(nc, es, "p1_hb", [128, 4, D], BF16)
        b.junk = sb(nc, es, "p1_junk", [128, D], F32)
        b.ss = sb(nc, es, "p1_ss", [128, 4], F32)
        b.rstd = sb(nc, es, "p1_rstd", [128, 4], F32)
        b.hT = sb(nc, es, "p1_hT", [128, 8, 512], BF16)
        b.fm = sb(nc, es, "p1_fm", [128, 2, 7, 512], BF16)
        b.vt = sb(nc, es, "p1_vt", [128, 2, 4, 512], BF16)
        b.pst = ps(nc, es, "p1_pst", [128, 2, 1024], BF16)
        b.pm = ps(nc, es, "p1_pm", [128, 4, 512], F32)
        return b

    def sp(self, e):
        S, b = self.S, self.b
        ld = self.sem("ld")
        ins = e.dma_start(out=b.identb[:, :], in_=self.identb_d[:, :])
        S.sig(ins, ld, "id", dma=True)
        ins = e.dma_start(out=b.gbc[:, :], in_=self.g_pre.partition_broadcast(128))
        S.sig(ins, ld, "g", dma=True)
        for t in range(NQ):
            if t >= 1:
                S.wait(e, self.nt.n + "_hb", (t - 1, 3))
            ins = e.dma_start(out=b.xt[:, :, :], in_=self.x[t * 512:(t + 1) * 512, :].rearrange("(s p) d -> p s d", p=128))
            S.sig(ins, self.sem("ldx"), t, dma=True)

    def pool(self, e):
        S, b = self.S, self.b
        lw = self.sem("lw")
        c = 0
        for (c0, c1) in self.FM:
            ins = e.dma_start(out=b.wfm[:, :, c:c + (c1 - c0)], in_=wview(self.w_in, c0, c1))
            S.sig(ins, lw, ("fm", c0), dma=True)
            c += c1 - c0
        c = 0
        for (c0, c1) in self.TM:
            ins = e.dma_start(out=b.wtm[:, :, c:c + (c1 - c0)], in_=wview(self.w_in, c0, c1))
            S.sig(ins, lw, ("tm", c0), dma=True)
            c += c1 - c0
        for t in range(NQ):
            sl = t % 2
            st = self.sem("st%d" % sl)
            cs = slice(t * 512, (t + 1) * 512)
            S.wait(e, self.sem("ev"), ("fm", t, 6))
            ins = e.dma_start(out=self.Qs[:, cs].rearrange("(b p) n -> p b n", p=128), in_=b.fm[:, sl, 0:3, :])
            S.sig(ins, st, ("q", t), dma=True)
            ins = e.dma_start(out=self.EK[:, cs].rearrange("(b p) n -> p b n", p=128), in_=b.fm[:, sl, 3:6, :])
            S.sig(ins, st, ("k", t), dma=True)
            ins = e.dma_start(out=self.ECK[:, cs], in_=b.fm[:, sl, 6, :])
            S.sig(ins, st, ("ck", t), dma=True)
            S.wait(e, self.sem("ev"), ("tm", t, 3))
            for hh in range(8):
                ch = 6 + hh if hh < 6 else 14 + (hh - 6)
                dst = self.E[ch, :].rearrange("(p t d) -> p t d", p=128, t=32)
                ins = e.dma_start(out=dst[:, 4 * t:4 * t + 4, :], in_=b.vt[:, sl, :, hh * 64:(hh + 1) * 64])
                S.sig(ins, st, ("cv" if hh == 7 else ("v", hh), t), dma=True)
        S.wait(e, self.sem("st0"), ("cv", NQ - 2))
        S.wait(e, self.sem("st1"), ("cv", NQ - 1))

    def dve(self, e):
        S, b = self.S, self.b
        S.wait(e, self.sem("ld"), "g")
        for t in range(NQ):
            self.nt.dve(e, t, b.xt, b.gbc, b.hb, b.junk, b.ss, b.rstd,
                        wait_x=lambda: S.wait(e, self.sem("ldx"), t),
                        wait_hb_free=lambda: (self.nt.wait_tr_done(e, t - 1) if t >= 1 else None))

    def pe(self, e):
        S, b = self.S, self.b
        S.wait(e, self.sem("ld"), "g")
        S.wait(e, self.sem("lw"), ("tm", self.TM[-1][0]))
        n = 0
        for t in range(NQ):
            self.nt.pe(e, t, b.hb, b.pst, b.identb)
            self.nt.wait_hT(e, t)
            for blk in range(7):
                if n >= 4:
                    S.wait(e, self.sem("ev"), self.evkeys[n - 4])
                for kc in range(8):
                    ins = e.matmul(b.pm[:, n % 4, :], lhsT=b.wfm[:, kc, blk * 128:(blk + 1) * 128], rhs=b.hT[:, kc, :],
                                   start=(kc == 0), stop=(kc == 7))
                S.sig(ins, self.sem("mm"), ("fm", t, blk))
                self.evkeys.append(("fm", t, blk))
                n += 1
            for s in range(4):
                if n >= 4:
                    S.wait(e, self.sem("ev"), self.evkeys[n - 4])
                for kc in range(8):
                    ins = e.matmul(b.pm[:, n % 4, :], lhsT=b.hT[:, kc, s * 128:(s + 1) * 128], rhs=b.wtm[:, kc, :],
                                   start=(kc == 0), stop=(kc == 7))
                S.sig(ins, self.sem("mm"), ("tm", t, s))
                self.evkeys.append(("tm", t, s))
                n += 1

    def reset(self):
        self.evkeys = []

    def act(self, e):
        S, b = self.S, self.b
        n = 0
        for t in range(NQ):
            sl = t % 2
            self.nt.act(e, t, b.pst, b.hT,
                        wait_hT_free=lambda: (S.wait(e, self.sem("mm"), ("tm", t - 1, 3)) if t >= 1 else None),
                        rstd=b.rstd)
            for blk in range(7):
                S.wait(e, self.sem("mm"), ("fm", t, blk))
                if t >= 2 and blk == 0:
                    S.wait(e, self.sem("st%d" % sl), ("cv", t - 2))
                ins = e.activation(out=b.fm[:, sl, blk, :], in_=b.pm[:, n % 4, :], func=AF.Copy)
                S.sig(ins, self.sem("ev"), ("fm", t, blk))
                n += 1
            for s in range(4):
                S.wait(e, self.sem("mm"), ("tm", t, s))
                if t >= 2 and s == 0:
                    S.wait(e, self.sem("st%d" % sl), ("cv", t - 2))
                ins = e.activation(out=b.vt[:, sl, s, :], in_=b.pm[:, n % 4, :], func=AF.Copy)
                S.sig(ins, self.sem("ev"), ("tm", t, s))
                n += 1


class AttnPipe(Phase):
    NSR = NSR

    def reset(self):
        self.gu = {"pe": 0, "act": 0, "dve": 0}
        self.prev_tp = {"pe": None}

    def pe_group(self, e, g):
        S, b = self.S, self.b
        U = len(g["units"])
        g0 = self.gu["pe"]
        for t in range(U + LA):
            if t < U:
                u = g["units"][t]
                gu = g0 + t
                if gu >= self.NSR:
                    S.wait(e, self.sem("p"), gu - self.NSR)
                if t == 0 and g.get("wait_pe") is not None:
                    g["wait_pe"](e)
                ins = e.matmul(b.sring[:, gu % self.NSR, :], lhsT=u["lhsT"], rhs=u["rhs"], start=True, stop=True)
                S.sig(ins, self.sem("qk"), gu)
            if t >= LA:
                v = t - LA
                u = g["units"][v]
                gv = g0 + v
                S.wait(e, self.sem("p"), gv)
                if u["first"] and g["acc_prev"] is not None:
                    S.wait(e, self.sem("ev"), (g["acc_prev"], u["acc"]))
                ins = e.matmul(b.oacc[:, g["par"] * 2 + u["acc"], :], lhsT=u["vl"], rhs=b.pring[:, gv % NPR, :],
                               start=u["first"], stop=u["last"])
                S.sig(ins, self.sem("pv"), gv)
        self.gu["pe"] = g0 + U

    def pe_group_pairs(self, e, g):
        S, b = self.S, self.b
        LAP = 2
        U = len(g["units"])
        NP_ = U // 2
        g0 = self.gu["pe"]
        for t in range(NP_ + LAP):
            if t >= LAP:
                S.wait(e, self.sem("p"), g0 + 2 * (t - LAP) + 1)
            elif g0 + 2 * (t - LAP) + 1 >= 0:
                S.wait(e, self.sem("p"), g0 + 2 * (t - LAP) + 1)
            if t == 0 and g.get("wait_pe") is not None:
                g["wait_pe"](e)
            if t < NP_:
                for c in (0, 1):
                    u = g["units"][2 * t + c]
                    gu = g0 + 2 * t + c
                    ins = e.matmul(b.sring[:, gu % self.NSR, :], lhsT=u["lhsT"], rhs=u["rhs"], start=True, stop=True)
                    S.sig(ins, self.sem("qk"), gu)
            if t >= LAP:
                for c in (0, 1):
                    v = 2 * (t - LAP) + c
                    u = g["units"][v]
                    gv = g0 + v
                    if u["first"] and g["acc_prev"] is not None:
                        S.wait(e, self.sem("ev"), (g["acc_prev"], u["acc"]))
                    ins = e.matmul(b.oacc[:, g["par"] * 2 + u["acc"], :], lhsT=u["vl"], rhs=b.pring[:, gv % NPR, :],
                                   start=u["first"], stop=u["last"])
                    S.sig(ins, self.sem("pv"), gv)
        self.gu["pe"] = g0 + U

    def pe_post(self, e, g):
        S, b = self.S, self.b
        for c in (0, 1):
            S.wait(e, self.sem("ev"), (g["gid"], c))
            if self.prev_tp["pe"] is not None:
                S.wait(e, self.sem("tpc"), self.prev_tp["pe"])
            for s in range(4):
                ins = e.transpose(b.tps[:, s, 0:65], b.oT[0:65, c, s * 128:(s + 1) * 128], b.identf[0:65, 0:65])
            S.sig(ins, self.sem("tp"), (g["gid"], c))
            self.prev_tp["pe"] = (g["gid"], c)

    def act_group(self, e, g):
        S, b = self.S, self.b
        g0 = self.gu["act"]
        for t, u in enumerate(g["units"]):
            gu = g0 + t
            if u["kind"] == "G":
                S.wait(e, self.sem("gb"), gu)
            else:
                S.wait(e, self.sem("qk"), gu)
            if gu >= NPR:
                S.wait(e, self.sem("pv"), gu - NPR)
            ins = e.activation(out=b.pring[:, gu % NPR, :], in_=b.sring[:, gu % self.NSR, :], func=AF.Exp,
                               bias=u["bias"], scale=u["scale"])
            S.sig(ins, self.sem("p"), gu)
        self.gu["act"] = g0 + len(g["units"])

    def dve_group(self, e, g, consume):
        S, b = self.S, self.b
        g0 = self.gu["dve"]
        last = {}
        for t, u in enumerate(g["units"]):
            gu = g0 + t
            last[u["acc"]] = gu
            if u["kind"] == "G":
                S.wait(e, self.sem("qk"), gu)
                ins = e.scalar_tensor_tensor(out=b.sring[:, gu % self.NSR, :], in0=u["dD"], scalar=u["dcoef"],
                                             in1=b.sring[:, gu % self.NSR, :], op0=ALU.mult, op1=ALU.add)
                S.sig(ins, self.sem("gb"), gu)
        self.gu["dve"] = g0 + len(g["units"])
        for c in (0, 1):
            S.wait(e, self.sem("pv"), last[c])
            if g["gid"] >= 1:
                S.wait(e, self.sem("tp"), (g["gid"] - 1, c))
            ins = e.tensor_copy(out=b.oT[0:65, c, :], in_=b.oacc[0:65, g["par"] * 2 + c, :])
            S.sig(ins, self.sem("ev"), (g["gid"], c))
        for c in (0, 1):
            S.wait(e, self.sem("tp"), (g["gid"], c))
            consume(e, g, c)


class P2a(AttnPipe):
    NSR = 4

    def __init__(self, prog, tag, G, Qs, ctab, dn, kaug, qaug, identf, lamv, lamc, gsub, ya_d):
        super().__init__(prog, tag)
        self.G, self.Qs, self.ctab_d, self.dn_d = G, Qs, ctab, dn
        self.kaug, self.qaug, self.identf_d, self.lamv, self.lamc_d, self.gsub_d, self.ya_d = kaug, qaug, identf, lamv, lamc, gsub, ya_d
        self.ncol = cidx_map()[1]

    def alloc(self, nc, es):
        b = Bufs()
        b.KT = sb(nc, es, "a_KT", [100, 2, SEQ], BF16)
        b.V = sb(nc, es, "a_V", [128, 2, 129, 65], BF16)
        b.QA = sb(nc, es, "a_QA", [100, 2, NTOK], BF16)
        b.QB = sb(nc, es, "a_QB", [100, 2, NTOK], BF16)
        b.ctab = sb(nc, es, "a_ctab", [128, self.ncol], F32)
        b.dn = sb(nc, es, "a_dn", [128, 16, 512], F32)
        b.identf = sb(nc, es, "a_idf", [128, 128], F32)
        b.pring = sb(nc, es, "a_pr", [128, NPR, 512], BF16)
        b.oT = sb(nc, es, "a_oT", [65, 2, 512], F32)
        b.lv = sb(nc, es, "a_lv", [128, 4, 32], F32)
        b.lt = sb(nc, es, "a_lt", [128, 8], F32)
        b.lamc = sb(nc, es, "a_lamc", [128, 2], F32)
        b.gsub = sb(nc, es, "a_gsub", [128, 64], F32)
        b.r = sb(nc, es, "a_r", [128, 8], F32)
        b.t0 = sb(nc, es, "a_t0", [128, 4, 64], F32)
        b.y = sb(nc, es, "a_y", [128, 4, 64], F32)
        b.sq = sb(nc, es, "a_sq", [128, 4, 64], F32)
        b.sst = sb(nc, es, "a_sst", [128, 32, 6], F32)
        b.ya = sb(nc, es, "a_ya", [128, 32, 384], BF16)
        b.sring = ps(nc, es, "a_sr", [128, 4, 512], F32)
        b.oacc = ps(nc, es, "a_oacc", [128, 2, 512], F32)
        b.tps = ps(nc, es, "a_tps", [128, 4, 128], F32)
        return b

    def groups(self):
        b = self.b
        cm, _ = cidx_map()
        out = []
        gid = 0
        for h in range(6):
            sl = h % 2
            sp = float(SL_DIFF[h] / SC_A)
            for i in range(NQ):
                units = []
                tl = band_tiles(h, i)
                qs = slice(512 * i, 512 * i + 512)
                for n, j in enumerate(tl):
                    r, lt = j2idx(j)
                    ks = slice(r * 4096 + lt * 128, r * 4096 + lt * 128 + 128)
                    for comp in (0, 1):
                        base = 64 * comp
                        vi = (r * 32 + lt) * 65
                        u = {"acc": comp, "first": n == 0, "last": n == len(tl) - 1, "scale": SC_A,
                             "vl": b.V[:, sl, :, :].rearrange("p t d -> p (t d)")[:, vi:vi + 128]}
                        if j < 16 * i or j >= 16 * (i + 1):
                            Q = b.QA if j < 16 * i else b.QB
                            u["kind"] = "F"
                            u["lhsT"] = b.KT[base:base + 36, sl, ks]
                            u["rhs"] = Q[base:base + 36, sl, qs]
                            u["bias"] = b.ctab[:, cm[(h, i, j)]:cm[(h, i, j)] + 1]
                        else:
                            u["kind"] = "G"
                            u["lhsT"] = b.KT[base:base + 32, sl, ks]
                            u["rhs"] = b.QA[base:base + 32, sl, qs]
                            u["bias"] = b.ctab[:, 0:1]
                            u["dD"] = b.dn[:, j - 16 * i, :]
                            u["dcoef"] = -sp
                        units.append(u)
                out.append({"gid": gid, "par": 0, "units": units, "h": h, "i": i,
                            "acc_prev": gid - 1 if gid >= 1 else None})
                gid += 1
        return out

    def sp(self, e):
        S, b = self.S, self.b
        ld = self.sem("ldc")
        lst = [(b.ctab[:, :], self.ctab_d[:, :]), (b.dn[:, :, :], self.dn_d[:, :, :]),
               (b.identf[:, :], self.identf_d[:, :]), (b.lamc[:, :], self.lamc_d[:, :]),
               (b.gsub[:, :], self.gsub_d.partition_broadcast(128))]
        for n, (dst, src) in enumerate(lst):
            ins = e.dma_start(out=dst, in_=src)
            S.sig(ins, ld, ("c", n), dma=True)
        for k in range(4):
            ins = e.dma_start(out=b.lv[:, k, :], in_=self.lamv[k, :].partition_broadcast(128))
            S.sig(ins, ld, ("lv", k), dma=True)
        for h in range(6):
            sl = h % 2
            sem = self.sem("ld%d" % sl)
            if h >= 2:
                S.wait(e, self.sem("hd"), h - 2)
            for comp in (0, 1):
                rows = slice(h * 64 + comp * 32, h * 64 + comp * 32 + 32)
                base = 64 * comp
                ins = e.dma_start(out=b.KT[base:base + 32, sl, :].rearrange("p (r n) -> p r n", r=4),
                                  in_=self.G[h, :, :].rearrange("r (a n) -> a r n", n=NTOK)[comp * 32:(comp + 1) * 32])
                S.sig(ins, sem, ("k", h, comp), dma=True)
                ins = e.dma_start(out=b.KT[base + 32:base + 36, sl, :], in_=self.kaug[h, :, :])
                S.sig(ins, sem, ("ka", h, comp), dma=True)
                for var, Q in ((0, b.QA), (1, b.QB)):
                    ins = e.dma_start(out=Q[base:base + 32, sl, :], in_=self.Qs[rows, :])
                    S.sig(ins, sem, ("q", h, comp, var), dma=True)
                    ins = e.dma_start(out=Q[base + 32:base + 36, sl, :], in_=self.qaug[h, var, :, :])
                    S.sig(ins, sem, ("qa", h, comp, var), dma=True)
            for r in range(4):
                ins = e.dma_start(out=b.V[:, sl, r * 32:(r + 1) * 32, 0:64],
                                  in_=self.G[6 + h, r, :].rearrange("(p t d) -> p t d", p=128, t=32))
                S.sig(ins, sem, ("v", h, r), dma=True)

    def pe(self, e):
        S, b = self.S, self.b
        S.wait(e, self.sem("ldc"), ("lv", 3))
        S.wait(e, self.sem("ones"), 0)
        prev = None
        for g in self.groups():
            if g["i"] == 0:
                h = g["h"]
                g["wait_pe"] = lambda e, h=h: S.wait(e, self.sem("ld%d" % (h % 2)), ("v", h, 3))
            if prev is not None:
                self.pe_post(e, prev)
            self.pe_group_pairs(e, g)
            prev = g
        self.pe_post(e, prev)

    def act(self, e):
        S, b = self.S, self.b
        S.wait(e, self.sem("ldc"), ("lv", 3))
        S.wait(e, self.sem("lm"), "dots")
        ins = e.activation(out=b.lt[:, 2:4], in_=b.lt[:, 0:2], func=AF.Exp)
        S.sig(ins, self.sem("lma"), "exp")
        for g in self.groups():
            self.act_group(e, g)
        S.wait(e, self.sem("fin"), "v")
        ins = e.activation(out=b.sst[:, 0:4 * NQ, :], in_=b.sst[:, 0:4 * NQ, :], func=AF.Sqrt)
        S.sig(ins, self.sem("lma"), "sqrt")

    def consume(self, e, g, c):
        S, b = self.S, self.b
        f = self.sem("f")
        h, i = g["h"], g["i"]
        ins = e.reciprocal(out=b.r[:, 4 * c:4 * c + 4], in_=b.tps[:, :, 64])
        S.fence(e, ins, f)
        if c == 0:
            for s in range(4):
                ins = e.tensor_scalar(out=b.t0[:, s, :], in0=b.tps[:, s, 0:64], scalar1=b.r[:, s:s + 1], scalar2=None,
                                      op0=ALU.mult)
            S.sig(ins, self.sem("tpc"), (g["gid"], 0))
        else:
            ins = e.tensor_scalar(out=b.r[:, 4:8], in0=b.r[:, 4:8], scalar1=b.lt[:, 4:5], scalar2=None, op0=ALU.mult)
            S.fence(e, ins, f)
            for s in range(4):
                ins = e.scalar_tensor_tensor(out=b.y[:, s, :], in0=b.tps[:, s, 0:64], scalar=b.r[:, 4 + s:5 + s],
                                             in1=b.t0[:, s, :], op0=ALU.mult, op1=ALU.add)
            S.sig(ins, self.sem("tpc"), (g["gid"], 1))
            S.wait(e, self.sem("tpc"), (g["gid"], 1))
            ins = e.tensor_tensor(out=b.sq[:, :, :], in0=b.y[:, :, :], in1=b.y[:, :, :], op=ALU.mult)
            S.fence(e, ins, f)
            ins = e.tensor_reduce(out=b.sst[:, 4 * i:4 * i + 4, h], in_=b.sq[:, :, :], axis=AX.X, op=ALU.add)
            ins = e.tensor_copy(out=b.ya[:, 4 * i:4 * i + 4, h * 64:(h + 1) * 64], in_=b.y[:, :, :])
            S.fence(e, ins, f)
            if i == NQ - 1:
                S.sig(e.tensor_copy(out=b.lt[:, 7:8], in_=b.lt[:, 4:5]), self.sem("hd"), h)

    def dve(self, e):
        S, b = self.S, self.b
        f = self.sem("f")
        for sl in (0, 1):
            ins = e.memset(b.V[:, sl, 128, :], 0.0)
            ins = e.memset(b.V[:, sl, 0:128, 64:65], 1.0)
        S.sig(ins, self.sem("ones"), 0)
        S.wait(e, self.sem("ldc"), ("lv", 3))
        ins = e.tensor_tensor(out=b.lv[:, 0, :], in0=b.lv[:, 0, :], in1=b.lv[:, 1, :], op=ALU.mult)
        ins = e.tensor_tensor(out=b.lv[:, 2, :], in0=b.lv[:, 2, :], in1=b.lv[:, 3, :], op=ALU.mult)
        S.fence(e, ins, f)
        ins = e.tensor_reduce(out=b.lt[:, 0:1], in_=b.lv[:, 0, :], axis=AX.X, op=ALU.add)
        ins = e.tensor_reduce(out=b.lt[:, 1:2], in_=b.lv[:, 2, :], axis=AX.X, op=ALU.add)
        S.sig(ins, self.sem("lm"), "dots")
        S.wait(e, self.sem("lma"), "exp")
        ins = e.tensor_tensor(out=b.lt[:, 5:6], in0=b.lt[:, 3:4], in1=b.lt[:, 2:3], op=ALU.subtract)
        S.fence(e, ins, f)
        ins = e.tensor_tensor(out=b.lt[:, 4:5], in0=b.lt[:, 5:6], in1=b.lamc[:, 0:1], op=ALU.subtract)
        ins = e.tensor_scalar(out=b.gsub[:, :], in0=b.gsub[:, :], scalar1=b.lamc[:, 1:2], scalar2=None, op0=ALU.mult)
        S.fence(e, ins, f)
        for g in self.groups():
            self.dve_group(e, g, self.consume)
        ins = e.tensor_scalar(out=b.sst[:, 0:4 * NQ, :], in0=b.sst[:, 0:4 * NQ, :], scalar1=1.0 / 64, scalar2=EPS,
                              op0=ALU.mult, op1=ALU.add)
        S.sig(ins, self.sem("fin"), "v")
        S.wait(e, self.sem("lma"), "sqrt")
        ins = e.reciprocal(out=b.sst[:, 0:4 * NQ, :], in_=b.sst[:, 0:4 * NQ, :])
        S.fence(e, ins, f)
        for lt in range(4 * NQ):
            for h in range(6):
                ins = e.scalar_tensor_tensor(out=b.ya[:, lt, h * 64:(h + 1) * 64], in0=b.ya[:, lt, h * 64:(h + 1) * 64],
                                             scalar=b.sst[:, lt, h:h + 1], in1=b.gsub[:, :], op0=ALU.mult, op1=ALU.mult)
        S.sig(ins, self.sem("fin"), "ya")

    def pool(self, e):
        S, b = self.S, self.b
        S.wait(e, self.sem("fin"), "ya")
        ins = e.dma_start(out=self.ya_d[:, 0:4 * NQ, :], in_=b.ya[:, 0:4 * NQ, :])
        S.sig(ins, self.sem("st"), 0, dma=True)
        S.wait(e, self.sem("st"), 0)


class P2b(AttnPipe):
    NSR = 4

    def __init__(self, prog, tag, x, w_in, g_pre, G, ya_d, ln_g, ln_b, w_sp, b_sp, sinks, w_out, g_post,
                 dw, identb, identf, x1):
        super().__init__(prog, tag)
        self.x, self.w_in, self.g_pre, self.G, self.ya_d = x, w_in, g_pre, G, ya_d
        self.ln_g, self.ln_b, self.w_sp, self.b_sp, self.sinks, self.w_out, self.g_post = ln_g, ln_b, w_sp, b_sp, sinks, w_out, g_post
        self.dw_d, self.identb_d, self.identf_d, self.x1 = dw, identb, identf, x1
        self.nt = NormT(self, "nt")

    def reset(self):
        super().reset()
        self.m = {"pe": 0, "act": 0, "dve": 0}

    def alloc(self, nc, es):
        b = Bufs()
        b.wuv = sb(nc, es, "b_wuv", [128, 8, 512], BF16)
        b.wcq = sb(nc, es, "b_wcq", [128, 8, 3, 128], BF16)
        b.wout = sb(nc, es, "b_wout", [128, 8, D], BF16)
        b.dw = sb(nc, es, "b_dw", [128, 18, 512], F32)
        b.gbc = sb(nc, es, "b_gbc", [128, D], F32)
        b.gpost = sb(nc, es, "b_gpost", [128, D], F32)
        b.lng = sb(nc, es, "b_lng", [128, 256], F32)
        b.lnb = sb(nc, es, "b_lnb", [128, 256], F32)
        b.identb = sb(nc, es, "b_idb", [128, 128], BF16)
        b.identf = sb(nc, es, "b_idf", [128, 128], F32)
        b.wsp = sb(nc, es, "b_wsp", [128, 4, 128], F32)
        b.wsT = sb(nc, es, "b_wsT", [128, 4, 128], BF16)
        b.bsp = sb(nc, es, "b_bsp", [4, 128], F32)
        b.bsT = sb(nc, es, "b_bsT", [128, 4], F32)
        b.esink = sb(nc, es, "b_esink", [128, 6], F32)
        b.zero = sb(nc, es, "b_zero", [128, 1], F32)
        b.xt = sb(nc, es, "b_xt", [128, 2, 4, D], F32)
        b.hb = sb(nc, es, "b_hb", [128, 4, D], BF16)
        b.junk = sb(nc, es, "b_junk", [128, D], F32)
        b.ss = sb(nc, es, "b_ss", [128, 4], F32)
        b.rstd = sb(nc, es, "b_rstd", [128, 4], F32)
        b.hT = sb(nc, es, "b_hT", [128, 8, 512], BF16)
        b.cq = sb(nc, es, "b_cq", [128, 3, 512], BF16)
        b.ckx = sb(nc, es, "b_ckx", [128, 18 * 128], BF16)
        b.cv = sb(nc, es, "b_cv", [128, 19, 2, 65], BF16)
        b.pring = sb(nc, es, "b_pr", [128, NPR, 512], BF16)
        b.oT = sb(nc, es, "b_oT", [65, 2, 512], F32)
        b.r = sb(nc, es, "b_r", [128, 8], F32)
        b.y = sb(nc, es, "b_y", [128, 4, D], BF16)
        b.u = sb(nc, es, "b_u", [128, 4, 256], F32)
        b.vt = sb(nc, es, "b_vt", [128, 4, 256], F32)
        b.vnb = sb(nc, es, "b_vnb", [128, 4, 256], BF16)
        b.st = sb(nc, es, "b_st", [128, 16], F32)
        b.osb = sb(nc, es, "b_osb", [128, 4, D], F32)
        b.os2 = sb(nc, es, "b_os2", [128, 8], F32)
        b.pst = ps(nc, es, "b_pst", [128, 2, 1024], BF16)
        b.sring = ps(nc, es, "b_sr", [128, 4, 512], F32)
        b.oacc = ps(nc, es, "b_oacc", [128, 2, 512], F32)
        b.tps = b.pst[:, 0, :].bitcast(F32).rearrange("p (s d) -> p s d", d=128)
        return b

    def kts(self, i):
        return [kt for kt in range(18) if not (kt == 0 and i == 0) and not (kt == 17 and i == 7)]

    def groups(self, i):
        b = self.b
        out = []
        kts = self.kts(i)
        for g in range(3):
            units = []
            for n, kt in enumerate(kts):
                for kv in (0, 1):
                    head = kv * 3 + g
                    units.append({"acc": kv, "first": n == 0, "last": n == len(kts) - 1, "scale": SC_W, "kind": "G",
                                  "lhsT": b.ckx[kv * 64:(kv + 1) * 64, kt * 128:(kt + 1) * 128],
                                  "rhs": b.cq[kv * 64:(kv + 1) * 64, g, :], "bias": b.zero[:, 0:1],
                                  "dD": b.dw[:, kt, :], "dcoef": -float(SL_WIN[head] / SC_W),
                                  "vl": b.cv[:, :, :, :].rearrange("p t k d -> p (t k d)")[:, (kt * 2 + kv) * 65:(kt * 2 + kv) * 65 + 128]})
            gid = 3 * i + g
            out.append({"gid": gid, "par": 0, "units": units, "g": g, "i": i, "acc_prev": gid - 1 if gid >= 1 else None})
        return out

    MSEM = {"cq": "mA", "uv": "mD", "sp": "mD", "op": "mA", "ws": "mA"}

    def misc_begin(self, e, kind):
        S = self.S
        m = self.m["pe"]
        self.mk.append(kind)
        if m >= self.NSR:
            S.wait(e, self.sem(self.MSEM[self.mk[m - self.NSR]]), ("rel", m - self.NSR))
        self.m["pe"] = m + 1
        return self.b.sring[:, m % self.NSR, :], m

    def sp(self, e):
        S, b = self.S, self.b
        ld = self.sem("ldc")
        lst = [(b.dw[:, :, :], self.dw_d[:, :, :]), (b.identb[:, :], self.identb_d[:, :]), (b.identf[:, :], self.identf_d[:, :]),
               (b.gbc[:, :], self.g_pre.partition_broadcast(128)), (b.gpost[:, :], self.g_post.partition_broadcast(128)),
               (b.lng[:, :], self.ln_g.partition_broadcast(128)), (b.lnb[:, :], self.ln_b.partition_broadcast(128)),
               (b.esink[:, :], self.sinks.partition_broadcast(128)),
               (b.wsp[:, :, :], self.w_sp.rearrange("g t s -> t g s")), (b.bsp[:, :], self.b_sp[:, :])]
        for n, (dst, src) in enumerate(lst):
            ins = e.dma_start(out=dst, in_=src)
            S.sig(ins, ld, ("c", n), dma=True)
        for i in range(NQ):
            sl = i % 2
            if i >= 2:
                S.wait(e, self.sem("stx%d" % sl), i - 2)
            ins = e.dma_start(out=b.xt[:, sl, :, :], in_=self.x[i * 512:(i + 1) * 512, :].rearrange("(s p) d -> p s d", p=128))
            S.sig(ins, self.sem("ldx%d" % sl), i, dma=True)
            if i >= 1:
                S.wait(e, self.sem("tpc"), (3 * (i - 1) + 2, 1))
            lk = self.sem("ldk")
            for kt in self.kts(i):
                if 1 <= kt <= 16 and (kt - 1) % 4 != 0:
                    continue
                if kt == 0:
                    r, lt, n = j2idx(16 * i - 1) + (1,)
                elif kt == 17:
                    r, lt, n = j2idx(16 * (i + 1)) + (1,)
                else:
                    r, lt, n = (kt - 1) // 4, 4 * i, 4
                for kv in (0, 1):
                    ins = e.dma_start(out=b.ckx[kv * 64:(kv + 1) * 64, kt * 128:(kt + n) * 128],
                                      in_=self.G[12 + kv, r, :].rearrange("(a n) -> a n", n=NTOK)[:, lt * 128:(lt + n) * 128])
                    S.sig(ins, lk, ("k", i, kt, kv), dma=True)
                for kv in (0, 1):
                    ins = e.dma_start(out=b.cv[:, kt:kt + n, kv, 0:64],
                                      in_=self.G[14 + kv, r, :].rearrange("(p t d) -> p t d", p=128, t=32)[:, lt:lt + n, :])
                    S.sig(ins, lk, ("v", i, kt) if kv == 1 else ("v0", i, kt), dma=True)
            if i >= 1:
                S.wait(e, self.sem("ytr"), i - 1)
            ins = e.dma_start(out=b.y[:, :, 0:384], in_=self.ya_d[:, 4 * i:4 * i + 4, :])
            S.sig(ins, self.sem("ldy"), i, dma=True)

    def pool(self, e):
        S, b = self.S, self.b
        lw = self.sem("lw")
        ins = e.dma_start(out=b.wuv[:, :, :], in_=wview(self.w_in, 1152, 1664))
        S.sig(ins, lw, "uv", dma=True)
        for g in range(3):
            for kv in (0, 1):
                c0 = 1664 + (kv * 3 + g) * 64
                ins = e.dma_start(out=b.wcq[:, :, g, kv * 64:(kv + 1) * 64], in_=wview(self.w_in, c0, c0 + 64))
                S.sig(ins, lw, ("cq", g, kv), dma=True)
        ins = e.dma_start(out=b.wout[:, :, :], in_=wview(self.w_out, 0, D))
        S.sig(ins, lw, "out", dma=True)
        for i in range(NQ):
            sl = i % 2
            S.wait(e, self.sem("res"), i)
            ins = e.dma_start(out=self.x1[i * 512:(i + 1) * 512, :].rearrange("(s p) d -> p s d", p=128), in_=b.xt[:, sl, :, :])
            S.sig(ins, self.sem("stx%d" % sl), i, dma=True)
        for i in (NQ - 2, NQ - 1):
            if i >= 0:
                S.wait(e, self.sem("stx%d" % (i % 2)), i)

    def pe(self, e):
        S, b = self.S, self.b
        self.mk = []
        S.wait(e, self.sem("ldc"), ("c", 9))
        S.wait(e, self.sem("lw"), "out")
        for g4 in range(4):
            if g4 >= 1:
                S.wait(e, self.sem("mA"), ("ws", g4 - 1))
            ins = e.transpose(b.tps[:, 0, :], b.wsp[:, g4, :], b.identf[:, :])
            S.sig(ins, self.sem("wst"), g4)
        S.wait(e, self.sem("mA"), ("ws", 3))
        ins = e.transpose(b.tps[:, 1, 0:4], b.bsp[0:4, :], b.identf[0:4, 0:4])
        S.sig(ins, self.sem("wst"), 4)
        S.wait(e, self.sem("ones"), 0)
        for i in range(NQ):
            self.nt.pe(e, i, b.hb, b.pst, b.identb)
            self.nt.wait_hT(e, i)
            for g in range(3):
                bank, m = self.misc_begin(e, "cq")
                for kc in range(8):
                    ins = e.matmul(bank, lhsT=b.wcq[:, kc, g, :], rhs=b.hT[:, kc, :], start=(kc == 0), stop=(kc == 7))
                S.sig(ins, self.sem("mm"), m)
            for s in range(4):
                bank, m = self.misc_begin(e, "uv")
                for kc in range(8):
                    ins = e.matmul(bank, lhsT=b.hT[:, kc, s * 128:(s + 1) * 128], rhs=b.wuv[:, kc, :], start=(kc == 0), stop=(kc == 7))
                S.sig(ins, self.sem("mm"), m)
            for s in range(4):
                bank, m = self.misc_begin(e, "sp")
                S.wait(e, self.sem("vn"), (i, s))
                for g4 in range(4):
                    ins = e.matmul(bank[:, g4 * 64:(g4 + 1) * 64], lhsT=b.wsT[:, g4, :], rhs=b.vnb[:, s, g4 * 64:(g4 + 1) * 64],
                                   start=True, stop=True)
                S.sig(ins, self.sem("mm"), m)
            mlast = self.m["pe"] - 1
            for gi, g in enumerate(self.groups(i)):
                if gi == 0:
                    def w0(e, mlast=mlast, i=i):
                        for mm_ in range(max(0, mlast - self.NSR + 1), mlast + 1):
                            S.wait(e, self.sem(self.MSEM[self.mk[mm_]]), ("rel", mm_))
                        S.wait(e, self.sem("ldk"), ("v", i, self.kts(i)[-1] if self.kts(i)[-1] == 17 else 13))
                        S.wait(e, self.sem("mA"), ("rel", mlast - 8))
                    g["wait_pe"] = w0
                self.pe_group_pairs(e, g)
                self.pe_post(e, g)
            S.wait(e, self.sem("ldy"), i)
            S.wait(e, self.sem("gate"), (i, 3))
            S.wait(e, self.sem("tpc"), (3 * i + 2, 1))
            for kc in range(8):
                gq = (NQ + i) * 8 + kc
                S.wait(e, self.nt.n + "_ev", gq - 2 if kc >= 2 else i * 8 + 6 + kc)
                for s in range(4):
                    ins = e.transpose(b.pst[:, gq % 2, s * 128:(s + 1) * 128], b.y[:, s, kc * 128:(kc + 1) * 128], b.identb[:, :])
                S.sig(ins, self.nt.n + "_tr", gq)
            S.wait(e, self.nt.n + "_ev", (NQ + i) * 8 + 7)
            S.wait(e, self.sem("p"), self.gu["pe"] - 1)
            for s in range(4):
                for half in range(2):
                    bank, m = self.misc_begin(e, "op")
                    for kc in range(8):
                        ins = e.matmul(bank, lhsT=b.hT[:, kc, s * 128:(s + 1) * 128], rhs=b.wout[:, kc, half * 512:(half + 1) * 512],
                                       start=(kc == 0), stop=(kc == 7))
                    S.sig(ins, self.sem("mm"), m)
            S.sig(e.transpose(b.pst[:, 0, 0:128], b.identb[:, :], b.identb[:, :]), self.sem("ytr"), i)

    def act(self, e):
        S, b = self.S, self.b
        S.wait(e, self.sem("ldc"), ("c", 9))
        ins = e.activation(out=b.esink[:, :], in_=b.esink[:, :], func=AF.Exp)
        S.sig(ins, self.sem("es"), 0)
        for g4 in range(4):
            S.wait(e, self.sem("wst"), g4)
            ins = e.activation(out=b.wsT[:, g4, :], in_=b.tps[:, 0, :], func=AF.Copy)
            S.sig(ins, self.sem("mA"), ("ws", g4))
        S.wait(e, self.sem("wst"), 4)
        ins = e.activation(out=b.bsT[:, :], in_=b.tps[:, 1, 0:4], func=AF.Copy)
        S.sig(ins, self.sem("es"), 1)
        m = 0
        for i in range(NQ):
            self.nt.act(e, i, b.pst, b.hT,
                        wait_hT_free=lambda: (S.wait(e, self.sem("mm"), self.m_last_op) if i >= 1 else None), rstd=b.rstd)
            for g in range(3):
                S.wait(e, self.sem("mm"), m)
                if i >= 1 and g == 0:
                    S.wait(e, self.sem("pv"), self.gu["act"] - 1)
                ins = e.activation(out=b.cq[:, g, :], in_=b.sring[:, m % self.NSR, :], func=AF.Copy)
                S.sig(ins, self.sem("mA"), ("rel", m))
                m += 1
            m += 8
            S.wait(e, self.sem("lnv"), i)
            ins = e.activation(out=b.st[:, 8:12], in_=b.st[:, 8:12], func=AF.Sqrt)
            S.sig(ins, self.sem("lnq"), i)
            for g in self.groups(i):
                self.act_group(e, g)
            for kc in range(8):
                gq = (NQ + i) * 8 + kc
                S.wait(e, self.nt.n + "_tr", gq)
                ins = e.activation(out=b.hT[:, kc, :], in_=b.pst[:, gq % 2, 0:512], func=AF.Copy)
                S.sig(ins, self.nt.n + "_ev", gq)
            for s in range(4):
                for half in range(2):
                    S.wait(e, self.sem("mm"), m)
                    if s == 0 and half == 0 and i >= 1:
                        S.wait(e, self.sem("res"), i - 1)
                    ins = e.activation(out=b.osb[:, s, half * 512:(half + 1) * 512], in_=b.sring[:, m % self.NSR, :], func=AF.Copy)
                    S.sig(ins, self.sem("mA"), ("rel", m))
                    self.m_last_op = m
                    m += 1
            S.wait(e, self.sem("onv"), i)
            ins = e.activation(out=b.os2[:, 4:8], in_=b.os2[:, 4:8], func=AF.Sqrt)
            S.sig(ins, self.sem("onq"), i)

    def consume(self, e, g, c):
        S, b = self.S, self.b
        f = self.sem("f")
        head = c * 3 + g["g"]
        ins = e.tensor_scalar(out=b.r[:, 0:4], in0=b.tps[:, :, 64], scalar1=b.esink[:, head:head + 1], scalar2=None, op0=ALU.add)
        S.fence(e, ins, f)
        ins = e.reciprocal(out=b.r[:, 0:4], in_=b.r[:, 0:4])
        S.fence(e, ins, f)
        for s in range(4):
            ins = e.tensor_scalar(out=b.y[:, s, 640 + head * 64:640 + (head + 1) * 64], in0=b.tps[:, s, 0:64],
                                  scalar1=b.r[:, s:s + 1], scalar2=None, op0=ALU.mult)
        S.sig(ins, self.sem("tpc"), (g["gid"], c))

    def dve(self, e):
        S, b = self.S, self.b
        f = self.sem("f")
        ins = e.memset(b.cv[:, 18, :, :], 0.0)
        ins = e.memset(b.cv[:, 0:18, :, 64:65], 1.0)
        ins = e.memset(b.zero[:, :], 0.0)
        S.sig(ins, self.sem("ones"), 0)
        S.wait(e, self.sem("ldc"), ("c", 9))
        S.wait(e, self.sem("es"), 1)
        m = 0
        for i in range(NQ):
            sl = i % 2
            xt = b.xt[:, sl, :, :]
            self.nt.dve(e, i, xt, b.gbc, b.hb, b.junk, b.ss, b.rstd,
                        wait_x=lambda: S.wait(e, self.sem("ldx%d" % sl), i),
                        wait_hb_free=lambda: (self.nt.wait_tr_done(e, i - 1) if i >= 1 else None))
            m += 3
            for s in range(4):
                S.wait(e, self.sem("mm"), m)
                if i >= 1 and s == 0:
                    S.wait(e, self.sem("gate"), (i - 1, 3))
                ins = e.tensor_copy(out=b.u[:, s, :], in_=b.sring[:, m % self.NSR, 0:256])
                ins = e.tensor_copy(out=b.vt[:, s, :], in_=b.sring[:, m % self.NSR, 256:512])
                S.sig(ins, self.sem("mD"), ("rel", m))
                S.wait(e, self.sem("mD"), ("rel", m))
                ins = e.tensor_reduce(out=b.st[:, s:s + 1], in_=b.vt[:, s, :], axis=AX.X, op=ALU.add)
                ins = e.tensor_tensor(out=b.junk[:, 0:256], in0=b.vt[:, s, :], in1=b.vt[:, s, :], op=ALU.mult)
                S.fence(e, ins, f)
                ins = e.tensor_reduce(out=b.st[:, 4 + s:5 + s], in_=b.junk[:, 0:256], axis=AX.X, op=ALU.add)
                S.fence(e, ins, f)
                m += 1
            ins = e.tensor_scalar(out=b.st[:, 0:4], in0=b.st[:, 0:4], scalar1=1.0 / 256, scalar2=None, op0=ALU.mult)
            S.fence(e, ins, f)
            ins = e.tensor_tensor(out=b.st[:, 12:16], in0=b.st[:, 0:4], in1=b.st[:, 0:4], op=ALU.mult)
            S.fence(e, ins, f)
            ins = e.scalar_tensor_tensor(out=b.st[:, 8:12], in0=b.st[:, 4:8], scalar=1.0 / 256, in1=b.st[:, 12:16],
                                         op0=ALU.mult, op1=ALU.subtract)
            S.fence(e, ins, f)
            ins = e.tensor_scalar(out=b.st[:, 8:12], in0=b.st[:, 8:12], scalar1=EPS, scalar2=None, op0=ALU.add)
            S.sig(ins, self.sem("lnv"), i)
            S.wait(e, self.sem("lnq"), i)
            ins = e.reciprocal(out=b.st[:, 8:12], in_=b.st[:, 8:12])
            S.fence(e, ins, f)
            for s in range(4):
                ins = e.tensor_scalar(out=b.vt[:, s, :], in0=b.vt[:, s, :], scalar1=b.st[:, s:s + 1], scalar2=b.st[:, 8 + s:9 + s],
                                      op0=ALU.subtract, op1=ALU.mult)
                S.fence(e, ins, f)
                ins = e.tensor_tensor(out=b.vt[:, s, :], in0=b.vt[:, s, :], in1=b.lng[:, :], op=ALU.mult)
                S.fence(e, ins, f)
                ins = e.tensor_tensor(out=b.vnb[:, s, :], in0=b.vt[:, s, :], in1=b.lnb[:, :], op=ALU.add)
                S.sig(ins, self.sem("vn"), (i, s))
            for s in range(4):
                S.wait(e, self.sem("mm"), m)
                if i >= 1 and s == 0:
                    S.wait(e, self.sem("ytr"), i - 1)
                for g4 in range(4):
                    ins = e.scalar_tensor_tensor(out=b.y[:, s, 384 + g4 * 64:384 + (g4 + 1) * 64],
                                                 in0=b.sring[:, m % self.NSR, g4 * 64:(g4 + 1) * 64], scalar=b.bsT[:, g4:g4 + 1],
                                                 in1=b.u[:, s, g4 * 64:(g4 + 1) * 64], op0=ALU.add, op1=ALU.mult)
                S.sig(ins, self.sem("mD"), ("rel", m))
                S.wait(e, self.sem("mD"), ("rel", m))
                S.sig(e.tensor_copy(out=b.r[:, 4:5], in_=b.zero[:, 0:1]), self.sem("gate"), (i, s))
                m += 1
            for g in self.groups(i):
                self.dve_group(e, g, self.consume)
            for s in range(4):
                S.wait(e, self.sem("mA"), ("rel", m + 1))
                ins = e.tensor_tensor(out=b.junk[:, :], in0=b.osb[:, s, :], in1=b.osb[:, s, :], op=ALU.mult)
                S.fence(e, ins, f)
                ins = e.tensor_reduce(out=b.os2[:, s:s + 1], in_=b.junk[:, :], axis=AX.X, op=ALU.add)
                S.fence(e, ins, f)
                m += 2
            ins = e.tensor_scalar(out=b.os2[:, 4:8], in0=b.os2[:, 0:4], scalar1=1.0 / D, scalar2=EPS, op0=ALU.mult, op1=ALU.add)
            S.sig(ins, self.sem("onv"), i)
            S.wait(e, self.sem("onq"), i)
            ins = e.reciprocal(out=b.os2[:, 4:8], in_=b.os2[:, 4:8])
            S.fence(e, ins, f)
            for s in range(4):
                ins = e.tensor_tensor(out=b.osb[:, s, :], in0=b.osb[:, s, :], in1=b.gpost[:, :], op=ALU.mult)
                S.fence(e, ins, f)
                ins = e.scalar_tensor_tensor(out=xt[:, s, :], in0=b.osb[:, s, :], scalar=b.os2[:, 4 + s:5 + s], in1=xt[:, s, :],
                                             op0=ALU.mult, op1=ALU.add)
            S.sig(ins, self.sem("res"), i)
            S.wait(e, self.sem("res"), i)


NMR = 6


class TokPhase(Phase):
    def reset(self):
        self.m = 0
        self.mk = []

    def ring(self, e, rel_sem):
        S = self.S
        m = self.m
        self.mk.append(rel_sem)
        if m >= NMR:
            S.wait(e, self.sem(self.mk[m - NMR]), ("rel", m - NMR))
        self.m = m + 1
        return self.b.ring[:, m % NMR, :], m

    def dve_postnorm(self, e, i, s, xt):
        S, b = self.S, self.b
        f = self.sem("f")
        o = b.osb[:, s % 2, :]
        ins = e.tensor_tensor(out=b.junk[:, :], in0=o, in1=o, op=ALU.mult)
        S.fence(e, ins, f)
        ins = e.tensor_reduce(out=b.os2[:, 0:1], in_=b.junk[:, :], axis=AX.X, op=ALU.add)
        S.fence(e, ins, f)
        ins = e.tensor_scalar(out=b.os2[:, 1:2], in0=b.os2[:, 0:1], scalar1=1.0 / D, scalar2=EPS, op0=ALU.mult, op1=ALU.add)
        S.sig(ins, self.sem("onv"), (i, s))
        S.wait(e, self.sem("onq"), (i, s))
        ins = e.reciprocal(out=b.os2[:, 1:2], in_=b.os2[:, 1:2])
        ins2 = e.tensor_tensor(out=o, in0=o, in1=b.gpost[:, :], op=ALU.mult)
        S.fence(e, ins2, f)
        ins = e.scalar_tensor_tensor(out=xt[:, s, :], in0=o, scalar=b.os2[:, 1:2], in1=xt[:, s, :], op0=ALU.mult, op1=ALU.add)
        S.sig(ins, self.sem("res"), (i, s))
        S.wait(e, self.sem("res"), (i, s))

    def act_postnorm(self, e, i, s):
        S, b = self.S, self.b
        S.wait(e, self.sem("onv"), (i, s))
        ins = e.activation(out=b.os2[:, 1:2], in_=b.os2[:, 1:2], func=AF.Sqrt)
        S.sig(ins, self.sem("onq"), (i, s))

    def pool_store(self, e, xo):
        S, b = self.S, self.b
        for i in range(NQ):
            S.wait(e, self.sem("res"), (i, 3))
            ins = e.dma_start(out=xo[i * 512:(i + 1) * 512, :].rearrange("(s p) d -> p s d", p=128), in_=b.xt[:, :, :])
            S.sig(ins, self.sem("stx"), i, dma=True)
        S.wait(e, self.sem("stx"), NQ - 1)

    def sp_loadx(self, e, i, xin):
        S, b = self.S, self.b
        if i >= 1:
            S.wait(e, self.sem("stx"), i - 1)
        ins = e.dma_start(out=b.xt[:, :, :], in_=xin[i * 512:(i + 1) * 512, :].rearrange("(s p) d -> p s d", p=128))
        S.sig(ins, self.sem("ldx"), i, dma=True)


class P3a(TokPhase):
    NBLK = DFF // 128

    def __init__(self, prog, tag, x1, w_fi, w_fo, g_pre, g_post, identb, x2):
        super().__init__(prog, tag)
        self.x1, self.w_fi, self.w_fo, self.g_pre, self.g_post, self.identb_d, self.x2 = x1, w_fi, w_fo, g_pre, g_post, identb, x2
        self.nt = NormT(self, "nt")

    def alloc(self, nc, es):
        b = Bufs()
        b.wfi = sb(nc, es, "f_wfi", [128, 8, 2 * DFF], BF16)
        b.wfo = sb(nc, es, "f_wfo", [128, self.NBLK, D], BF16)
        b.gbc = sb(nc, es, "f_gbc", [128, D], F32)
        b.gpost = sb(nc, es, "f_gpost", [128, D], F32)
        b.identb = sb(nc, es, "f_idb", [128, 128], BF16)
        b.xt = sb(nc, es, "f_xt", [128, 4, D], F32)
        b.hb = sb(nc, es, "f_hb", [128, 4, D], BF16)
        b.ss = sb(nc, es, "f_ss", [128, 4], F32)
        b.rstd = sb(nc, es, "f_rstd", [128, 4], F32)
        b.hT = sb(nc, es, "f_hT", [128, 8, 512], BF16)
        b.actT = sb(nc, es, "f_actT", [128, self.NBLK, 512], BF16)
        b.sg = sb(nc, es, "f_sg", [128, 2, 512], F32)
        b.junk = b.sg[:, :, :].rearrange("p a b -> p (a b)")
        b.osb = sb(nc, es, "f_osb", [128, 2, D], F32)
        b.os2 = sb(nc, es, "f_os2", [128, 4], F32)
        b.pst = ps(nc, es, "f_pst", [128, 2, 1024], BF16)
        b.ring = ps(nc, es, "f_ring", [128, NMR, 512], F32)
        return b

    def sp(self, e):
        S, b = self.S, self.b
        ld = self.sem("ldc")
        for n, (dst, src) in enumerate([(b.identb[:, :], self.identb_d[:, :]), (b.gbc[:, :], self.g_pre.partition_broadcast(128)),
                                        (b.gpost[:, :], self.g_post.partition_broadcast(128))]):
            ins = e.dma_start(out=dst, in_=src)
            S.sig(ins, ld, ("c", n), dma=True)
        for i in range(NQ):
            self.sp_loadx(e, i, self.x1)

    def pool(self, e):
        S, b = self.S, self.b
        lw = self.sem("lw")
        nch = 8
        w = 2 * DFF // nch
        for k in range(nch):
            ins = e.dma_start(out=b.wfi[:, :, k * w:(k + 1) * w], in_=wview(self.w_fi, k * w, (k + 1) * w))
            S.sig(ins, lw, ("fi", k), dma=True)
        for k in range(2):
            ins = e.dma_start(out=b.wfo[:, :, k * 512:(k + 1) * 512], in_=wview(self.w_fo, k * 512, (k + 1) * 512))
            S.sig(ins, lw, ("fo", k), dma=True)
        self.pool_store(e, self.x2)

    def pe(self, e):
        S, b = self.S, self.b
        S.wait(e, self.sem("ldc"), ("c", 2))
        S.wait(e, self.sem("lw"), ("fo", 1))
        for i in range(NQ):
            self.nt.pe(e, i, b.hb, b.pst, b.identb)
            self.nt.wait_hT(e, i)
            for blk in range(self.NBLK):
                for part, rel in ((0, "mA"), (1, "mD")):
                    bank, m = self.ring(e, rel)
                    c0 = part * DFF + blk * 128
                    for kc in range(8):
                        ins = e.matmul(bank, lhsT=b.wfi[:, kc, c0:c0 + 128], rhs=b.hT[:, kc, :], start=(kc == 0), stop=(kc == 7))
                    S.sig(ins, self.sem("mm"), m)
            S.wait(e, self.sem("mD"), ("rel", self.m - 1))
            for s in range(4):
                for half in range(2):
                    bank, m = self.ring(e, "mA")
                    for blk in range(self.NBLK):
                        ins = e.matmul(bank, lhsT=b.actT[:, blk, s * 128:(s + 1) * 128], rhs=b.wfo[:, blk, half * 512:(half + 1) * 512],
                                       start=(blk == 0), stop=(blk == self.NBLK - 1))
                    S.sig(ins, self.sem("mm"), m)

    def act(self, e):
        S, b = self.S, self.b
        m = 0
        nb = 0
        for i in range(NQ):
            self.nt.act(e, i, b.pst, b.hT,
                        wait_hT_free=lambda: (S.wait(e, self.sem("mm"), m - 9) if i >= 1 else None), rstd=b.rstd)
            for blk in range(self.NBLK):
                S.wait(e, self.sem("mm"), m)
                if nb >= 2:
                    S.wait(e, self.sem("mD"), ("rel", self.sgrel[nb - 2]))
                ins = e.activation(out=b.sg[:, nb % 2, :], in_=b.ring[:, m % NMR, :], func=AF.Silu)
                S.sig(ins, self.sem("mA"), ("rel", m))
                self.sgrel.append(m + 1)
                nb += 1
                m += 2
            for s in range(4):
                for half in range(2):
                    S.wait(e, self.sem("mm"), m)
                    if half == 0 and (i, s) >= (0, 2):
                        ps_ = (i, s - 2) if s >= 2 else (i - 1, s + 2)
                        S.wait(e, self.sem("res"), ps_)
                    ins = e.activation(out=b.osb[:, s % 2, half * 512:(half + 1) * 512], in_=b.ring[:, m % NMR, :], func=AF.Copy)
                    S.sig(ins, self.sem("mA"), ("rel", m))
                    m += 1
                self.act_postnorm(e, i, s)

    def reset(self):
        super().reset()
        self.sgrel = []

    def dve(self, e):
        S, b = self.S, self.b
        S.wait(e, self.sem("ldc"), ("c", 2))
        m = 0
        nb = 0
        for i in range(NQ):
            self.nt.dve(e, i, b.xt, b.gbc, b.hb, b.junk, b.ss, b.rstd,
                        wait_x=lambda: S.wait(e, self.sem("ldx"), i),
                        wait_hb_free=lambda: (self.nt.wait_tr_done(e, i - 1) if i >= 1 else None))
            for blk in range(self.NBLK):
                S.wait(e, self.sem("mm"), m + 1)
                S.wait(e, self.sem("mA"), ("rel", m))
                if i >= 1 and blk == 0:
                    S.wait(e, self.sem("mm"), m - 1)
                ins = e.tensor_tensor(out=b.actT[:, blk, :], in0=b.sg[:, nb % 2, :], in1=b.ring[:, (m + 1) % NMR, :], op=ALU.mult)
                S.sig(ins, self.sem("mD"), ("rel", m + 1))
                nb += 1
                m += 2
            for s in range(4):
                S.wait(e, self.sem("mA"), ("rel", m + 1))
                self.dve_postnorm(e, i, s, b.xt)
                m += 2


class P3b(TokPhase):
    def __init__(self, prog, tag, x2, p, w_up, w_gate, g_gate, g_post, identb, x3):
        super().__init__(prog, tag)
        self.x2, self.p, self.w_up, self.w_gate, self.g_gate, self.g_post, self.identb_d, self.x3 = x2, p, w_up, w_gate, g_gate, g_post, identb, x3
        self.nt = NormT(self, "nt")

    def alloc(self, nc, es):
        b = Bufs()
        b.wg = sb(nc, es, "e_wg", [128, 8, D], BF16)
        b.wu = sb(nc, es, "e_wu", [128, 2, D], BF16)
        b.gbc = sb(nc, es, "e_gbc", [128, D], F32)
        b.gpost = sb(nc, es, "e_gpost", [128, D], F32)
        b.identb = sb(nc, es, "e_idb", [128, 128], BF16)
        b.xt = sb(nc, es, "e_xt", [128, 4, D], F32)
        b.pt = sb(nc, es, "e_pt", [128, 4, 256], F32)
        b.pb = sb(nc, es, "e_pb", [128, 4, 256], BF16)
        b.pT = sb(nc, es, "e_pT", [128, 2, 512], BF16)
        b.hb = sb(nc, es, "e_hb", [128, 4, D], BF16)
        b.junk = sb(nc, es, "e_junk", [128, D], F32)
        b.ss = sb(nc, es, "e_ss", [128, 4], F32)
        b.rstd = sb(nc, es, "e_rstd", [128, 4], F32)
        b.hT = sb(nc, es, "e_hT", [128, 8, 512], BF16)
        b.sgm = sb(nc, es, "e_sgm", [128, 2, D], F32)
        b.osb = sb(nc, es, "e_osb", [128, 2, D], F32)
        b.os2 = sb(nc, es, "e_os2", [128, 4], F32)
        b.pst = ps(nc, es, "e_pst", [128, 2, 1024], BF16)
        b.ring = ps(nc, es, "e_ring", [128, NMR, 512], F32)
        return b

    def sp(self, e):
        S, b = self.S, self.b
        ld = self.sem("ldc")
        for n, (dst, src) in enumerate([(b.identb[:, :], self.identb_d[:, :]), (b.gbc[:, :], self.g_gate.partition_broadcast(128)),
                                        (b.gpost[:, :], self.g_post.partition_broadcast(128))]):
            ins = e.dma_start(out=dst, in_=src)
            S.sig(ins, ld, ("c", n), dma=True)
        for i in range(NQ):
            self.sp_loadx(e, i, self.x2)
            if i >= 1:
                S.wait(e, self.sem("pb"), i - 1)
            ins = e.dma_start(out=b.pt[:, :, :], in_=self.p[i * 512:(i + 1) * 512, :].rearrange("(s p) d -> p s d", p=128))
            S.sig(ins, self.sem("ldp"), i, dma=True)

    def pool(self, e):
        S, b = self.S, self.b
        lw = self.sem("lw")
        for k in range(2):
            ins = e.dma_start(out=b.wg[:, :, k * 512:(k + 1) * 512], in_=wview(self.w_gate, k * 512, (k + 1) * 512))
            S.sig(ins, lw, ("g", k), dma=True)
        ins = e.dma_start(out=b.wu[:, :, :], in_=wview(self.w_up, 0, D))
        S.sig(ins, lw, "u", dma=True)
        self.pool_store(e, self.x3)

    def pe(self, e):
        S, b = self.S, self.b
        S.wait(e, self.sem("ldc"), ("c", 2))
        S.wait(e, self.sem("lw"), "u")
        for i in range(NQ):
            self.nt.pe(e, i, b.hb, b.pst, b.identb)
            S.wait(e, self.sem("pb"), i)
            for kc in range(2):
                gq = (NQ + i) * 8 + kc
                S.wait(e, self.nt.n + "_ev", i * 8 + 6 + kc)
                for s in range(4):
                    ins = e.transpose(b.pst[:, gq % 2, s * 128:(s + 1) * 128], b.pb[:, s, kc * 128:(kc + 1) * 128], b.identb[:, :])
                S.sig(ins, self.nt.n + "_tr", gq)
            S.wait(e, self.nt.n + "_ev", (NQ + i) * 8 + 1)
            for s in range(4):
                for half in range(2):
                    bank, m = self.ring(e, "mD")
                    for kc in range(2):
                        ins = e.matmul(bank, lhsT=b.pT[:, kc, s * 128:(s + 1) * 128], rhs=b.wu[:, kc, half * 512:(half + 1) * 512],
                                       start=(kc == 0), stop=(kc == 1))
                    S.sig(ins, self.sem("mm"), m)
                for half in range(2):
                    bank, m = self.ring(e, "mA")
                    for kc in range(8):
                        ins = e.matmul(bank, lhsT=b.hT[:, kc, s * 128:(s + 1) * 128], rhs=b.wg[:, kc, half * 512:(half + 1) * 512],
                                       start=(kc == 0), stop=(kc == 7))
                    S.sig(ins, self.sem("mm"), m)

    def act(self, e):
        S, b = self.S, self.b
        m = 0
        for i in range(NQ):
            self.nt.act(e, i, b.pst, b.hT,
                        wait_hT_free=lambda: (S.wait(e, self.sem("mm"), m - 1) if i >= 1 else None), rstd=b.rstd)
            for kc in range(2):
                gq = (NQ + i) * 8 + kc
                S.wait(e, self.nt.n + "_tr", gq)
                if i >= 1 and kc == 0:
                    S.wait(e, self.sem("mm"), m - 3)
                ins = e.activation(out=b.pT[:, kc, :], in_=b.pst[:, gq % 2, 0:512], func=AF.Copy)
                S.sig(ins, self.nt.n + "_ev", gq)
            for s in range(4):
                for half in range(2):
                    mz = m + 2 + half
                    S.wait(e, self.sem("mm"), mz)
                    if half == 0 and (i, s) >= (0, 2):
                        ps_ = (i, s - 2) if s >= 2 else (i - 1, s + 2)
                        S.wait(e, self.sem("eg"), ps_)
                    ins = e.activation(out=b.sgm[:, s % 2, half * 512:(half + 1) * 512], in_=b.ring[:, mz % NMR, :], func=AF.Sigmoid)
                    S.sig(ins, self.sem("mA"), ("rel", mz))
                m += 4
                self.act_postnorm(e, i, s)

    def dve(self, e):
        S, b = self.S, self.b
        S.wait(e, self.sem("ldc"), ("c", 2))
        m = 0
        for i in range(NQ):
            self.nt.dve(e, i, b.xt, b.gbc, b.hb, b.junk, b.ss, b.rstd,
                        wait_x=lambda: S.wait(e, self.sem("ldx"), i),
                        wait_hb_free=lambda: (self.nt.wait_tr_done(e, i - 1) if i >= 1 else None))
            S.wait(e, self.sem("ldp"), i)
            if i >= 1:
                S.wait(e, self.nt.n + "_tr", (NQ + i - 1) * 8 + 1)
            ins = e.tensor_copy(out=b.pb[:, :, :], in_=b.pt[:, :, :])
            S.sig(ins, self.sem("pb"), i)
            for s in range(4):
                for half in range(2):
                    S.wait(e, self.sem("mm"), m + half)
                    S.wait(e, self.sem("mA"), ("rel", m + 2 + half))
                    ins = e.tensor_tensor(out=b.osb[:, s % 2, half * 512:(half + 1) * 512], in0=b.sgm[:, s % 2, half * 512:(half + 1) * 512],
                                          in1=b.ring[:, (m + half) % NMR, :], op=ALU.mult)
                    S.sig(ins, self.sem("mD"), ("rel", m + half))
                S.wait(e, self.sem("mD"), ("rel", m + 1))
                S.sig(e.memset(b.os2[:, 2:3], 0.0), self.sem("eg"), (i, s))
                m += 4
                self.dve_postnorm(e, i, s, b.xt)


class AG(Phase):
    def __init__(self, prog, tag, E, G):
        super().__init__(prog, tag)
        self.E, self.G = E, G

    def pool(self, e):
        S = self.S
        for k in range(NCH):
            ins = e.collective_compute("AllGather", ALU.bypass, replica_groups=[[0, 1, 2, 3], [4, 5, 6, 7]],
                                       ins=[self.E[k, :].rearrange("(p c) -> p c", c=2048)],
                                       outs=[self.G[k, :, :].rearrange("r (p c) -> (r p) c", c=2048)])
            S.sig(ins, "cc", k)
        S.wait(e, "cc", NCH - 1)


def _local_tokens(c):
    ii = np.arange(8)[:, None]
    t = np.arange(512)[None, :]
    return (2048 * ii + 512 * c + t).reshape(-1)


WKEYS = [("g_pre_mix", [D]), ("w_in", [D, 2304]), ("g_diff_sub", [64]), ("gmlp_ln_g", [256]), ("gmlp_ln_b", [256]),
         ("w_spatial", [4, 128, 128]), ("b_spatial", [4, 128]), ("swa_sinks", [6]), ("w_out", [D, D]), ("g_post_mix", [D]),
         ("g_pre_ffn", [D]), ("w_ffn_in", [D, 2 * DFF]), ("w_ffn_out", [DFF, D]), ("g_post_ffn", [D]),
         ("w_ple_up", [256, D]), ("w_ple_gate", [D, D]), ("g_ple_gate", [D]), ("g_ple_post", [D])]


def build_fused(nl=L):
    P = Prog()
    ncol = cidx_map()[1]
    x = P.din("x", [NTOK, D], F32)
    pp = P.din("p", [L, NTOK, 256], F32)
    ctab = P.din("ctab", [128, ncol], F32)
    dn = P.din("dn", [128, 16, 512], F32)
    dw = P.din("dw", [128, 18, 512], F32)
    kaug = P.din("kaug", [6, 4, SEQ], BF16)
    qaug = P.din("qaug", [6, 2, 4, NTOK], BF16)
    identb = P.din("identb", [128, 128], BF16)
    identf = P.din("identf", [128, 128], F32)
    lamv = P.din("lamv", [L, 4, 32], F32)
    lamc = P.din("lamc", [L, 128, 2], F32)
    W = {k: P.din(k, [L] + shp, F32) for k, shp in WKEYS}
    E = P.dint("E", [NCH, CHE], BF16)
    G = P.dint("G", [NCH, 4, CHE], BF16)
    Qs = P.dint("Qs", [384, NTOK], BF16)
    ya_d = P.dint("ya_d", [128, 32, 384], BF16)
    x1 = P.dint("x1", [NTOK, D], F32)
    x2 = P.dint("x2", [NTOK, D], F32)
    x3 = P.dint("x3", [NTOK, D], F32)
    xo = P.dout("xo", [NTOK, D], F32)
    for l in range(nl):
        xin = x if l == 0 else x3
        xout = xo if l == nl - 1 else x3
        P.phases.append(P1(P, "p1", xin, W["w_in"][l], W["g_pre_mix"][l], identb, Qs, E))
        P.phases.append(AG(P, "ag", E, G))
        P.phases.append(P2a(P, "a", G, Qs, ctab, dn, kaug, qaug, identf, lamv[l], lamc[l], W["g_diff_sub"][l], ya_d))
        P.phases.append(P2b(P, "b", xin, W["w_in"][l], W["g_pre_mix"][l], G, ya_d, W["gmlp_ln_g"][l], W["gmlp_ln_b"][l],
                            W["w_spatial"][l], W["b_spatial"][l], W["swa_sinks"][l], W["w_out"][l], W["g_post_mix"][l],
                            dw, identb, identf, x1))
        P.phases.append(P3a(P, "f", x1, W["w_ffn_in"][l], W["w_ffn_out"][l], W["g_pre_ffn"][l], W["g_post_ffn"][l], identb, x2))
        P.phases.append(P3b(P, "e", x2, pp[l], W["w_ple_up"][l], W["w_ple_gate"][l], W["g_ple_gate"][l], W["g_ple_post"][l],
                            identb, xout))
    return P.build()


_PROGS = {}


def kernel(**inputs):
    if "F" not in _PROGS:
        _PROGS["F"] = build_fused()
    consts = host_consts()
    tabs = [host_tables(c) for c in range(4)]
    x = np.asarray(inputs["x"], np.float32)
    p = np.asarray(inputs["p"], np.float32)
    lamv = np.stack([np.stack([np.asarray(inputs[k][l], np.float32) for k in ("lam_q1", "lam_k1", "lam_q2", "lam_k2")])
                     for l in range(L)])
    lamc = np.stack([np.tile(np.array([[lam_init(l), 1.0 - lam_init(l)]], np.float32), (128, 1)) for l in range(L)])
    shared = {k: np.ascontiguousarray(np.asarray(inputs[k], np.float32)) for k, _ in WKEYS}
    shared.update(kaug=consts["kaug"], qaug=consts["qaug"], identb=consts["identb"], identf=consts["identf"],
                  lamv=lamv, lamc=lamc)
    cores = list(range(8))
    in_maps = []
    for core in cores:
        bb, c = core // 4, core % 4
        tok = _local_tokens(c)
        m = dict(shared)
        m["x"] = np.ascontiguousarray(x[bb][tok])
        m["p"] = np.ascontiguousarray(p[:, bb][:, tok])
        m["ctab"], m["dn"], m["dw"] = tabs[c]["ctab"], tabs[c]["dn"], tabs[c]["dw"]
        in_maps.append(m)
    res = run_bass_kernel_spmd(_PROGS["F"], in_maps, core_ids=cores)
    out = np.empty((NB, SEQ, D), np.float32)
    for core in cores:
        out[core // 4][_local_tokens(core % 4)] = np.asarray(res.results[core]["xo"], np.float32)
    return out
```

```python
import numpy as np
import ml_dtypes
from contextlib import ExitStack
import concourse.bass as bass
import concourse.mybir as mybir
from concourse.bass_utils import run_bass_kernel_spmd

F32 = mybir.dt.float32
BF16 = mybir.dt.bfloat16
AF = mybir.ActivationFunctionType
ALU = mybir.AluOpType
AX = mybir.AxisListType
NPBF = ml_dtypes.bfloat16

D = 1024
SEQ = 16384
NB = 2
L = 4
NTOK = 4096
NQ = 8
DFF = 2816
EPS = 1e-6
SC_A = 32 ** -0.5
SC_W = 64 ** -0.5
_k = np.arange(1, 13, dtype=np.float64)
_sl = np.exp2(-8.0 * _k / 12.0)
SL_DIFF = _sl[6:]
SL_WIN = _sl[:6]
CUT = 30.0
BIG = 30000.0
LA = 2
NSR = 3
NPR = 4


def lam_init(l):
    return 0.8 - 0.6 * float(np.exp(-0.3 * l))


class Dummy:
    def __getattr__(self, n):
        return self

    def __call__(self, *a, **k):
        return self

    def __getitem__(self, k):
        return self

    def __enter__(self):
        return self

    def __exit__(self, *a):
        return False


class Sync:
    def __init__(self):
        self.dry = True
        self.total = {}
        self.count = {}
        self.seq = {}
        self.rpos = {}
        self.handles = {}
        self.scope = None

    def _k(self, key, glob):
        return key if glob else (self.scope, key)

    def sig(self, ins, sem, key, dma=False, glob=False):
        inc = 16 if dma else 1
        k = (sem, self._k(key, glob))
        if self.dry:
            assert k not in self.count, k
            t = self.total.get(sem, 0) + inc
            self.total[sem] = t
            self.count[k] = t
            self.seq.setdefault(sem, []).append((k, t))
        else:
            i = self.rpos.get(sem, 0)
            kk, t = self.seq[sem][i]
            assert kk == k, (kk, k)
            self.rpos[sem] = i + 1
            ins.then_inc(self.handles[sem], inc)

    def wait(self, e, sem, key, glob=False):
        if not self.dry:
            e.wait_ge(self.handles[sem], self.count[(sem, self._k(key, glob))])

    def fence(self, e, ins, sem):
        if self.dry:
            t = self.total.get(sem, 0) + 1
            self.total[sem] = t
            self.seq.setdefault(sem, []).append((None, t))
        else:
            i = self.rpos.get(sem, 0)
            kk, t = self.seq[sem][i]
            assert kk is None, kk
            self.rpos[sem] = i + 1
            ins.then_inc(self.handles[sem], 1)
            e.wait_ge(self.handles[sem], t)


def band_tiles(h, i):
    out = []
    for j in range(128):
        if j < 16 * i:
            dmin = 2048 * i - (128 * j + 127)
        elif j >= 16 * (i + 1):
            dmin = 128 * j - (2048 * i + 2047)
        else:
            dmin = 0
        if SL_DIFF[h] * dmin <= CUT:
            out.append(j)
    return out


def j2idx(j):
    T = j // 4
    sub = j % 4
    r = T % 4
    ii = T // 4
    return r, 4 * ii + sub


_CIDX = None


def cidx_map():
    global _CIDX
    if _CIDX is None:
        m = {}
        n = 1
        for h in range(6):
            for i in range(NQ):
                for j in band_tiles(h, i):
                    if j < 16 * i or j >= 16 * (i + 1):
                        m[(h, i, j)] = n
                        n += 1
        _CIDX = (m, n)
    return _CIDX


def split_bf(v):
    hi = np.float32(np.asarray(v, np.float32).astype(NPBF).astype(np.float32))
    lo = np.float32(np.asarray(np.float32(v) - hi, np.float32).astype(NPBF).astype(np.float32))
    return hi, lo


def host_tables(c):
    m, n = cidx_map()
    ct = np.zeros((n,), np.float32)
    for (h, i, j), col in m.items():
        qc = 2048 * i + 512 * c + 256
        ct[col] = -SL_DIFF[h] * abs(128 * j - qc)
    ctab = np.ascontiguousarray(np.broadcast_to(ct[None, :], (128, n))).astype(np.float32)
    ki = np.arange(128)[:, None, None]
    qi = np.arange(512)[None, None, :]
    tg = np.arange(16)[None, :, None]
    dn = np.abs(128 * tg + ki - (512 * c + qi)).astype(np.float32)
    koff = np.array([-128] + [128 * t for t in range(16)] + [2048])[None, :, None]
    dw = np.abs(koff + ki - (512 * c + qi)).astype(np.float32)
    dw = np.where(dw <= 128.0, dw, BIG).astype(np.float32)
    return {"ctab": ctab, "dn": np.ascontiguousarray(dn), "dw": np.ascontiguousarray(dw)}


def host_consts():
    kaug = np.zeros((6, 4, 128), np.float32)
    qaug = np.zeros((6, 2, 4, NTOK), np.float32)
    qip = (np.arange(NTOK) % 512 - 256).astype(np.float32)
    for h in range(6):
        sp = SL_DIFF[h] / SC_A
        hi, lo = split_bf(sp)
        kaug[h, 0, :] = -hi
        kaug[h, 1, :] = -lo
        kaug[h, 2, :] = np.arange(128)
        kaug[h, 3, :] = np.arange(128)
        qaug[h, 0, 0, :] = qip
        qaug[h, 0, 1, :] = qip
        qaug[h, 0, 2, :] = hi
        qaug[h, 0, 3, :] = lo
        qaug[h, 1] = -qaug[h, 0]
    kaug = np.tile(kaug, (1, 1, 128))
    return {
        "kaug": kaug.astype(NPBF),
        "qaug": qaug.astype(NPBF),
        "identb": np.eye(128, dtype=np.float32).astype(NPBF),
        "identf": np.eye(128, dtype=np.float32),
    }


class Prog:
    def __init__(self):
        self.nc = bass.Bass("TRN2", target_bir_lowering=False)
        self.S = Sync()
        self.dram = {}
        self.phases = []

    def din(self, name, shape, dt):
        t = self.nc.dram_tensor(name, list(shape), dt, kind="ExternalInput").ap()
        self.dram[name] = t
        return t

    def dout(self, name, shape, dt):
        t = self.nc.dram_tensor(name, list(shape), dt, kind="ExternalOutput").ap()
        self.dram[name] = t
        return t

    def dint(self, name, shape, dt):
        t = self.nc.dram_tensor(name, list(shape), dt, kind="Internal").ap()
        self.dram[name] = t
        return t

    def build(self):
        S = self.S
        nph = len(self.phases)
        engs = ["sp", "act", "pe", "dve", "pool"]
        S.dry = True
        for pi, ph in enumerate(self.phases):
            S.scope = pi
            ph.reset()
            ph.bind(Dummy())
            for en in engs:
                e = Dummy()
                self._barrier_pre(en, e, pi)
                getattr(ph, en)(e)
                self._barrier_post(en, e, pi)
        S.dry = False
        nc = self.nc
        with ExitStack() as es:
            for sem in S.seq:
                S.handles[sem] = es.enter_context(nc.semaphore(sem))
            self.bar_tile = es.enter_context(nc.sbuf_tensor("bar_tile", [128, 8], F32))
            for pi, ph in enumerate(self.phases):
                S.scope = pi
                ph.reset()
                with ExitStack() as pes:
                    bufs = ph.alloc(nc, pes)
                    ph.bind(bufs)
                    with nc.Block() as block:
                        def mk(en, ph=ph, pi=pi):
                            def f(e):
                                S.scope = pi
                                self._barrier_pre(en, e, pi)
                                getattr(ph, en)(e)
                                self._barrier_post(en, e, pi)
                            return f
                        block.sync(mk("sp"))
                        block.scalar(mk("act"))
                        block.tensor(mk("pe"))
                        block.vector(mk("dve"))
                        block.gpsimd(mk("pool"))
        for sem in S.seq:
            assert S.rpos.get(sem, 0) == len(S.seq[sem]), sem
        return nc

    def _barrier_pre(self, en, e, pi):
        if pi == 0 or en == "pool":
            return
        self.S.wait(e, "bar", ("end", pi - 1), glob=True)

    def _barrier_post(self, en, e, pi):
        if en != "pool":
            return
        S = self.S
        if S.dry:
            S.sig(None, "bar", ("end", pi), glob=True)
        else:
            ins = e.memset(self.bar_tile[:, :], 0.0)
            S.sig(ins, "bar", ("end", pi), glob=True)


class Phase:
    def __init__(self, prog, tag):
        self.P = prog
        self.S = prog.S
        self.tag = tag

    def reset(self):
        pass

    def bind(self, bufs):
        self.b = bufs

    def alloc(self, nc, es):
        return Dummy()

    def sp(self, e):
        pass

    def act(self, e):
        pass

    def pe(self, e):
        pass

    def dve(self, e):
        pass

    def pool(self, e):
        pass

    def sem(self, n):
        return n


class Bufs:
    pass


_UID = [0]


def _uname(name):
    _UID[0] += 1
    return "%s_%d" % (name, _UID[0])


def sb(nc, es, name, shape, dt):
    return es.enter_context(nc.sbuf_tensor(_uname(name), list(shape), dt))


def ps(nc, es, name, shape, dt):
    return es.enter_context(nc.psum_tensor(_uname(name), list(shape), dt))


NCH = 16
CHE = 262144


def wview(w, c0, c1):
    return w[:, c0:c1].rearrange("(kc p) c -> p kc c", p=128)


class NormT:
    def __init__(self, ph, name):
        self.ph = ph
        self.S = ph.S
        self.n = ph.sem(name)

    def dve(self, e, t, xt, gbc, hb, junk, ss, rstd, wait_x, wait_hb_free):
        S = self.S
        n = self.n
        wait_x()
        wait_hb_free()
        for s in range(4):
            ins = e.tensor_tensor(out=junk[:, :], in0=xt[:, s, :], in1=xt[:, s, :], op=ALU.mult)
            S.fence(e, ins, n + "_f")
            ins = e.tensor_reduce(out=ss[:, s:s + 1], in_=junk[:, :], axis=AX.X, op=ALU.add)
            S.fence(e, ins, n + "_f")
        ins = e.tensor_scalar(out=rstd[:, 0:4], in0=ss[:, 0:4], scalar1=1.0 / D, scalar2=EPS, op0=ALU.mult, op1=ALU.add)
        S.sig(ins, n + "_v", t)
        S.wait(e, n + "_sq", t)
        ins = e.reciprocal(out=rstd[:, 0:4], in_=rstd[:, 0:4])
        S.fence(e, ins, n + "_f")
        for s in range(4):
            ins = e.scalar_tensor_tensor(out=hb[:, s, :], in0=xt[:, s, :], scalar=rstd[:, s:s + 1], in1=gbc[:, :],
                                         op0=ALU.mult, op1=ALU.mult)
            S.sig(ins, n + "_hb", (t, s))

    def pe(self, e, t, hb, pst, identb):
        S = self.S
        n = self.n
        for kc in range(8):
            g = t * 8 + kc
            if g >= 2:
                S.wait(e, n + "_ev", g - 2)
            for s in range(4):
                if kc == 0:
                    S.wait(e, n + "_hb", (t, s))
                ins = e.transpose(pst[:, g % 2, s * 128:(s + 1) * 128], hb[:, s, kc * 128:(kc + 1) * 128], identb[:, :])
            S.sig(ins, n + "_tr", g)

    def act(self, e, t, pst, hT, wait_hT_free, rstd=None):
        S = self.S
        n = self.n
        S.wait(e, n + "_v", t)
        ins = e.activation(out=rstd[:, 0:4], in_=rstd[:, 0:4], func=AF.Sqrt)
        S.sig(ins, n + "_sq", t)
        wait_hT_free()
        for kc in range(8):
            g = t * 8 + kc
            S.wait(e, n + "_tr", g)
            ins = e.activation(out=hT[:, kc, :], in_=pst[:, g % 2, 0:512], func=AF.Copy)
            S.sig(ins, n + "_ev", g)

    def wait_hT(self, e, t):
        self.S.wait(e, self.n + "_ev", t * 8 + 7)

    def wait_tr_done(self, e, t):
        self.S.wait(e, self.n + "_tr", t * 8 + 7)


class P1(Phase):
    FM = [(0, 384), (384, 768), (2048, 2176)]
    TM = [(768, 1152), (2176, 2304)]

    def __init__(self, prog, tag, x, w_in, g_pre, identb, Qs, E):
        super().__init__(prog, tag)
        self.x, self.w_in, self.g_pre, self.identb_d = x, w_in, g_pre, identb
        self.Qs, self.E = Qs, E
        self.EK = E[0:6, :].rearrange("h (a n) -> (h a) n", n=NTOK)
        self.ECK = E[12:14, :].rearrange("h (a n) -> (h a) n", n=NTOK)
        self.nt = NormT(self, "nt")

    def alloc(self, nc, es):
        b = Bufs()
        b.wfm = sb(nc, es, "p1_wfm", [128, 8, 896], BF16)
        b.wtm = sb(nc, es, "p1_wtm", [128, 8, 512], BF16)
        b.gbc = sb(nc, es, "p1_gbc", [128, D], F32)
        b.identb = sb(nc, es, "p1_id", [128, 128], BF16)
        b.xt = sb(nc, es, "p1_xt", [128, 4, D], F32)
        b.hb = sb(nc, es, "p1_hb", [128, 4, D], BF16)
        b.junk = sb(nc, es, "p1_junk", [128, D], F32)
        b.ss = sb(nc, es, "p1_ss", [128, 4], F32)
        b.rstd = sb(nc, es, "p1_rstd", [128, 4], F32)
        b.hT = sb(nc, es, "p1_hT", [128, 8, 512], BF16)
        b.fm = sb(nc, es, "p1_fm", [128, 2, 7, 512], BF16)
        b.vt = sb(nc, es, "p1_vt", [128, 2, 4, 512], BF16)
        b.pst = ps(nc, es, "p1_pst", [128, 2, 1024], BF16)
        b.pm = ps(nc, es, "p1_pm", [128, 4, 512], F32)
        return b

    def sp(self, e):
        S, b = self.S, self.b
        ld = self.sem("ld")
        ins = e.dma_start(out=b.identb[:, :], in_=self.identb_d[:, :])
        S.sig(ins, ld, "id", dma=True)
        ins = e.dma_start(out=b.gbc[:, :], in_=self.g_pre.partition_broadcast(128))
        S.sig(ins, ld, "g", dma=True)
        for t in range(NQ):
            if t >= 1:
                S.wait(e, self.nt.n + "_hb", (t - 1, 3))
            ins = e.dma_start(out=b.xt[:, :, :], in_=self.x[t * 512:(t + 1) * 512, :].rearrange("(s p) d -> p s d", p=128))
            S.sig(ins, self.sem("ldx"), t, dma=True)

    def pool(self, e):
        S, b = self.S, self.b
        lw = self.sem("lw")
        c = 0
        for (c0, c1) in self.FM:
            ins = e.dma_start(out=b.wfm[:, :, c:c + (c1 - c0)], in_=wview(self.w_in, c0, c1))
            S.sig(ins, lw, ("fm", c0), dma=True)
            c += c1 - c0
        c = 0
        for (c0, c1) in self.TM:
            ins = e.dma_start(out=b.wtm[:, :, c:c + (c1 - c0)], in_=wview(self.w_in, c0, c1))
            S.sig(ins, lw, ("tm", c0), dma=True)
            c += c1 - c0
        for t in range(NQ):
            sl = t % 2
            st = self.sem("st%d" % sl)
            cs = slice(t * 512, (t + 1) * 512)
            S.wait(e, self.sem("ev"), ("fm", t, 6))
            ins = e.dma_start(out=self.Qs[:, cs].rearrange("(b p) n -> p b n", p=128), in_=b.fm[:, sl, 0:3, :])
            S.sig(ins, st, ("q", t), dma=True)
            ins = e.dma_start(out=self.EK[:, cs].rearrange("(b p) n -> p b n", p=128), in_=b.fm[:, sl, 3:6, :])
            S.sig(ins, st, ("k", t), dma=True)
            ins = e.dma_start(out=self.ECK[:, cs], in_=b.fm[:, sl, 6, :])
            S.sig(ins, st, ("ck", t), dma=True)
            S.wait(e, self.sem("ev"), ("tm", t, 3))
            for hh in range(8):
                ch = 6 + hh if hh < 6 else 14 + (hh - 6)
                dst = self.E[ch, :].rearrange("(p t d) -> p t d", p=128, t=32)
                ins = e.dma_start(out=dst[:, 4 * t:4 * t + 4, :], in_=b.vt[:, sl, :, hh * 64:(hh + 1) * 64])
                S.sig(ins, st, ("cv" if hh == 7 else ("v", hh), t), dma=True)
        S.wait(e, self.sem("st0"), ("cv", NQ - 2))
        S.wait(e, self.sem("st1"), ("cv", NQ - 1))

    def dve(self, e):
        S, b = self.S, self.b
        S.wait(e, self.sem("ld"), "g")
        for t in range(NQ):
            self.nt.dve(e, t, b.xt, b.gbc, b.hb, b.junk, b.ss, b.rstd,
                        wait_x=lambda: S.wait(e, self.sem("ldx"), t),
                        wait_hb_free=lambda: (self.nt.wait_tr_done(e, t - 1) if t >= 1 else None))

    def pe(self, e):
        S, b = self.S, self.b
        S.wait(e, self.sem("ld"), "g")
        S.wait(e, self.sem("lw"), ("tm", self.TM[-1][0]))
        n = 0
        for t in range(NQ):
            self.nt.pe(e, t, b.hb, b.pst, b.identb)
            self.nt.wait_hT(e, t)
            for blk in range(7):
                if n >= 4:
                    S.wait(e, self.sem("ev"), self.evkeys[n - 4])
                for kc in range(8):
                    ins = e.matmul(b.pm[:, n % 4, :], lhsT=b.wfm[:, kc, blk * 128:(blk + 1) * 128], rhs=b.hT[:, kc, :],
                                   start=(kc == 0), stop=(kc == 7))
                S.sig(ins, self.sem("mm"), ("fm", t, blk))
                self.evkeys.append(("fm", t, blk))
                n += 1
            for s in range(4):
                if n >= 4:
                    S.wait(e, self.sem("ev"), self.evkeys[n - 4])
                for kc in range(8):
                    ins = e.matmul(b.pm[:, n % 4, :], lhsT=b.hT[:, kc, s * 128:(s + 1) * 128], rhs=b.wtm[:, kc, :],
                                   start=(kc == 0), stop=(kc == 7))
                S.sig(ins, self.sem("mm"), ("tm", t, s))
                self.evkeys.append(("tm", t, s))
                n += 1

    def reset(self):
        self.evkeys = []

    def act(self, e):
        S, b = self.S, self.b
        n = 0
        for t in range(NQ):
            sl = t % 2
            self.nt.act(e, t, b.pst, b.hT,
                        wait_hT_free=lambda: (S.wait(e, self.sem("mm"), ("tm", t - 1, 3)) if t >= 1 else None),
                        rstd=b.rstd)
            for blk in range(7):
                S.wait(e, self.sem("mm"), ("fm", t, blk))
                if t >= 2 and blk == 0:
                    S.wait(e, self.sem("st%d" % sl), ("cv", t - 2))
                ins = e.activation(out=b.fm[:, sl, blk, :], in_=b.pm[:, n % 4, :], func=AF.Copy)
                S.sig(ins, self.sem("ev"), ("fm", t, blk))
                n += 1
            for s in range(4):
                S.wait(e, self.sem("mm"), ("tm", t, s))
                if t >= 2 and s == 0:
                    S.wait(e, self.sem("st%d" % sl), ("cv", t - 2))
                ins = e.activation(out=b.vt[:, sl, s, :], in_=b.pm[:, n % 4, :], func=AF.Copy)
                S.sig(ins, self.sem("ev"), ("tm", t, s))
                n += 1


class AttnPipe(Phase):
    NSR = NSR

    def reset(self):
        self.gu = {"pe": 0, "act": 0, "dve": 0}
        self.prev_tp = {"pe": None}

    def pe_group(self, e, g):
        S, b = self.S, self.b
        U = len(g["units"])
        g0 = self.gu["pe"]
        for t in range(U + LA):
            if t < U:
                u = g["units"][t]
                gu = g0 + t
                if gu >= self.NSR:
                    S.wait(e, self.sem("p"), gu - self.NSR)
                if t == 0 and g.get("wait_pe") is not None:
                    g["wait_pe"](e)
                ins = e.matmul(b.sring[:, gu % self.NSR, :], lhsT=u["lhsT"], rhs=u["rhs"], start=True, stop=True)
                S.sig(ins, self.sem("qk"), gu)
            if t >= LA:
                v = t - LA
                u = g["units"][v]
                gv = g0 + v
                S.wait(e, self.sem("p"), gv)
                if u["first"] and g["acc_prev"] is not None:
                    S.wait(e, self.sem("ev"), (g["acc_prev"], u["acc"]))
                ins = e.matmul(b.oacc[:, g["par"] * 2 + u["acc"], :], lhsT=u["vl"], rhs=b.pring[:, gv % NPR, :],
                               start=u["first"], stop=u["last"])
                S.sig(ins, self.sem("pv"), gv)
        self.gu["pe"] = g0 + U

    def pe_group_pairs(self, e, g):
        S, b = self.S, self.b
        LAP = 2
        U = len(g["units"])
        NP_ = U // 2
        g0 = self.gu["pe"]
        for t in range(NP_ + LAP):
            if t >= LAP:
                S.wait(e, self.sem("p"), g0 + 2 * (t - LAP) + 1)
            elif g0 + 2 * (t - LAP) + 1 >= 0:
                S.wait(e, self.sem("p"), g0 + 2 * (t - LAP) + 1)
            if t == 0 and g.get("wait_pe") is not None:
                g["wait_pe"](e)
            if t < NP_:
                for c in (0, 1):
                    u = g["units"][2 * t + c]
                    gu = g0 + 2 * t + c
                    ins = e.matmul(b.sring[:, gu % self.NSR, :], lhsT=u["lhsT"], rhs=u["rhs"], start=True, stop=True)
                    S.sig(ins, self.sem("qk"), gu)
            if t >= LAP:
                for c in (0, 1):
                    v = 2 * (t - LAP) + c
                    u = g["units"][v]
                    gv = g0 + v
                    if u["first"] and g["acc_prev"] is not None:
                        S.wait(e, self.sem("ev"), (g["acc_prev"], u["acc"]))
                    ins = e.matmul(b.oacc[:, g["par"] * 2 + u["acc"], :], lhsT=u["vl"], rhs=b.pring[:, gv % NPR, :],
                                   start=u["first"], stop=u["last"])
                    S.sig(ins, self.sem("pv"), gv)
        self.gu["pe"] = g0 + U

    def pe_post(self, e, g):
        S, b = self.S, self.b
        for c in (0, 1):
            S.wait(e, self.sem("ev"), (g["gid"], c))
            if self.prev_tp["pe"] is not None:
                S.wait(e, self.sem("tpc"), self.prev_tp["pe"])
            for s in range(4):
                ins = e.transpose(b.tps[:, s, 0:65], b.oT[0:65, c, s * 128:(s + 1) * 128], b.identf[0:65, 0:65])
            S.sig(ins, self.sem("tp"), (g["gid"], c))
            self.prev_tp["pe"] = (g["gid"], c)

    def act_group(self, e, g):
        S, b = self.S, self.b
        g0 = self.gu["act"]
        for t, u in enumerate(g["units"]):
            gu = g0 + t
            if u["kind"] == "G":
                S.wait(e, self.sem("gb"), gu)
            else:
                S.wait(e, self.sem("qk"), gu)
            if gu >= NPR:
                S.wait(e, self.sem("pv"), gu - NPR)
            ins = e.activation(out=b.pring[:, gu % NPR, :], in_=b.sring[:, gu % self.NSR, :], func=AF.Exp,
                               bias=u["bias"], scale=u["scale"])
            S.sig(ins, self.sem("p"), gu)
        self.gu["act"] = g0 + len(g["units"])

    def dve_group(self, e, g, consume):
        S, b = self.S, self.b
        g0 = self.gu["dve"]
        last = {}
        for t, u in enumerate(g["units"]):
            gu = g0 + t
            last[u["acc"]] = gu
            if u["kind"] == "G":
                S.wait(e, self.sem("qk"), gu)
                ins = e.scalar_tensor_tensor(out=b.sring[:, gu % self.NSR, :], in0=u["dD"], scalar=u["dcoef"],
                                             in1=b.sring[:, gu % self.NSR, :], op0=ALU.mult, op1=ALU.add)
                S.sig(ins, self.sem("gb"), gu)
        self.gu["dve"] = g0 + len(g["units"])
        for c in (0, 1):
            S.wait(e, self.sem("pv"), last[c])
            if g["gid"] >= 1:
                S.wait(e, self.sem("tp"), (g["gid"] - 1, c))
            ins = e.tensor_copy(out=b.oT[0:65, c, :], in_=b.oacc[0:65, g["par"] * 2 + c, :])
            S.sig(ins, self.sem("ev"), (g["gid"], c))
        for c in (0, 1):
            S.wait(e, self.sem("tp"), (g["gid"], c))
            consume(e, g, c)


class P2a(AttnPipe):
    NSR = 4

    AG_ORDER = [0, 6, 1, 7, 2, 8, 3, 9, 4, 10, 5, 11, 12, 13, 14, 15]

    def __init__(self, prog, tag, G, Qs, ctab, dn, kaug, qaug, identf, lamv, lamc, gsub, ya_d, E=None):
        super().__init__(prog, tag)
        self.E = E
        self.G, self.Qs, self.ctab_d, self.dn_d = G, Qs, ctab, dn
        self.kaug, self.qaug, self.identf_d, self.lamv, self.lamc_d, self.gsub_d, self.ya_d = kaug, qaug, identf, lamv, lamc, gsub, ya_d
        self.ncol = cidx_map()[1]

    def alloc(self, nc, es):
        b = Bufs()
        b.KT = sb(nc, es, "a_KT", [100, 2, SEQ], BF16)
        b.V = sb(nc, es, "a_V", [128, 2, 129, 65], BF16)
        b.QA = sb(nc, es, "a_QA", [100, 2, NTOK], BF16)
        b.QB = sb(nc, es, "a_QB", [100, 2, NTOK], BF16)
        b.ctab = sb(nc, es, "a_ctab", [128, self.ncol], F32)
        b.dn = sb(nc, es, "a_dn", [128, 16, 512], F32)
        b.identf = sb(nc, es, "a_idf", [128, 128], F32)
        b.pring = sb(nc, es, "a_pr", [128, NPR, 512], BF16)
        b.oT = sb(nc, es, "a_oT", [65, 2, 512], F32)
        b.lv = sb(nc, es, "a_lv", [128, 4, 32], F32)
        b.lt = sb(nc, es, "a_lt", [128, 8], F32)
        b.lamc = sb(nc, es, "a_lamc", [128, 2], F32)
        b.gsub = sb(nc, es, "a_gsub", [128, 64], F32)
        b.r = sb(nc, es, "a_r", [128, 8], F32)
        b.t0 = sb(nc, es, "a_t0", [128, 4, 64], F32)
        b.y = sb(nc, es, "a_y", [128, 4, 64], F32)
        b.sq = sb(nc, es, "a_sq", [128, 4, 64], F32)
        b.sst = sb(nc, es, "a_sst", [128, 32, 6], F32)
        b.ya = sb(nc, es, "a_ya", [128, 32, 384], BF16)
        b.sring = ps(nc, es, "a_sr", [128, 4, 512], F32)
        b.oacc = ps(nc, es, "a_oacc", [128, 2, 512], F32)
        b.tps = ps(nc, es, "a_tps", [128, 4, 128], F32)
        return b

    def groups(self):
        b = self.b
        cm, _ = cidx_map()
        out = []
        gid = 0
        for h in range(6):
            sl = h % 2
            sp = float(SL_DIFF[h] / SC_A)
            for i in range(NQ):
                units = []
                tl = band_tiles(h, i)
                qs = slice(512 * i, 512 * i + 512)
                for n, j in enumerate(tl):
                    r, lt = j2idx(j)
                    ks = slice(r * 4096 + lt * 128, r * 4096 + lt * 128 + 128)
                    for comp in (0, 1):
                        base = 64 * comp
                        vi = (r * 32 + lt) * 65
                        u = {"acc": comp, "first": n == 0, "last": n == len(tl) - 1, "scale": SC_A,
                             "vl": b.V[:, sl, :, :].rearrange("p t d -> p (t d)")[:, vi:vi + 128]}
                        if j < 16 * i or j >= 16 * (i + 1):
                            Q = b.QA if j < 16 * i else b.QB
                            u["kind"] = "F"
                            u["lhsT"] = b.KT[base:base + 36, sl, ks]
                            u["rhs"] = Q[base:base + 36, sl, qs]
                            u["bias"] = b.ctab[:, cm[(h, i, j)]:cm[(h, i, j)] + 1]
                        else:
                            u["kind"] = "G"
                            u["lhsT"] = b.KT[base:base + 32, sl, ks]
                            u["rhs"] = b.QA[base:base + 32, sl, qs]
                            u["bias"] = b.ctab[:, 0:1]
                            u["dD"] = b.dn[:, j - 16 * i, :]
                            u["dcoef"] = -sp
                        units.append(u)
                out.append({"gid": gid, "par": 0, "units": units, "h": h, "i": i,
                            "acc_prev": gid - 1 if gid >= 1 else None})
                gid += 1
        return out

    def sp(self, e):
        S, b = self.S, self.b
        ld = self.sem("ldc")
        lst = [(b.ctab[:, :], self.ctab_d[:, :]), (b.dn[:, :, :], self.dn_d[:, :, :]),
               (b.identf[:, :], self.identf_d[:, :]), (b.lamc[:, :], self.lamc_d[:, :]),
               (b.gsub[:, :], self.gsub_d.partition_broadcast(128))]
        for n, (dst, src) in enumerate(lst):
            ins = e.dma_start(out=dst, in_=src)
            S.sig(ins, ld, ("c", n), dma=True)
        for k in range(4):
            ins = e.dma_start(out=b.lv[:, k, :], in_=self.lamv[k, :].partition_broadcast(128))
            S.sig(ins, ld, ("lv", k), dma=True)
        for h in range(6):
            sl = h % 2
            sem = self.sem("ld%d" % sl)
            if h >= 2:
                S.wait(e, self.sem("hd"), h - 2)
            if self.E is not None:
                S.wait(e, "cc", 6 + h)
            for comp in (0, 1):
                rows = slice(h * 64 + comp * 32, h * 64 + comp * 32 + 32)
                base = 64 * comp
                ins = e.dma_start(out=b.KT[base:base + 32, sl, :].rearrange("p (r n) -> p r n", r=4),
                                  in_=self.G[h, :, :].rearrange("r (a n) -> a r n", n=NTOK)[comp * 32:(comp + 1) * 32])
                S.sig(ins, sem, ("k", h, comp), dma=True)
                ins = e.dma_start(out=b.KT[base + 32:base + 36, sl, :], in_=self.kaug[h, :, :])
                S.sig(ins, sem, ("ka", h, comp), dma=True)
                for var, Q in ((0, b.QA), (1, b.QB)):
                    ins = e.dma_start(out=Q[base:base + 32, sl, :], in_=self.Qs[rows, :])
                    S.sig(ins, sem, ("q", h, comp, var), dma=True)
                    ins = e.dma_start(out=Q[base + 32:base + 36, sl, :], in_=self.qaug[h, var, :, :])
                    S.sig(ins, sem, ("qa", h, comp, var), dma=True)
            for r in range(4):
                ins = e.dma_start(out=b.V[:, sl, r * 32:(r + 1) * 32, 0:64],
                                  in_=self.G[6 + h, r, :].rearrange("(p t d) -> p t d", p=128, t=32))
                S.sig(ins, sem, ("v", h, r), dma=True)

    def pe(self, e):
        S, b = self.S, self.b
        S.wait(e, self.sem("ldc"), ("lv", 3))
        S.wait(e, self.sem("ones"), 0)
        prev = None
        for g in self.groups():
            if g["i"] == 0:
                h = g["h"]
                g["wait_pe"] = lambda e, h=h: S.wait(e, self.sem("ld%d" % (h % 2)), ("v", h, 3))
            if prev is not None:
                self.pe_post(e, prev)
            self.pe_group_pairs(e, g)
            prev = g
        self.pe_post(e, prev)

    def act(self, e):
        S, b = self.S, self.b
        S.wait(e, self.sem("ldc"), ("lv", 3))
        S.wait(e, self.sem("lm"), "dots")
        ins = e.activation(out=b.lt[:, 2:4], in_=b.lt[:, 0:2], func=AF.Exp)
        S.sig(ins, self.sem("lma"), "exp")
        for g in self.groups():
            self.act_group(e, g)
        S.wait(e, self.sem("fin"), "v")
        ins = e.activation(out=b.sst[:, 0:4 * NQ, :], in_=b.sst[:, 0:4 * NQ, :], func=AF.Sqrt)
        S.sig(ins, self.sem("lma"), "sqrt")

    def consume(self, e, g, c):
        S, b = self.S, self.b
        f = self.sem("f")
        h, i = g["h"], g["i"]
        ins = e.reciprocal(out=b.r[:, 4 * c:4 * c + 4], in_=b.tps[:, :, 64])
        S.fence(e, ins, f)
        if c == 0:
            for s in range(4):
                ins = e.tensor_scalar(out=b.t0[:, s, :], in0=b.tps[:, s, 0:64], scalar1=b.r[:, s:s + 1], scalar2=None,
                                      op0=ALU.mult)
            S.sig(ins, self.sem("tpc"), (g["gid"], 0))
        else:
            ins = e.tensor_scalar(out=b.r[:, 4:8], in0=b.r[:, 4:8], scalar1=b.lt[:, 4:5], scalar2=None, op0=ALU.mult)
            S.fence(e, ins, f)
            for s in range(4):
                ins = e.scalar_tensor_tensor(out=b.y[:, s, :], in0=b.tps[:, s, 0:64], scalar=b.r[:, 4 + s:5 + s],
                                             in1=b.t0[:, s, :], op0=ALU.mult, op1=ALU.add)
            S.sig(ins, self.sem("tpc"), (g["gid"], 1))
            S.wait(e, self.sem("tpc"), (g["gid"], 1))
            ins = e.tensor_tensor(out=b.sq[:, :, :], in0=b.y[:, :, :], in1=b.y[:, :, :], op=ALU.mult)
            S.fence(e, ins, f)
            ins = e.tensor_reduce(out=b.sst[:, 4 * i:4 * i + 4, h], in_=b.sq[:, :, :], axis=AX.X, op=ALU.add)
            ins = e.tensor_copy(out=b.ya[:, 4 * i:4 * i + 4, h * 64:(h + 1) * 64], in_=b.y[:, :, :])
            S.fence(e, ins, f)
            if i == NQ - 1:
                S.sig(e.tensor_copy(out=b.lt[:, 7:8], in_=b.lt[:, 4:5]), self.sem("hd"), h)

    def dve(self, e):
        S, b = self.S, self.b
        f = self.sem("f")
        for sl in (0, 1):
            ins = e.memset(b.V[:, sl, 128, :], 0.0)
            ins = e.memset(b.V[:, sl, 0:128, 64:65], 1.0)
        S.sig(ins, self.sem("ones"), 0)
        S.wait(e, self.sem("ldc"), ("lv", 3))
        ins = e.tensor_tensor(out=b.lv[:, 0, :], in0=b.lv[:, 0, :], in1=b.lv[:, 1, :], op=ALU.mult)
        ins = e.tensor_tensor(out=b.lv[:, 2, :], in0=b.lv[:, 2, :], in1=b.lv[:, 3, :], op=ALU.mult)
        S.fence(e, ins, f)
        ins = e.tensor_reduce(out=b.lt[:, 0:1], in_=b.lv[:, 0, :], axis=AX.X, op=ALU.add)
        ins = e.tensor_reduce(out=b.lt[:, 1:2], in_=b.lv[:, 2, :], axis=AX.X, op=ALU.add)
        S.sig(ins, self.sem("lm"), "dots")
        S.wait(e, self.sem("lma"), "exp")
        ins = e.tensor_tensor(out=b.lt[:, 5:6], in0=b.lt[:, 3:4], in1=b.lt[:, 2:3], op=ALU.subtract)
        S.fence(e, ins, f)
        ins = e.tensor_tensor(out=b.lt[:, 4:5], in0=b.lt[:, 5:6], in1=b.lamc[:, 0:1], op=ALU.subtract)
        ins = e.tensor_scalar(out=b.gsub[:, :], in0=b.gsub[:, :], scalar1=b.lamc[:, 1:2], scalar2=None, op0=ALU.mult)
        S.fence(e, ins, f)
        for g in self.groups():
            self.dve_group(e, g, self.consume)
        ins = e.tensor_scalar(out=b.sst[:, 0:4 * NQ, :], in0=b.sst[:, 0:4 * NQ, :], scalar1=1.0 / 64, scalar2=EPS,
                              op0=ALU.mult, op1=ALU.add)
        S.sig(ins, self.sem("fin"), "v")
        S.wait(e, self.sem("lma"), "sqrt")
        ins = e.reciprocal(out=b.sst[:, 0:4 * NQ, :], in_=b.sst[:, 0:4 * NQ, :])
        S.fence(e, ins, f)
        for lt in range(4 * NQ):
            for h in range(6):
                ins = e.scalar_tensor_tensor(out=b.ya[:, lt, h * 64:(h + 1) * 64], in0=b.ya[:, lt, h * 64:(h + 1) * 64],
                                             scalar=b.sst[:, lt, h:h + 1], in1=b.gsub[:, :], op0=ALU.mult, op1=ALU.mult)
        S.sig(ins, self.sem("fin"), "ya")

    def pool(self, e):
        S, b = self.S, self.b
        if self.E is not None:
            for k in self.AG_ORDER:
                ins = e.collective_compute("AllGather", ALU.bypass, replica_groups=[[0, 1, 2, 3], [4, 5, 6, 7]],
                                           ins=[self.E[k, :].rearrange("(p c) -> p c", c=2048)],
                                           outs=[self.G[k, :, :].rearrange("r (p c) -> (r p) c", c=2048)])
                S.sig(ins, "cc", k)
            S.wait(e, "cc", 15)
        S.wait(e, self.sem("fin"), "ya")
        ins = e.dma_start(out=self.ya_d[:, 0:4 * NQ, :], in_=b.ya[:, 0:4 * NQ, :])
        S.sig(ins, self.sem("st"), 0, dma=True)
        S.wait(e, self.sem("st"), 0)


class P2b(AttnPipe):
    NSR = 4

    def __init__(self, prog, tag, x, w_in, g_pre, G, ya_d, ln_g, ln_b, w_sp, b_sp, sinks, w_out, g_post,
                 dw, identb, identf, x1):
        super().__init__(prog, tag)
        self.x, self.w_in, self.g_pre, self.G, self.ya_d = x, w_in, g_pre, G, ya_d
        self.ln_g, self.ln_b, self.w_sp, self.b_sp, self.sinks, self.w_out, self.g_post = ln_g, ln_b, w_sp, b_sp, sinks, w_out, g_post
        self.dw_d, self.identb_d, self.identf_d, self.x1 = dw, identb, identf, x1
        self.nt = NormT(self, "nt")

    def reset(self):
        super().reset()
        self.m = {"pe": 0, "act": 0, "dve": 0}

    def alloc(self, nc, es):
        b = Bufs()
        b.wuv = sb(nc, es, "b_wuv", [128, 8, 512], BF16)
        b.wcq = sb(nc, es, "b_wcq", [128, 8, 3, 128], BF16)
        b.wout = sb(nc, es, "b_wout", [128, 8, D], BF16)
        b.dw = sb(nc, es, "b_dw", [128, 18, 512], F32)
        b.gbc = sb(nc, es, "b_gbc", [128, D], F32)
        b.gpost = sb(nc, es, "b_gpost", [128, D], F32)
        b.lng = sb(nc, es, "b_lng", [128, 256], F32)
        b.lnb = sb(nc, es, "b_lnb", [128, 256], F32)
        b.identb = sb(nc, es, "b_idb", [128, 128], BF16)
        b.identf = sb(nc, es, "b_idf", [128, 128], F32)
        b.wsp = sb(nc, es, "b_wsp", [128, 4, 128], F32)
        b.wsT = sb(nc, es, "b_wsT", [128, 4, 128], BF16)
        b.bsp = sb(nc, es, "b_bsp", [4, 128], F32)
        b.bsT = sb(nc, es, "b_bsT", [128, 4], F32)
        b.esink = sb(nc, es, "b_esink", [128, 6], F32)
        b.zero = sb(nc, es, "b_zero", [128, 1], F32)
        b.xt = sb(nc, es, "b_xt", [128, 2, 4, D], F32)
        b.hb = sb(nc, es, "b_hb", [128, 4, D], BF16)
        b.junk = sb(nc, es, "b_junk", [128, D], F32)
        b.ss = sb(nc, es, "b_ss", [128, 4], F32)
        b.rstd = sb(nc, es, "b_rstd", [128, 4], F32)
        b.hT = sb(nc, es, "b_hT", [128, 8, 512], BF16)
        b.cq = sb(nc, es, "b_cq", [128, 3, 512], BF16)
        b.ckx = sb(nc, es, "b_ckx", [128, 18 * 128], BF16)
        b.cv = sb(nc, es, "b_cv", [128, 19, 2, 65], BF16)
        b.pring = sb(nc, es, "b_pr", [128, NPR, 512], BF16)
        b.oT = sb(nc, es, "b_oT", [65, 2, 512], F32)
        b.r = sb(nc, es, "b_r", [128, 8], F32)
        b.y = sb(nc, es, "b_y", [128, 4, D], BF16)
        b.u = sb(nc, es, "b_u", [128, 4, 256], F32)
        b.vt = sb(nc, es, "b_vt", [128, 4, 256], F32)
        b.vnb = sb(nc, es, "b_vnb", [128, 4, 256], BF16)
        b.st = sb(nc, es, "b_st", [128, 16], F32)
        b.osb = sb(nc, es, "b_osb", [128, 4, D], F32)
        b.os2 = sb(nc, es, "b_os2", [128, 8], F32)
        b.pst = ps(nc, es, "b_pst", [128, 2, 1024], BF16)
        b.sring = ps(nc, es, "b_sr", [128, 4, 512], F32)
        b.oacc = ps(nc, es, "b_oacc", [128, 2, 512], F32)
        b.tps = b.pst[:, 0, :].bitcast(F32).rearrange("p (s d) -> p s d", d=128)
        return b

    def kts(self, i):
        return [kt for kt in range(18) if not (kt == 0 and i == 0) and not (kt == 17 and i == 7)]

    def groups(self, i):
        b = self.b
        out = []
        kts = self.kts(i)
        for g in range(3):
            units = []
            for n, kt in enumerate(kts):
                for kv in (0, 1):
                    head = kv * 3 + g
                    units.append({"acc": kv, "first": n == 0, "last": n == len(kts) - 1, "scale": SC_W, "kind": "G",
                                  "lhsT": b.ckx[kv * 64:(kv + 1) * 64, kt * 128:(kt + 1) * 128],
                                  "rhs": b.cq[kv * 64:(kv + 1) * 64, g, :], "bias": b.zero[:, 0:1],
                                  "dD": b.dw[:, kt, :], "dcoef": -float(SL_WIN[head] / SC_W),
                                  "vl": b.cv[:, :, :, :].rearrange("p t k d -> p (t k d)")[:, (kt * 2 + kv) * 65:(kt * 2 + kv) * 65 + 128]})
            gid = 3 * i + g
            out.append({"gid": gid, "par": 0, "units": units, "g": g, "i": i, "acc_prev": gid - 1 if gid >= 1 else None})
        return out

    MSEM = {"cq": "mA", "uv": "mD", "sp": "mD", "op": "mA", "ws": "mA"}

    def misc_begin(self, e, kind):
        S = self.S
        m = self.m["pe"]
        self.mk.append(kind)
        if m >= self.NSR:
            S.wait(e, self.sem(self.MSEM[self.mk[m - self.NSR]]), ("rel", m - self.NSR))
        self.m["pe"] = m + 1
        return self.b.sring[:, m % self.NSR, :], m

    def sp(self, e):
        S, b = self.S, self.b
        ld = self.sem("ldc")
        lst = [(b.dw[:, :, :], self.dw_d[:, :, :]), (b.identb[:, :], self.identb_d[:, :]), (b.identf[:, :], self.identf_d[:, :]),
               (b.gbc[:, :], self.g_pre.partition_broadcast(128)), (b.gpost[:, :], self.g_post.partition_broadcast(128)),
               (b.lng[:, :], self.ln_g.partition_broadcast(128)), (b.lnb[:, :], self.ln_b.partition_broadcast(128)),
               (b.esink[:, :], self.sinks.partition_broadcast(128)),
               (b.wsp[:, :, :], self.w_sp.rearrange("g t s -> t g s")), (b.bsp[:, :], self.b_sp[:, :])]
        for n, (dst, src) in enumerate(lst):
            ins = e.dma_start(out=dst, in_=src)
            S.sig(ins, ld, ("c", n), dma=True)
        for i in range(NQ):
            sl = i % 2
            if i >= 2:
                S.wait(e, self.sem("stx%d" % sl), i - 2)
            ins = e.dma_start(out=b.xt[:, sl, :, :], in_=self.x[i * 512:(i + 1) * 512, :].rearrange("(s p) d -> p s d", p=128))
            S.sig(ins, self.sem("ldx%d" % sl), i, dma=True)
            if i >= 1:
                S.wait(e, self.sem("tpc"), (3 * (i - 1) + 2, 1))
            lk = self.sem("ldk")
            for kt in self.kts(i):
                if 1 <= kt <= 16 and (kt - 1) % 4 != 0:
                    continue
                if kt == 0:
                    r, lt, n = j2idx(16 * i - 1) + (1,)
                elif kt == 17:
                    r, lt, n = j2idx(16 * (i + 1)) + (1,)
                else:
                    r, lt, n = (kt - 1) // 4, 4 * i, 4
                for kv in (0, 1):
                    ins = e.dma_start(out=b.ckx[kv * 64:(kv + 1) * 64, kt * 128:(kt + n) * 128],
                                      in_=self.G[12 + kv, r, :].rearrange("(a n) -> a n", n=NTOK)[:, lt * 128:(lt + n) * 128])
                    S.sig(ins, lk, ("k", i, kt, kv), dma=True)
                for kv in (0, 1):
                    ins = e.dma_start(out=b.cv[:, kt:kt + n, kv, 0:64],
                                      in_=self.G[14 + kv, r, :].rearrange("(p t d) -> p t d", p=128, t=32)[:, lt:lt + n, :])
                    S.sig(ins, lk, ("v", i, kt) if kv == 1 else ("v0", i, kt), dma=True)
            if i >= 1:
                S.wait(e, self.sem("ytr"), i - 1)
            ins = e.dma_start(out=b.y[:, :, 0:384], in_=self.ya_d[:, 4 * i:4 * i + 4, :])
            S.sig(ins, self.sem("ldy"), i, dma=True)

    def pool(self, e):
        S, b = self.S, self.b
        lw = self.sem("lw")
        ins = e.dma_start(out=b.wuv[:, :, :], in_=wview(self.w_in, 1152, 1664))
        S.sig(ins, lw, "uv", dma=True)
        for g in range(3):
            for kv in (0, 1):
                c0 = 1664 + (kv * 3 + g) * 64
                ins = e.dma_start(out=b.wcq[:, :, g, kv * 64:(kv + 1) * 64], in_=wview(self.w_in, c0, c0 + 64))
                S.sig(ins, lw, ("cq", g, kv), dma=True)
        ins = e.dma_start(out=b.wout[:, :, :], in_=wview(self.w_out, 0, D))
        S.sig(ins, lw, "out", dma=True)
        for i in range(NQ):
            sl = i % 2
            S.wait(e, self.sem("res"), i)
            ins = e.dma_start(out=self.x1[i * 512:(i + 1) * 512, :].rearrange("(s p) d -> p s d", p=128), in_=b.xt[:, sl, :, :])
            S.sig(ins, self.sem("stx%d" % sl), i, dma=True)
        for i in (NQ - 2, NQ - 1):
            if i >= 0:
                S.wait(e, self.sem("stx%d" % (i % 2)), i)

    def pe(self, e):
        S, b = self.S, self.b
        self.mk = []
        S.wait(e, self.sem("ldc"), ("c", 9))
        S.wait(e, self.sem("lw"), "out")
        for g4 in range(4):
            if g4 >= 1:
                S.wait(e, self.sem("mA"), ("ws", g4 - 1))
            ins = e.transpose(b.tps[:, 0, :], b.wsp[:, g4, :], b.identf[:, :])
            S.sig(ins, self.sem("wst"), g4)
        S.wait(e, self.sem("mA"), ("ws", 3))
        ins = e.transpose(b.tps[:, 1, 0:4], b.bsp[0:4, :], b.identf[0:4, 0:4])
        S.sig(ins, self.sem("wst"), 4)
        S.wait(e, self.sem("ones"), 0)
        for i in range(NQ):
            self.nt.pe(e, i, b.hb, b.pst, b.identb)
            self.nt.wait_hT(e, i)
            for g in range(3):
                bank, m = self.misc_begin(e, "cq")
                for kc in range(8):
                    ins = e.matmul(bank, lhsT=b.wcq[:, kc, g, :], rhs=b.hT[:, kc, :], start=(kc == 0), stop=(kc == 7))
                S.sig(ins, self.sem("mm"), m)
            for s in range(4):
                bank, m = self.misc_begin(e, "uv")
                for kc in range(8):
                    ins = e.matmul(bank, lhsT=b.hT[:, kc, s * 128:(s + 1) * 128], rhs=b.wuv[:, kc, :], start=(kc == 0), stop=(kc == 7))
                S.sig(ins, self.sem("mm"), m)
            for s in range(4):
                bank, m = self.misc_begin(e, "sp")
                S.wait(e, self.sem("vn"), (i, s))
                for g4 in range(4):
                    ins = e.matmul(bank[:, g4 * 64:(g4 + 1) * 64], lhsT=b.wsT[:, g4, :], rhs=b.vnb[:, s, g4 * 64:(g4 + 1) * 64],
                                   start=True, stop=True)
                S.sig(ins, self.sem("mm"), m)
            mlast = self.m["pe"] - 1
            for gi, g in enumerate(self.groups(i)):
                if gi == 0:
                    def w0(e, mlast=mlast, i=i):
                        for mm_ in range(max(0, mlast - self.NSR + 1), mlast + 1):
                            S.wait(e, self.sem(self.MSEM[self.mk[mm_]]), ("rel", mm_))
                        S.wait(e, self.sem("ldk"), ("v", i, self.kts(i)[-1] if self.kts(i)[-1] == 17 else 13))
                        S.wait(e, self.sem("mA"), ("rel", mlast - 8))
                    g["wait_pe"] = w0
                self.pe_group_pairs(e, g)
                self.pe_post(e, g)
            S.wait(e, self.sem("ldy"), i)
            S.wait(e, self.sem("gate"), (i, 3))
            S.wait(e, self.sem("tpc"), (3 * i + 2, 1))
            for kc in range(8):
                gq = (NQ + i) * 8 + kc
                S.wait(e, self.nt.n + "_ev", gq - 2 if kc >= 2 else i * 8 + 6 + kc)
                for s in range(4):
                    ins = e.transpose(b.pst[:, gq % 2, s * 128:(s + 1) * 128], b.y[:, s, kc * 128:(kc + 1) * 128], b.identb[:, :])
                S.sig(ins, self.nt.n + "_tr", gq)
            S.wait(e, self.nt.n + "_ev", (NQ + i) * 8 + 7)
            S.wait(e, self.sem("p"), self.gu["pe"] - 1)
            for s in range(4):
                for half in range(2):
                    bank, m = self.misc_begin(e, "op")
                    for kc in range(8):
                        ins = e.matmul(bank, lhsT=b.hT[:, kc, s * 128:(s + 1) * 128], rhs=b.wout[:, kc, half * 512:(half + 1) * 512],
                                       start=(kc == 0), stop=(kc == 7))
                    S.sig(ins, self.sem("mm"), m)
            S.sig(e.transpose(b.pst[:, 0, 0:128], b.identb[:, :], b.identb[:, :]), self.sem("ytr"), i)

    def act(self, e):
        S, b = self.S, self.b
        S.wait(e, self.sem("ldc"), ("c", 9))
        ins = e.activation(out=b.esink[:, :], in_=b.esink[:, :], func=AF.Exp)
        S.sig(ins, self.sem("es"), 0)
        for g4 in range(4):
            S.wait(e, self.sem("wst"), g4)
            ins = e.activation(out=b.wsT[:, g4, :], in_=b.tps[:, 0, :], func=AF.Copy)
            S.sig(ins, self.sem("mA"), ("ws", g4))
        S.wait(e, self.sem("wst"), 4)
        ins = e.activation(out=b.bsT[:, :], in_=b.tps[:, 1, 0:4], func=AF.Copy)
        S.sig(ins, self.sem("es"), 1)
        m = 0
        for i in range(NQ):
            self.nt.act(e, i, b.pst, b.hT,
                        wait_hT_free=lambda: (S.wait(e, self.sem("mm"), self.m_last_op) if i >= 1 else None), rstd=b.rstd)
            for g in range(3):
                S.wait(e, self.sem("mm"), m)
                if i >= 1 and g == 0:
                    S.wait(e, self.sem("pv"), self.gu["act"] - 1)
                ins = e.activation(out=b.cq[:, g, :], in_=b.sring[:, m % self.NSR, :], func=AF.Copy)
                S.sig(ins, self.sem("mA"), ("rel", m))
                m += 1
            m += 8
            S.wait(e, self.sem("lnv"), i)
            ins = e.activation(out=b.st[:, 8:12], in_=b.st[:, 8:12], func=AF.Sqrt)
            S.sig(ins, self.sem("lnq"), i)
            for g in self.groups(i):
                self.act_group(e, g)
            for kc in range(8):
                gq = (NQ + i) * 8 + kc
                S.wait(e, self.nt.n + "_tr", gq)
                ins = e.activation(out=b.hT[:, kc, :], in_=b.pst[:, gq % 2, 0:512], func=AF.Copy)
                S.sig(ins, self.nt.n + "_ev", gq)
            for s in range(4):
                for half in range(2):
                    S.wait(e, self.sem("mm"), m)
                    if s == 0 and half == 0 and i >= 1:
                        S.wait(e, self.sem("res"), i - 1)
                    ins = e.activation(out=b.osb[:, s, half * 512:(half + 1) * 512], in_=b.sring[:, m % self.NSR, :], func=AF.Copy)
                    S.sig(ins, self.sem("mA"), ("rel", m))
                    self.m_last_op = m
                    m += 1
            S.wait(e, self.sem("onv"), i)
            ins = e.activation(out=b.os2[:, 4:8], in_=b.os2[:, 4:8], func=AF.Sqrt)
            S.sig(ins, self.sem("onq"), i)

    def consume(self, e, g, c):
        S, b = self.S, self.b
        f = self.sem("f")
        head = c * 3 + g["g"]
        ins = e.tensor_scalar(out=b.r[:, 0:4], in0=b.tps[:, :, 64], scalar1=b.esink[:, head:head + 1], scalar2=None, op0=ALU.add)
        S.fence(e, ins, f)
        ins = e.reciprocal(out=b.r[:, 0:4], in_=b.r[:, 0:4])
        S.fence(e, ins, f)
        for s in range(4):
            ins = e.tensor_scalar(out=b.y[:, s, 640 + head * 64:640 + (head + 1) * 64], in0=b.tps[:, s, 0:64],
                                  scalar1=b.r[:, s:s + 1], scalar2=None, op0=ALU.mult)
        S.sig(ins, self.sem("tpc"), (g["gid"], c))

    def dve(self, e):
        S, b = self.S, self.b
        f = self.sem("f")
        ins = e.memset(b.cv[:, 18, :, :], 0.0)
        ins = e.memset(b.cv[:, 0:18, :, 64:65], 1.0)
        ins = e.memset(b.zero[:, :], 0.0)
        S.sig(ins, self.sem("ones"), 0)
        S.wait(e, self.sem("ldc"), ("c", 9))
        S.wait(e, self.sem("es"), 1)
        m = 0
        for i in range(NQ):
            sl = i % 2
            xt = b.xt[:, sl, :, :]
            self.nt.dve(e, i, xt, b.gbc, b.hb, b.junk, b.ss, b.rstd,
                        wait_x=lambda: S.wait(e, self.sem("ldx%d" % sl), i),
                        wait_hb_free=lambda: (self.nt.wait_tr_done(e, i - 1) if i >= 1 else None))
            m += 3
            for s in range(4):
                S.wait(e, self.sem("mm"), m)
                if i >= 1 and s == 0:
                    S.wait(e, self.sem("gate"), (i - 1, 3))
                ins = e.tensor_copy(out=b.u[:, s, :], in_=b.sring[:, m % self.NSR, 0:256])
                ins = e.tensor_copy(out=b.vt[:, s, :], in_=b.sring[:, m % self.NSR, 256:512])
                S.sig(ins, self.sem("mD"), ("rel", m))
                S.wait(e, self.sem("mD"), ("rel", m))
                ins = e.tensor_reduce(out=b.st[:, s:s + 1], in_=b.vt[:, s, :], axis=AX.X, op=ALU.add)
                ins = e.tensor_tensor(out=b.junk[:, 0:256], in0=b.vt[:, s, :], in1=b.vt[:, s, :], op=ALU.mult)
                S.fence(e, ins, f)
                ins = e.tensor_reduce(out=b.st[:, 4 + s:5 + s], in_=b.junk[:, 0:256], axis=AX.X, op=ALU.add)
                S.fence(e, ins, f)
                m += 1
            ins = e.tensor_scalar(out=b.st[:, 0:4], in0=b.st[:, 0:4], scalar1=1.0 / 256, scalar2=None, op0=ALU.mult)
            S.fence(e, ins, f)
            ins = e.tensor_tensor(out=b.st[:, 12:16], in0=b.st[:, 0:4], in1=b.st[:, 0:4], op=ALU.mult)
            S.fence(e, ins, f)
            ins = e.scalar_tensor_tensor(out=b.st[:, 8:12], in0=b.st[:, 4:8], scalar=1.0 / 256, in1=b.st[:, 12:16],
                                         op0=ALU.mult, op1=ALU.subtract)
            S.fence(e, ins, f)
            ins = e.tensor_scalar(out=b.st[:, 8:12], in0=b.st[:, 8:12], scalar1=EPS, scalar2=None, op0=ALU.add)
            S.sig(ins, self.sem("lnv"), i)
            S.wait(e, self.sem("lnq"), i)
            ins = e.reciprocal(out=b.st[:, 8:12], in_=b.st[:, 8:12])
            S.fence(e, ins, f)
            for s in range(4):
                ins = e.tensor_scalar(out=b.vt[:, s, :], in0=b.vt[:, s, :], scalar1=b.st[:, s:s + 1], scalar2=b.st[:, 8 + s:9 + s],
                                      op0=ALU.subtract, op1=ALU.mult)
                S.fence(e, ins, f)
                ins = e.tensor_tensor(out=b.vt[:, s, :], in0=b.vt[:, s, :], in1=b.lng[:, :], op=ALU.mult)
                S.fence(e, ins, f)
                ins = e.tensor_tensor(out=b.vnb[:, s, :], in0=b.vt[:, s, :], in1=b.lnb[:, :], op=ALU.add)
                S.sig(ins, self.sem("vn"), (i, s))
            for s in range(4):
                S.wait(e, self.sem("mm"), m)
                if i >= 1 and s == 0:
                    S.wait(e, self.sem("ytr"), i - 1)
                for g4 in range(4):
                    ins = e.scalar_tensor_tensor(out=b.y[:, s, 384 + g4 * 64:384 + (g4 + 1) * 64],
                                                 in0=b.sring[:, m % self.NSR, g4 * 64:(g4 + 1) * 64], scalar=b.bsT[:, g4:g4 + 1],
                                                 in1=b.u[:, s, g4 * 64:(g4 + 1) * 64], op0=ALU.add, op1=ALU.mult)
                S.sig(ins, self.sem("mD"), ("rel", m))
                S.wait(e, self.sem("mD"), ("rel", m))
                S.sig(e.tensor_copy(out=b.r[:, 4:5], in_=b.zero[:, 0:1]), self.sem("gate"), (i, s))
                m += 1
            for g in self.groups(i):
                self.dve_group(e, g, self.consume)
            for s in range(4):
                S.wait(e, self.sem("mA"), ("rel", m + 1))
                ins = e.tensor_tensor(out=b.junk[:, :], in0=b.osb[:, s, :], in1=b.osb[:, s, :], op=ALU.mult)
                S.fence(e, ins, f)
                ins = e.tensor_reduce(out=b.os2[:, s:s + 1], in_=b.junk[:, :], axis=AX.X, op=ALU.add)
                S.fence(e, ins, f)
                m += 2
            ins = e.tensor_scalar(out=b.os2[:, 4:8], in0=b.os2[:, 0:4], scalar1=1.0 / D, scalar2=EPS, op0=ALU.mult, op1=ALU.add)
            S.sig(ins, self.sem("onv"), i)
            S.wait(e, self.sem("onq"), i)
            ins = e.reciprocal(out=b.os2[:, 4:8], in_=b.os2[:, 4:8])
            S.fence(e, ins, f)
            for s in range(4):
                ins = e.tensor_tensor(out=b.osb[:, s, :], in0=b.osb[:, s, :], in1=b.gpost[:, :], op=ALU.mult)
                S.fence(e, ins, f)
                ins = e.scalar_tensor_tensor(out=xt[:, s, :], in0=b.osb[:, s, :], scalar=b.os2[:, 4 + s:5 + s], in1=xt[:, s, :],
                                             op0=ALU.mult, op1=ALU.add)
            S.sig(ins, self.sem("res"), i)
            S.wait(e, self.sem("res"), i)


NMR = 6


class TokPhase(Phase):
    def reset(self):
        self.m = 0
        self.mk = []

    def ring(self, e, rel_sem):
        S = self.S
        m = self.m
        self.mk.append(rel_sem)
        if m >= NMR:
            S.wait(e, self.sem(self.mk[m - NMR]), ("rel", m - NMR))
        self.m = m + 1
        return self.b.ring[:, m % NMR, :], m

    def dve_postnorm(self, e, i, s, xt):
        S, b = self.S, self.b
        f = self.sem("f")
        o = b.osb[:, s % 2, :]
        ins = e.tensor_tensor(out=b.junk[:, :], in0=o, in1=o, op=ALU.mult)
        S.fence(e, ins, f)
        ins = e.tensor_reduce(out=b.os2[:, 0:1], in_=b.junk[:, :], axis=AX.X, op=ALU.add)
        S.fence(e, ins, f)
        ins = e.tensor_scalar(out=b.os2[:, 1:2], in0=b.os2[:, 0:1], scalar1=1.0 / D, scalar2=EPS, op0=ALU.mult, op1=ALU.add)
        S.sig(ins, self.sem("onv"), (i, s))
        S.wait(e, self.sem("onq"), (i, s))
        ins = e.reciprocal(out=b.os2[:, 1:2], in_=b.os2[:, 1:2])
        ins2 = e.tensor_tensor(out=o, in0=o, in1=b.gpost[:, :], op=ALU.mult)
        S.fence(e, ins2, f)
        ins = e.scalar_tensor_tensor(out=xt[:, s, :], in0=o, scalar=b.os2[:, 1:2], in1=xt[:, s, :], op0=ALU.mult, op1=ALU.add)
        S.sig(ins, self.sem("res"), (i, s))
        S.wait(e, self.sem("res"), (i, s))

    def act_postnorm(self, e, i, s):
        S, b = self.S, self.b
        S.wait(e, self.sem("onv"), (i, s))
        ins = e.activation(out=b.os2[:, 1:2], in_=b.os2[:, 1:2], func=AF.Sqrt)
        S.sig(ins, self.sem("onq"), (i, s))

    def pool_store(self, e, xo):
        S, b = self.S, self.b
        for i in range(NQ):
            S.wait(e, self.sem("res"), (i, 3))
            ins = e.dma_start(out=xo[i * 512:(i + 1) * 512, :].rearrange("(s p) d -> p s d", p=128), in_=b.xt[:, :, :])
            S.sig(ins, self.sem("stx"), i, dma=True)
        S.wait(e, self.sem("stx"), NQ - 1)

    def sp_loadx(self, e, i, xin):
        S, b = self.S, self.b
        if i >= 1:
            S.wait(e, self.sem("stx"), i - 1)
        ins = e.dma_start(out=b.xt[:, :, :], in_=xin[i * 512:(i + 1) * 512, :].rearrange("(s p) d -> p s d", p=128))
        S.sig(ins, self.sem("ldx"), i, dma=True)


class P3a(TokPhase):
    NBLK = DFF // 128

    def __init__(self, prog, tag, x1, w_fi, w_fo, g_pre, g_post, identb, x2):
        super().__init__(prog, tag)
        self.x1, self.w_fi, self.w_fo, self.g_pre, self.g_post, self.identb_d, self.x2 = x1, w_fi, w_fo, g_pre, g_post, identb, x2
        self.nt = NormT(self, "nt")

    def alloc(self, nc, es):
        b = Bufs()
        b.wfi = sb(nc, es, "f_wfi", [128, 8, 2 * DFF], BF16)
        b.wfo = sb(nc, es, "f_wfo", [128, self.NBLK, D], BF16)
        b.gbc = sb(nc, es, "f_gbc", [128, D], F32)
        b.gpost = sb(nc, es, "f_gpost", [128, D], F32)
        b.identb = sb(nc, es, "f_idb", [128, 128], BF16)
        b.xt = sb(nc, es, "f_xt", [128, 4, D], F32)
        b.hb = sb(nc, es, "f_hb", [128, 4, D], BF16)
        b.ss = sb(nc, es, "f_ss", [128, 4], F32)
        b.rstd = sb(nc, es, "f_rstd", [128, 4], F32)
        b.hT = sb(nc, es, "f_hT", [128, 8, 512], BF16)
        b.actT = sb(nc, es, "f_actT", [128, self.NBLK, 512], BF16)
        b.sg = sb(nc, es, "f_sg", [128, 2, 512], F32)
        b.junk = b.sg[:, :, :].rearrange("p a b -> p (a b)")
        b.osb = sb(nc, es, "f_osb", [128, 2, D], F32)
        b.os2 = sb(nc, es, "f_os2", [128, 4], F32)
        b.pst = ps(nc, es, "f_pst", [128, 2, 1024], BF16)
        b.ring = ps(nc, es, "f_ring", [128, NMR, 512], F32)
        return b

    def sp(self, e):
        S, b = self.S, self.b
        ld = self.sem("ldc")
        for n, (dst, src) in enumerate([(b.identb[:, :], self.identb_d[:, :]), (b.gbc[:, :], self.g_pre.partition_broadcast(128)),
                                        (b.gpost[:, :], self.g_post.partition_broadcast(128))]):
            ins = e.dma_start(out=dst, in_=src)
            S.sig(ins, ld, ("c", n), dma=True)
        for i in range(NQ):
            self.sp_loadx(e, i, self.x1)

    def pool(self, e):
        S, b = self.S, self.b
        lw = self.sem("lw")
        nch = 8
        w = 2 * DFF // nch
        for k in range(nch):
            ins = e.dma_start(out=b.wfi[:, :, k * w:(k + 1) * w], in_=wview(self.w_fi, k * w, (k + 1) * w))
            S.sig(ins, lw, ("fi", k), dma=True)
        for k in range(2):
            ins = e.dma_start(out=b.wfo[:, :, k * 512:(k + 1) * 512], in_=wview(self.w_fo, k * 512, (k + 1) * 512))
            S.sig(ins, lw, ("fo", k), dma=True)
        self.pool_store(e, self.x2)

    def pe(self, e):
        S, b = self.S, self.b
        S.wait(e, self.sem("ldc"), ("c", 2))
        S.wait(e, self.sem("lw"), ("fo", 1))
        for i in range(NQ):
            self.nt.pe(e, i, b.hb, b.pst, b.identb)
            self.nt.wait_hT(e, i)
            for blk in range(self.NBLK):
                for part, rel in ((0, "mA"), (1, "mD")):
                    bank, m = self.ring(e, rel)
                    c0 = part * DFF + blk * 128
                    for kc in range(8):
                        ins = e.matmul(bank, lhsT=b.wfi[:, kc, c0:c0 + 128], rhs=b.hT[:, kc, :], start=(kc == 0), stop=(kc == 7))
                    S.sig(ins, self.sem("mm"), m)
            S.wait(e, self.sem("mD"), ("rel", self.m - 1))
            for s in range(4):
                for half in range(2):
                    bank, m = self.ring(e, "mA")
                    for blk in range(self.NBLK):
                        ins = e.matmul(bank, lhsT=b.actT[:, blk, s * 128:(s + 1) * 128], rhs=b.wfo[:, blk, half * 512:(half + 1) * 512],
                                       start=(blk == 0), stop=(blk == self.NBLK - 1))
                    S.sig(ins, self.sem("mm"), m)

    def act(self, e):
        S, b = self.S, self.b
        m = 0
        nb = 0
        for i in range(NQ):
            self.nt.act(e, i, b.pst, b.hT,
                        wait_hT_free=lambda: (S.wait(e, self.sem("mm"), m - 9) if i >= 1 else None), rstd=b.rstd)
            for blk in range(self.NBLK):
                S.wait(e, self.sem("mm"), m)
                if nb >= 2:
                    S.wait(e, self.sem("mD"), ("rel", self.sgrel[nb - 2]))
                ins = e.activation(out=b.sg[:, nb % 2, :], in_=b.ring[:, m % NMR, :], func=AF.Silu)
                S.sig(ins, self.sem("mA"), ("rel", m))
                self.sgrel.append(m + 1)
                nb += 1
                m += 2
            for s in range(4):
                for half in range(2):
                    S.wait(e, self.sem("mm"), m)
                    if half == 0 and (i, s) >= (0, 2):
                        ps_ = (i, s - 2) if s >= 2 else (i - 1, s + 2)
                        S.wait(e, self.sem("res"), ps_)
                    ins = e.activation(out=b.osb[:, s % 2, half * 512:(half + 1) * 512], in_=b.ring[:, m % NMR, :], func=AF.Copy)
                    S.sig(ins, self.sem("mA"), ("rel", m))
                    m += 1
                self.act_postnorm(e, i, s)

    def reset(self):
        super().reset()
        self.sgrel = []

    def dve(self, e):
        S, b = self.S, self.b
        S.wait(e, self.sem("ldc"), ("c", 2))
        m = 0
        nb = 0
        for i in range(NQ):
            self.nt.dve(e, i, b.xt, b.gbc, b.hb, b.junk, b.ss, b.rstd,
                        wait_x=lambda: S.wait(e, self.sem("ldx"), i),
                        wait_hb_free=lambda: (self.nt.wait_tr_done(e, i - 1) if i >= 1 else None))
            for blk in range(self.NBLK):
                S.wait(e, self.sem("mm"), m + 1)
                S.wait(e, self.sem("mA"), ("rel", m))
                if i >= 1 and blk == 0:
                    S.wait(e, self.sem("mm"), m - 1)
                ins = e.tensor_tensor(out=b.actT[:, blk, :], in0=b.sg[:, nb % 2, :], in1=b.ring[:, (m + 1) % NMR, :], op=ALU.mult)
                S.sig(ins, self.sem("mD"), ("rel", m + 1))
                nb += 1
                m += 2
            for s in range(4):
                S.wait(e, self.sem("mA"), ("rel", m + 1))
                self.dve_postnorm(e, i, s, b.xt)
                m += 2


class P3b(TokPhase):
    def __init__(self, prog, tag, x2, p, w_up, w_gate, g_gate, g_post, identb, x3):
        super().__init__(prog, tag)
        self.x2, self.p, self.w_up, self.w_gate, self.g_gate, self.g_post, self.identb_d, self.x3 = x2, p, w_up, w_gate, g_gate, g_post, identb, x3
        self.nt = NormT(self, "nt")

    def alloc(self, nc, es):
        b = Bufs()
        b.wg = sb(nc, es, "e_wg", [128, 8, D], BF16)
        b.wu = sb(nc, es, "e_wu", [128, 2, D], BF16)
        b.gbc = sb(nc, es, "e_gbc", [128, D], F32)
        b.gpost = sb(nc, es, "e_gpost", [128, D], F32)
        b.identb = sb(nc, es, "e_idb", [128, 128], BF16)
        b.xt = sb(nc, es, "e_xt", [128, 4, D], F32)
        b.pt = sb(nc, es, "e_pt", [128, 4, 256], F32)
        b.pb = sb(nc, es, "e_pb", [128, 4, 256], BF16)
        b.pT = sb(nc, es, "e_pT", [128, 2, 512], BF16)
        b.hb = sb(nc, es, "e_hb", [128, 4, D], BF16)
        b.junk = sb(nc, es, "e_junk", [128, D], F32)
        b.ss = sb(nc, es, "e_ss", [128, 4], F32)
        b.rstd = sb(nc, es, "e_rstd", [128, 4], F32)
        b.hT = sb(nc, es, "e_hT", [128, 8, 512], BF16)
        b.sgm = sb(nc, es, "e_sgm", [128, 2, D], F32)
        b.osb = sb(nc, es, "e_osb", [128, 2, D], F32)
        b.os2 = sb(nc, es, "e_os2", [128, 4], F32)
        b.pst = ps(nc, es, "e_pst", [128, 2, 1024], BF16)
        b.ring = ps(nc, es, "e_ring", [128, NMR, 512], F32)
        return b

    def sp(self, e):
        S, b = self.S, self.b
        ld = self.sem("ldc")
        for n, (dst, src) in enumerate([(b.identb[:, :], self.identb_d[:, :]), (b.gbc[:, :], self.g_gate.partition_broadcast(128)),
                                        (b.gpost[:, :], self.g_post.partition_broadcast(128))]):
            ins = e.dma_start(out=dst, in_=src)
            S.sig(ins, ld, ("c", n), dma=True)
        for i in range(NQ):
            self.sp_loadx(e, i, self.x2)
            if i >= 1:
                S.wait(e, self.sem("pb"), i - 1)
            ins = e.dma_start(out=b.pt[:, :, :], in_=self.p[i * 512:(i + 1) * 512, :].rearrange("(s p) d -> p s d", p=128))
            S.sig(ins, self.sem("ldp"), i, dma=True)

    def pool(self, e):
        S, b = self.S, self.b
        lw = self.sem("lw")
        for k in range(2):
            ins = e.dma_start(out=b.wg[:, :, k * 512:(k + 1) * 512], in_=wview(self.w_gate, k * 512, (k + 1) * 512))
            S.sig(ins, lw, ("g", k), dma=True)
        ins = e.dma_start(out=b.wu[:, :, :], in_=wview(self.w_up, 0, D))
        S.sig(ins, lw, "u", dma=True)
        self.pool_store(e, self.x3)

    def pe(self, e):
        S, b = self.S, self.b
        S.wait(e, self.sem("ldc"), ("c", 2))
        S.wait(e, self.sem("lw"), "u")
        for i in range(NQ):
            self.nt.pe(e, i, b.hb, b.pst, b.identb)
            S.wait(e, self.sem("pb"), i)
            for kc in range(2):
                gq = (NQ + i) * 8 + kc
                S.wait(e, self.nt.n + "_ev", i * 8 + 6 + kc)
                for s in range(4):
                    ins = e.transpose(b.pst[:, gq % 2, s * 128:(s + 1) * 128], b.pb[:, s, kc * 128:(kc + 1) * 128], b.identb[:, :])
                S.sig(ins, self.nt.n + "_tr", gq)
            S.wait(e, self.nt.n + "_ev", (NQ + i) * 8 + 1)
            for s in range(4):
                for half in range(2):
                    bank, m = self.ring(e, "mD")
                    for kc in range(2):
                        ins = e.matmul(bank, lhsT=b.pT[:, kc, s * 128:(s + 1) * 128], rhs=b.wu[:, kc, half * 512:(half + 1) * 512],
                                       start=(kc == 0), stop=(kc == 1))
                    S.sig(ins, self.sem("mm"), m)
                for half in range(2):
                    bank, m = self.ring(e, "mA")
                    for kc in range(8):
                        ins = e.matmul(bank, lhsT=b.hT[:, kc, s * 128:(s + 1) * 128], rhs=b.wg[:, kc, half * 512:(half + 1) * 512],
                                       start=(kc == 0), stop=(kc == 7))
                    S.sig(ins, self.sem("mm"), m)

    def act(self, e):
        S, b = self.S, self.b
        m = 0
        for i in range(NQ):
            self.nt.act(e, i, b.pst, b.hT,
                        wait_hT_free=lambda: (S.wait(e, self.sem("mm"), m - 1) if i >= 1 else None), rstd=b.rstd)
            for kc in range(2):
                gq = (NQ + i) * 8 + kc
                S.wait(e, self.nt.n + "_tr", gq)
                if i >= 1 and kc == 0:
                    S.wait(e, self.sem("mm"), m - 3)
                ins = e.activation(out=b.pT[:, kc, :], in_=b.pst[:, gq % 2, 0:512], func=AF.Copy)
                S.sig(ins, self.nt.n + "_ev", gq)
            for s in range(4):
                for half in range(2):
                    mz = m + 2 + half
                    S.wait(e, self.sem("mm"), mz)
                    if half == 0 and (i, s) >= (0, 2):
                        ps_ = (i, s - 2) if s >= 2 else (i - 1, s + 2)
                        S.wait(e, self.sem("eg"), ps_)
                    ins = e.activation(out=b.sgm[:, s % 2, half * 512:(half + 1) * 512], in_=b.ring[:, mz % NMR, :], func=AF.Sigmoid)
                    S.sig(ins, self.sem("mA"), ("rel", mz))
                m += 4
                self.act_postnorm(e, i, s)

    def dve(self, e):
        S, b = self.S, self.b
        S.wait(e, self.sem("ldc"), ("c", 2))
        m = 0
        for i in range(NQ):
            self.nt.dve(e, i, b.xt, b.gbc, b.hb, b.junk, b.ss, b.rstd,
                        wait_x=lambda: S.wait(e, self.sem("ldx"), i),
                        wait_hb_free=lambda: (self.nt.wait_tr_done(e, i - 1) if i >= 1 else None))
            S.wait(e, self.sem("ldp"), i)
            if i >= 1:
                S.wait(e, self.nt.n + "_tr", (NQ + i - 1) * 8 + 1)
            ins = e.tensor_copy(out=b.pb[:, :, :], in_=b.pt[:, :, :])
            S.sig(ins, self.sem("pb"), i)
            for s in range(4):
                for half in range(2):
                    S.wait(e, self.sem("mm"), m + half)
                    S.wait(e, self.sem("mA"), ("rel", m + 2 + half))
                    ins = e.tensor_tensor(out=b.osb[:, s % 2, half * 512:(half + 1) * 512], in0=b.sgm[:, s % 2, half * 512:(half + 1) * 512],
                                          in1=b.ring[:, (m + half) % NMR, :], op=ALU.mult)
                    S.sig(ins, self.sem("mD"), ("rel", m + half))
                S.wait(e, self.sem("mD"), ("rel", m + 1))
                S.sig(e.memset(b.os2[:, 2:3], 0.0), self.sem("eg"), (i, s))
                m += 4
                self.dve_postnorm(e, i, s, b.xt)


class AG(Phase):
    def __init__(self, prog, tag, E, G):
        super().__init__(prog, tag)
        self.E, self.G = E, G

    def pool(self, e):
        S = self.S
        for k in range(NCH):
            ins = e.collective_compute("AllGather", ALU.bypass, replica_groups=[[0, 1, 2, 3], [4, 5, 6, 7]],
                                       ins=[self.E[k, :].rearrange("(p c) -> p c", c=2048)],
                                       outs=[self.G[k, :, :].rearrange("r (p c) -> (r p) c", c=2048)])
            S.sig(ins, "cc", k)
        S.wait(e, "cc", NCH - 1)


def _local_tokens(c):
    ii = np.arange(8)[:, None]
    t = np.arange(512)[None, :]
    return (2048 * ii + 512 * c + t).reshape(-1)


WKEYS = [("g_pre_mix", [D]), ("w_in", [D, 2304]), ("g_diff_sub", [64]), ("gmlp_ln_g", [256]), ("gmlp_ln_b", [256]),
         ("w_spatial", [4, 128, 128]), ("b_spatial", [4, 128]), ("swa_sinks", [6]), ("w_out", [D, D]), ("g_post_mix", [D]),
         ("g_pre_ffn", [D]), ("w_ffn_in", [D, 2 * DFF]), ("w_ffn_out", [DFF, D]), ("g_post_ffn", [D]),
         ("w_ple_up", [256, D]), ("w_ple_gate", [D, D]), ("g_ple_gate", [D]), ("g_ple_post", [D])]


def build_fused(nl=L):
    P = Prog()
    ncol = cidx_map()[1]
    x = P.din("x", [NTOK, D], F32)
    pp = P.din("p", [L, NTOK, 256], F32)
    ctab = P.din("ctab", [128, ncol], F32)
    dn = P.din("dn", [128, 16, 512], F32)
    dw = P.din("dw", [128, 18, 512], F32)
    kaug = P.din("kaug", [6, 4, SEQ], BF16)
    qaug = P.din("qaug", [6, 2, 4, NTOK], BF16)
    identb = P.din("identb", [128, 128], BF16)
    identf = P.din("identf", [128, 128], F32)
    lamv = P.din("lamv", [L, 4, 32], F32)
    lamc = P.din("lamc", [L, 128, 2], F32)
    W = {k: P.din(k, [L] + shp, F32) for k, shp in WKEYS}
    E = P.dint("E", [NCH, CHE], BF16)
    G = P.dint("G", [NCH, 4, CHE], BF16)
    Qs = P.dint("Qs", [384, NTOK], BF16)
    ya_d = P.dint("ya_d", [128, 32, 384], BF16)
    x1 = P.dint("x1", [NTOK, D], F32)
    x2 = P.dint("x2", [NTOK, D], F32)
    x3 = P.dint("x3", [NTOK, D], F32)
    xo = P.dout("xo", [NTOK, D], F32)
    for l in range(nl):
        xin = x if l == 0 else x3
        xout = xo if l == nl - 1 else x3
        P.phases.append(P1(P, "p1", xin, W["w_in"][l], W["g_pre_mix"][l], identb, Qs, E))
        P.phases.append(P2a(P, "a", G, Qs, ctab, dn, kaug, qaug, identf, lamv[l], lamc[l], W["g_diff_sub"][l], ya_d, E=E))
        P.phases.append(P2b(P, "b", xin, W["w_in"][l], W["g_pre_mix"][l], G, ya_d, W["gmlp_ln_g"][l], W["gmlp_ln_b"][l],
                            W["w_spatial"][l], W["b_spatial"][l], W["swa_sinks"][l], W["w_out"][l], W["g_post_mix"][l],
                            dw, identb, identf, x1))
        P.phases.append(P3a(P, "f", x1, W["w_ffn_in"][l], W["w_ffn_out"][l], W["g_pre_ffn"][l], W["g_post_ffn"][l], identb, x2))
        P.phases.append(P3b(P, "e", x2, pp[l], W["w_ple_up"][l], W["w_ple_gate"][l], W["g_ple_gate"][l], W["g_ple_post"][l],
                            identb, xout))
    return P.build()


_PROGS = {}


def kernel(**inputs):
    if "F" not in _PROGS:
        _PROGS["F"] = build_fused()
    consts = host_consts()
    tabs = [host_tables(c) for c in range(4)]
    x = np.asarray(inputs["x"], np.float32)
    p = np.asarray(inputs["p"], np.float32)
    lamv = np.stack([np.stack([np.asarray(inputs[k][l], np.float32) for k in ("lam_q1", "lam_k1", "lam_q2", "lam_k2")])
                     for l in range(L)])
    lamc = np.stack([np.tile(np.array([[lam_init(l), 1.0 - lam_init(l)]], np.float32), (128, 1)) for l in range(L)])
    shared = {k: np.ascontiguousarray(np.asarray(inputs[k], np.float32)) for k, _ in WKEYS}
    shared.update(kaug=consts["kaug"], qaug=consts["qaug"], identb=consts["identb"], identf=consts["identf"],
                  lamv=lamv, lamc=lamc)
    cores = list(range(8))
    in_maps = []
    for core in cores:
        bb, c = core // 4, core % 4
        tok = _local_tokens(c)
        m = dict(shared)
        m["x"] = np.ascontiguousarray(x[bb][tok])
        m["p"] = np.ascontiguousarray(p[:, bb][:, tok])
        m["ctab"], m["dn"], m["dw"] = tabs[c]["ctab"], tabs[c]["dn"], tabs[c]["dw"]
        in_maps.append(m)
    res = run_bass_kernel_spmd(_PROGS["F"], in_maps, core_ids=cores)
    out = np.empty((NB, SEQ, D), np.float32)
    for core in cores:
        out[core // 4][_local_tokens(core % 4)] = np.asarray(res.results[core]["xo"], np.float32)
    return out
```

```python
import numpy as np
import ml_dtypes
from contextlib import ExitStack
import concourse.bass as bass
import concourse.mybir as mybir
from concourse.bass_utils import run_bass_kernel_spmd

F32 = mybir.dt.float32
BF16 = mybir.dt.bfloat16
AF = mybir.ActivationFunctionType
ALU = mybir.AluOpType
AX = mybir.AxisListType
NPBF = ml_dtypes.bfloat16

D = 1024
SEQ = 16384
NB = 2
L = 4
NTOK = 4096
NQ = 8
DFF = 2816
EPS = 1e-6
SC_A = 32 ** -0.5
SC_W = 64 ** -0.5
_k = np.arange(1, 13, dtype=np.float64)
_sl = np.exp2(-8.0 * _k / 12.0)
SL_DIFF = _sl[6:]
SL_WIN = _sl[:6]
CUT = 27.0
BIG = 30000.0
LA = 2
NSR = 3
NPR = 4


def lam_init(l):
    return 0.8 - 0.6 * float(np.exp(-0.3 * l))


class Dummy:
    def __getattr__(self, n):
        return self

    def __call__(self, *a, **k):
        return self

    def __getitem__(self, k):
        return self

    def __enter__(self):
        return self

    def __exit__(self, *a):
        return False


class Sync:
    def __init__(self):
        self.dry = True
        self.total = {}
        self.count = {}
        self.seq = {}
        self.rpos = {}
        self.handles = {}
        self.scope = None

    def _k(self, key, glob):
        return key if glob else (self.scope, key)

    def sig(self, ins, sem, key, dma=False, glob=False):
        inc = 16 if dma else 1
        k = (sem, self._k(key, glob))
        if self.dry:
            assert k not in self.count, k
            t = self.total.get(sem, 0) + inc
            self.total[sem] = t
            self.count[k] = t
            self.seq.setdefault(sem, []).append((k, t))
        else:
            i = self.rpos.get(sem, 0)
            kk, t = self.seq[sem][i]
            assert kk == k, (kk, k)
            self.rpos[sem] = i + 1
            ins.then_inc(self.handles[sem], inc)

    def wait(self, e, sem, key, glob=False):
        if not self.dry:
            e.wait_ge(self.handles[sem], self.count[(sem, self._k(key, glob))])

    def fence(self, e, ins, sem):
        if self.dry:
            t = self.total.get(sem, 0) + 1
            self.total[sem] = t
            self.seq.setdefault(sem, []).append((None, t))
        else:
            i = self.rpos.get(sem, 0)
            kk, t = self.seq[sem][i]
            assert kk is None, kk
            self.rpos[sem] = i + 1
            ins.then_inc(self.handles[sem], 1)
            e.wait_ge(self.handles[sem], t)


def band_tiles(h, i):
    out = []
    for j in range(128):
        if j < 16 * i:
            dmin = 2048 * i - (128 * j + 127)
        elif j >= 16 * (i + 1):
            dmin = 128 * j - (2048 * i + 2047)
        else:
            dmin = 0
        if SL_DIFF[h] * dmin <= CUT:
            out.append(j)
    return out


def j2idx(j):
    T = j // 4
    sub = j % 4
    r = T % 4
    ii = T // 4
    return r, 4 * ii + sub


_CIDX = None


def cidx_map():
    global _CIDX
    if _CIDX is None:
        m = {}
        n = 1
        for h in range(6):
            for i in range(NQ):
                for j in band_tiles(h, i):
                    if j < 16 * i or j >= 16 * (i + 1):
                        m[(h, i, j)] = n
                        n += 1
        _CIDX = (m, n)
    return _CIDX


def split_bf(v):
    hi = np.float32(np.asarray(v, np.float32).astype(NPBF).astype(np.float32))
    lo = np.float32(np.asarray(np.float32(v) - hi, np.float32).astype(NPBF).astype(np.float32))
    return hi, lo


def host_tables(c):
    m, n = cidx_map()
    ct = np.zeros((n,), np.float32)
    for (h, i, j), col in m.items():
        qc = 2048 * i + 512 * c + 256
        ct[col] = -SL_DIFF[h] * abs(128 * j - qc)
    ctab = np.ascontiguousarray(np.broadcast_to(ct[None, :], (128, n))).astype(np.float32)
    ki = np.arange(128)[:, None, None]
    qi = np.arange(512)[None, None, :]
    tg = np.arange(16)[None, :, None]
    dn = np.abs(128 * tg + ki - (512 * c + qi)).astype(np.float32)
    koff = np.array([-128] + [128 * t for t in range(16)] + [2048])[None, :, None]
    dw = np.abs(koff + ki - (512 * c + qi)).astype(np.float32)
    dw = np.where(dw <= 128.0, dw, BIG).astype(np.float32)
    return {"ctab": ctab, "dn": np.ascontiguousarray(dn), "dw": np.ascontiguousarray(dw)}


def host_consts():
    kaug = np.zeros((6, 4, 128), np.float32)
    qaug = np.zeros((6, 2, 4, NTOK), np.float32)
    qip = (np.arange(NTOK) % 512 - 256).astype(np.float32)
    for h in range(6):
        sp = SL_DIFF[h] / SC_A
        hi, lo = split_bf(sp)
        kaug[h, 0, :] = -hi
        kaug[h, 1, :] = -lo
        kaug[h, 2, :] = np.arange(128)
        kaug[h, 3, :] = np.arange(128)
        qaug[h, 0, 0, :] = qip
        qaug[h, 0, 1, :] = qip
        qaug[h, 0, 2, :] = hi
        qaug[h, 0, 3, :] = lo
        qaug[h, 1] = -qaug[h, 0]
    kaug = np.tile(kaug, (1, 1, 128))
    return {
        "kaug": kaug.astype(NPBF),
        "qaug": qaug.astype(NPBF),
        "identb": np.eye(128, dtype=np.float32).astype(NPBF),
        "identf": np.eye(128, dtype=np.float32),
    }


class Prog:
    def __init__(self):
        self.nc = bass.Bass("TRN2", target_bir_lowering=False)
        self.S = Sync()
        self.dram = {}
        self.phases = []

    def din(self, name, shape, dt):
        t = self.nc.dram_tensor(name, list(shape), dt, kind="ExternalInput").ap()
        self.dram[name] = t
        return t

    def dout(self, name, shape, dt):
        t = self.nc.dram_tensor(name, list(shape), dt, kind="ExternalOutput").ap()
        self.dram[name] = t
        return t

    def dint(self, name, shape, dt):
        t = self.nc.dram_tensor(name, list(shape), dt, kind="Internal").ap()
        self.dram[name] = t
        return t

    def build(self):
        S = self.S
        nph = len(self.phases)
        engs = ["sp", "act", "pe", "dve", "pool"]
        S.dry = True
        for pi, ph in enumerate(self.phases):
            S.scope = pi
            ph.reset()
            ph.bind(Dummy())
            for en in engs:
                e = Dummy()
                self._barrier_pre(en, e, pi)
                getattr(ph, en)(e)
                self._barrier_post(en, e, pi)
        S.dry = False
        nc = self.nc
        with ExitStack() as es:
            for sem in S.seq:
                S.handles[sem] = es.enter_context(nc.semaphore(sem))
            self.bar_tile = es.enter_context(nc.sbuf_tensor("bar_tile", [128, 8], F32))
            for pi, ph in enumerate(self.phases):
                S.scope = pi
                ph.reset()
                with ExitStack() as pes:
                    bufs = ph.alloc(nc, pes)
                    ph.bind(bufs)
                    with nc.Block() as block:
                        def mk(en, ph=ph, pi=pi):
                            def f(e):
                                S.scope = pi
                                self._barrier_pre(en, e, pi)
                                getattr(ph, en)(e)
                                self._barrier_post(en, e, pi)
                            return f
                        block.sync(mk("sp"))
                        block.scalar(mk("act"))
                        block.tensor(mk("pe"))
                        block.vector(mk("dve"))
                        block.gpsimd(mk("pool"))
        for sem in S.seq:
            assert S.rpos.get(sem, 0) == len(S.seq[sem]), sem
        return nc

    def _barrier_pre(self, en, e, pi):
        if pi == 0 or en == "pool":
            return
        self.S.wait(e, "bar", ("end", pi - 1), glob=True)

    def _barrier_post(self, en, e, pi):
        if en != "pool":
            return
        S = self.S
        if S.dry:
            S.sig(None, "bar", ("end", pi), glob=True)
        else:
            ins = e.memset(self.bar_tile[:, :], 0.0)
            S.sig(ins, "bar", ("end", pi), glob=True)


class Phase:
    def __init__(self, prog, tag):
        self.P = prog
        self.S = prog.S
        self.tag = tag

    def reset(self):
        pass

    def bind(self, bufs):
        self.b = bufs

    def alloc(self, nc, es):
        return Dummy()

    def sp(self, e):
        pass

    def act(self, e):
        pass

    def pe(self, e):
        pass

    def dve(self, e):
        pass

    def pool(self, e):
        pass

    def sem(self, n):
        return n


class Bufs:
    pass


_UID = [0]


def _uname(name):
    _UID[0] += 1
    return "%s_%d" % (name, _UID[0])


def sb(nc, es, name, shape, dt):
    return es.enter_context(nc.sbuf_tensor(_uname(name), list(shape), dt))


def ps(nc, es, name, shape, dt):
    return es.enter_context(nc.psum_tensor(_uname(name), list(shape), dt))


NCH = 16
CHE = 262144


def wview(w, c0, c1):
    return w[:, c0:c1].rearrange("(kc p) c -> p kc c", p=128)


class NormT:
    def __init__(self, ph, name):
        self.ph = ph
        self.S = ph.S
        self.n = ph.sem(name)

    def dve(self, e, t, xt, gbc, hb, junk, ss, rstd, wait_x, wait_hb_free):
        S = self.S
        n = self.n
        wait_x()
        wait_hb_free()
        for s in range(4):
            ins = e.tensor_tensor(out=junk[:, :], in0=xt[:, s, :], in1=xt[:, s, :], op=ALU.mult)
            S.fence(e, ins, n + "_f")
            ins = e.tensor_reduce(out=ss[:, s:s + 1], in_=junk[:, :], axis=AX.X, op=ALU.add)
            S.fence(e, ins, n + "_f")
        ins = e.tensor_scalar(out=rstd[:, 0:4], in0=ss[:, 0:4], scalar1=1.0 / D, scalar2=EPS, op0=ALU.mult, op1=ALU.add)
        S.sig(ins, n + "_v", t)
        S.wait(e, n + "_sq", t)
        ins = e.reciprocal(out=rstd[:, 0:4], in_=rstd[:, 0:4])
        S.fence(e, ins, n + "_f")
        for s in range(4):
            ins = e.scalar_tensor_tensor(out=hb[:, s, :], in0=xt[:, s, :], scalar=rstd[:, s:s + 1], in1=gbc[:, :],
                                         op0=ALU.mult, op1=ALU.mult)
            S.sig(ins, n + "_hb", (t, s))

    def pe(self, e, t, hb, pst, identb):
        S = self.S
        n = self.n
        for kc in range(8):
            g = t * 8 + kc
            if g >= 2:
                S.wait(e, n + "_ev", g - 2)
            for s in range(4):
                if kc == 0:
                    S.wait(e, n + "_hb", (t, s))
                ins = e.transpose(pst[:, g % 2, s * 128:(s + 1) * 128], hb[:, s, kc * 128:(kc + 1) * 128], identb[:, :])
            S.sig(ins, n + "_tr", g)

    def act(self, e, t, pst, hT, wait_hT_free, rstd=None):
        S = self.S
        n = self.n
        S.wait(e, n + "_v", t)
        ins = e.activation(out=rstd[:, 0:4], in_=rstd[:, 0:4], func=AF.Sqrt)
        S.sig(ins, n + "_sq", t)
        wait_hT_free()
        for kc in range(8):
            g = t * 8 + kc
            S.wait(e, n + "_tr", g)
            ins = e.activation(out=hT[:, kc, :], in_=pst[:, g % 2, 0:512], func=AF.Copy)
            S.sig(ins, n + "_ev", g)

    def wait_hT(self, e, t):
        self.S.wait(e, self.n + "_ev", t * 8 + 7)

    def wait_tr_done(self, e, t):
        self.S.wait(e, self.n + "_tr", t * 8 + 7)


class P1(Phase):
    FM = [(0, 384), (384, 768), (2048, 2176)]
    TM = [(768, 1152), (2176, 2304)]

    def __init__(self, prog, tag, x, w_in, g_pre, identb, Qs, E):
        super().__init__(prog, tag)
        self.x, self.w_in, self.g_pre, self.identb_d = x, w_in, g_pre, identb
        self.Qs, self.E = Qs, E
        self.EK = E[0:6, :].rearrange("h (a n) -> (h a) n", n=NTOK)
        self.ECK = E[12:14, :].rearrange("h (a n) -> (h a) n", n=NTOK)
        self.nt = NormT(self, "nt")

    def alloc(self, nc, es):
        b = Bufs()
        b.wfm = sb(nc, es, "p1_wfm", [128, 8, 896], BF16)
        b.wtm = sb(nc, es, "p1_wtm", [128, 8, 512], BF16)
        b.gbc = sb(nc, es, "p1_gbc", [128, D], F32)
        b.identb = sb(nc, es, "p1_id", [128, 128], BF16)
        b.xt = sb(nc, es, "p1_xt", [128, 4, D], F32)
        b.hb = sb(nc, es, "p1_hb", [128, 4, D], BF16)
        b.junk = sb(nc, es, "p1_junk", [128, D], F32)
        b.ss = sb(nc, es, "p1_ss", [128, 4], F32)
        b.rstd = sb(nc, es, "p1_rstd", [128, 4], F32)
        b.hT = sb(nc, es, "p1_hT", [128, 8, 512], BF16)
        b.fm = sb(nc, es, "p1_fm", [128, 2, 7, 512], BF16)
        b.vt = sb(nc, es, "p1_vt", [128, 2, 4, 512], BF16)
        b.pst = ps(nc, es, "p1_pst", [128, 2, 1024], BF16)
        b.pm = ps(nc, es, "p1_pm", [128, 4, 512], F32)
        return b

    def sp(self, e):
        S, b = self.S, self.b
        ld = self.sem("ld")
        ins = e.dma_start(out=b.identb[:, :], in_=self.identb_d[:, :])
        S.sig(ins, ld, "id", dma=True)
        ins = e.dma_start(out=b.gbc[:, :], in_=self.g_pre.partition_broadcast(128))
        S.sig(ins, ld, "g", dma=True)
        for t in range(NQ):
            if t >= 1:
                S.wait(e, self.nt.n + "_hb", (t - 1, 3))
            ins = e.dma_start(out=b.xt[:, :, :], in_=self.x[t * 512:(t + 1) * 512, :].rearrange("(s p) d -> p s d", p=128))
            S.sig(ins, self.sem("ldx"), t, dma=True)

    def pool(self, e):
        S, b = self.S, self.b
        lw = self.sem("lw")
        c = 0
        for (c0, c1) in self.FM:
            ins = e.dma_start(out=b.wfm[:, :, c:c + (c1 - c0)], in_=wview(self.w_in, c0, c1))
            S.sig(ins, lw, ("fm", c0), dma=True)
            c += c1 - c0
        c = 0
        for (c0, c1) in self.TM:
            ins = e.dma_start(out=b.wtm[:, :, c:c + (c1 - c0)], in_=wview(self.w_in, c0, c1))
            S.sig(ins, lw, ("tm", c0), dma=True)
            c += c1 - c0
        for t in range(NQ):
            sl = t % 2
            st = self.sem("st%d" % sl)
            cs = slice(t * 512, (t + 1) * 512)
            S.wait(e, self.sem("ev"), ("fm", t, 6))
            ins = e.dma_start(out=self.Qs[:, cs].rearrange("(b p) n -> p b n", p=128), in_=b.fm[:, sl, 0:3, :])
            S.sig(ins, st, ("q", t), dma=True)
            ins = e.dma_start(out=self.EK[:, cs].rearrange("(b p) n -> p b n", p=128), in_=b.fm[:, sl, 3:6, :])
            S.sig(ins, st, ("k", t), dma=True)
            ins = e.dma_start(out=self.ECK[:, cs], in_=b.fm[:, sl, 6, :])
            S.sig(ins, st, ("ck", t), dma=True)
            S.wait(e, self.sem("ev"), ("tm", t, 3))
            for hh in range(8):
                ch = 6 + hh if hh < 6 else 14 + (hh - 6)
                dst = self.E[ch, :].rearrange("(p t d) -> p t d", p=128, t=32)
                ins = e.dma_start(out=dst[:, 4 * t:4 * t + 4, :], in_=b.vt[:, sl, :, hh * 64:(hh + 1) * 64])
                S.sig(ins, st, ("cv" if hh == 7 else ("v", hh), t), dma=True)
        S.wait(e, self.sem("st0"), ("cv", NQ - 2))
        S.wait(e, self.sem("st1"), ("cv", NQ - 1))

    def dve(self, e):
        S, b = self.S, self.b
        S.wait(e, self.sem("ld"), "g")
        for t in range(NQ):
            self.nt.dve(e, t, b.xt, b.gbc, b.hb, b.junk, b.ss, b.rstd,
                        wait_x=lambda: S.wait(e, self.sem("ldx"), t),
                        wait_hb_free=lambda: (self.nt.wait_tr_done(e, t - 1) if t >= 1 else None))

    def pe(self, e):
        S, b = self.S, self.b
        S.wait(e, self.sem("ld"), "g")
        S.wait(e, self.sem("lw"), ("tm", self.TM[-1][0]))
        n = 0
        for t in range(NQ):
            self.nt.pe(e, t, b.hb, b.pst, b.identb)
            self.nt.wait_hT(e, t)
            for blk in range(7):
                if n >= 4:
                    S.wait(e, self.sem("ev"), self.evkeys[n - 4])
                for kc in range(8):
                    ins = e.matmul(b.pm[:, n % 4, :], lhsT=b.wfm[:, kc, blk * 128:(blk + 1) * 128], rhs=b.hT[:, kc, :],
                                   start=(kc == 0), stop=(kc == 7))
                S.sig(ins, self.sem("mm"), ("fm", t, blk))
                self.evkeys.append(("fm", t, blk))
                n += 1
            for s in range(4):
                if n >= 4:
                    S.wait(e, self.sem("ev"), self.evkeys[n - 4])
                for kc in range(8):
                    ins = e.matmul(b.pm[:, n % 4, :], lhsT=b.hT[:, kc, s * 128:(s + 1) * 128], rhs=b.wtm[:, kc, :],
                                   start=(kc == 0), stop=(kc == 7))
                S.sig(ins, self.sem("mm"), ("tm", t, s))
                self.evkeys.append(("tm", t, s))
                n += 1

    def reset(self):
        self.evkeys = []

    def act(self, e):
        S, b = self.S, self.b
        n = 0
        for t in range(NQ):
            sl = t % 2
            self.nt.act(e, t, b.pst, b.hT,
                        wait_hT_free=lambda: (S.wait(e, self.sem("mm"), ("tm", t - 1, 3)) if t >= 1 else None),
                        rstd=b.rstd)
            for blk in range(7):
                S.wait(e, self.sem("mm"), ("fm", t, blk))
                if t >= 2 and blk == 0:
                    S.wait(e, self.sem("st%d" % sl), ("cv", t - 2))
                ins = e.activation(out=b.fm[:, sl, blk, :], in_=b.pm[:, n % 4, :], func=AF.Copy)
                S.sig(ins, self.sem("ev"), ("fm", t, blk))
                n += 1
            for s in range(4):
                S.wait(e, self.sem("mm"), ("tm", t, s))
                if t >= 2 and s == 0:
                    S.wait(e, self.sem("st%d" % sl), ("cv", t - 2))
                ins = e.activation(out=b.vt[:, sl, s, :], in_=b.pm[:, n % 4, :], func=AF.Copy)
                S.sig(ins, self.sem("ev"), ("tm", t, s))
                n += 1


class AttnPipe(Phase):
    NSR = NSR

    def reset(self):
        self.gu = {"pe": 0, "act": 0, "dve": 0}
        self.prev_tp = {"pe": None}

    def pe_group(self, e, g):
        S, b = self.S, self.b
        U = len(g["units"])
        g0 = self.gu["pe"]
        for t in range(U + LA):
            if t < U:
                u = g["units"][t]
                gu = g0 + t
                if gu >= self.NSR:
                    S.wait(e, self.sem("p"), gu - self.NSR)
                if t == 0 and g.get("wait_pe") is not None:
                    g["wait_pe"](e)
                ins = e.matmul(b.sring[:, gu % self.NSR, :], lhsT=u["lhsT"], rhs=u["rhs"], start=True, stop=True)
                S.sig(ins, self.sem("qk"), gu)
            if t >= LA:
                v = t - LA
                u = g["units"][v]
                gv = g0 + v
                S.wait(e, self.sem("p"), gv)
                if u["first"] and g["acc_prev"] is not None:
                    S.wait(e, self.sem("ev"), (g["acc_prev"], u["acc"]))
                ins = e.matmul(b.oacc[:, g["par"] * 2 + u["acc"], :], lhsT=u["vl"], rhs=b.pring[:, gv % NPR, :],
                               start=u["first"], stop=u["last"])
                S.sig(ins, self.sem("pv"), gv)
        self.gu["pe"] = g0 + U

    def pe_group_pairs(self, e, g):
        S, b = self.S, self.b
        LAP = 2
        U = len(g["units"])
        NP_ = U // 2
        g0 = self.gu["pe"]
        for t in range(NP_ + LAP):
            if t >= LAP:
                S.wait(e, self.sem("p"), g0 + 2 * (t - LAP) + 1)
            elif g0 + 2 * (t - LAP) + 1 >= 0:
                S.wait(e, self.sem("p"), g0 + 2 * (t - LAP) + 1)
            if t == 0 and g.get("wait_pe") is not None:
                g["wait_pe"](e)
            if t < NP_:
                for c in (0, 1):
                    u = g["units"][2 * t + c]
                    gu = g0 + 2 * t + c
                    ins = e.matmul(b.sring[:, gu % self.NSR, :], lhsT=u["lhsT"], rhs=u["rhs"], start=True, stop=True)
                    S.sig(ins, self.sem("qk"), gu)
            if t >= LAP:
                for c in (0, 1):
                    v = 2 * (t - LAP) + c
                    u = g["units"][v]
                    gv = g0 + v
                    if u["first"] and g["acc_prev"] is not None:
                        S.wait(e, self.sem("ev"), (g["acc_prev"], u["acc"]))
                    ins = e.matmul(b.oacc[:, g["par"] * 2 + u["acc"], :], lhsT=u["vl"], rhs=b.pring[:, gv % NPR, :],
                                   start=u["first"], stop=u["last"])
                    S.sig(ins, self.sem("pv"), gv)
        self.gu["pe"] = g0 + U

    def pe_post(self, e, g):
        S, b = self.S, self.b
        for c in (0, 1):
            S.wait(e, self.sem("ev"), (g["gid"], c))
            if self.prev_tp["pe"] is not None:
                S.wait(e, self.sem("tpc"), self.prev_tp["pe"])
            for s in range(4):
                ins = e.transpose(b.tps[:, s, 0:65], b.oT[0:65, c, s * 128:(s + 1) * 128], b.identf[0:65, 0:65])
            S.sig(ins, self.sem("tp"), (g["gid"], c))
            self.prev_tp["pe"] = (g["gid"], c)

    def act_group(self, e, g):
        S, b = self.S, self.b
        g0 = self.gu["act"]
        for t, u in enumerate(g["units"]):
            gu = g0 + t
            if u["kind"] == "G":
                S.wait(e, self.sem("gb"), gu)
            else:
                S.wait(e, self.sem("qk"), gu)
            if gu >= NPR:
                S.wait(e, self.sem("pv"), gu - NPR)
            ins = e.activation(out=b.pring[:, gu % NPR, :], in_=b.sring[:, gu % self.NSR, :], func=AF.Exp,
                               bias=u["bias"], scale=u["scale"])
            S.sig(ins, self.sem("p"), gu)
        self.gu["act"] = g0 + len(g["units"])

    def dve_group(self, e, g, consume):
        S, b = self.S, self.b
        g0 = self.gu["dve"]
        last = {}
        for t, u in enumerate(g["units"]):
            gu = g0 + t
            last[u["acc"]] = gu
            if u["kind"] == "G":
                S.wait(e, self.sem("qk"), gu)
                ins = e.scalar_tensor_tensor(out=b.sring[:, gu % self.NSR, :], in0=u["dD"], scalar=u["dcoef"],
                                             in1=b.sring[:, gu % self.NSR, :], op0=ALU.mult, op1=ALU.add)
                S.sig(ins, self.sem("gb"), gu)
        self.gu["dve"] = g0 + len(g["units"])
        for c in (0, 1):
            S.wait(e, self.sem("pv"), last[c])
            if g["gid"] >= 1:
                S.wait(e, self.sem("tp"), (g["gid"] - 1, c))
            ins = e.tensor_copy(out=b.oT[0:65, c, :], in_=b.oacc[0:65, g["par"] * 2 + c, :])
            S.sig(ins, self.sem("ev"), (g["gid"], c))
        for c in (0, 1):
            S.wait(e, self.sem("tp"), (g["gid"], c))
            consume(e, g, c)


class P2a(AttnPipe):
    NSR = 4

    AG_ORDER = [0, 6, 1, 7, 2, 8, 3, 9, 4, 10, 5, 11, 12, 13, 14, 15]

    def __init__(self, prog, tag, G, Qs, ctab, dn, kaug, qaug, identf, lamv, lamc, gsub, ya_d, E=None):
        super().__init__(prog, tag)
        self.E = E
        self.G, self.Qs, self.ctab_d, self.dn_d = G, Qs, ctab, dn
        self.kaug, self.qaug, self.identf_d, self.lamv, self.lamc_d, self.gsub_d, self.ya_d = kaug, qaug, identf, lamv, lamc, gsub, ya_d
        self.ncol = cidx_map()[1]

    def alloc(self, nc, es):
        b = Bufs()
        b.KT = sb(nc, es, "a_KT", [100, 2, SEQ], BF16)
        b.V = sb(nc, es, "a_V", [128, 2, 129, 65], BF16)
        b.QA = sb(nc, es, "a_QA", [100, 2, NTOK], BF16)
        b.QB = sb(nc, es, "a_QB", [100, 2, NTOK], BF16)
        b.ctab = sb(nc, es, "a_ctab", [128, self.ncol], F32)
        b.dn = sb(nc, es, "a_dn", [128, 16, 512], F32)
        b.identf = sb(nc, es, "a_idf", [128, 128], F32)
        b.pring = sb(nc, es, "a_pr", [128, NPR, 512], BF16)
        b.oT = sb(nc, es, "a_oT", [65, 2, 512], F32)
        b.lv = sb(nc, es, "a_lv", [128, 4, 32], F32)
        b.lt = sb(nc, es, "a_lt", [128, 8], F32)
        b.lamc = sb(nc, es, "a_lamc", [128, 2], F32)
        b.gsub = sb(nc, es, "a_gsub", [128, 64], F32)
        b.r = sb(nc, es, "a_r", [128, 8], F32)
        b.t0 = sb(nc, es, "a_t0", [128, 4, 64], F32)
        b.y = sb(nc, es, "a_y", [128, 4, 64], F32)
        b.sq = sb(nc, es, "a_sq", [128, 4, 64], F32)
        b.sst = sb(nc, es, "a_sst", [128, 32, 6], F32)
        b.ya = sb(nc, es, "a_ya", [128, 32, 384], BF16)
        b.sring = ps(nc, es, "a_sr", [128, 4, 512], F32)
        b.oacc = ps(nc, es, "a_oacc", [128, 2, 512], F32)
        b.tps = ps(nc, es, "a_tps", [128, 4, 128], F32)
        return b

    def groups(self):
        b = self.b
        cm, _ = cidx_map()
        out = []
        gid = 0
        for h in range(6):
            sl = h % 2
            sp = float(SL_DIFF[h] / SC_A)
            for i in range(NQ):
                units = []
                tl = band_tiles(h, i)
                qs = slice(512 * i, 512 * i + 512)
                for n, j in enumerate(tl):
                    r, lt = j2idx(j)
                    ks = slice(r * 4096 + lt * 128, r * 4096 + lt * 128 + 128)
                    for comp in (0, 1):
                        base = 64 * comp
                        vi = (r * 32 + lt) * 65
                        u = {"acc": comp, "first": n == 0, "last": n == len(tl) - 1, "scale": SC_A,
                             "vl": b.V[:, sl, :, :].rearrange("p t d -> p (t d)")[:, vi:vi + 128]}
                        if j < 16 * i or j >= 16 * (i + 1):
                            Q = b.QA if j < 16 * i else b.QB
                            u["kind"] = "F"
                            u["lhsT"] = b.KT[base:base + 36, sl, ks]
                            u["rhs"] = Q[base:base + 36, sl, qs]
                            u["bias"] = b.ctab[:, cm[(h, i, j)]:cm[(h, i, j)] + 1]
                        else:
                            u["kind"] = "G"
                            u["lhsT"] = b.KT[base:base + 32, sl, ks]
                            u["rhs"] = b.QA[base:base + 32, sl, qs]
                            u["bias"] = b.ctab[:, 0:1]
                            u["dD"] = b.dn[:, j - 16 * i, :]
                            u["dcoef"] = -sp
                        units.append(u)
                out.append({"gid": gid, "par": 0, "units": units, "h": h, "i": i,
                            "acc_prev": gid - 1 if gid >= 1 else None})
                gid += 1
        return out

    def sp(self, e):
        S, b = self.S, self.b
        ld = self.sem("ldc")
        lst = [(b.ctab[:, :], self.ctab_d[:, :]), (b.dn[:, :, :], self.dn_d[:, :, :]),
               (b.identf[:, :], self.identf_d[:, :]), (b.lamc[:, :], self.lamc_d[:, :]),
               (b.gsub[:, :], self.gsub_d.partition_broadcast(128))]
        for n, (dst, src) in enumerate(lst):
            ins = e.dma_start(out=dst, in_=src)
            S.sig(ins, ld, ("c", n), dma=True)
        for k in range(4):
            ins = e.dma_start(out=b.lv[:, k, :], in_=self.lamv[k, :].partition_broadcast(128))
            S.sig(ins, ld, ("lv", k), dma=True)
        for h in range(6):
            sl = h % 2
            sem = self.sem("ld%d" % sl)
            if h >= 2:
                S.wait(e, self.sem("hd"), h - 2)
            if self.E is not None:
                S.wait(e, "cc", 6 + h)
            for comp in (0, 1):
                rows = slice(h * 64 + comp * 32, h * 64 + comp * 32 + 32)
                base = 64 * comp
                ins = e.dma_start(out=b.KT[base:base + 32, sl, :].rearrange("p (r n) -> p r n", r=4),
                                  in_=self.G[h, :, :].rearrange("r (a n) -> a r n", n=NTOK)[comp * 32:(comp + 1) * 32])
                S.sig(ins, sem, ("k", h, comp), dma=True)
                ins = e.dma_start(out=b.KT[base + 32:base + 36, sl, :], in_=self.kaug[h, :, :])
                S.sig(ins, sem, ("ka", h, comp), dma=True)
                for var, Q in ((0, b.QA), (1, b.QB)):
                    ins = e.dma_start(out=Q[base:base + 32, sl, :], in_=self.Qs[rows, :])
                    S.sig(ins, sem, ("q", h, comp, var), dma=True)
                    ins = e.dma_start(out=Q[base + 32:base + 36, sl, :], in_=self.qaug[h, var, :, :])
                    S.sig(ins, sem, ("qa", h, comp, var), dma=True)
            for r in range(4):
                ins = e.dma_start(out=b.V[:, sl, r * 32:(r + 1) * 32, 0:64],
                                  in_=self.G[6 + h, r, :].rearrange("(p t d) -> p t d", p=128, t=32))
                S.sig(ins, sem, ("v", h, r), dma=True)

    def pe(self, e):
        S, b = self.S, self.b
        S.wait(e, self.sem("ldc"), ("lv", 3))
        S.wait(e, self.sem("ones"), 0)
        prev = None
        for g in self.groups():
            if g["i"] == 0:
                h = g["h"]
                g["wait_pe"] = lambda e, h=h: S.wait(e, self.sem("ld%d" % (h % 2)), ("v", h, 3))
            if prev is not None:
                self.pe_post(e, prev)
            self.pe_group_pairs(e, g)
            prev = g
        self.pe_post(e, prev)

    def act(self, e):
        S, b = self.S, self.b
        S.wait(e, self.sem("ldc"), ("lv", 3))
        S.wait(e, self.sem("lm"), "dots")
        ins = e.activation(out=b.lt[:, 2:4], in_=b.lt[:, 0:2], func=AF.Exp)
        S.sig(ins, self.sem("lma"), "exp")
        for g in self.groups():
            self.act_group(e, g)
        S.wait(e, self.sem("fin"), "v")
        ins = e.activation(out=b.sst[:, 0:4 * NQ, :], in_=b.sst[:, 0:4 * NQ, :], func=AF.Sqrt)
        S.sig(ins, self.sem("lma"), "sqrt")

    def consume(self, e, g, c):
        S, b = self.S, self.b
        f = self.sem("f")
        h, i = g["h"], g["i"]
        ins = e.reciprocal(out=b.r[:, 4 * c:4 * c + 4], in_=b.tps[:, :, 64])
        S.fence(e, ins, f)
        if c == 0:
            for s in range(4):
                ins = e.tensor_scalar(out=b.t0[:, s, :], in0=b.tps[:, s, 0:64], scalar1=b.r[:, s:s + 1], scalar2=None,
                                      op0=ALU.mult)
            S.sig(ins, self.sem("tpc"), (g["gid"], 0))
        else:
            ins = e.tensor_scalar(out=b.r[:, 4:8], in0=b.r[:, 4:8], scalar1=b.lt[:, 4:5], scalar2=None, op0=ALU.mult)
            S.fence(e, ins, f)
            for s in range(4):
                ins = e.scalar_tensor_tensor(out=b.y[:, s, :], in0=b.tps[:, s, 0:64], scalar=b.r[:, 4 + s:5 + s],
                                             in1=b.t0[:, s, :], op0=ALU.mult, op1=ALU.add)
            S.sig(ins, self.sem("tpc"), (g["gid"], 1))
            S.wait(e, self.sem("tpc"), (g["gid"], 1))
            ins = e.tensor_tensor(out=b.sq[:, :, :], in0=b.y[:, :, :], in1=b.y[:, :, :], op=ALU.mult)
            S.fence(e, ins, f)
            ins = e.tensor_reduce(out=b.sst[:, 4 * i:4 * i + 4, h], in_=b.sq[:, :, :], axis=AX.X, op=ALU.add)
            ins = e.tensor_copy(out=b.ya[:, 4 * i:4 * i + 4, h * 64:(h + 1) * 64], in_=b.y[:, :, :])
            S.fence(e, ins, f)
            if i == NQ - 1:
                S.sig(e.tensor_copy(out=b.lt[:, 7:8], in_=b.lt[:, 4:5]), self.sem("hd"), h)

    def dve(self, e):
        S, b = self.S, self.b
        f = self.sem("f")
        for sl in (0, 1):
            ins = e.memset(b.V[:, sl, 128, :], 0.0)
            ins = e.memset(b.V[:, sl, 0:128, 64:65], 1.0)
        S.sig(ins, self.sem("ones"), 0)
        S.wait(e, self.sem("ldc"), ("lv", 3))
        ins = e.tensor_tensor(out=b.lv[:, 0, :], in0=b.lv[:, 0, :], in1=b.lv[:, 1, :], op=ALU.mult)
        ins = e.tensor_tensor(out=b.lv[:, 2, :], in0=b.lv[:, 2, :], in1=b.lv[:, 3, :], op=ALU.mult)
        S.fence(e, ins, f)
        ins = e.tensor_reduce(out=b.lt[:, 0:1], in_=b.lv[:, 0, :], axis=AX.X, op=ALU.add)
        ins = e.tensor_reduce(out=b.lt[:, 1:2], in_=b.lv[:, 2, :], axis=AX.X, op=ALU.add)
        S.sig(ins, self.sem("lm"), "dots")
        S.wait(e, self.sem("lma"), "exp")
        ins = e.tensor_tensor(out=b.lt[:, 5:6], in0=b.lt[:, 3:4], in1=b.lt[:, 2:3], op=ALU.subtract)
        S.fence(e, ins, f)
        ins = e.tensor_tensor(out=b.lt[:, 4:5], in0=b.lt[:, 5:6], in1=b.lamc[:, 0:1], op=ALU.subtract)
        ins = e.tensor_scalar(out=b.gsub[:, :], in0=b.gsub[:, :], scalar1=b.lamc[:, 1:2], scalar2=None, op0=ALU.mult)
        S.fence(e, ins, f)
        for g in self.groups():
            self.dve_group(e, g, self.consume)
        ins = e.tensor_scalar(out=b.sst[:, 0:4 * NQ, :], in0=b.sst[:, 0:4 * NQ, :], scalar1=1.0 / 64, scalar2=EPS,
                              op0=ALU.mult, op1=ALU.add)
        S.sig(ins, self.sem("fin"), "v")
        S.wait(e, self.sem("lma"), "sqrt")
        ins = e.reciprocal(out=b.sst[:, 0:4 * NQ, :], in_=b.sst[:, 0:4 * NQ, :])
        S.fence(e, ins, f)
        for lt in range(4 * NQ):
            for h in range(6):
                ins = e.scalar_tensor_tensor(out=b.ya[:, lt, h * 64:(h + 1) * 64], in0=b.ya[:, lt, h * 64:(h + 1) * 64],
                                             scalar=b.sst[:, lt, h:h + 1], in1=b.gsub[:, :], op0=ALU.mult, op1=ALU.mult)
        S.sig(ins, self.sem("fin"), "ya")

    def pool(self, e):
        S, b = self.S, self.b
        if self.E is not None:
            for k in self.AG_ORDER:
                ins = e.collective_compute("AllGather", ALU.bypass, replica_groups=[[0, 1, 2, 3], [4, 5, 6, 7]],
                                           ins=[self.E[k, :].rearrange("(p c) -> p c", c=2048)],
                                           outs=[self.G[k, :, :].rearrange("r (p c) -> (r p) c", c=2048)])
                S.sig(ins, "cc", k)
            S.wait(e, "cc", 15)
        S.wait(e, self.sem("fin"), "ya")
        ins = e.dma_start(out=self.ya_d[:, 0:4 * NQ, :], in_=b.ya[:, 0:4 * NQ, :])
        S.sig(ins, self.sem("st"), 0, dma=True)
        S.wait(e, self.sem("st"), 0)


class P2b(AttnPipe):
    NSR = 4

    def __init__(self, prog, tag, x, w_in, g_pre, G, ya_d, ln_g, ln_b, w_sp, b_sp, sinks, w_out, g_post,
                 dw, identb, identf, x1):
        super().__init__(prog, tag)
        self.x, self.w_in, self.g_pre, self.G, self.ya_d = x, w_in, g_pre, G, ya_d
        self.ln_g, self.ln_b, self.w_sp, self.b_sp, self.sinks, self.w_out, self.g_post = ln_g, ln_b, w_sp, b_sp, sinks, w_out, g_post
        self.dw_d, self.identb_d, self.identf_d, self.x1 = dw, identb, identf, x1
        self.nt = NormT(self, "nt")

    def reset(self):
        super().reset()
        self.m = {"pe": 0, "act": 0, "dve": 0}

    def alloc(self, nc, es):
        b = Bufs()
        b.wuv = sb(nc, es, "b_wuv", [128, 8, 512], BF16)
        b.wcq = sb(nc, es, "b_wcq", [128, 8, 3, 128], BF16)
        b.wout = sb(nc, es, "b_wout", [128, 8, D], BF16)
        b.dw = sb(nc, es, "b_dw", [128, 18, 512], F32)
        b.gbc = sb(nc, es, "b_gbc", [128, D], F32)
        b.gpost = sb(nc, es, "b_gpost", [128, D], F32)
        b.lng = sb(nc, es, "b_lng", [128, 256], F32)
        b.lnb = sb(nc, es, "b_lnb", [128, 256], F32)
        b.identb = sb(nc, es, "b_idb", [128, 128], BF16)
        b.identf = sb(nc, es, "b_idf", [128, 128], F32)
        b.wsp = sb(nc, es, "b_wsp", [128, 4, 128], F32)
        b.wsT = sb(nc, es, "b_wsT", [128, 4, 128], BF16)
        b.bsp = sb(nc, es, "b_bsp", [4, 128], F32)
        b.bsT = sb(nc, es, "b_bsT", [128, 4], F32)
        b.esink = sb(nc, es, "b_esink", [128, 6], F32)
        b.zero = sb(nc, es, "b_zero", [128, 1], F32)
        b.xt = sb(nc, es, "b_xt", [128, 2, 4, D], F32)
        b.hb = sb(nc, es, "b_hb", [128, 4, D], BF16)
        b.junk = sb(nc, es, "b_junk", [128, D], F32)
        b.ss = sb(nc, es, "b_ss", [128, 4], F32)
        b.rstd = sb(nc, es, "b_rstd", [128, 4], F32)
        b.hT = sb(nc, es, "b_hT", [128, 8, 512], BF16)
        b.cq = sb(nc, es, "b_cq", [128, 3, 512], BF16)
        b.ckx = sb(nc, es, "b_ckx", [128, 18 * 128], BF16)
        b.cv = sb(nc, es, "b_cv", [128, 19, 2, 65], BF16)
        b.pring = sb(nc, es, "b_pr", [128, NPR, 512], BF16)
        b.oT = sb(nc, es, "b_oT", [65, 2, 512], F32)
        b.r = sb(nc, es, "b_r", [128, 8], F32)
        b.y = sb(nc, es, "b_y", [128, 4, D], BF16)
        b.u = sb(nc, es, "b_u", [128, 4, 256], F32)
        b.vt = sb(nc, es, "b_vt", [128, 4, 256], F32)
        b.vnb = sb(nc, es, "b_vnb", [128, 4, 256], BF16)
        b.st = sb(nc, es, "b_st", [128, 16], F32)
        b.osb = sb(nc, es, "b_osb", [128, 4, D], F32)
        b.os2 = sb(nc, es, "b_os2", [128, 8], F32)
        b.pst = ps(nc, es, "b_pst", [128, 2, 1024], BF16)
        b.sring = ps(nc, es, "b_sr", [128, 4, 512], F32)
        b.oacc = ps(nc, es, "b_oacc", [128, 2, 512], F32)
        b.tps = b.pst[:, 0, :].bitcast(F32).rearrange("p (s d) -> p s d", d=128)
        return b

    def kts(self, i):
        return [kt for kt in range(18) if not (kt == 0 and i == 0) and not (kt == 17 and i == 7)]

    def groups(self, i):
        b = self.b
        out = []
        kts = self.kts(i)
        for g in range(3):
            units = []
            for n, kt in enumerate(kts):
                for kv in (0, 1):
                    head = kv * 3 + g
                    units.append({"acc": kv, "first": n == 0, "last": n == len(kts) - 1, "scale": SC_W, "kind": "G",
                                  "lhsT": b.ckx[kv * 64:(kv + 1) * 64, kt * 128:(kt + 1) * 128],
                                  "rhs": b.cq[kv * 64:(kv + 1) * 64, g, :], "bias": b.zero[:, 0:1],
                                  "dD": b.dw[:, kt, :], "dcoef": -float(SL_WIN[head] / SC_W),
                                  "vl": b.cv[:, :, :, :].rearrange("p t k d -> p (t k d)")[:, (kt * 2 + kv) * 65:(kt * 2 + kv) * 65 + 128]})
            gid = 3 * i + g
            out.append({"gid": gid, "par": 0, "units": units, "g": g, "i": i, "acc_prev": gid - 1 if gid >= 1 else None})
        return out

    MSEM = {"cq": "mA", "uv": "mD", "sp": "mD", "op": "mA", "ws": "mA"}

    def misc_begin(self, e, kind):
        S = self.S
        m = self.m["pe"]
        self.mk.append(kind)
        if m >= self.NSR:
            S.wait(e, self.sem(self.MSEM[self.mk[m - self.NSR]]), ("rel", m - self.NSR))
        self.m["pe"] = m + 1
        return self.b.sring[:, m % self.NSR, :], m

    def sp(self, e):
        S, b = self.S, self.b
        ld = self.sem("ldc")
        lst = [(b.dw[:, :, :], self.dw_d[:, :, :]), (b.identb[:, :], self.identb_d[:, :]), (b.identf[:, :], self.identf_d[:, :]),
               (b.gbc[:, :], self.g_pre.partition_broadcast(128)), (b.gpost[:, :], self.g_post.partition_broadcast(128)),
               (b.lng[:, :], self.ln_g.partition_broadcast(128)), (b.lnb[:, :], self.ln_b.partition_broadcast(128)),
               (b.esink[:, :], self.sinks.partition_broadcast(128)),
               (b.wsp[:, :, :], self.w_sp.rearrange("g t s -> t g s")), (b.bsp[:, :], self.b_sp[:, :])]
        for n, (dst, src) in enumerate(lst):
            ins = e.dma_start(out=dst, in_=src)
            S.sig(ins, ld, ("c", n), dma=True)
        for i in range(NQ):
            sl = i % 2
            if i >= 2:
                S.wait(e, self.sem("stx%d" % sl), i - 2)
            ins = e.dma_start(out=b.xt[:, sl, :, :], in_=self.x[i * 512:(i + 1) * 512, :].rearrange("(s p) d -> p s d", p=128))
            S.sig(ins, self.sem("ldx%d" % sl), i, dma=True)
            if i >= 1:
                S.wait(e, self.sem("tpc"), (3 * (i - 1) + 2, 1))
            lk = self.sem("ldk")
            for kt in self.kts(i):
                if 1 <= kt <= 16 and (kt - 1) % 4 != 0:
                    continue
                if kt == 0:
                    r, lt, n = j2idx(16 * i - 1) + (1,)
                elif kt == 17:
                    r, lt, n = j2idx(16 * (i + 1)) + (1,)
                else:
                    r, lt, n = (kt - 1) // 4, 4 * i, 4
                for kv in (0, 1):
                    ins = e.dma_start(out=b.ckx[kv * 64:(kv + 1) * 64, kt * 128:(kt + n) * 128],
                                      in_=self.G[12 + kv, r, :].rearrange("(a n) -> a n", n=NTOK)[:, lt * 128:(lt + n) * 128])
                    S.sig(ins, lk, ("k", i, kt, kv), dma=True)
                for kv in (0, 1):
                    ins = e.dma_start(out=b.cv[:, kt:kt + n, kv, 0:64],
                                      in_=self.G[14 + kv, r, :].rearrange("(p t d) -> p t d", p=128, t=32)[:, lt:lt + n, :])
                    S.sig(ins, lk, ("v", i, kt) if kv == 1 else ("v0", i, kt), dma=True)
            if i >= 1:
                S.wait(e, self.sem("ytr"), i - 1)
            ins = e.dma_start(out=b.y[:, :, 0:384], in_=self.ya_d[:, 4 * i:4 * i + 4, :])
            S.sig(ins, self.sem("ldy"), i, dma=True)

    def pool(self, e):
        S, b = self.S, self.b
        lw = self.sem("lw")
        ins = e.dma_start(out=b.wuv[:, :, :], in_=wview(self.w_in, 1152, 1664))
        S.sig(ins, lw, "uv", dma=True)
        for g in range(3):
            for kv in (0, 1):
                c0 = 1664 + (kv * 3 + g) * 64
                ins = e.dma_start(out=b.wcq[:, :, g, kv * 64:(kv + 1) * 64], in_=wview(self.w_in, c0, c0 + 64))
                S.sig(ins, lw, ("cq", g, kv), dma=True)
        ins = e.dma_start(out=b.wout[:, :, :], in_=wview(self.w_out, 0, D))
        S.sig(ins, lw, "out", dma=True)
        for i in range(NQ):
            sl = i % 2
            S.wait(e, self.sem("res"), i)
            ins = e.dma_start(out=self.x1[i * 512:(i + 1) * 512, :].rearrange("(s p) d -> p s d", p=128), in_=b.xt[:, sl, :, :])
            S.sig(ins, self.sem("stx%d" % sl), i, dma=True)
        for i in (NQ - 2, NQ - 1):
            if i >= 0:
                S.wait(e, self.sem("stx%d" % (i % 2)), i)

    def pe(self, e):
        S, b = self.S, self.b
        self.mk = []
        S.wait(e, self.sem("ldc"), ("c", 9))
        S.wait(e, self.sem("lw"), "out")
        for g4 in range(4):
            if g4 >= 1:
                S.wait(e, self.sem("mA"), ("ws", g4 - 1))
            ins = e.transpose(b.tps[:, 0, :], b.wsp[:, g4, :], b.identf[:, :])
            S.sig(ins, self.sem("wst"), g4)
        S.wait(e, self.sem("mA"), ("ws", 3))
        ins = e.transpose(b.tps[:, 1, 0:4], b.bsp[0:4, :], b.identf[0:4, 0:4])
        S.sig(ins, self.sem("wst"), 4)
        S.wait(e, self.sem("ones"), 0)
        for i in range(NQ):
            self.nt.pe(e, i, b.hb, b.pst, b.identb)
            self.nt.wait_hT(e, i)
            for g in range(3):
                bank, m = self.misc_begin(e, "cq")
                for kc in range(8):
                    ins = e.matmul(bank, lhsT=b.wcq[:, kc, g, :], rhs=b.hT[:, kc, :], start=(kc == 0), stop=(kc == 7))
                S.sig(ins, self.sem("mm"), m)
            for s in range(4):
                bank, m = self.misc_begin(e, "uv")
                for kc in range(8):
                    ins = e.matmul(bank, lhsT=b.hT[:, kc, s * 128:(s + 1) * 128], rhs=b.wuv[:, kc, :], start=(kc == 0), stop=(kc == 7))
                S.sig(ins, self.sem("mm"), m)
            for s in range(4):
                bank, m = self.misc_begin(e, "sp")
                S.wait(e, self.sem("vn"), (i, s))
                for g4 in range(4):
                    ins = e.matmul(bank[:, g4 * 64:(g4 + 1) * 64], lhsT=b.wsT[:, g4, :], rhs=b.vnb[:, s, g4 * 64:(g4 + 1) * 64],
                                   start=True, stop=True)
                S.sig(ins, self.sem("mm"), m)
            mlast = self.m["pe"] - 1
            for gi, g in enumerate(self.groups(i)):
                if gi == 0:
                    def w0(e, mlast=mlast, i=i):
                        for mm_ in range(max(0, mlast - self.NSR + 1), mlast + 1):
                            S.wait(e, self.sem(self.MSEM[self.mk[mm_]]), ("rel", mm_))
                        S.wait(e, self.sem("ldk"), ("v", i, self.kts(i)[-1] if self.kts(i)[-1] == 17 else 13))
                        S.wait(e, self.sem("mA"), ("rel", mlast - 8))
                    g["wait_pe"] = w0
                self.pe_group_pairs(e, g)
                self.pe_post(e, g)
            S.wait(e, self.sem("ldy"), i)
            S.wait(e, self.sem("gate"), (i, 3))
            S.wait(e, self.sem("tpc"), (3 * i + 2, 1))
            for kc in range(8):
                gq = (NQ + i) * 8 + kc
                S.wait(e, self.nt.n + "_ev", gq - 2 if kc >= 2 else i * 8 + 6 + kc)
                for s in range(4):
                    ins = e.transpose(b.pst[:, gq % 2, s * 128:(s + 1) * 128], b.y[:, s, kc * 128:(kc + 1) * 128], b.identb[:, :])
                S.sig(ins, self.nt.n + "_tr", gq)
            S.wait(e, self.nt.n + "_ev", (NQ + i) * 8 + 7)
            S.wait(e, self.sem("p"), self.gu["pe"] - 1)
            for s in range(4):
                for half in range(2):
                    bank, m = self.misc_begin(e, "op")
                    for kc in range(8):
                        ins = e.matmul(bank, lhsT=b.hT[:, kc, s * 128:(s + 1) * 128], rhs=b.wout[:, kc, half * 512:(half + 1) * 512],
                                       start=(kc == 0), stop=(kc == 7))
                    S.sig(ins, self.sem("mm"), m)
            S.sig(e.transpose(b.pst[:, 0, 0:128], b.identb[:, :], b.identb[:, :]), self.sem("ytr"), i)

    def act(self, e):
        S, b = self.S, self.b
        S.wait(e, self.sem("ldc"), ("c", 9))
        ins = e.activation(out=b.esink[:, :], in_=b.esink[:, :], func=AF.Exp)
        S.sig(ins, self.sem("es"), 0)
        for g4 in range(4):
            S.wait(e, self.sem("wst"), g4)
            ins = e.activation(out=b.wsT[:, g4, :], in_=b.tps[:, 0, :], func=AF.Copy)
            S.sig(ins, self.sem("mA"), ("ws", g4))
        S.wait(e, self.sem("wst"), 4)
        ins = e.activation(out=b.bsT[:, :], in_=b.tps[:, 1, 0:4], func=AF.Copy)
        S.sig(ins, self.sem("es"), 1)
        m = 0
        for i in range(NQ):
            self.nt.act(e, i, b.pst, b.hT,
                        wait_hT_free=lambda: (S.wait(e, self.sem("mm"), self.m_last_op) if i >= 1 else None), rstd=b.rstd)
            for g in range(3):
                S.wait(e, self.sem("mm"), m)
                if i >= 1 and g == 0:
                    S.wait(e, self.sem("pv"), self.gu["act"] - 1)
                ins = e.activation(out=b.cq[:, g, :], in_=b.sring[:, m % self.NSR, :], func=AF.Copy)
                S.sig(ins, self.sem("mA"), ("rel", m))
                m += 1
            m += 8
            S.wait(e, self.sem("lnv"), i)
            ins = e.activation(out=b.st[:, 8:12], in_=b.st[:, 8:12], func=AF.Sqrt)
            S.sig(ins, self.sem("lnq"), i)
            for g in self.groups(i):
                self.act_group(e, g)
            for kc in range(8):
                gq = (NQ + i) * 8 + kc
                S.wait(e, self.nt.n + "_tr", gq)
                ins = e.activation(out=b.hT[:, kc, :], in_=b.pst[:, gq % 2, 0:512], func=AF.Copy)
                S.sig(ins, self.nt.n + "_ev", gq)
            for s in range(4):
                for half in range(2):
                    S.wait(e, self.sem("mm"), m)
                    if s == 0 and half == 0 and i >= 1:
                        S.wait(e, self.sem("res"), i - 1)
                    ins = e.activation(out=b.osb[:, s, half * 512:(half + 1) * 512], in_=b.sring[:, m % self.NSR, :], func=AF.Copy)
                    S.sig(ins, self.sem("mA"), ("rel", m))
                    self.m_last_op = m
                    m += 1
            S.wait(e, self.sem("onv"), i)
            ins = e.activation(out=b.os2[:, 4:8], in_=b.os2[:, 4:8], func=AF.Sqrt)
            S.sig(ins, self.sem("onq"), i)

    def consume(self, e, g, c):
        S, b = self.S, self.b
        f = self.sem("f")
        head = c * 3 + g["g"]
        ins = e.tensor_scalar(out=b.r[:, 0:4], in0=b.tps[:, :, 64], scalar1=b.esink[:, head:head + 1], scalar2=None, op0=ALU.add)
        S.fence(e, ins, f)
        ins = e.reciprocal(out=b.r[:, 0:4], in_=b.r[:, 0:4])
        S.fence(e, ins, f)
        for s in range(4):
            ins = e.tensor_scalar(out=b.y[:, s, 640 + head * 64:640 + (head + 1) * 64], in0=b.tps[:, s, 0:64],
                                  scalar1=b.r[:, s:s + 1], scalar2=None, op0=ALU.mult)
        S.sig(ins, self.sem("tpc"), (g["gid"], c))

    def dve(self, e):
        S, b = self.S, self.b
        f = self.sem("f")
        ins = e.memset(b.cv[:, 18, :, :], 0.0)
        ins = e.memset(b.cv[:, 0:18, :, 64:65], 1.0)
        ins = e.memset(b.zero[:, :], 0.0)
        S.sig(ins, self.sem("ones"), 0)
        S.wait(e, self.sem("ldc"), ("c", 9))
        S.wait(e, self.sem("es"), 1)
        m = 0
        for i in range(NQ):
            sl = i % 2
            xt = b.xt[:, sl, :, :]
            self.nt.dve(e, i, xt, b.gbc, b.hb, b.junk, b.ss, b.rstd,
                        wait_x=lambda: S.wait(e, self.sem("ldx%d" % sl), i),
                        wait_hb_free=lambda: (self.nt.wait_tr_done(e, i - 1) if i >= 1 else None))
            m += 3
            for s in range(4):
                S.wait(e, self.sem("mm"), m)
                if i >= 1 and s == 0:
                    S.wait(e, self.sem("gate"), (i - 1, 3))
                ins = e.tensor_copy(out=b.u[:, s, :], in_=b.sring[:, m % self.NSR, 0:256])
                ins = e.tensor_copy(out=b.vt[:, s, :], in_=b.sring[:, m % self.NSR, 256:512])
                S.sig(ins, self.sem("mD"), ("rel", m))
                S.wait(e, self.sem("mD"), ("rel", m))
                ins = e.tensor_reduce(out=b.st[:, s:s + 1], in_=b.vt[:, s, :], axis=AX.X, op=ALU.add)
                ins = e.tensor_tensor(out=b.junk[:, 0:256], in0=b.vt[:, s, :], in1=b.vt[:, s, :], op=ALU.mult)
                S.fence(e, ins, f)
                ins = e.tensor_reduce(out=b.st[:, 4 + s:5 + s], in_=b.junk[:, 0:256], axis=AX.X, op=ALU.add)
                S.fence(e, ins, f)
                m += 1
            ins = e.tensor_scalar(out=b.st[:, 0:4], in0=b.st[:, 0:4], scalar1=1.0 / 256, scalar2=None, op0=ALU.mult)
            S.fence(e, ins, f)
            ins = e.tensor_tensor(out=b.st[:, 12:16], in0=b.st[:, 0:4], in1=b.st[:, 0:4], op=ALU.mult)
            S.fence(e, ins, f)
            ins = e.scalar_tensor_tensor(out=b.st[:, 8:12], in0=b.st[:, 4:8], scalar=1.0 / 256, in1=b.st[:, 12:16],
                                         op0=ALU.mult, op1=ALU.subtract)
            S.fence(e, ins, f)
            ins = e.tensor_scalar(out=b.st[:, 8:12], in0=b.st[:, 8:12], scalar1=EPS, scalar2=None, op0=ALU.add)
            S.sig(ins, self.sem("lnv"), i)
            S.wait(e, self.sem("lnq"), i)
            ins = e.reciprocal(out=b.st[:, 8:12], in_=b.st[:, 8:12])
            S.fence(e, ins, f)
            for s in range(4):
                ins = e.tensor_scalar(out=b.vt[:, s, :], in0=b.vt[:, s, :], scalar1=b.st[:, s:s + 1], scalar2=b.st[:, 8 + s:9 + s],
                                      op0=ALU.subtract, op1=ALU.mult)
                S.fence(e, ins, f)
                ins = e.tensor_tensor(out=b.vt[:, s, :], in0=b.vt[:, s, :], in1=b.lng[:, :], op=ALU.mult)
                S.fence(e, ins, f)
                ins = e.tensor_tensor(out=b.vnb[:, s, :], in0=b.vt[:, s, :], in1=b.lnb[:, :], op=ALU.add)
                S.sig(ins, self.sem("vn"), (i, s))
            for s in range(4):
                S.wait(e, self.sem("mm"), m)
                if i >= 1 and s == 0:
                    S.wait(e, self.sem("ytr"), i - 1)
                for g4 in range(4):
                    ins = e.scalar_tensor_tensor(out=b.y[:, s, 384 + g4 * 64:384 + (g4 + 1) * 64],
                                                 in0=b.sring[:, m % self.NSR, g4 * 64:(g4 + 1) * 64], scalar=b.bsT[:, g4:g4 + 1],
                                                 in1=b.u[:, s, g4 * 64:(g4 + 1) * 64], op0=ALU.add, op1=ALU.mult)
                S.sig(ins, self.sem("mD"), ("rel", m))
                S.wait(e, self.sem("mD"), ("rel", m))
                S.sig(e.tensor_copy(out=b.r[:, 4:5], in_=b.zero[:, 0:1]), self.sem("gate"), (i, s))
                m += 1
            for g in self.groups(i):
                self.dve_group(e, g, self.consume)
            for s in range(4):
                S.wait(e, self.sem("mA"), ("rel", m + 1))
                ins = e.tensor_tensor(out=b.junk[:, :], in0=b.osb[:, s, :], in1=b.osb[:, s, :], op=ALU.mult)
                S.fence(e, ins, f)
                ins = e.tensor_reduce(out=b.os2[:, s:s + 1], in_=b.junk[:, :], axis=AX.X, op=ALU.add)
                S.fence(e, ins, f)
                m += 2
            ins = e.tensor_scalar(out=b.os2[:, 4:8], in0=b.os2[:, 0:4], scalar1=1.0 / D, scalar2=EPS, op0=ALU.mult, op1=ALU.add)
            S.sig(ins, self.sem("onv"), i)
            S.wait(e, self.sem("onq"), i)
            ins = e.reciprocal(out=b.os2[:, 4:8], in_=b.os2[:, 4:8])
            S.fence(e, ins, f)
            for s in range(4):
                ins = e.tensor_tensor(out=b.osb[:, s, :], in0=b.osb[:, s, :], in1=b.gpost[:, :], op=ALU.mult)
                S.fence(e, ins, f)
                ins = e.scalar_tensor_tensor(out=xt[:, s, :], in0=b.osb[:, s, :], scalar=b.os2[:, 4 + s:5 + s], in1=xt[:, s, :],
                                             op0=ALU.mult, op1=ALU.add)
            S.sig(ins, self.sem("res"), i)
            S.wait(e, self.sem("res"), i)


NMR = 6


class TokPhase(Phase):
    def reset(self):
        self.m = 0
        self.mk = []

    def ring(self, e, rel_sem):
        S = self.S
        m = self.m
        self.mk.append(rel_sem)
        if m >= NMR:
            S.wait(e, self.sem(self.mk[m - NMR]), ("rel", m - NMR))
        self.m = m + 1
        return self.b.ring[:, m % NMR, :], m

    def dve_postnorm(self, e, i, s, xt):
        S, b = self.S, self.b
        f = self.sem("f")
        o = b.osb[:, s % 2, :]
        ins = e.tensor_tensor(out=b.junk[:, :], in0=o, in1=o, op=ALU.mult)
        S.fence(e, ins, f)
        ins = e.tensor_reduce(out=b.os2[:, 0:1], in_=b.junk[:, :], axis=AX.X, op=ALU.add)
        S.fence(e, ins, f)
        ins = e.tensor_scalar(out=b.os2[:, 1:2], in0=b.os2[:, 0:1], scalar1=1.0 / D, scalar2=EPS, op0=ALU.mult, op1=ALU.add)
        S.sig(ins, self.sem("onv"), (i, s))
        S.wait(e, self.sem("onq"), (i, s))
        ins = e.reciprocal(out=b.os2[:, 1:2], in_=b.os2[:, 1:2])
        ins2 = e.tensor_tensor(out=o, in0=o, in1=b.gpost[:, :], op=ALU.mult)
        S.fence(e, ins2, f)
        ins = e.scalar_tensor_tensor(out=xt[:, s, :], in0=o, scalar=b.os2[:, 1:2], in1=xt[:, s, :], op0=ALU.mult, op1=ALU.add)
        S.sig(ins, self.sem("res"), (i, s))
        S.wait(e, self.sem("res"), (i, s))

    def act_postnorm(self, e, i, s):
        S, b = self.S, self.b
        S.wait(e, self.sem("onv"), (i, s))
        ins = e.activation(out=b.os2[:, 1:2], in_=b.os2[:, 1:2], func=AF.Sqrt)
        S.sig(ins, self.sem("onq"), (i, s))

    def pool_store(self, e, xo):
        S, b = self.S, self.b
        for i in range(NQ):
            S.wait(e, self.sem("res"), (i, 3))
            ins = e.dma_start(out=xo[i * 512:(i + 1) * 512, :].rearrange("(s p) d -> p s d", p=128), in_=b.xt[:, :, :])
            S.sig(ins, self.sem("stx"), i, dma=True)
        S.wait(e, self.sem("stx"), NQ - 1)

    def sp_loadx(self, e, i, xin):
        S, b = self.S, self.b
        if i >= 1:
            S.wait(e, self.sem("stx"), i - 1)
        ins = e.dma_start(out=b.xt[:, :, :], in_=xin[i * 512:(i + 1) * 512, :].rearrange("(s p) d -> p s d", p=128))
        S.sig(ins, self.sem("ldx"), i, dma=True)


class P3a(TokPhase):
    NBLK = DFF // 128

    def __init__(self, prog, tag, x1, w_fi, w_fo, g_pre, g_post, identb, x2):
        super().__init__(prog, tag)
        self.x1, self.w_fi, self.w_fo, self.g_pre, self.g_post, self.identb_d, self.x2 = x1, w_fi, w_fo, g_pre, g_post, identb, x2
        self.nt = NormT(self, "nt")

    def alloc(self, nc, es):
        b = Bufs()
        b.wfi = sb(nc, es, "f_wfi", [128, 8, 2 * DFF], BF16)
        b.wfo = sb(nc, es, "f_wfo", [128, self.NBLK, D], BF16)
        b.gbc = sb(nc, es, "f_gbc", [128, D], F32)
        b.gpost = sb(nc, es, "f_gpost", [128, D], F32)
        b.identb = sb(nc, es, "f_idb", [128, 128], BF16)
        b.xt = sb(nc, es, "f_xt", [128, 4, D], F32)
        b.hb = sb(nc, es, "f_hb", [128, 4, D], BF16)
        b.ss = sb(nc, es, "f_ss", [128, 4], F32)
        b.rstd = sb(nc, es, "f_rstd", [128, 4], F32)
        b.hT = sb(nc, es, "f_hT", [128, 8, 512], BF16)
        b.actT = sb(nc, es, "f_actT", [128, self.NBLK, 512], BF16)
        b.sg = sb(nc, es, "f_sg", [128, 2, 512], F32)
        b.junk = b.sg[:, :, :].rearrange("p a b -> p (a b)")
        b.osb = sb(nc, es, "f_osb", [128, 2, D], F32)
        b.os2 = sb(nc, es, "f_os2", [128, 4], F32)
        b.pst = ps(nc, es, "f_pst", [128, 2, 1024], BF16)
        b.ring = ps(nc, es, "f_ring", [128, NMR, 512], F32)
        return b

    def sp(self, e):
        S, b = self.S, self.b
        ld = self.sem("ldc")
        for n, (dst, src) in enumerate([(b.identb[:, :], self.identb_d[:, :]), (b.gbc[:, :], self.g_pre.partition_broadcast(128)),
                                        (b.gpost[:, :], self.g_post.partition_broadcast(128))]):
            ins = e.dma_start(out=dst, in_=src)
            S.sig(ins, ld, ("c", n), dma=True)
        for i in range(NQ):
            self.sp_loadx(e, i, self.x1)

    def pool(self, e):
        S, b = self.S, self.b
        lw = self.sem("lw")
        nch = 8
        w = 2 * DFF // nch
        for k in range(nch):
            ins = e.dma_start(out=b.wfi[:, :, k * w:(k + 1) * w], in_=wview(self.w_fi, k * w, (k + 1) * w))
            S.sig(ins, lw, ("fi", k), dma=True)
        for k in range(2):
            ins = e.dma_start(out=b.wfo[:, :, k * 512:(k + 1) * 512], in_=wview(self.w_fo, k * 512, (k + 1) * 512))
            S.sig(ins, lw, ("fo", k), dma=True)
        self.pool_store(e, self.x2)

    def pe(self, e):
        S, b = self.S, self.b
        S.wait(e, self.sem("ldc"), ("c", 2))
        S.wait(e, self.sem("lw"), ("fo", 1))
        for i in range(NQ):
            self.nt.pe(e, i, b.hb, b.pst, b.identb)
            self.nt.wait_hT(e, i)
            for blk in range(self.NBLK):
                for part, rel in ((0, "mA"), (1, "mD")):
                    bank, m = self.ring(e, rel)
                    c0 = part * DFF + blk * 128
                    for kc in range(8):
                        ins = e.matmul(bank, lhsT=b.wfi[:, kc, c0:c0 + 128], rhs=b.hT[:, kc, :], start=(kc == 0), stop=(kc == 7))
                    S.sig(ins, self.sem("mm"), m)
            S.wait(e, self.sem("mD"), ("rel", self.m - 1))
            for s in range(4):
                for half in range(2):
                    bank, m = self.ring(e, "mA")
                    for blk in range(self.NBLK):
                        ins = e.matmul(bank, lhsT=b.actT[:, blk, s * 128:(s + 1) * 128], rhs=b.wfo[:, blk, half * 512:(half + 1) * 512],
                                       start=(blk == 0), stop=(blk == self.NBLK - 1))
                    S.sig(ins, self.sem("mm"), m)

    def act(self, e):
        S, b = self.S, self.b
        m = 0
        nb = 0
        for i in range(NQ):
            self.nt.act(e, i, b.pst, b.hT,
                        wait_hT_free=lambda: (S.wait(e, self.sem("mm"), m - 9) if i >= 1 else None), rstd=b.rstd)
            for blk in range(self.NBLK):
                S.wait(e, self.sem("mm"), m)
                if nb >= 2:
                    S.wait(e, self.sem("mD"), ("rel", self.sgrel[nb - 2]))
                ins = e.activation(out=b.sg[:, nb % 2, :], in_=b.ring[:, m % NMR, :], func=AF.Silu)
                S.sig(ins, self.sem("mA"), ("rel", m))
                self.sgrel.append(m + 1)
                nb += 1
                m += 2
            for s in range(4):
                for half in range(2):
                    S.wait(e, self.sem("mm"), m)
                    if half == 0 and (i, s) >= (0, 2):
                        ps_ = (i, s - 2) if s >= 2 else (i - 1, s + 2)
                        S.wait(e, self.sem("res"), ps_)
                    ins = e.activation(out=b.osb[:, s % 2, half * 512:(half + 1) * 512], in_=b.ring[:, m % NMR, :], func=AF.Copy)
                    S.sig(ins, self.sem("mA"), ("rel", m))
                    m += 1
                self.act_postnorm(e, i, s)

    def reset(self):
        super().reset()
        self.sgrel = []

    def dve(self, e):
        S, b = self.S, self.b
        S.wait(e, self.sem("ldc"), ("c", 2))
        m = 0
        nb = 0
        for i in range(NQ):
            self.nt.dve(e, i, b.xt, b.gbc, b.hb, b.junk, b.ss, b.rstd,
                        wait_x=lambda: S.wait(e, self.sem("ldx"), i),
                        wait_hb_free=lambda: (self.nt.wait_tr_done(e, i - 1) if i >= 1 else None))
            for blk in range(self.NBLK):
                S.wait(e, self.sem("mm"), m + 1)
                S.wait(e, self.sem("mA"), ("rel", m))
                if i >= 1 and blk == 0:
                    S.wait(e, self.sem("mm"), m - 1)
                ins = e.tensor_tensor(out=b.actT[:, blk, :], in0=b.sg[:, nb % 2, :], in1=b.ring[:, (m + 1) % NMR, :], op=ALU.mult)
                S.sig(ins, self.sem("mD"), ("rel", m + 1))
                nb += 1
                m += 2
            for s in range(4):
                S.wait(e, self.sem("mA"), ("rel", m + 1))
                self.dve_postnorm(e, i, s, b.xt)
                m += 2


class P3b(TokPhase):
    def __init__(self, prog, tag, x2, p, w_up, w_gate, g_gate, g_post, identb, x3):
        super().__init__(prog, tag)
        self.x2, self.p, self.w_up, self.w_gate, self.g_gate, self.g_post, self.identb_d, self.x3 = x2, p, w_up, w_gate, g_gate, g_post, identb, x3
        self.nt = NormT(self, "nt")

    def alloc(self, nc, es):
        b = Bufs()
        b.wg = sb(nc, es, "e_wg", [128, 8, D], BF16)
        b.wu = sb(nc, es, "e_wu", [128, 2, D], BF16)
        b.gbc = sb(nc, es, "e_gbc", [128, D], F32)
        b.gpost = sb(nc, es, "e_gpost", [128, D], F32)
        b.identb = sb(nc, es, "e_idb", [128, 128], BF16)
        b.xt = sb(nc, es, "e_xt", [128, 4, D], F32)
        b.pt = sb(nc, es, "e_pt", [128, 4, 256], F32)
        b.pb = sb(nc, es, "e_pb", [128, 4, 256], BF16)
        b.pT = sb(nc, es, "e_pT", [128, 2, 512], BF16)
        b.hb = sb(nc, es, "e_hb", [128, 4, D], BF16)
        b.junk = sb(nc, es, "e_junk", [128, D], F32)
        b.ss = sb(nc, es, "e_ss", [128, 4], F32)
        b.rstd = sb(nc, es, "e_rstd", [128, 4], F32)
        b.hT = sb(nc, es, "e_hT", [128, 8, 512], BF16)
        b.sgm = sb(nc, es, "e_sgm", [128, 2, D], F32)
        b.osb = sb(nc, es, "e_osb", [128, 2, D], F32)
        b.os2 = sb(nc, es, "e_os2", [128, 4], F32)
        b.pst = ps(nc, es, "e_pst", [128, 2, 1024], BF16)
        b.ring = ps(nc, es, "e_ring", [128, NMR, 512], F32)
        return b

    def sp(self, e):
        S, b = self.S, self.b
        ld = self.sem("ldc")
        for n, (dst, src) in enumerate([(b.identb[:, :], self.identb_d[:, :]), (b.gbc[:, :], self.g_gate.partition_broadcast(128)),
                                        (b.gpost[:, :], self.g_post.partition_broadcast(128))]):
            ins = e.dma_start(out=dst, in_=src)
            S.sig(ins, ld, ("c", n), dma=True)
        for i in range(NQ):
            self.sp_loadx(e, i, self.x2)
            if i >= 1:
                S.wait(e, self.sem("pb"), i - 1)
            ins = e.dma_start(out=b.pt[:, :, :], in_=self.p[i * 512:(i + 1) * 512, :].rearrange("(s p) d -> p s d", p=128))
            S.sig(ins, self.sem("ldp"), i, dma=True)

    def pool(self, e):
        S, b = self.S, self.b
        lw = self.sem("lw")
        for k in range(2):
            ins = e.dma_start(out=b.wg[:, :, k * 512:(k + 1) * 512], in_=wview(self.w_gate, k * 512, (k + 1) * 512))
            S.sig(ins, lw, ("g", k), dma=True)
        ins = e.dma_start(out=b.wu[:, :, :], in_=wview(self.w_up, 0, D))
        S.sig(ins, lw, "u", dma=True)
        self.pool_store(e, self.x3)

    def pe(self, e):
        S, b = self.S, self.b
        S.wait(e, self.sem("ldc"), ("c", 2))
        S.wait(e, self.sem("lw"), "u")
        for i in range(NQ):
            self.nt.pe(e, i, b.hb, b.pst, b.identb)
            S.wait(e, self.sem("pb"), i)
            for kc in range(2):
                gq = (NQ + i) * 8 + kc
                S.wait(e, self.nt.n + "_ev", i * 8 + 6 + kc)
                for s in range(4):
                    ins = e.transpose(b.pst[:, gq % 2, s * 128:(s + 1) * 128], b.pb[:, s, kc * 128:(kc + 1) * 128], b.identb[:, :])
                S.sig(ins, self.nt.n + "_tr", gq)
            S.wait(e, self.nt.n + "_ev", (NQ + i) * 8 + 1)
            for s in range(4):
                for half in range(2):
                    bank, m = self.ring(e, "mD")
                    for kc in range(2):
                        ins = e.matmul(bank, lhsT=b.pT[:, kc, s * 128:(s + 1) * 128], rhs=b.wu[:, kc, half * 512:(half + 1) * 512],
                                       start=(kc == 0), stop=(kc == 1))
                    S.sig(ins, self.sem("mm"), m)
                for half in range(2):
                    bank, m = self.ring(e, "mA")
                    for kc in range(8):
                        ins = e.matmul(bank, lhsT=b.hT[:, kc, s * 128:(s + 1) * 128], rhs=b.wg[:, kc, half * 512:(half + 1) * 512],
                                       start=(kc == 0), stop=(kc == 7))
                    S.sig(ins, self.sem("mm"), m)

    def act(self, e):
        S, b = self.S, self.b
        m = 0
        for i in range(NQ):
            self.nt.act(e, i, b.pst, b.hT,
                        wait_hT_free=lambda: (S.wait(e, self.sem("mm"), m - 1) if i >= 1 else None), rstd=b.rstd)
            for kc in range(2):
                gq = (NQ + i) * 8 + kc
                S.wait(e, self.nt.n + "_tr", gq)
                if i >= 1 and kc == 0:
                    S.wait(e, self.sem("mm"), m - 3)
                ins = e.activation(out=b.pT[:, kc, :], in_=b.pst[:, gq % 2, 0:512], func=AF.Copy)
                S.sig(ins, self.nt.n + "_ev", gq)
            for s in range(4):
                for half in range(2):
                    mz = m + 2 + half
                    S.wait(e, self.sem("mm"), mz)
                    if half == 0 and (i, s) >= (0, 2):
                        ps_ = (i, s - 2) if s >= 2 else (i - 1, s + 2)
                        S.wait(e, self.sem("eg"), ps_)
                    ins = e.activation(out=b.sgm[:, s % 2, half * 512:(half + 1) * 512], in_=b.ring[:, mz % NMR, :], func=AF.Sigmoid)
                    S.sig(ins, self.sem("mA"), ("rel", mz))
                m += 4
                self.act_postnorm(e, i, s)

    def dve(self, e):
        S, b = self.S, self.b
        S.wait(e, self.sem("ldc"), ("c", 2))
        m = 0
        for i in range(NQ):
            self.nt.dve(e, i, b.xt, b.gbc, b.hb, b.junk, b.ss, b.rstd,
                        wait_x=lambda: S.wait(e, self.sem("ldx"), i),
                        wait_hb_free=lambda: (self.nt.wait_tr_done(e, i - 1) if i >= 1 else None))
            S.wait(e, self.sem("ldp"), i)
            if i >= 1:
                S.wait(e, self.nt.n + "_tr", (NQ + i - 1) * 8 + 1)
            ins = e.tensor_copy(out=b.pb[:, :, :], in_=b.pt[:, :, :])
            S.sig(ins, self.sem("pb"), i)
            for s in range(4):
                for half in range(2):
                    S.wait(e, self.sem("mm"), m + half)
                    S.wait(e, self.sem("mA"), ("rel", m + 2 + half))
                    ins = e.tensor_tensor(out=b.osb[:, s % 2, half * 512:(half + 1) * 512], in0=b.sgm[:, s % 2, half * 512:(half + 1) * 512],
                                          in1=b.ring[:, (m + half) % NMR, :], op=ALU.mult)
                    S.sig(ins, self.sem("mD"), ("rel", m + half))
                S.wait(e, self.sem("mD"), ("rel", m + 1))
                S.sig(e.memset(b.os2[:, 2:3], 0.0), self.sem("eg"), (i, s))
                m += 4
                self.dve_postnorm(e, i, s, b.xt)


class AG(Phase):
    def __init__(self, prog, tag, E, G):
        super().__init__(prog, tag)
        self.E, self.G = E, G

    def pool(self, e):
        S = self.S
        for k in range(NCH):
            ins = e.collective_compute("AllGather", ALU.bypass, replica_groups=[[0, 1, 2, 3], [4, 5, 6, 7]],
                                       ins=[self.E[k, :].rearrange("(p c) -> p c", c=2048)],
                                       outs=[self.G[k, :, :].rearrange("r (p c) -> (r p) c", c=2048)])
            S.sig(ins, "cc", k)
        S.wait(e, "cc", NCH - 1)


def _local_tokens(c):
    ii = np.arange(8)[:, None]
    t = np.arange(512)[None, :]
    return (2048 * ii + 512 * c + t).reshape(-1)


WKEYS = [("g_pre_mix", [D]), ("w_in", [D, 2304]), ("g_diff_sub", [64]), ("gmlp_ln_g", [256]), ("gmlp_ln_b", [256]),
         ("w_spatial", [4, 128, 128]), ("b_spatial", [4, 128]), ("swa_sinks", [6]), ("w_out", [D, D]), ("g_post_mix", [D]),
         ("g_pre_ffn", [D]), ("w_ffn_in", [D, 2 * DFF]), ("w_ffn_out", [DFF, D]), ("g_post_ffn", [D]),
         ("w_ple_up", [256, D]), ("w_ple_gate", [D, D]), ("g_ple_gate", [D]), ("g_ple_post", [D])]


def build_fused(nl=L):
    P = Prog()
    ncol = cidx_map()[1]
    x = P.din("x", [NTOK, D], F32)
    pp = P.din("p", [L, NTOK, 256], F32)
    ctab = P.din("ctab", [128, ncol], F32)
    dn = P.din("dn", [128, 16, 512], F32)
    dw = P.din("dw", [128, 18, 512], F32)
    kaug = P.din("kaug", [6, 4, SEQ], BF16)
    qaug = P.din("qaug", [6, 2, 4, NTOK], BF16)
    identb = P.din("identb", [128, 128], BF16)
    identf = P.din("identf", [128, 128], F32)
    lamv = P.din("lamv", [L, 4, 32], F32)
    lamc = P.din("lamc", [L, 128, 2], F32)
    W = {k: P.din(k, [L] + shp, F32) for k, shp in WKEYS}
    E = P.dint("E", [NCH, CHE], BF16)
    G = P.dint("G", [NCH, 4, CHE], BF16)
    Qs = P.dint("Qs", [384, NTOK], BF16)
    ya_d = P.dint("ya_d", [128, 32, 384], BF16)
    x1 = P.dint("x1", [NTOK, D], F32)
    x2 = P.dint("x2", [NTOK, D], F32)
    x3 = P.dint("x3", [NTOK, D], F32)
    xo = P.dout("xo", [NTOK, D], F32)
    for l in range(nl):
        xin = x if l == 0 else x3
        xout = xo if l == nl - 1 else x3
        P.phases.append(P1(P, "p1", xin, W["w_in"][l], W["g_pre_mix"][l], identb, Qs, E))
        P.phases.append(P2a(P, "a", G, Qs, ctab, dn, kaug, qaug, identf, lamv[l], lamc[l], W["g_diff_sub"][l], ya_d, E=E))
        P.phases.append(P2b(P, "b", xin, W["w_in"][l], W["g_pre_mix"][l], G, ya_d, W["gmlp_ln_g"][l], W["gmlp_ln_b"][l],
                            W["w_spatial"][l], W["b_spatial"][l], W["swa_sinks"][l], W["w_out"][l], W["g_post_mix"][l],
                            dw, identb, identf, x1))
        P.phases.append(P3a(P, "f", x1, W["w_ffn_in"][l], W["w_ffn_out"][l], W["g_pre_ffn"][l], W["g_post_ffn"][l], identb, x2))
        P.phases.append(P3b(P, "e", x2, pp[l], W["w_ple_up"][l], W["w_ple_gate"][l], W["g_ple_gate"][l], W["g_ple_post"][l],
                            identb, xout))
    return P.build()


_PROGS = {}


def kernel(**inputs):
    if "F" not in _PROGS:
        _PROGS["F"] = build_fused()
    consts = host_consts()
    tabs = [host_tables(c) for c in range(4)]
    x = np.asarray(inputs["x"], np.float32)
    p = np.asarray(inputs["p"], np.float32)
    lamv = np.stack([np.stack([np.asarray(inputs[k][l], np.float32) for k in ("lam_q1", "lam_k1", "lam_q2", "lam_k2")])
                     for l in range(L)])
    lamc = np.stack([np.tile(np.array([[lam_init(l), 1.0 - lam_init(l)]], np.float32), (128, 1)) for l in range(L)])
    shared = {k: np.ascontiguousarray(np.asarray(inputs[k], np.float32)) for k, _ in WKEYS}
    shared.update(kaug=consts["kaug"], qaug=consts["qaug"], identb=consts["identb"], identf=consts["identf"],
                  lamv=lamv, lamc=lamc)
    cores = list(range(8))
    in_maps = []
    for core in cores:
        bb, c = core // 4, core % 4
        tok = _local_tokens(c)
        m = dict(shared)
        m["x"] = np.ascontiguousarray(x[bb][tok])
        m["p"] = np.ascontiguousarray(p[:, bb][:, tok])
        m["ctab"], m["dn"], m["dw"] = tabs[c]["ctab"], tabs[c]["dn"], tabs[c]["dw"]
        in_maps.append(m)
    res = run_bass_kernel_spmd(_PROGS["F"], in_maps, core_ids=cores)
    out = np.empty((NB, SEQ, D), np.float32)
    for core in cores:
        out[core // 4][_local_tokens(core % 4)] = np.asarray(res.results[core]["xo"], np.float32)
    return out
```

```python
import numpy as np
import ml_dtypes
from contextlib import ExitStack
import concourse.bass as bass
import concourse.mybir as mybir
from concourse.bass_utils import run_bass_kernel_spmd

F32 = mybir.dt.float32
BF16 = mybir.dt.bfloat16
AF = mybir.ActivationFunctionType
ALU = mybir.AluOpType
AX = mybir.AxisListType
NPBF = ml_dtypes.bfloat16

D = 1024
SEQ = 16384
NB = 2
L = 4
NTOK = 4096
NQ = 8
DFF = 2816
EPS = 1e-6
SC_A = 32 ** -0.5
SC_W = 64 ** -0.5
_k = np.arange(1, 13, dtype=np.float64)
_sl = np.exp2(-8.0 * _k / 12.0)
SL_DIFF = _sl[6:]
SL_WIN = _sl[:6]
CUT = 27.0
BIG = 30000.0
LA = 2
NSR = 3
NPR = 4


def lam_init(l):
    return 0.8 - 0.6 * float(np.exp(-0.3 * l))


class Dummy:
    def __getattr__(self, n):
        return self

    def __call__(self, *a, **k):
        return self

    def __getitem__(self, k):
        return self

    def __enter__(self):
        return self

    def __exit__(self, *a):
        return False


class Sync:
    def __init__(self):
        self.dry = True
        self.total = {}
        self.count = {}
        self.seq = {}
        self.rpos = {}
        self.handles = {}
        self.scope = None

    def _k(self, key, glob):
        return key if glob else (self.scope, key)

    def sig(self, ins, sem, key, dma=False, glob=False):
        inc = 16 if dma else 1
        k = (sem, self._k(key, glob))
        if self.dry:
            assert k not in self.count, k
            t = self.total.get(sem, 0) + inc
            self.total[sem] = t
            self.count[k] = t
            self.seq.setdefault(sem, []).append((k, t))
        else:
            i = self.rpos.get(sem, 0)
            kk, t = self.seq[sem][i]
            assert kk == k, (kk, k)
            self.rpos[sem] = i + 1
            ins.then_inc(self.handles[sem], inc)

    def wait(self, e, sem, key, glob=False):
        if not self.dry:
            e.wait_ge(self.handles[sem], self.count[(sem, self._k(key, glob))])

    def fence(self, e, ins, sem):
        if self.dry:
            t = self.total.get(sem, 0) + 1
            self.total[sem] = t
            self.seq.setdefault(sem, []).append((None, t))
        else:
            i = self.rpos.get(sem, 0)
            kk, t = self.seq[sem][i]
            assert kk is None, kk
            self.rpos[sem] = i + 1
            ins.then_inc(self.handles[sem], 1)
            e.wait_ge(self.handles[sem], t)


def band_tiles(h, i):
    out = []
    for j in range(128):
        if j < 16 * i:
            dmin = 2048 * i - (128 * j + 127)
        elif j >= 16 * (i + 1):
            dmin = 128 * j - (2048 * i + 2047)
        else:
            dmin = 0
        if SL_DIFF[h] * dmin <= CUT:
            out.append(j)
    return out


def j2idx(j):
    T = j // 4
    sub = j % 4
    r = T % 4
    ii = T // 4
    return r, 4 * ii + sub


_CIDX = None


def cidx_map():
    global _CIDX
    if _CIDX is None:
        m = {}
        n = 1
        for h in range(6):
            for i in range(NQ):
                for j in band_tiles(h, i):
                    if j < 16 * i or j >= 16 * (i + 1):
                        m[(h, i, j)] = n
                        n += 1
        _CIDX = (m, n)
    return _CIDX


def split_bf(v):
    hi = np.float32(np.asarray(v, np.float32).astype(NPBF).astype(np.float32))
    lo = np.float32(np.asarray(np.float32(v) - hi, np.float32).astype(NPBF).astype(np.float32))
    return hi, lo


def host_tables(c):
    m, n = cidx_map()
    ct = np.zeros((n,), np.float32)
    for (h, i, j), col in m.items():
        qc = 2048 * i + 512 * c + 256
        ct[col] = -SL_DIFF[h] * abs(128 * j - qc)
    ctab = np.ascontiguousarray(np.broadcast_to(ct[None, :], (128, n))).astype(np.float32)
    ki = np.arange(128)[:, None, None]
    qi = np.arange(512)[None, None, :]
    tg = np.arange(16)[None, :, None]
    dn = np.abs(128 * tg + ki - (512 * c + qi)).astype(np.float32)
    koff = np.array([-128] + [128 * t for t in range(16)] + [2048])[None, :, None]
    dw = np.abs(koff + ki - (512 * c + qi)).astype(np.float32)
    dw = np.where(dw <= 128.0, dw, BIG).astype(np.float32)
    return {"ctab": ctab, "dn": np.ascontiguousarray(dn), "dw": np.ascontiguousarray(dw)}


def host_consts():
    kaug = np.zeros((6, 4, 128), np.float32)
    qaug = np.zeros((6, 2, 4, NTOK), np.float32)
    qip = (np.arange(NTOK) % 512 - 256).astype(np.float32)
    for h in range(6):
        sp = SL_DIFF[h] / SC_A
        hi, lo = split_bf(sp)
        kaug[h, 0, :] = -hi
        kaug[h, 1, :] = -lo
        kaug[h, 2, :] = np.arange(128)
        kaug[h, 3, :] = np.arange(128)
        qaug[h, 0, 0, :] = qip
        qaug[h, 0, 1, :] = qip
        qaug[h, 0, 2, :] = hi
        qaug[h, 0, 3, :] = lo
        qaug[h, 1] = -qaug[h, 0]
    kaug = np.tile(kaug, (1, 1, 128))
    return {
        "kaug": kaug.astype(NPBF),
        "qaug": qaug.astype(NPBF),
        "identb": np.eye(128, dtype=np.float32).astype(NPBF),
        "identf": np.eye(128, dtype=np.float32),
    }


class Prog:
    def __init__(self):
        self.nc = bass.Bass("TRN2", target_bir_lowering=False)
        self.S = Sync()
        self.dram = {}
        self.phases = []

    def din(self, name, shape, dt):
        t = self.nc.dram_tensor(name, list(shape), dt, kind="ExternalInput").ap()
        self.dram[name] = t
        return t

    def dout(self, name, shape, dt):
        t = self.nc.dram_tensor(name, list(shape), dt, kind="ExternalOutput").ap()
        self.dram[name] = t
        return t

    def dint(self, name, shape, dt):
        t = self.nc.dram_tensor(name, list(shape), dt, kind="Internal").ap()
        self.dram[name] = t
        return t

    def build(self):
        S = self.S
        nph = len(self.phases)
        engs = ["sp", "act", "pe", "dve", "pool"]
        S.dry = True
        for pi, ph in enumerate(self.phases):
            S.scope = pi
            ph.reset()
            ph.bind(Dummy())
            for en in engs:
                e = Dummy()
                self._barrier_pre(en, e, pi)
                getattr(ph, en)(e)
                self._barrier_post(en, e, pi)
        S.dry = False
        nc = self.nc
        with ExitStack() as es:
            for sem in S.seq:
                S.handles[sem] = es.enter_context(nc.semaphore(sem))
            self.bar_tile = es.enter_context(nc.sbuf_tensor("bar_tile", [128, 8], F32))
            for pi, ph in enumerate(self.phases):
                S.scope = pi
                ph.reset()
                with ExitStack() as pes:
                    bufs = ph.alloc(nc, pes)
                    ph.bind(bufs)
                    with nc.Block() as block:
                        def mk(en, ph=ph, pi=pi):
                            def f(e):
                                S.scope = pi
                                self._barrier_pre(en, e, pi)
                                getattr(ph, en)(e)
                                self._barrier_post(en, e, pi)
                            return f
                        block.sync(mk("sp"))
                        block.scalar(mk("act"))
                        block.tensor(mk("pe"))
                        block.vector(mk("dve"))
                        block.gpsimd(mk("pool"))
        for sem in S.seq:
            assert S.rpos.get(sem, 0) == len(S.seq[sem]), sem
        return nc

    def _barrier_pre(self, en, e, pi):
        if pi == 0 or en == "pool":
            return
        self.S.wait(e, "bar", ("end", pi - 1), glob=True)

    def _barrier_post(self, en, e, pi):
        if en != "pool":
            return
        S = self.S
        if S.dry:
            S.sig(None, "bar", ("end", pi), glob=True)
        else:
            ins = e.memset(self.bar_tile[:, :], 0.0)
            S.sig(ins, "bar", ("end", pi), glob=True)


class Phase:
    def __init__(self, prog, tag):
        self.P = prog
        self.S = prog.S
        self.tag = tag

    def reset(self):
        pass

    def bind(self, bufs):
        self.b = bufs

    def alloc(self, nc, es):
        return Dummy()

    def sp(self, e):
        pass

    def act(self, e):
        pass

    def pe(self, e):
        pass

    def dve(self, e):
        pass

    def pool(self, e):
        pass

    def sem(self, n):
        return n


class Bufs:
    pass


_UID = [0]


def _uname(name):
    _UID[0] += 1
    return "%s_%d" % (name, _UID[0])


def sb(nc, es, name, shape, dt):
    return es.enter_context(nc.sbuf_tensor(_uname(name), list(shape), dt))


def ps(nc, es, name, shape, dt):
    return es.enter_context(nc.psum_tensor(_uname(name), list(shape), dt))


NCH = 16
CHE = 262144


def wview(w, c0, c1):
    return w[:, c0:c1].rearrange("(kc p) c -> p kc c", p=128)


class NormT:
    def __init__(self, ph, name):
        self.ph = ph
        self.S = ph.S
        self.n = ph.sem(name)

    def dve(self, e, t, xt, gbc, hb, junk, ss, rstd, wait_x, wait_hb_free):
        S = self.S
        n = self.n
        wait_x()
        wait_hb_free()
        for s in range(4):
            ins = e.tensor_tensor(out=junk[:, :], in0=xt[:, s, :], in1=xt[:, s, :], op=ALU.mult)
            S.fence(e, ins, n + "_f")
            ins = e.tensor_reduce(out=ss[:, s:s + 1], in_=junk[:, :], axis=AX.X, op=ALU.add)
            S.fence(e, ins, n + "_f")
        ins = e.tensor_scalar(out=rstd[:, 0:4], in0=ss[:, 0:4], scalar1=1.0 / D, scalar2=EPS, op0=ALU.mult, op1=ALU.add)
        S.sig(ins, n + "_v", t)
        S.wait(e, n + "_sq", t)
        ins = e.reciprocal(out=rstd[:, 0:4], in_=rstd[:, 0:4])
        S.fence(e, ins, n + "_f")
        for s in range(4):
            ins = e.scalar_tensor_tensor(out=hb[:, s, :], in0=xt[:, s, :], scalar=rstd[:, s:s + 1], in1=gbc[:, :],
                                         op0=ALU.mult, op1=ALU.mult)
            S.sig(ins, n + "_hb", (t, s))

    def pe(self, e, t, hb, pst, identb):
        S = self.S
        n = self.n
        for kc in range(8):
            g = t * 8 + kc
            if g >= 2:
                S.wait(e, n + "_ev", g - 2)
            for s in range(4):
                if kc == 0:
                    S.wait(e, n + "_hb", (t, s))
                ins = e.transpose(pst[:, g % 2, s * 128:(s + 1) * 128], hb[:, s, kc * 128:(kc + 1) * 128], identb[:, :])
            S.sig(ins, n + "_tr", g)

    def act(self, e, t, pst, hT, wait_hT_free, rstd=None):
        S = self.S
        n = self.n
        S.wait(e, n + "_v", t)
        ins = e.activation(out=rstd[:, 0:4], in_=rstd[:, 0:4], func=AF.Sqrt)
        S.sig(ins, n + "_sq", t)
        wait_hT_free()
        for kc in range(8):
            g = t * 8 + kc
            S.wait(e, n + "_tr", g)
            ins = e.activation(out=hT[:, kc, :], in_=pst[:, g % 2, 0:512], func=AF.Copy)
            S.sig(ins, n + "_ev", g)

    def wait_hT(self, e, t):
        self.S.wait(e, self.n + "_ev", t * 8 + 7)

    def wait_tr_done(self, e, t):
        self.S.wait(e, self.n + "_tr", t * 8 + 7)


class P1(Phase):
    FM = [(0, 384), (384, 768), (2048, 2176)]
    TM = [(768, 1152), (2176, 2304)]

    def __init__(self, prog, tag, x, w_in, g_pre, identb, Qs, E):
        super().__init__(prog, tag)
        self.x, self.w_in, self.g_pre, self.identb_d = x, w_in, g_pre, identb
        self.Qs, self.E = Qs, E
        self.EK = E[0:6, :].rearrange("h (a n) -> (h a) n", n=NTOK)
        self.ECK = E[12:14, :].rearrange("h (a n) -> (h a) n", n=NTOK)
        self.nt = NormT(self, "nt")

    def alloc(self, nc, es):
        b = Bufs()
        b.wfm = sb(nc, es, "p1_wfm", [128, 8, 896], BF16)
        b.wtm = sb(nc, es, "p1_wtm", [128, 8, 512], BF16)
        b.gbc = sb(nc, es, "p1_gbc", [128, D], F32)
        b.identb = sb(nc, es, "p1_id", [128, 128], BF16)
        b.xt = sb(nc, es, "p1_xt", [128, 4, D], F32)
        b.hb = sb(nc, es, "p1_hb", [128, 4, D], BF16)
        b.junk = sb(nc, es, "p1_junk", [128, D], F32)
        b.ss = sb(nc, es, "p1_ss", [128, 4], F32)
        b.rstd = sb(nc, es, "p1_rstd", [128, 4], F32)
        b.hT = sb(nc, es, "p1_hT", [128, 8, 512], BF16)
        b.fm = sb(nc, es, "p1_fm", [128, 2, 7, 512], BF16)
        b.vt = sb(nc, es, "p1_vt", [128, 2, 4, 512], BF16)
        b.pst = ps(nc, es, "p1_pst", [128, 2, 1024], BF16)
        b.pm = ps(nc, es, "p1_pm", [128, 4, 512], F32)
        return b

    def sp(self, e):
        S, b = self.S, self.b
        ld = self.sem("ld")
        ins = e.dma_start(out=b.identb[:, :], in_=self.identb_d[:, :])
        S.sig(ins, ld, "id", dma=True)
        ins = e.dma_start(out=b.gbc[:, :], in_=self.g_pre.partition_broadcast(128))
        S.sig(ins, ld, "g", dma=True)
        for t in range(NQ):
            if t >= 1:
                S.wait(e, self.nt.n + "_hb", (t - 1, 3))
            ins = e.dma_start(out=b.xt[:, :, :], in_=self.x[t * 512:(t + 1) * 512, :].rearrange("(s p) d -> p s d", p=128))
            S.sig(ins, self.sem("ldx"), t, dma=True)

    def pool(self, e):
        S, b = self.S, self.b
        lw = self.sem("lw")
        c = 0
        for (c0, c1) in self.FM:
            ins = e.dma_start(out=b.wfm[:, :, c:c + (c1 - c0)], in_=wview(self.w_in, c0, c1))
            S.sig(ins, lw, ("fm", c0), dma=True)
            c += c1 - c0
        c = 0
        for (c0, c1) in self.TM:
            ins = e.dma_start(out=b.wtm[:, :, c:c + (c1 - c0)], in_=wview(self.w_in, c0, c1))
            S.sig(ins, lw, ("tm", c0), dma=True)
            c += c1 - c0
        for t in range(NQ):
            sl = t % 2
            st = self.sem("st%d" % sl)
            cs = slice(t * 512, (t + 1) * 512)
            S.wait(e, self.sem("ev"), ("fm", t, 6))
            ins = e.dma_start(out=self.Qs[:, cs].rearrange("(b p) n -> p b n", p=128), in_=b.fm[:, sl, 0:3, :])
            S.sig(ins, st, ("q", t), dma=True)
            ins = e.dma_start(out=self.EK[:, cs].rearrange("(b p) n -> p b n", p=128), in_=b.fm[:, sl, 3:6, :])
            S.sig(ins, st, ("k", t), dma=True)
            ins = e.dma_start(out=self.ECK[:, cs], in_=b.fm[:, sl, 6, :])
            S.sig(ins, st, ("ck", t), dma=True)
            S.wait(e, self.sem("ev"), ("tm", t, 3))
            for hh in range(8):
                ch = 6 + hh if hh < 6 else 14 + (hh - 6)
                dst = self.E[ch, :].rearrange("(p t d) -> p t d", p=128, t=32)
                ins = e.dma_start(out=dst[:, 4 * t:4 * t + 4, :], in_=b.vt[:, sl, :, hh * 64:(hh + 1) * 64])
                S.sig(ins, st, ("cv" if hh == 7 else ("v", hh), t), dma=True)
        S.wait(e, self.sem("st0"), ("cv", NQ - 2))
        S.wait(e, self.sem("st1"), ("cv", NQ - 1))

    def dve(self, e):
        S, b = self.S, self.b
        S.wait(e, self.sem("ld"), "g")
        for t in range(NQ):
            self.nt.dve(e, t, b.xt, b.gbc, b.hb, b.junk, b.ss, b.rstd,
                        wait_x=lambda: S.wait(e, self.sem("ldx"), t),
                        wait_hb_free=lambda: (self.nt.wait_tr_done(e, t - 1) if t >= 1 else None))

    def pe(self, e):
        S, b = self.S, self.b
        S.wait(e, self.sem("ld"), "g")
        S.wait(e, self.sem("lw"), ("tm", self.TM[-1][0]))
        n = 0
        for t in range(NQ):
            self.nt.pe(e, t, b.hb, b.pst, b.identb)
            self.nt.wait_hT(e, t)
            for blk in range(7):
                if n >= 4:
                    S.wait(e, self.sem("ev"), self.evkeys[n - 4])
                for kc in range(8):
                    ins = e.matmul(b.pm[:, n % 4, :], lhsT=b.wfm[:, kc, blk * 128:(blk + 1) * 128], rhs=b.hT[:, kc, :],
                                   start=(kc == 0), stop=(kc == 7))
                S.sig(ins, self.sem("mm"), ("fm", t, blk))
                self.evkeys.append(("fm", t, blk))
                n += 1
            for s in range(4):
                if n >= 4:
                    S.wait(e, self.sem("ev"), self.evkeys[n - 4])
                for kc in range(8):
                    ins = e.matmul(b.pm[:, n % 4, :], lhsT=b.hT[:, kc, s * 128:(s + 1) * 128], rhs=b.wtm[:, kc, :],
                                   start=(kc == 0), stop=(kc == 7))
                S.sig(ins, self.sem("mm"), ("tm", t, s))
                self.evkeys.append(("tm", t, s))
                n += 1

    def reset(self):
        self.evkeys = []

    def act(self, e):
        S, b = self.S, self.b
        n = 0
        for t in range(NQ):
            sl = t % 2
            self.nt.act(e, t, b.pst, b.hT,
                        wait_hT_free=lambda: (S.wait(e, self.sem("mm"), ("tm", t - 1, 3)) if t >= 1 else None),
                        rstd=b.rstd)
            for blk in range(7):
                S.wait(e, self.sem("mm"), ("fm", t, blk))
                if t >= 2 and blk == 0:
                    S.wait(e, self.sem("st%d" % sl), ("cv", t - 2))
                ins = e.activation(out=b.fm[:, sl, blk, :], in_=b.pm[:, n % 4, :], func=AF.Copy)
                S.sig(ins, self.sem("ev"), ("fm", t, blk))
                n += 1
            for s in range(4):
                S.wait(e, self.sem("mm"), ("tm", t, s))
                if t >= 2 and s == 0:
                    S.wait(e, self.sem("st%d" % sl), ("cv", t - 2))
                ins = e.activation(out=b.vt[:, sl, s, :], in_=b.pm[:, n % 4, :], func=AF.Copy)
                S.sig(ins, self.sem("ev"), ("tm", t, s))
                n += 1


class AttnPipe(Phase):
    NSR = NSR

    def reset(self):
        self.gu = {"pe": 0, "act": 0, "dve": 0}
        self.prev_tp = {"pe": None}

    def pe_group(self, e, g):
        S, b = self.S, self.b
        U = len(g["units"])
        g0 = self.gu["pe"]
        for t in range(U + LA):
            if t < U:
                u = g["units"][t]
                gu = g0 + t
                if gu >= self.NSR:
                    S.wait(e, self.sem("p"), gu - self.NSR)
                if t == 0 and g.get("wait_pe") is not None:
                    g["wait_pe"](e)
                ins = e.matmul(b.sring[:, gu % self.NSR, :], lhsT=u["lhsT"], rhs=u["rhs"], start=True, stop=True)
                S.sig(ins, self.sem("qk"), gu)
            if t >= LA:
                v = t - LA
                u = g["units"][v]
                gv = g0 + v
                S.wait(e, self.sem("p"), gv)
                if u["first"] and g["acc_prev"] is not None:
                    S.wait(e, self.sem("ev"), (g["acc_prev"], u["acc"]))
                ins = e.matmul(b.oacc[:, g["par"] * 2 + u["acc"], :], lhsT=u["vl"], rhs=b.pring[:, gv % NPR, :],
                               start=u["first"], stop=u["last"])
                S.sig(ins, self.sem("pv"), gv)
        self.gu["pe"] = g0 + U

    def pe_group_pairs(self, e, g, after_prologue=None):
        S, b = self.S, self.b
        LAP = 2
        U = len(g["units"])
        NP_ = U // 2
        g0 = self.gu["pe"]
        for t in range(NP_ + LAP):
            if t == min(LAP, NP_) and after_prologue is not None:
                after_prologue()
                after_prologue = None
            if t >= LAP:
                S.wait(e, self.sem("p"), g0 + 2 * (t - LAP) + 1)
            elif g0 + 2 * (t - LAP) + 1 >= 0:
                S.wait(e, self.sem("p"), g0 + 2 * (t - LAP) + 1)
            if t == 0 and g.get("wait_pe") is not None:
                g["wait_pe"](e)
            if t < NP_:
                for c in (0, 1):
                    u = g["units"][2 * t + c]
                    gu = g0 + 2 * t + c
                    ins = e.matmul(b.sring[:, gu % self.NSR, :], lhsT=u["lhsT"], rhs=u["rhs"], start=True, stop=True)
                    S.sig(ins, self.sem("qk"), gu)
            if t >= LAP:
                for c in (0, 1):
                    v = 2 * (t - LAP) + c
                    u = g["units"][v]
                    gv = g0 + v
                    if u["first"] and g["acc_prev"] is not None:
                        S.wait(e, self.sem("ev"), (g["acc_prev"], u["acc"]))
                    ins = e.matmul(b.oacc[:, g["par"] * 2 + u["acc"], :], lhsT=u["vl"], rhs=b.pring[:, gv % NPR, :],
                                   start=u["first"], stop=u["last"])
                    S.sig(ins, self.sem("pv"), gv)
        if after_prologue is not None:
            after_prologue()
        self.gu["pe"] = g0 + U

    def pe_post(self, e, g):
        S, b = self.S, self.b
        for c in (0, 1):
            S.wait(e, self.sem("ev"), (g["gid"], c))
            if self.prev_tp["pe"] is not None:
                S.wait(e, self.sem("tpc"), self.prev_tp["pe"])
            for s in range(4):
                ins = e.transpose(b.tps[:, s, 0:65], b.oT[0:65, c, s * 128:(s + 1) * 128], b.identf[0:65, 0:65])
            S.sig(ins, self.sem("tp"), (g["gid"], c))
            self.prev_tp["pe"] = (g["gid"], c)

    def act_group(self, e, g):
        S, b = self.S, self.b
        g0 = self.gu["act"]
        for t, u in enumerate(g["units"]):
            gu = g0 + t
            if u["kind"] == "G":
                S.wait(e, self.sem("gb"), gu)
            else:
                S.wait(e, self.sem("qk"), gu)
            if gu >= NPR:
                S.wait(e, self.sem("pv"), gu - NPR)
            ins = e.activation(out=b.pring[:, gu % NPR, :], in_=b.sring[:, gu % self.NSR, :], func=AF.Exp,
                               bias=u["bias"], scale=u["scale"])
            S.sig(ins, self.sem("p"), gu)
        self.gu["act"] = g0 + len(g["units"])

    def dve_group(self, e, g, consume):
        S, b = self.S, self.b
        g0 = self.gu["dve"]
        last = {}
        for t, u in enumerate(g["units"]):
            gu = g0 + t
            last[u["acc"]] = gu
            if u["kind"] == "G":
                S.wait(e, self.sem("qk"), gu)
                ins = e.scalar_tensor_tensor(out=b.sring[:, gu % self.NSR, :], in0=u["dD"], scalar=u["dcoef"],
                                             in1=b.sring[:, gu % self.NSR, :], op0=ALU.mult, op1=ALU.add)
                S.sig(ins, self.sem("gb"), gu)
        self.gu["dve"] = g0 + len(g["units"])
        for c in (0, 1):
            S.wait(e, self.sem("pv"), last[c])
            if g["gid"] >= 1:
                S.wait(e, self.sem("tp"), (g["gid"] - 1, c))
            ins = e.tensor_copy(out=b.oT[0:65, c, :], in_=b.oacc[0:65, g["par"] * 2 + c, :])
            S.sig(ins, self.sem("ev"), (g["gid"], c))
        for c in (0, 1):
            S.wait(e, self.sem("tp"), (g["gid"], c))
            consume(e, g, c)


class P2a(AttnPipe):
    NSR = 4

    AG_ORDER = [0, 6, 1, 7, 2, 8, 3, 9, 4, 10, 5, 11, 12, 13, 14, 15]

    def __init__(self, prog, tag, G, Qs, ctab, dn, kaug, qaug, identf, lamv, lamc, gsub, ya_d, E=None):
        super().__init__(prog, tag)
        self.E = E
        self.G, self.Qs, self.ctab_d, self.dn_d = G, Qs, ctab, dn
        self.kaug, self.qaug, self.identf_d, self.lamv, self.lamc_d, self.gsub_d, self.ya_d = kaug, qaug, identf, lamv, lamc, gsub, ya_d
        self.ncol = cidx_map()[1]

    def alloc(self, nc, es):
        b = Bufs()
        b.KT = sb(nc, es, "a_KT", [100, 2, SEQ], BF16)
        b.V = sb(nc, es, "a_V", [128, 2, 129, 65], BF16)
        b.QA = sb(nc, es, "a_QA", [100, 2, NTOK], BF16)
        b.QB = sb(nc, es, "a_QB", [100, 2, NTOK], BF16)
        b.ctab = sb(nc, es, "a_ctab", [128, self.ncol], F32)
        b.dn = sb(nc, es, "a_dn", [128, 16, 512], F32)
        b.identf = sb(nc, es, "a_idf", [128, 128], F32)
        b.pring = sb(nc, es, "a_pr", [128, NPR, 512], BF16)
        b.oT = sb(nc, es, "a_oT", [65, 2, 512], F32)
        b.lv = sb(nc, es, "a_lv", [128, 4, 32], F32)
        b.lt = sb(nc, es, "a_lt", [128, 8], F32)
        b.lamc = sb(nc, es, "a_lamc", [128, 2], F32)
        b.gsub = sb(nc, es, "a_gsub", [128, 64], F32)
        b.r = sb(nc, es, "a_r", [128, 8], F32)
        b.t0 = sb(nc, es, "a_t0", [128, 4, 64], F32)
        b.y = sb(nc, es, "a_y", [128, 4, 64], F32)
        b.sq = sb(nc, es, "a_sq", [128, 4, 64], F32)
        b.sst = sb(nc, es, "a_sst", [128, 32, 6], F32)
        b.ya = sb(nc, es, "a_ya", [128, 32, 384], BF16)
        b.sring = ps(nc, es, "a_sr", [128, 4, 512], F32)
        b.oacc = ps(nc, es, "a_oacc", [128, 2, 512], F32)
        b.tps = ps(nc, es, "a_tps", [128, 4, 128], F32)
        return b

    def groups(self):
        b = self.b
        cm, _ = cidx_map()
        out = []
        gid = 0
        for h in range(6):
            sl = h % 2
            sp = float(SL_DIFF[h] / SC_A)
            for i in range(NQ):
                units = []
                tl = band_tiles(h, i)
                qs = slice(512 * i, 512 * i + 512)
                for n, j in enumerate(tl):
                    r, lt = j2idx(j)
                    ks = slice(r * 4096 + lt * 128, r * 4096 + lt * 128 + 128)
                    for comp in (0, 1):
                        base = 64 * comp
                        vi = (r * 32 + lt) * 65
                        u = {"acc": comp, "first": n == 0, "last": n == len(tl) - 1, "scale": SC_A,
                             "vl": b.V[:, sl, :, :].rearrange("p t d -> p (t d)")[:, vi:vi + 128]}
                        if j < 16 * i or j >= 16 * (i + 1):
                            Q = b.QA if j < 16 * i else b.QB
                            u["kind"] = "F"
                            u["lhsT"] = b.KT[base:base + 36, sl, ks]
                            u["rhs"] = Q[base:base + 36, sl, qs]
                            u["bias"] = b.ctab[:, cm[(h, i, j)]:cm[(h, i, j)] + 1]
                        else:
                            u["kind"] = "G"
                            u["lhsT"] = b.KT[base:base + 32, sl, ks]
                            u["rhs"] = b.QA[base:base + 32, sl, qs]
                            u["bias"] = b.ctab[:, 0:1]
                            u["dD"] = b.dn[:, j - 16 * i, :]
                            u["dcoef"] = -sp
                        units.append(u)
                out.append({"gid": gid, "par": 0, "units": units, "h": h, "i": i,
                            "acc_prev": gid - 1 if gid >= 1 else None})
                gid += 1
        return out

    def sp(self, e):
        S, b = self.S, self.b
        ld = self.sem("ldc")
        lst = [(b.ctab[:, :], self.ctab_d[:, :]), (b.dn[:, :, :], self.dn_d[:, :, :]),
               (b.identf[:, :], self.identf_d[:, :]), (b.lamc[:, :], self.lamc_d[:, :]),
               (b.gsub[:, :], self.gsub_d.partition_broadcast(128))]
        for n, (dst, src) in enumerate(lst):
            ins = e.dma_start(out=dst, in_=src)
            S.sig(ins, ld, ("c", n), dma=True)
        for k in range(4):
            ins = e.dma_start(out=b.lv[:, k, :], in_=self.lamv[k, :].partition_broadcast(128))
            S.sig(ins, ld, ("lv", k), dma=True)
        for h in range(6):
            sl = h % 2
            sem = self.sem("ld%d" % sl)
            if h >= 2:
                S.wait(e, self.sem("hd"), h - 2)
            if self.E is not None:
                S.wait(e, "cc", 6 + h)
            for comp in (0, 1):
                rows = slice(h * 64 + comp * 32, h * 64 + comp * 32 + 32)
                base = 64 * comp
                ins = e.dma_start(out=b.KT[base:base + 32, sl, :].rearrange("p (r n) -> p r n", r=4),
                                  in_=self.G[h, :, :].rearrange("r (a n) -> a r n", n=NTOK)[comp * 32:(comp + 1) * 32])
                S.sig(ins, sem, ("k", h, comp), dma=True)
                ins = e.dma_start(out=b.KT[base + 32:base + 36, sl, :], in_=self.kaug[h, :, :])
                S.sig(ins, sem, ("ka", h, comp), dma=True)
                for var, Q in ((0, b.QA), (1, b.QB)):
                    ins = e.dma_start(out=Q[base:base + 32, sl, :], in_=self.Qs[rows, :])
                    S.sig(ins, sem, ("q", h, comp, var), dma=True)
                    ins = e.dma_start(out=Q[base + 32:base + 36, sl, :], in_=self.qaug[h, var, :, :])
                    S.sig(ins, sem, ("qa", h, comp, var), dma=True)
            for r in range(4):
                ins = e.dma_start(out=b.V[:, sl, r * 32:(r + 1) * 32, 0:64],
                                  in_=self.G[6 + h, r, :].rearrange("(p t d) -> p t d", p=128, t=32))
                S.sig(ins, sem, ("v", h, r), dma=True)

    def pe(self, e):
        S, b = self.S, self.b
        S.wait(e, self.sem("ldc"), ("lv", 3))
        S.wait(e, self.sem("ones"), 0)
        prev = None
        for g in self.groups():
            if g["i"] == 0:
                h = g["h"]
                g["wait_pe"] = lambda e, h=h: S.wait(e, self.sem("ld%d" % (h % 2)), ("v", h, 3))
            self.pe_group_pairs(e, g, after_prologue=(lambda p=prev: self.pe_post(e, p)) if prev is not None else None)
            prev = g
        self.pe_post(e, prev)

    def act(self, e):
        S, b = self.S, self.b
        S.wait(e, self.sem("ldc"), ("lv", 3))
        S.wait(e, self.sem("lm"), "dots")
        ins = e.activation(out=b.lt[:, 2:4], in_=b.lt[:, 0:2], func=AF.Exp)
        S.sig(ins, self.sem("lma"), "exp")
        for g in self.groups():
            self.act_group(e, g)
        S.wait(e, self.sem("fin"), "v")
        ins = e.activation(out=b.sst[:, 0:4 * NQ, :], in_=b.sst[:, 0:4 * NQ, :], func=AF.Sqrt)
        S.sig(ins, self.sem("lma"), "sqrt")

    def consume(self, e, g, c):
        S, b = self.S, self.b
        f = self.sem("f")
        h, i = g["h"], g["i"]
        ins = e.reciprocal(out=b.r[:, 4 * c:4 * c + 4], in_=b.tps[:, :, 64])
        S.fence(e, ins, f)
        if c == 0:
            for s in range(4):
                ins = e.tensor_scalar(out=b.t0[:, s, :], in0=b.tps[:, s, 0:64], scalar1=b.r[:, s:s + 1], scalar2=None,
                                      op0=ALU.mult)
            S.sig(ins, self.sem("tpc"), (g["gid"], 0))
        else:
            ins = e.tensor_scalar(out=b.r[:, 4:8], in0=b.r[:, 4:8], scalar1=b.lt[:, 4:5], scalar2=None, op0=ALU.mult)
            S.fence(e, ins, f)
            for s in range(4):
                ins = e.scalar_tensor_tensor(out=b.y[:, s, :], in0=b.tps[:, s, 0:64], scalar=b.r[:, 4 + s:5 + s],
                                             in1=b.t0[:, s, :], op0=ALU.mult, op1=ALU.add)
            S.sig(ins, self.sem("tpc"), (g["gid"], 1))
            S.wait(e, self.sem("tpc"), (g["gid"], 1))
            ins = e.tensor_tensor(out=b.sq[:, :, :], in0=b.y[:, :, :], in1=b.y[:, :, :], op=ALU.mult)
            S.fence(e, ins, f)
            ins = e.tensor_reduce(out=b.sst[:, 4 * i:4 * i + 4, h], in_=b.sq[:, :, :], axis=AX.X, op=ALU.add)
            ins = e.tensor_copy(out=b.ya[:, 4 * i:4 * i + 4, h * 64:(h + 1) * 64], in_=b.y[:, :, :])
            S.fence(e, ins, f)
            if i == NQ - 1:
                S.sig(e.tensor_copy(out=b.lt[:, 7:8], in_=b.lt[:, 4:5]), self.sem("hd"), h)

    def dve(self, e):
        S, b = self.S, self.b
        f = self.sem("f")
        for sl in (0, 1):
            ins = e.memset(b.V[:, sl, 128, :], 0.0)
            ins = e.memset(b.V[:, sl, 0:128, 64:65], 1.0)
        S.sig(ins, self.sem("ones"), 0)
        S.wait(e, self.sem("ldc"), ("lv", 3))
        ins = e.tensor_tensor(out=b.lv[:, 0, :], in0=b.lv[:, 0, :], in1=b.lv[:, 1, :], op=ALU.mult)
        ins = e.tensor_tensor(out=b.lv[:, 2, :], in0=b.lv[:, 2, :], in1=b.lv[:, 3, :], op=ALU.mult)
        S.fence(e, ins, f)
        ins = e.tensor_reduce(out=b.lt[:, 0:1], in_=b.lv[:, 0, :], axis=AX.X, op=ALU.add)
        ins = e.tensor_reduce(out=b.lt[:, 1:2], in_=b.lv[:, 2, :], axis=AX.X, op=ALU.add)
        S.sig(ins, self.sem("lm"), "dots")
        S.wait(e, self.sem("lma"), "exp")
        ins = e.tensor_tensor(out=b.lt[:, 5:6], in0=b.lt[:, 3:4], in1=b.lt[:, 2:3], op=ALU.subtract)
        S.fence(e, ins, f)
        ins = e.tensor_tensor(out=b.lt[:, 4:5], in0=b.lt[:, 5:6], in1=b.lamc[:, 0:1], op=ALU.subtract)
        ins = e.tensor_scalar(out=b.gsub[:, :], in0=b.gsub[:, :], scalar1=b.lamc[:, 1:2], scalar2=None, op0=ALU.mult)
        S.fence(e, ins, f)
        for g in self.groups():
            self.dve_group(e, g, self.consume)
        ins = e.tensor_scalar(out=b.sst[:, 0:4 * NQ, :], in0=b.sst[:, 0:4 * NQ, :], scalar1=1.0 / 64, scalar2=EPS,
                              op0=ALU.mult, op1=ALU.add)
        S.sig(ins, self.sem("fin"), "v")
        S.wait(e, self.sem("lma"), "sqrt")
        ins = e.reciprocal(out=b.sst[:, 0:4 * NQ, :], in_=b.sst[:, 0:4 * NQ, :])
        S.fence(e, ins, f)
        for lt in range(4 * NQ):
            for h in range(6):
                ins = e.scalar_tensor_tensor(out=b.ya[:, lt, h * 64:(h + 1) * 64], in0=b.ya[:, lt, h * 64:(h + 1) * 64],
                                             scalar=b.sst[:, lt, h:h + 1], in1=b.gsub[:, :], op0=ALU.mult, op1=ALU.mult)
        S.sig(ins, self.sem("fin"), "ya")

    def pool(self, e):
        S, b = self.S, self.b
        if self.E is not None:
            for k in self.AG_ORDER:
                ins = e.collective_compute("AllGather", ALU.bypass, replica_groups=[[0, 1, 2, 3], [4, 5, 6, 7]],
                                           ins=[self.E[k, :].rearrange("(p c) -> p c", c=2048)],
                                           outs=[self.G[k, :, :].rearrange("r (p c) -> (r p) c", c=2048)])
                S.sig(ins, "cc", k)
            S.wait(e, "cc", 15)
        S.wait(e, self.sem("fin"), "ya")
        ins = e.dma_start(out=self.ya_d[:, 0:4 * NQ, :], in_=b.ya[:, 0:4 * NQ, :])
        S.sig(ins, self.sem("st"), 0, dma=True)
        S.wait(e, self.sem("st"), 0)


class P2b(AttnPipe):
    NSR = 4

    def __init__(self, prog, tag, x, w_in, g_pre, G, ya_d, ln_g, ln_b, w_sp, b_sp, sinks, w_out, g_post,
                 dw, identb, identf, x1):
        super().__init__(prog, tag)
        self.x, self.w_in, self.g_pre, self.G, self.ya_d = x, w_in, g_pre, G, ya_d
        self.ln_g, self.ln_b, self.w_sp, self.b_sp, self.sinks, self.w_out, self.g_post = ln_g, ln_b, w_sp, b_sp, sinks, w_out, g_post
        self.dw_d, self.identb_d, self.identf_d, self.x1 = dw, identb, identf, x1
        self.nt = NormT(self, "nt")

    def reset(self):
        super().reset()
        self.m = {"pe": 0, "act": 0, "dve": 0}

    def alloc(self, nc, es):
        b = Bufs()
        b.wuv = sb(nc, es, "b_wuv", [128, 8, 512], BF16)
        b.wcq = sb(nc, es, "b_wcq", [128, 8, 3, 128], BF16)
        b.wout = sb(nc, es, "b_wout", [128, 8, D], BF16)
        b.dw = sb(nc, es, "b_dw", [128, 18, 512], F32)
        b.gbc = sb(nc, es, "b_gbc", [128, D], F32)
        b.gpost = sb(nc, es, "b_gpost", [128, D], F32)
        b.lng = sb(nc, es, "b_lng", [128, 256], F32)
        b.lnb = sb(nc, es, "b_lnb", [128, 256], F32)
        b.identb = sb(nc, es, "b_idb", [128, 128], BF16)
        b.identf = sb(nc, es, "b_idf", [128, 128], F32)
        b.wsp = sb(nc, es, "b_wsp", [128, 4, 128], F32)
        b.wsT = sb(nc, es, "b_wsT", [128, 4, 128], BF16)
        b.bsp = sb(nc, es, "b_bsp", [4, 128], F32)
        b.bsT = sb(nc, es, "b_bsT", [128, 4], F32)
        b.esink = sb(nc, es, "b_esink", [128, 6], F32)
        b.zero = sb(nc, es, "b_zero", [128, 1], F32)
        b.xt = sb(nc, es, "b_xt", [128, 2, 4, D], F32)
        b.hb = sb(nc, es, "b_hb", [128, 4, D], BF16)
        b.junk = sb(nc, es, "b_junk", [128, D], F32)
        b.ss = sb(nc, es, "b_ss", [128, 4], F32)
        b.rstd = sb(nc, es, "b_rstd", [128, 4], F32)
        b.hT = sb(nc, es, "b_hT", [128, 8, 512], BF16)
        b.cq = sb(nc, es, "b_cq", [128, 3, 512], BF16)
        b.ckx = sb(nc, es, "b_ckx", [128, 18 * 128], BF16)
        b.cv = sb(nc, es, "b_cv", [128, 19, 2, 65], BF16)
        b.pring = sb(nc, es, "b_pr", [128, NPR, 512], BF16)
        b.oT = sb(nc, es, "b_oT", [65, 2, 512], F32)
        b.r = sb(nc, es, "b_r", [128, 8], F32)
        b.y = sb(nc, es, "b_y", [128, 4, D], BF16)
        b.u = sb(nc, es, "b_u", [128, 4, 256], F32)
        b.vt = sb(nc, es, "b_vt", [128, 4, 256], F32)
        b.vnb = sb(nc, es, "b_vnb", [128, 4, 256], BF16)
        b.st = sb(nc, es, "b_st", [128, 16], F32)
        b.osb = sb(nc, es, "b_osb", [128, 4, D], F32)
        b.os2 = sb(nc, es, "b_os2", [128, 8], F32)
        b.pst = ps(nc, es, "b_pst", [128, 2, 1024], BF16)
        b.sring = ps(nc, es, "b_sr", [128, 4, 512], F32)
        b.oacc = ps(nc, es, "b_oacc", [128, 2, 512], F32)
        b.tps = b.pst[:, 0, :].bitcast(F32).rearrange("p (s d) -> p s d", d=128)
        return b

    def kts(self, i):
        return [kt for kt in range(18) if not (kt == 0 and i == 0) and not (kt == 17 and i == 7)]

    def groups(self, i):
        b = self.b
        out = []
        kts = self.kts(i)
        for g in range(3):
            units = []
            for n, kt in enumerate(kts):
                for kv in (0, 1):
                    head = kv * 3 + g
                    units.append({"acc": kv, "first": n == 0, "last": n == len(kts) - 1, "scale": SC_W, "kind": "G",
                                  "lhsT": b.ckx[kv * 64:(kv + 1) * 64, kt * 128:(kt + 1) * 128],
                                  "rhs": b.cq[kv * 64:(kv + 1) * 64, g, :], "bias": b.zero[:, 0:1],
                                  "dD": b.dw[:, kt, :], "dcoef": -float(SL_WIN[head] / SC_W),
                                  "vl": b.cv[:, :, :, :].rearrange("p t k d -> p (t k d)")[:, (kt * 2 + kv) * 65:(kt * 2 + kv) * 65 + 128]})
            gid = 3 * i + g
            out.append({"gid": gid, "par": 0, "units": units, "g": g, "i": i, "acc_prev": gid - 1 if gid >= 1 else None})
        return out

    MSEM = {"cq": "mA", "uv": "mD", "sp": "mD", "op": "mA", "ws": "mA"}

    def misc_begin(self, e, kind):
        S = self.S
        m = self.m["pe"]
        self.mk.append(kind)
        if m >= self.NSR:
            S.wait(e, self.sem(self.MSEM[self.mk[m - self.NSR]]), ("rel", m - self.NSR))
        self.m["pe"] = m + 1
        return self.b.sring[:, m % self.NSR, :], m

    def sp(self, e):
        S, b = self.S, self.b
        ld = self.sem("ldc")
        lst = [(b.dw[:, :, :], self.dw_d[:, :, :]), (b.identb[:, :], self.identb_d[:, :]), (b.identf[:, :], self.identf_d[:, :]),
               (b.gbc[:, :], self.g_pre.partition_broadcast(128)), (b.gpost[:, :], self.g_post.partition_broadcast(128)),
               (b.lng[:, :], self.ln_g.partition_broadcast(128)), (b.lnb[:, :], self.ln_b.partition_broadcast(128)),
               (b.esink[:, :], self.sinks.partition_broadcast(128)),
               (b.wsp[:, :, :], self.w_sp.rearrange("g t s -> t g s")), (b.bsp[:, :], self.b_sp[:, :])]
        for n, (dst, src) in enumerate(lst):
            ins = e.dma_start(out=dst, in_=src)
            S.sig(ins, ld, ("c", n), dma=True)
        for i in range(NQ):
            sl = i % 2
            if i >= 2:
                S.wait(e, self.sem("stx%d" % sl), i - 2)
            ins = e.dma_start(out=b.xt[:, sl, :, :], in_=self.x[i * 512:(i + 1) * 512, :].rearrange("(s p) d -> p s d", p=128))
            S.sig(ins, self.sem("ldx%d" % sl), i, dma=True)
            if i >= 1:
                S.wait(e, self.sem("tpc"), (3 * (i - 1) + 2, 1))
            lk = self.sem("ldk")
            for kt in self.kts(i):
                if 1 <= kt <= 16 and (kt - 1) % 4 != 0:
                    continue
                if kt == 0:
                    r, lt, n = j2idx(16 * i - 1) + (1,)
                elif kt == 17:
                    r, lt, n = j2idx(16 * (i + 1)) + (1,)
                else:
                    r, lt, n = (kt - 1) // 4, 4 * i, 4
                for kv in (0, 1):
                    ins = e.dma_start(out=b.ckx[kv * 64:(kv + 1) * 64, kt * 128:(kt + n) * 128],
                                      in_=self.G[12 + kv, r, :].rearrange("(a n) -> a n", n=NTOK)[:, lt * 128:(lt + n) * 128])
                    S.sig(ins, lk, ("k", i, kt, kv), dma=True)
                for kv in (0, 1):
                    ins = e.dma_start(out=b.cv[:, kt:kt + n, kv, 0:64],
                                      in_=self.G[14 + kv, r, :].rearrange("(p t d) -> p t d", p=128, t=32)[:, lt:lt + n, :])
                    S.sig(ins, lk, ("v", i, kt) if kv == 1 else ("v0", i, kt), dma=True)
            if i >= 1:
                S.wait(e, self.sem("ytr"), i - 1)
            ins = e.dma_start(out=b.y[:, :, 0:384], in_=self.ya_d[:, 4 * i:4 * i + 4, :])
            S.sig(ins, self.sem("ldy"), i, dma=True)

    def pool(self, e):
        S, b = self.S, self.b
        lw = self.sem("lw")
        ins = e.dma_start(out=b.wuv[:, :, :], in_=wview(self.w_in, 1152, 1664))
        S.sig(ins, lw, "uv", dma=True)
        for g in range(3):
            for kv in (0, 1):
                c0 = 1664 + (kv * 3 + g) * 64
                ins = e.dma_start(out=b.wcq[:, :, g, kv * 64:(kv + 1) * 64], in_=wview(self.w_in, c0, c0 + 64))
                S.sig(ins, lw, ("cq", g, kv), dma=True)
        ins = e.dma_start(out=b.wout[:, :, :], in_=wview(self.w_out, 0, D))
        S.sig(ins, lw, "out", dma=True)
        for i in range(NQ):
            sl = i % 2
            S.wait(e, self.sem("res"), i)
            ins = e.dma_start(out=self.x1[i * 512:(i + 1) * 512, :].rearrange("(s p) d -> p s d", p=128), in_=b.xt[:, sl, :, :])
            S.sig(ins, self.sem("stx%d" % sl), i, dma=True)
        for i in (NQ - 2, NQ - 1):
            if i >= 0:
                S.wait(e, self.sem("stx%d" % (i % 2)), i)

    def pe(self, e):
        S, b = self.S, self.b
        self.mk = []
        S.wait(e, self.sem("ldc"), ("c", 9))
        S.wait(e, self.sem("lw"), "out")
        for g4 in range(4):
            if g4 >= 1:
                S.wait(e, self.sem("mA"), ("ws", g4 - 1))
            ins = e.transpose(b.tps[:, 0, :], b.wsp[:, g4, :], b.identf[:, :])
            S.sig(ins, self.sem("wst"), g4)
        S.wait(e, self.sem("mA"), ("ws", 3))
        ins = e.transpose(b.tps[:, 1, 0:4], b.bsp[0:4, :], b.identf[0:4, 0:4])
        S.sig(ins, self.sem("wst"), 4)
        S.wait(e, self.sem("ones"), 0)
        for i in range(NQ):
            self.nt.pe(e, i, b.hb, b.pst, b.identb)
            self.nt.wait_hT(e, i)
            for g in range(3):
                bank, m = self.misc_begin(e, "cq")
                for kc in range(8):
                    ins = e.matmul(bank, lhsT=b.wcq[:, kc, g, :], rhs=b.hT[:, kc, :], start=(kc == 0), stop=(kc == 7))
                S.sig(ins, self.sem("mm"), m)
            for s in range(4):
                bank, m = self.misc_begin(e, "uv")
                for kc in range(8):
                    ins = e.matmul(bank, lhsT=b.hT[:, kc, s * 128:(s + 1) * 128], rhs=b.wuv[:, kc, :], start=(kc == 0), stop=(kc == 7))
                S.sig(ins, self.sem("mm"), m)
            for s in range(4):
                bank, m = self.misc_begin(e, "sp")
                S.wait(e, self.sem("vn"), (i, s))
                for g4 in range(4):
                    ins = e.matmul(bank[:, g4 * 64:(g4 + 1) * 64], lhsT=b.wsT[:, g4, :], rhs=b.vnb[:, s, g4 * 64:(g4 + 1) * 64],
                                   start=True, stop=True)
                S.sig(ins, self.sem("mm"), m)
            mlast = self.m["pe"] - 1
            for gi, g in enumerate(self.groups(i)):
                if gi == 0:
                    def w0(e, mlast=mlast, i=i):
                        for mm_ in range(max(0, mlast - self.NSR + 1), mlast + 1):
                            S.wait(e, self.sem(self.MSEM[self.mk[mm_]]), ("rel", mm_))
                        S.wait(e, self.sem("ldk"), ("v", i, self.kts(i)[-1] if self.kts(i)[-1] == 17 else 13))
                        S.wait(e, self.sem("mA"), ("rel", mlast - 8))
                    g["wait_pe"] = w0
                self.pe_group_pairs(e, g)
                self.pe_post(e, g)
            S.wait(e, self.sem("ldy"), i)
            S.wait(e, self.sem("gate"), (i, 3))
            S.wait(e, self.sem("tpc"), (3 * i + 2, 1))
            for kc in range(8):
                gq = (NQ + i) * 8 + kc
                S.wait(e, self.nt.n + "_ev", gq - 2 if kc >= 2 else i * 8 + 6 + kc)
                for s in range(4):
                    ins = e.transpose(b.pst[:, gq % 2, s * 128:(s + 1) * 128], b.y[:, s, kc * 128:(kc + 1) * 128], b.identb[:, :])
                S.sig(ins, self.nt.n + "_tr", gq)
            S.wait(e, self.nt.n + "_ev", (NQ + i) * 8 + 7)
            S.wait(e, self.sem("p"), self.gu["pe"] - 1)
            for s in range(4):
                for half in range(2):
                    bank, m = self.misc_begin(e, "op")
                    for kc in range(8):
                        ins = e.matmul(bank, lhsT=b.hT[:, kc, s * 128:(s + 1) * 128], rhs=b.wout[:, kc, half * 512:(half + 1) * 512],
                                       start=(kc == 0), stop=(kc == 7))
                    S.sig(ins, self.sem("mm"), m)
            S.sig(e.transpose(b.pst[:, 0, 0:128], b.identb[:, :], b.identb[:, :]), self.sem("ytr"), i)

    def act(self, e):
        S, b = self.S, self.b
        S.wait(e, self.sem("ldc"), ("c", 9))
        ins = e.activation(out=b.esink[:, :], in_=b.esink[:, :], func=AF.Exp)
        S.sig(ins, self.sem("es"), 0)
        for g4 in range(4):
            S.wait(e, self.sem("wst"), g4)
            ins = e.activation(out=b.wsT[:, g4, :], in_=b.tps[:, 0, :], func=AF.Copy)
            S.sig(ins, self.sem("mA"), ("ws", g4))
        S.wait(e, self.sem("wst"), 4)
        ins = e.activation(out=b.bsT[:, :], in_=b.tps[:, 1, 0:4], func=AF.Copy)
        S.sig(ins, self.sem("es"), 1)
        m = 0
        for i in range(NQ):
            self.nt.act(e, i, b.pst, b.hT,
                        wait_hT_free=lambda: (S.wait(e, self.sem("mm"), self.m_last_op) if i >= 1 else None), rstd=b.rstd)
            for g in range(3):
                S.wait(e, self.sem("mm"), m)
                if i >= 1 and g == 0:
                    S.wait(e, self.sem("pv"), self.gu["act"] - 1)
                ins = e.activation(out=b.cq[:, g, :], in_=b.sring[:, m % self.NSR, :], func=AF.Copy)
                S.sig(ins, self.sem("mA"), ("rel", m))
                m += 1
            m += 8
            S.wait(e, self.sem("lnv"), i)
            ins = e.activation(out=b.st[:, 8:12], in_=b.st[:, 8:12], func=AF.Sqrt)
            S.sig(ins, self.sem("lnq"), i)
            for g in self.groups(i):
                self.act_group(e, g)
            for kc in range(8):
                gq = (NQ + i) * 8 + kc
                S.wait(e, self.nt.n + "_tr", gq)
                ins = e.activation(out=b.hT[:, kc, :], in_=b.pst[:, gq % 2, 0:512], func=AF.Copy)
                S.sig(ins, self.nt.n + "_ev", gq)
            for s in range(4):
                for half in range(2):
                    S.wait(e, self.sem("mm"), m)
                    if s == 0 and half == 0 and i >= 1:
                        S.wait(e, self.sem("res"), i - 1)
                    ins = e.activation(out=b.osb[:, s, half * 512:(half + 1) * 512], in_=b.sring[:, m % self.NSR, :], func=AF.Copy)
                    S.sig(ins, self.sem("mA"), ("rel", m))
                    self.m_last_op = m
                    m += 1
            S.wait(e, self.sem("onv"), i)
            ins = e.activation(out=b.os2[:, 4:8], in_=b.os2[:, 4:8], func=AF.Sqrt)
            S.sig(ins, self.sem("onq"), i)

    def consume(self, e, g, c):
        S, b = self.S, self.b
        f = self.sem("f")
        head = c * 3 + g["g"]
        ins = e.tensor_scalar(out=b.r[:, 0:4], in0=b.tps[:, :, 64], scalar1=b.esink[:, head:head + 1], scalar2=None, op0=ALU.add)
        S.fence(e, ins, f)
        ins = e.reciprocal(out=b.r[:, 0:4], in_=b.r[:, 0:4])
        S.fence(e, ins, f)
        for s in range(4):
            ins = e.tensor_scalar(out=b.y[:, s, 640 + head * 64:640 + (head + 1) * 64], in0=b.tps[:, s, 0:64],
                                  scalar1=b.r[:, s:s + 1], scalar2=None, op0=ALU.mult)
        S.sig(ins, self.sem("tpc"), (g["gid"], c))

    def dve(self, e):
        S, b = self.S, self.b
        f = self.sem("f")
        ins = e.memset(b.cv[:, 18, :, :], 0.0)
        ins = e.memset(b.cv[:, 0:18, :, 64:65], 1.0)
        ins = e.memset(b.zero[:, :], 0.0)
        S.sig(ins, self.sem("ones"), 0)
        S.wait(e, self.sem("ldc"), ("c", 9))
        S.wait(e, self.sem("es"), 1)
        m = 0
        for i in range(NQ):
            sl = i % 2
            xt = b.xt[:, sl, :, :]
            self.nt.dve(e, i, xt, b.gbc, b.hb, b.junk, b.ss, b.rstd,
                        wait_x=lambda: S.wait(e, self.sem("ldx%d" % sl), i),
                        wait_hb_free=lambda: (self.nt.wait_tr_done(e, i - 1) if i >= 1 else None))
            m += 3
            for s in range(4):
                S.wait(e, self.sem("mm"), m)
                if i >= 1 and s == 0:
                    S.wait(e, self.sem("gate"), (i - 1, 3))
                ins = e.tensor_copy(out=b.u[:, s, :], in_=b.sring[:, m % self.NSR, 0:256])
                ins = e.tensor_copy(out=b.vt[:, s, :], in_=b.sring[:, m % self.NSR, 256:512])
                S.sig(ins, self.sem("mD"), ("rel", m))
                S.wait(e, self.sem("mD"), ("rel", m))
                ins = e.tensor_reduce(out=b.st[:, s:s + 1], in_=b.vt[:, s, :], axis=AX.X, op=ALU.add)
                ins = e.tensor_tensor(out=b.junk[:, 0:256], in0=b.vt[:, s, :], in1=b.vt[:, s, :], op=ALU.mult)
                S.fence(e, ins, f)
                ins = e.tensor_reduce(out=b.st[:, 4 + s:5 + s], in_=b.junk[:, 0:256], axis=AX.X, op=ALU.add)
                S.fence(e, ins, f)
                m += 1
            ins = e.tensor_scalar(out=b.st[:, 0:4], in0=b.st[:, 0:4], scalar1=1.0 / 256, scalar2=None, op0=ALU.mult)
            S.fence(e, ins, f)
            ins = e.tensor_tensor(out=b.st[:, 12:16], in0=b.st[:, 0:4], in1=b.st[:, 0:4], op=ALU.mult)
            S.fence(e, ins, f)
            ins = e.scalar_tensor_tensor(out=b.st[:, 8:12], in0=b.st[:, 4:8], scalar=1.0 / 256, in1=b.st[:, 12:16],
                                         op0=ALU.mult, op1=ALU.subtract)
            S.fence(e, ins, f)
            ins = e.tensor_scalar(out=b.st[:, 8:12], in0=b.st[:, 8:12], scalar1=EPS, scalar2=None, op0=ALU.add)
            S.sig(ins, self.sem("lnv"), i)
            S.wait(e, self.sem("lnq"), i)
            ins = e.reciprocal(out=b.st[:, 8:12], in_=b.st[:, 8:12])
            S.fence(e, ins, f)
            for s in range(4):
                ins = e.tensor_scalar(out=b.vt[:, s, :], in0=b.vt[:, s, :], scalar1=b.st[:, s:s + 1], scalar2=b.st[:, 8 + s:9 + s],
                                      op0=ALU.subtract, op1=ALU.mult)
                S.fence(e, ins, f)
                ins = e.tensor_tensor(out=b.vt[:, s, :], in0=b.vt[:, s, :], in1=b.lng[:, :], op=ALU.mult)
                S.fence(e, ins, f)
                ins = e.tensor_tensor(out=b.vnb[:, s, :], in0=b.vt[:, s, :], in1=b.lnb[:, :], op=ALU.add)
                S.sig(ins, self.sem("vn"), (i, s))
            for s in range(4):
                S.wait(e, self.sem("mm"), m)
                if i >= 1 and s == 0:
                    S.wait(e, self.sem("ytr"), i - 1)
                for g4 in range(4):
                    ins = e.scalar_tensor_tensor(out=b.y[:, s, 384 + g4 * 64:384 + (g4 + 1) * 64],
                                                 in0=b.sring[:, m % self.NSR, g4 * 64:(g4 + 1) * 64], scalar=b.bsT[:, g4:g4 + 1],
                                                 in1=b.u[:, s, g4 * 64:(g4 + 1) * 64], op0=ALU.add, op1=ALU.mult)
                S.sig(ins, self.sem("mD"), ("rel", m))
                S.wait(e, self.sem("mD"), ("rel", m))
                S.sig(e.tensor_copy(out=b.r[:, 4:5], in_=b.zero[:, 0:1]), self.sem("gate"), (i, s))
                m += 1
            for g in self.groups(i):
                self.dve_group(e, g, self.consume)
            for s in range(4):
                S.wait(e, self.sem("mA"), ("rel", m + 1))
                ins = e.tensor_tensor(out=b.junk[:, :], in0=b.osb[:, s, :], in1=b.osb[:, s, :], op=ALU.mult)
                S.fence(e, ins, f)
                ins = e.tensor_reduce(out=b.os2[:, s:s + 1], in_=b.junk[:, :], axis=AX.X, op=ALU.add)
                S.fence(e, ins, f)
                m += 2
            ins = e.tensor_scalar(out=b.os2[:, 4:8], in0=b.os2[:, 0:4], scalar1=1.0 / D, scalar2=EPS, op0=ALU.mult, op1=ALU.add)
            S.sig(ins, self.sem("onv"), i)
            S.wait(e, self.sem("onq"), i)
            ins = e.reciprocal(out=b.os2[:, 4:8], in_=b.os2[:, 4:8])
            S.fence(e, ins, f)
            for s in range(4):
                ins = e.tensor_tensor(out=b.osb[:, s, :], in0=b.osb[:, s, :], in1=b.gpost[:, :], op=ALU.mult)
                S.fence(e, ins, f)
                ins = e.scalar_tensor_tensor(out=xt[:, s, :], in0=b.osb[:, s, :], scalar=b.os2[:, 4 + s:5 + s], in1=xt[:, s, :],
                                             op0=ALU.mult, op1=ALU.add)
            S.sig(ins, self.sem("res"), i)
            S.wait(e, self.sem("res"), i)


NMR = 6


class TokPhase(Phase):
    def reset(self):
        self.m = 0
        self.mk = []

    def ring(self, e, rel_sem):
        S = self.S
        m = self.m
        self.mk.append(rel_sem)
        if m >= NMR:
            S.wait(e, self.sem(self.mk[m - NMR]), ("rel", m - NMR))
        self.m = m + 1
        return self.b.ring[:, m % NMR, :], m

    def dve_postnorm(self, e, i, s, xt):
        S, b = self.S, self.b
        f = self.sem("f")
        o = b.osb[:, s % 2, :]
        ins = e.tensor_tensor(out=b.junk[:, :], in0=o, in1=o, op=ALU.mult)
        S.fence(e, ins, f)
        ins = e.tensor_reduce(out=b.os2[:, 0:1], in_=b.junk[:, :], axis=AX.X, op=ALU.add)
        S.fence(e, ins, f)
        ins = e.tensor_scalar(out=b.os2[:, 1:2], in0=b.os2[:, 0:1], scalar1=1.0 / D, scalar2=EPS, op0=ALU.mult, op1=ALU.add)
        S.sig(ins, self.sem("onv"), (i, s))
        S.wait(e, self.sem("onq"), (i, s))
        ins = e.reciprocal(out=b.os2[:, 1:2], in_=b.os2[:, 1:2])
        ins2 = e.tensor_tensor(out=o, in0=o, in1=b.gpost[:, :], op=ALU.mult)
        S.fence(e, ins2, f)
        ins = e.scalar_tensor_tensor(out=xt[:, s, :], in0=o, scalar=b.os2[:, 1:2], in1=xt[:, s, :], op0=ALU.mult, op1=ALU.add)
        S.sig(ins, self.sem("res"), (i, s))
        S.wait(e, self.sem("res"), (i, s))

    def act_postnorm(self, e, i, s):
        S, b = self.S, self.b
        S.wait(e, self.sem("onv"), (i, s))
        ins = e.activation(out=b.os2[:, 1:2], in_=b.os2[:, 1:2], func=AF.Sqrt)
        S.sig(ins, self.sem("onq"), (i, s))

    def pool_store(self, e, xo):
        S, b = self.S, self.b
        for i in range(NQ):
            S.wait(e, self.sem("res"), (i, 3))
            ins = e.dma_start(out=xo[i * 512:(i + 1) * 512, :].rearrange("(s p) d -> p s d", p=128), in_=b.xt[:, :, :])
            S.sig(ins, self.sem("stx"), i, dma=True)
        S.wait(e, self.sem("stx"), NQ - 1)

    def sp_loadx(self, e, i, xin):
        S, b = self.S, self.b
        if i >= 1:
            S.wait(e, self.sem("stx"), i - 1)
        ins = e.dma_start(out=b.xt[:, :, :], in_=xin[i * 512:(i + 1) * 512, :].rearrange("(s p) d -> p s d", p=128))
        S.sig(ins, self.sem("ldx"), i, dma=True)


class P3a(TokPhase):
    NBLK = DFF // 128

    def __init__(self, prog, tag, x1, w_fi, w_fo, g_pre, g_post, identb, x2):
        super().__init__(prog, tag)
        self.x1, self.w_fi, self.w_fo, self.g_pre, self.g_post, self.identb_d, self.x2 = x1, w_fi, w_fo, g_pre, g_post, identb, x2
        self.nt = NormT(self, "nt")

    def alloc(self, nc, es):
        b = Bufs()
        b.wfi = sb(nc, es, "f_wfi", [128, 8, 2 * DFF], BF16)
        b.wfo = sb(nc, es, "f_wfo", [128, self.NBLK, D], BF16)
        b.gbc = sb(nc, es, "f_gbc", [128, D], F32)
        b.gpost = sb(nc, es, "f_gpost", [128, D], F32)
        b.identb = sb(nc, es, "f_idb", [128, 128], BF16)
        b.xt = sb(nc, es, "f_xt", [128, 4, D], F32)
        b.hb = sb(nc, es, "f_hb", [128, 4, D], BF16)
        b.ss = sb(nc, es, "f_ss", [128, 4], F32)
        b.rstd = sb(nc, es, "f_rstd", [128, 4], F32)
        b.hT = sb(nc, es, "f_hT", [128, 8, 512], BF16)
        b.actT = sb(nc, es, "f_actT", [128, self.NBLK, 512], BF16)
        b.sg = sb(nc, es, "f_sg", [128, 2, 512], F32)
        b.junk = b.sg[:, :, :].rearrange("p a b -> p (a b)")
        b.osb = sb(nc, es, "f_osb", [128, 2, D], F32)
        b.os2 = sb(nc, es, "f_os2", [128, 4], F32)
        b.pst = ps(nc, es, "f_pst", [128, 2, 1024], BF16)
        b.ring = ps(nc, es, "f_ring", [128, NMR, 512], F32)
        return b

    def sp(self, e):
        S, b = self.S, self.b
        ld = self.sem("ldc")
        for n, (dst, src) in enumerate([(b.identb[:, :], self.identb_d[:, :]), (b.gbc[:, :], self.g_pre.partition_broadcast(128)),
                                        (b.gpost[:, :], self.g_post.partition_broadcast(128))]):
            ins = e.dma_start(out=dst, in_=src)
            S.sig(ins, ld, ("c", n), dma=True)
        for i in range(NQ):
            self.sp_loadx(e, i, self.x1)

    def pool(self, e):
        S, b = self.S, self.b
        lw = self.sem("lw")
        nch = 8
        w = 2 * DFF // nch
        for k in range(nch):
            ins = e.dma_start(out=b.wfi[:, :, k * w:(k + 1) * w], in_=wview(self.w_fi, k * w, (k + 1) * w))
            S.sig(ins, lw, ("fi", k), dma=True)
        for k in range(2):
            ins = e.dma_start(out=b.wfo[:, :, k * 512:(k + 1) * 512], in_=wview(self.w_fo, k * 512, (k + 1) * 512))
            S.sig(ins, lw, ("fo", k), dma=True)
        self.pool_store(e, self.x2)

    def pe(self, e):
        S, b = self.S, self.b
        S.wait(e, self.sem("ldc"), ("c", 2))
        S.wait(e, self.sem("lw"), ("fo", 1))
        for i in range(NQ):
            self.nt.pe(e, i, b.hb, b.pst, b.identb)
            self.nt.wait_hT(e, i)
            for blk in range(self.NBLK):
                for part, rel in ((0, "mA"), (1, "mD")):
                    bank, m = self.ring(e, rel)
                    c0 = part * DFF + blk * 128
                    for kc in range(8):
                        ins = e.matmul(bank, lhsT=b.wfi[:, kc, c0:c0 + 128], rhs=b.hT[:, kc, :], start=(kc == 0), stop=(kc == 7))
                    S.sig(ins, self.sem("mm"), m)
            S.wait(e, self.sem("mD"), ("rel", self.m - 1))
            for s in range(4):
                for half in range(2):
                    bank, m = self.ring(e, "mA")
                    for blk in range(self.NBLK):
                        ins = e.matmul(bank, lhsT=b.actT[:, blk, s * 128:(s + 1) * 128], rhs=b.wfo[:, blk, half * 512:(half + 1) * 512],
                                       start=(blk == 0), stop=(blk == self.NBLK - 1))
                    S.sig(ins, self.sem("mm"), m)

    def act(self, e):
        S, b = self.S, self.b
        m = 0
        nb = 0
        for i in range(NQ):
            self.nt.act(e, i, b.pst, b.hT,
                        wait_hT_free=lambda: (S.wait(e, self.sem("mm"), m - 9) if i >= 1 else None), rstd=b.rstd)
            for blk in range(self.NBLK):
                S.wait(e, self.sem("mm"), m)
                if nb >= 2:
                    S.wait(e, self.sem("mD"), ("rel", self.sgrel[nb - 2]))
                ins = e.activation(out=b.sg[:, nb % 2, :], in_=b.ring[:, m % NMR, :], func=AF.Silu)
                S.sig(ins, self.sem("mA"), ("rel", m))
                self.sgrel.append(m + 1)
                nb += 1
                m += 2
            for s in range(4):
                for half in range(2):
                    S.wait(e, self.sem("mm"), m)
                    if half == 0 and (i, s) >= (0, 2):
                        ps_ = (i, s - 2) if s >= 2 else (i - 1, s + 2)
                        S.wait(e, self.sem("res"), ps_)
                    ins = e.activation(out=b.osb[:, s % 2, half * 512:(half + 1) * 512], in_=b.ring[:, m % NMR, :], func=AF.Copy)
                    S.sig(ins, self.sem("mA"), ("rel", m))
                    m += 1
                self.act_postnorm(e, i, s)

    def reset(self):
        super().reset()
        self.sgrel = []

    def dve(self, e):
        S, b = self.S, self.b
        S.wait(e, self.sem("ldc"), ("c", 2))
        m = 0
        nb = 0
        for i in range(NQ):
            self.nt.dve(e, i, b.xt, b.gbc, b.hb, b.junk, b.ss, b.rstd,
                        wait_x=lambda: S.wait(e, self.sem("ldx"), i),
                        wait_hb_free=lambda: (self.nt.wait_tr_done(e, i - 1) if i >= 1 else None))
            for blk in range(self.NBLK):
                S.wait(e, self.sem("mm"), m + 1)
                S.wait(e, self.sem("mA"), ("rel", m))
                if i >= 1 and blk == 0:
                    S.wait(e, self.sem("mm"), m - 1)
                ins = e.tensor_tensor(out=b.actT[:, blk, :], in0=b.sg[:, nb % 2, :], in1=b.ring[:, (m + 1) % NMR, :], op=ALU.mult)
                S.sig(ins, self.sem("mD"), ("rel", m + 1))
                nb += 1
                m += 2
            for s in range(4):
                S.wait(e, self.sem("mA"), ("rel", m + 1))
                self.dve_postnorm(e, i, s, b.xt)
                m += 2


class P3b(TokPhase):
    def __init__(self, prog, tag, x2, p, w_up, w_gate, g_gate, g_post, identb, x3):
        super().__init__(prog, tag)
        self.x2, self.p, self.w_up, self.w_gate, self.g_gate, self.g_post, self.identb_d, self.x3 = x2, p, w_up, w_gate, g_gate, g_post, identb, x3
        self.nt = NormT(self, "nt")

    def alloc(self, nc, es):
        b = Bufs()
        b.wg = sb(nc, es, "e_wg", [128, 8, D], BF16)
        b.wu = sb(nc, es, "e_wu", [128, 2, D], BF16)
        b.gbc = sb(nc, es, "e_gbc", [128, D], F32)
        b.gpost = sb(nc, es, "e_gpost", [128, D], F32)
        b.identb = sb(nc, es, "e_idb", [128, 128], BF16)
        b.xt = sb(nc, es, "e_xt", [128, 4, D], F32)
        b.pt = sb(nc, es, "e_pt", [128, 4, 256], F32)
        b.pb = sb(nc, es, "e_pb", [128, 4, 256], BF16)
        b.pT = sb(nc, es, "e_pT", [128, 2, 512], BF16)
        b.hb = sb(nc, es, "e_hb", [128, 4, D], BF16)
        b.junk = sb(nc, es, "e_junk", [128, D], F32)
        b.ss = sb(nc, es, "e_ss", [128, 4], F32)
        b.rstd = sb(nc, es, "e_rstd", [128, 4], F32)
        b.hT = sb(nc, es, "e_hT", [128, 8, 512], BF16)
        b.sgm = sb(nc, es, "e_sgm", [128, 2, D], F32)
        b.osb = sb(nc, es, "e_osb", [128, 2, D], F32)
        b.os2 = sb(nc, es, "e_os2", [128, 4], F32)
        b.pst = ps(nc, es, "e_pst", [128, 2, 1024], BF16)
        b.ring = ps(nc, es, "e_ring", [128, NMR, 512], F32)
        return b

    def sp(self, e):
        S, b = self.S, self.b
        ld = self.sem("ldc")
        for n, (dst, src) in enumerate([(b.identb[:, :], self.identb_d[:, :]), (b.gbc[:, :], self.g_gate.partition_broadcast(128)),
                                        (b.gpost[:, :], self.g_post.partition_broadcast(128))]):
            ins = e.dma_start(out=dst, in_=src)
            S.sig(ins, ld, ("c", n), dma=True)
        for i in range(NQ):
            self.sp_loadx(e, i, self.x2)
            if i >= 1:
                S.wait(e, self.sem("pb"), i - 1)
            ins = e.dma_start(out=b.pt[:, :, :], in_=self.p[i * 512:(i + 1) * 512, :].rearrange("(s p) d -> p s d", p=128))
            S.sig(ins, self.sem("ldp"), i, dma=True)

    def pool(self, e):
        S, b = self.S, self.b
        lw = self.sem("lw")
        for k in range(2):
            ins = e.dma_start(out=b.wg[:, :, k * 512:(k + 1) * 512], in_=wview(self.w_gate, k * 512, (k + 1) * 512))
            S.sig(ins, lw, ("g", k), dma=True)
        ins = e.dma_start(out=b.wu[:, :, :], in_=wview(self.w_up, 0, D))
        S.sig(ins, lw, "u", dma=True)
        self.pool_store(e, self.x3)

    def pe(self, e):
        S, b = self.S, self.b
        S.wait(e, self.sem("ldc"), ("c", 2))
        S.wait(e, self.sem("lw"), "u")
        for i in range(NQ):
            self.nt.pe(e, i, b.hb, b.pst, b.identb)
            S.wait(e, self.sem("pb"), i)
            for kc in range(2):
                gq = (NQ + i) * 8 + kc
                S.wait(e, self.nt.n + "_ev", i * 8 + 6 + kc)
                for s in range(4):
                    ins = e.transpose(b.pst[:, gq % 2, s * 128:(s + 1) * 128], b.pb[:, s, kc * 128:(kc + 1) * 128], b.identb[:, :])
                S.sig(ins, self.nt.n + "_tr", gq)
            S.wait(e, self.nt.n + "_ev", (NQ + i) * 8 + 1)
            for s in range(4):
                for half in range(2):
                    bank, m = self.ring(e, "mD")
                    for kc in range(2):
                        ins = e.matmul(bank, lhsT=b.pT[:, kc, s * 128:(s + 1) * 128], rhs=b.wu[:, kc, half * 512:(half + 1) * 512],
                                       start=(kc == 0), stop=(kc == 1))
                    S.sig(ins, self.sem("mm"), m)
                for half in range(2):
                    bank, m = self.ring(e, "mA")
                    for kc in range(8):
                        ins = e.matmul(bank, lhsT=b.hT[:, kc, s * 128:(s + 1) * 128], rhs=b.wg[:, kc, half * 512:(half + 1) * 512],
                                       start=(kc == 0), stop=(kc == 7))
                    S.sig(ins, self.sem("mm"), m)

    def act(self, e):
        S, b = self.S, self.b
        m = 0
        for i in range(NQ):
            self.nt.act(e, i, b.pst, b.hT,
                        wait_hT_free=lambda: (S.wait(e, self.sem("mm"), m - 1) if i >= 1 else None), rstd=b.rstd)
            for kc in range(2):
                gq = (NQ + i) * 8 + kc
                S.wait(e, self.nt.n + "_tr", gq)
                if i >= 1 and kc == 0:
                    S.wait(e, self.sem("mm"), m - 3)
                ins = e.activation(out=b.pT[:, kc, :], in_=b.pst[:, gq % 2, 0:512], func=AF.Copy)
                S.sig(ins, self.nt.n + "_ev", gq)
            for s in range(4):
                for half in range(2):
                    mz = m + 2 + half
                    S.wait(e, self.sem("mm"), mz)
                    if half == 0 and (i, s) >= (0, 2):
                        ps_ = (i, s - 2) if s >= 2 else (i - 1, s + 2)
                        S.wait(e, self.sem("eg"), ps_)
                    ins = e.activation(out=b.sgm[:, s % 2, half * 512:(half + 1) * 512], in_=b.ring[:, mz % NMR, :], func=AF.Sigmoid)
                    S.sig(ins, self.sem("mA"), ("rel", mz))
                m += 4
                self.act_postnorm(e, i, s)

    def dve(self, e):
        S, b = self.S, self.b
        S.wait(e, self.sem("ldc"), ("c", 2))
        m = 0
        for i in range(NQ):
            self.nt.dve(e, i, b.xt, b.gbc, b.hb, b.junk, b.ss, b.rstd,
                        wait_x=lambda: S.wait(e, self.sem("ldx"), i),
                        wait_hb_free=lambda: (self.nt.wait_tr_done(e, i - 1) if i >= 1 else None))
            S.wait(e, self.sem("ldp"), i)
            if i >= 1:
                S.wait(e, self.nt.n + "_tr", (NQ + i - 1) * 8 + 1)
            ins = e.tensor_copy(out=b.pb[:, :, :], in_=b.pt[:, :, :])
            S.sig(ins, self.sem("pb"), i)
            for s in range(4):
                for half in range(2):
                    S.wait(e, self.sem("mm"), m + half)
                    S.wait(e, self.sem("mA"), ("rel", m + 2 + half))
                    ins = e.tensor_tensor(out=b.osb[:, s % 2, half * 512:(half + 1) * 512], in0=b.sgm[:, s % 2, half * 512:(half + 1) * 512],
                                          in1=b.ring[:, (m + half) % NMR, :], op=ALU.mult)
                    S.sig(ins, self.sem("mD"), ("rel", m + half))
                S.wait(e, self.sem("mD"), ("rel", m + 1))
                S.sig(e.memset(b.os2[:, 2:3], 0.0), self.sem("eg"), (i, s))
                m += 4
                self.dve_postnorm(e, i, s, b.xt)


class AG(Phase):
    def __init__(self, prog, tag, E, G):
        super().__init__(prog, tag)
        self.E, self.G = E, G

    def pool(self, e):
        S = self.S
        for k in range(NCH):
            ins = e.collective_compute("AllGather", ALU.bypass, replica_groups=[[0, 1, 2, 3], [4, 5, 6, 7]],
                                       ins=[self.E[k, :].rearrange("(p c) -> p c", c=2048)],
                                       outs=[self.G[k, :, :].rearrange("r (p c) -> (r p) c", c=2048)])
            S.sig(ins, "cc", k)
        S.wait(e, "cc", NCH - 1)


def _local_tokens(c):
    ii = np.arange(8)[:, None]
    t = np.arange(512)[None, :]
    return (2048 * ii + 512 * c + t).reshape(-1)


WKEYS = [("g_pre_mix", [D]), ("w_in", [D, 2304]), ("g_diff_sub", [64]), ("gmlp_ln_g", [256]), ("gmlp_ln_b", [256]),
         ("w_spatial", [4, 128, 128]), ("b_spatial", [4, 128]), ("swa_sinks", [6]), ("w_out", [D, D]), ("g_post_mix", [D]),
         ("g_pre_ffn", [D]), ("w_ffn_in", [D, 2 * DFF]), ("w_ffn_out", [DFF, D]), ("g_post_ffn", [D]),
         ("w_ple_up", [256, D]), ("w_ple_gate", [D, D]), ("g_ple_gate", [D]), ("g_ple_post", [D])]


def build_fused(nl=L):
    P = Prog()
    ncol = cidx_map()[1]
    x = P.din("x", [NTOK, D], F32)
    pp = P.din("p", [L, NTOK, 256], F32)
    ctab = P.din("ctab", [128, ncol], F32)
    dn = P.din("dn", [128, 16, 512], F32)
    dw = P.din("dw", [128, 18, 512], F32)
    kaug = P.din("kaug", [6, 4, SEQ], BF16)
    qaug = P.din("qaug", [6, 2, 4, NTOK], BF16)
    identb = P.din("identb", [128, 128], BF16)
    identf = P.din("identf", [128, 128], F32)
    lamv = P.din("lamv", [L, 4, 32], F32)
    lamc = P.din("lamc", [L, 128, 2], F32)
    W = {k: P.din(k, [L] + shp, F32) for k, shp in WKEYS}
    E = P.dint("E", [NCH, CHE], BF16)
    G = P.dint("G", [NCH, 4, CHE], BF16)
    Qs = P.dint("Qs", [384, NTOK], BF16)
    ya_d = P.dint("ya_d", [128, 32, 384], BF16)
    x1 = P.dint("x1", [NTOK, D], F32)
    x2 = P.dint("x2", [NTOK, D], F32)
    x3 = P.dint("x3", [NTOK, D], F32)
    xo = P.dout("xo", [NTOK, D], F32)
    for l in range(nl):
        xin = x if l == 0 else x3
        xout = xo if l == nl - 1 else x3
        P.phases.append(P1(P, "p1", xin, W["w_in"][l], W["g_pre_mix"][l], identb, Qs, E))
        P.phases.append(P2a(P, "a", G, Qs, ctab, dn, kaug, qaug, identf, lamv[l], lamc[l], W["g_diff_sub"][l], ya_d, E=E))
        P.phases.append(P2b(P, "b", xin, W["w_in"][l], W["g_pre_mix"][l], G, ya_d, W["gmlp_ln_g"][l], W["gmlp_ln_b"][l],
                            W["w_spatial"][l], W["b_spatial"][l], W["swa_sinks"][l], W["w_out"][l], W["g_post_mix"][l],
                            dw, identb, identf, x1))
        P.phases.append(P3a(P, "f", x1, W["w_ffn_in"][l], W["w_ffn_out"][l], W["g_pre_ffn"][l], W["g_post_ffn"][l], identb, x2))
        P.phases.append(P3b(P, "e", x2, pp[l], W["w_ple_up"][l], W["w_ple_gate"][l], W["g_ple_gate"][l], W["g_ple_post"][l],
                            identb, xout))
    return P.build()


_PROGS = {}


def kernel(**inputs):
    if "F" not in _PROGS:
        _PROGS["F"] = build_fused()
    consts = host_consts()
    tabs = [host_tables(c) for c in range(4)]
    x = np.asarray(inputs["x"], np.float32)
    p = np.asarray(inputs["p"], np.float32)
    lamv = np.stack([np.stack([np.asarray(inputs[k][l], np.float32) for k in ("lam_q1", "lam_k1", "lam_q2", "lam_k2")])
                     for l in range(L)])
    lamc = np.stack([np.tile(np.array([[lam_init(l), 1.0 - lam_init(l)]], np.float32), (128, 1)) for l in range(L)])
    shared = {k: np.ascontiguousarray(np.asarray(inputs[k], np.float32)) for k, _ in WKEYS}
    shared.update(kaug=consts["kaug"], qaug=consts["qaug"], identb=consts["identb"], identf=consts["identf"],
                  lamv=lamv, lamc=lamc)
    cores = list(range(8))
    in_maps = []
    for core in cores:
        bb, c = core // 4, core % 4
        tok = _local_tokens(c)
        m = dict(shared)
        m["x"] = np.ascontiguousarray(x[bb][tok])
        m["p"] = np.ascontiguousarray(p[:, bb][:, tok])
        m["ctab"], m["dn"], m["dw"] = tabs[c]["ctab"], tabs[c]["dn"], tabs[c]["dw"]
        in_maps.append(m)
    res = run_bass_kernel_spmd(_PROGS["F"], in_maps, core_ids=cores)
    out = np.empty((NB, SEQ, D), np.float32)
    for core in cores:
        out[core // 4][_local_tokens(core % 4)] = np.asarray(res.results[core]["xo"], np.float32)
    return out
```

```python
import numpy as np
import ml_dtypes
from contextlib import ExitStack
import concourse.bass as bass
import concourse.mybir as mybir
from concourse.bass_utils import run_bass_kernel_spmd

F32 = mybir.dt.float32
BF16 = mybir.dt.bfloat16
AF = mybir.ActivationFunctionType
ALU = mybir.AluOpType
AX = mybir.AxisListType
NPBF = ml_dtypes.bfloat16

D = 1024
SEQ = 16384
NB = 2
L = 4
NTOK = 4096
NQ = 8
DFF = 2816
EPS = 1e-6
SC_A = 32 ** -0.5
SC_W = 64 ** -0.5
_k = np.arange(1, 13, dtype=np.float64)
_sl = np.exp2(-8.0 * _k / 12.0)
SL_DIFF = _sl[6:]
SL_WIN = _sl[:6]
CUT = 26.0
BIG = 30000.0
LA = 2
NSR = 3
NPR = 4


def lam_init(l):
    return 0.8 - 0.6 * float(np.exp(-0.3 * l))


class Dummy:
    def __getattr__(self, n):
        return self

    def __call__(self, *a, **k):
        return self

    def __getitem__(self, k):
        return self

    def __enter__(self):
        return self

    def __exit__(self, *a):
        return False


class Sync:
    def __init__(self):
        self.dry = True
        self.total = {}
        self.count = {}
        self.seq = {}
        self.rpos = {}
        self.handles = {}
        self.scope = None

    def _k(self, key, glob):
        return key if glob else (self.scope, key)

    def sig(self, ins, sem, key, dma=False, glob=False):
        inc = 16 if dma else 1
        k = (sem, self._k(key, glob))
        if self.dry:
            assert k not in self.count, k
            t = self.total.get(sem, 0) + inc
            self.total[sem] = t
            self.count[k] = t
            self.seq.setdefault(sem, []).append((k, t))
        else:
            i = self.rpos.get(sem, 0)
            kk, t = self.seq[sem][i]
            assert kk == k, (kk, k)
            self.rpos[sem] = i + 1
            ins.then_inc(self.handles[sem], inc)

    def wait(self, e, sem, key, glob=False):
        if not self.dry:
            e.wait_ge(self.handles[sem], self.count[(sem, self._k(key, glob))])

    def fence(self, e, ins, sem):
        if self.dry:
            t = self.total.get(sem, 0) + 1
            self.total[sem] = t
            self.seq.setdefault(sem, []).append((None, t))
        else:
            i = self.rpos.get(sem, 0)
            kk, t = self.seq[sem][i]
            assert kk is None, kk
            self.rpos[sem] = i + 1
            ins.then_inc(self.handles[sem], 1)
            e.wait_ge(self.handles[sem], t)


def band_tiles(h, i):
    out = []
    for j in range(128):
        if j < 16 * i:
            dmin = 2048 * i - (128 * j + 127)
        elif j >= 16 * (i + 1):
            dmin = 128 * j - (2048 * i + 2047)
        else:
            dmin = 0
        if SL_DIFF[h] * dmin <= CUT:
            out.append(j)
    return out


def j2idx(j):
    T = j // 4
    sub = j % 4
    r = T % 4
    ii = T // 4
    return r, 4 * ii + sub


_CIDX = None


def cidx_map():
    global _CIDX
    if _CIDX is None:
        m = {}
        n = 1
        for h in range(6):
            for i in range(NQ):
                for j in band_tiles(h, i):
                    if j < 16 * i or j >= 16 * (i + 1):
                        m[(h, i, j)] = n
                        n += 1
        _CIDX = (m, n)
    return _CIDX


def split_bf(v):
    hi = np.float32(np.asarray(v, np.float32).astype(NPBF).astype(np.float32))
    lo = np.float32(np.asarray(np.float32(v) - hi, np.float32).astype(NPBF).astype(np.float32))
    return hi, lo


def host_tables(c):
    m, n = cidx_map()
    ct = np.zeros((n,), np.float32)
    for (h, i, j), col in m.items():
        qc = 2048 * i + 512 * c + 256
        ct[col] = -SL_DIFF[h] * abs(128 * j - qc)
    ctab = np.ascontiguousarray(np.broadcast_to(ct[None, :], (128, n))).astype(np.float32)
    ki = np.arange(128)[:, None, None]
    qi = np.arange(512)[None, None, :]
    tg = np.arange(16)[None, :, None]
    dn = np.abs(128 * tg + ki - (512 * c + qi)).astype(np.float32)
    koff = np.array([-128] + [128 * t for t in range(16)] + [2048])[None, :, None]
    dw = np.abs(koff + ki - (512 * c + qi)).astype(np.float32)
    dw = np.where(dw <= 128.0, dw, BIG).astype(np.float32)
    return {"ctab": ctab, "dn": np.ascontiguousarray(dn), "dw": np.ascontiguousarray(dw)}


def host_consts():
    kaug = np.zeros((6, 4, 128), np.float32)
    qaug = np.zeros((6, 2, 4, NTOK), np.float32)
    qip = (np.arange(NTOK) % 512 - 256).astype(np.float32)
    for h in range(6):
        sp = SL_DIFF[h] / SC_A
        hi, lo = split_bf(sp)
        kaug[h, 0, :] = -hi
        kaug[h, 1, :] = -lo
        kaug[h, 2, :] = np.arange(128)
        kaug[h, 3, :] = np.arange(128)
        qaug[h, 0, 0, :] = qip
        qaug[h, 0, 1, :] = qip
        qaug[h, 0, 2, :] = hi
        qaug[h, 0, 3, :] = lo
        qaug[h, 1] = -qaug[h, 0]
    kaug = np.tile(kaug, (1, 1, 128))
    return {
        "kaug": kaug.astype(NPBF),
        "qaug": qaug.astype(NPBF),
        "identb": np.eye(128, dtype=np.float32).astype(NPBF),
        "identf": np.eye(128, dtype=np.float32),
    }


class Prog:
    def __init__(self):
        self.nc = bass.Bass("TRN2", target_bir_lowering=False)
        self.S = Sync()
        self.dram = {}
        self.phases = []

    def din(self, name, shape, dt):
        t = self.nc.dram_tensor(name, list(shape), dt, kind="ExternalInput").ap()
        self.dram[name] = t
        return t

    def dout(self, name, shape, dt):
        t = self.nc.dram_tensor(name, list(shape), dt, kind="ExternalOutput").ap()
        self.dram[name] = t
        return t

    def dint(self, name, shape, dt):
        t = self.nc.dram_tensor(name, list(shape), dt, kind="Internal").ap()
        self.dram[name] = t
        return t

    def build(self):
        S = self.S
        nph = len(self.phases)
        engs = ["sp", "act", "pe", "dve", "pool"]
        S.dry = True
        for pi, ph in enumerate(self.phases):
            S.scope = pi
            ph.reset()
            ph.bind(Dummy())
            for en in engs:
                e = Dummy()
                self._barrier_pre(en, e, pi)
                getattr(ph, en)(e)
                self._barrier_post(en, e, pi)
        S.dry = False
        nc = self.nc
        with ExitStack() as es:
            for sem in S.seq:
                S.handles[sem] = es.enter_context(nc.semaphore(sem))
            self.bar_tile = es.enter_context(nc.sbuf_tensor("bar_tile", [128, 8], F32))
            for pi, ph in enumerate(self.phases):
                S.scope = pi
                ph.reset()
                with ExitStack() as pes:
                    bufs = ph.alloc(nc, pes)
                    ph.bind(bufs)
                    with nc.Block() as block:
                        def mk(en, ph=ph, pi=pi):
                            def f(e):
                                S.scope = pi
                                self._barrier_pre(en, e, pi)
                                getattr(ph, en)(e)
                                self._barrier_post(en, e, pi)
                            return f
                        block.sync(mk("sp"))
                        block.scalar(mk("act"))
                        block.tensor(mk("pe"))
                        block.vector(mk("dve"))
                        block.gpsimd(mk("pool"))
        for sem in S.seq:
            assert S.rpos.get(sem, 0) == len(S.seq[sem]), sem
        return nc

    def _barrier_pre(self, en, e, pi):
        if pi == 0 or en == "pool":
            return
        self.S.wait(e, "bar", ("end", pi - 1), glob=True)

    def _barrier_post(self, en, e, pi):
        if en != "pool":
            return
        S = self.S
        if S.dry:
            S.sig(None, "bar", ("end", pi), glob=True)
        else:
            ins = e.memset(self.bar_tile[:, :], 0.0)
            S.sig(ins, "bar", ("end", pi), glob=True)


class Phase:
    def __init__(self, prog, tag):
        self.P = prog
        self.S = prog.S
        self.tag = tag

    def reset(self):
        pass

    def bind(self, bufs):
        self.b = bufs

    def alloc(self, nc, es):
        return Dummy()

    def sp(self, e):
        pass

    def act(self, e):
        pass

    def pe(self, e):
        pass

    def dve(self, e):
        pass

    def pool(self, e):
        pass

    def sem(self, n):
        return n


class Bufs:
    pass


_UID = [0]


def _uname(name):
    _UID[0] += 1
    return "%s_%d" % (name, _UID[0])


def sb(nc, es, name, shape, dt):
    return es.enter_context(nc.sbuf_tensor(_uname(name), list(shape), dt))


def ps(nc, es, name, shape, dt):
    return es.enter_context(nc.psum_tensor(_uname(name), list(shape), dt))


NCH = 16
CHE = 262144


def wview(w, c0, c1):
    return w[:, c0:c1].rearrange("(kc p) c -> p kc c", p=128)


class NormT:
    def __init__(self, ph, name):
        self.ph = ph
        self.S = ph.S
        self.n = ph.sem(name)

    def dve(self, e, t, xt, gbc, hb, junk, ss, rstd, wait_x, wait_hb_free):
        S = self.S
        n = self.n
        wait_x()
        wait_hb_free()
        for s in range(4):
            ins = e.tensor_tensor(out=junk[:, :], in0=xt[:, s, :], in1=xt[:, s, :], op=ALU.mult)
            S.fence(e, ins, n + "_f")
            ins = e.tensor_reduce(out=ss[:, s:s + 1], in_=junk[:, :], axis=AX.X, op=ALU.add)
            S.fence(e, ins, n + "_f")
        ins = e.tensor_scalar(out=rstd[:, 0:4], in0=ss[:, 0:4], scalar1=1.0 / D, scalar2=EPS, op0=ALU.mult, op1=ALU.add)
        S.sig(ins, n + "_v", t)
        S.wait(e, n + "_sq", t)
        ins = e.reciprocal(out=rstd[:, 0:4], in_=rstd[:, 0:4])
        S.fence(e, ins, n + "_f")
        for s in range(4):
            ins = e.scalar_tensor_tensor(out=hb[:, s, :], in0=xt[:, s, :], scalar=rstd[:, s:s + 1], in1=gbc[:, :],
                                         op0=ALU.mult, op1=ALU.mult)
            S.sig(ins, n + "_hb", (t, s))

    def pe(self, e, t, hb, pst, identb):
        S = self.S
        n = self.n
        for kc in range(8):
            g = t * 8 + kc
            if g >= 2:
                S.wait(e, n + "_ev", g - 2)
            for s in range(4):
                if kc == 0:
                    S.wait(e, n + "_hb", (t, s))
                ins = e.transpose(pst[:, g % 2, s * 128:(s + 1) * 128], hb[:, s, kc * 128:(kc + 1) * 128], identb[:, :])
            S.sig(ins, n + "_tr", g)

    def act(self, e, t, pst, hT, wait_hT_free, rstd=None):
        S = self.S
        n = self.n
        S.wait(e, n + "_v", t)
        ins = e.activation(out=rstd[:, 0:4], in_=rstd[:, 0:4], func=AF.Sqrt)
        S.sig(ins, n + "_sq", t)
        wait_hT_free()
        for kc in range(8):
            g = t * 8 + kc
            S.wait(e, n + "_tr", g)
            ins = e.activation(out=hT[:, kc, :], in_=pst[:, g % 2, 0:512], func=AF.Copy)
            S.sig(ins, n + "_ev", g)

    def wait_hT(self, e, t):
        self.S.wait(e, self.n + "_ev", t * 8 + 7)

    def wait_tr_done(self, e, t):
        self.S.wait(e, self.n + "_tr", t * 8 + 7)


class P1(Phase):
    FM = [(0, 384), (384, 768), (2048, 2176)]
    TM = [(768, 1152), (2176, 2304)]

    def __init__(self, prog, tag, x, w_in, g_pre, identb, Qs, E):
        super().__init__(prog, tag)
        self.x, self.w_in, self.g_pre, self.identb_d = x, w_in, g_pre, identb
        self.Qs, self.E = Qs, E
        self.EK = E[0:6, :].rearrange("h (a n) -> (h a) n", n=NTOK)
        self.ECK = E[12:14, :].rearrange("h (a n) -> (h a) n", n=NTOK)
        self.nt = NormT(self, "nt")

    def alloc(self, nc, es):
        b = Bufs()
        b.wfm = sb(nc, es, "p1_wfm", [128, 8, 896], BF16)
        b.wtm = sb(nc, es, "p1_wtm", [128, 8, 512], BF16)
        b.gbc = sb(nc, es, "p1_gbc", [128, D], F32)
        b.identb = sb(nc, es, "p1_id", [128, 128], BF16)
        b.xt = sb(nc, es, "p1_xt", [128, 4, D], F32)
        b.hb = sb(nc, es, "p1_hb", [128, 4, D], BF16)
        b.junk = sb(nc, es, "p1_junk", [128, D], F32)
        b.ss = sb(nc, es, "p1_ss", [128, 4], F32)
        b.rstd = sb(nc, es, "p1_rstd", [128, 4], F32)
        b.hT = sb(nc, es, "p1_hT", [128, 8, 512], BF16)
        b.fm = sb(nc, es, "p1_fm", [128, 2, 7, 512], BF16)
        b.vt = sb(nc, es, "p1_vt", [128, 2, 4, 512], BF16)
        b.pst = ps(nc, es, "p1_pst", [128, 2, 1024], BF16)
        b.pm = ps(nc, es, "p1_pm", [128, 4, 512], F32)
        return b

    def sp(self, e):
        S, b = self.S, self.b
        ld = self.sem("ld")
        ins = e.dma_start(out=b.identb[:, :], in_=self.identb_d[:, :])
        S.sig(ins, ld, "id", dma=True)
        ins = e.dma_start(out=b.gbc[:, :], in_=self.g_pre.partition_broadcast(128))
        S.sig(ins, ld, "g", dma=True)
        for t in range(NQ):
            if t >= 1:
                S.wait(e, self.nt.n + "_hb", (t - 1, 3))
            ins = e.dma_start(out=b.xt[:, :, :], in_=self.x[t * 512:(t + 1) * 512, :].rearrange("(s p) d -> p s d", p=128))
            S.sig(ins, self.sem("ldx"), t, dma=True)

    def pool(self, e):
        S, b = self.S, self.b
        lw = self.sem("lw")
        c = 0
        for (c0, c1) in self.FM:
            ins = e.dma_start(out=b.wfm[:, :, c:c + (c1 - c0)], in_=wview(self.w_in, c0, c1))
            S.sig(ins, lw, ("fm", c0), dma=True)
            c += c1 - c0
        c = 0
        for (c0, c1) in self.TM:
            ins = e.dma_start(out=b.wtm[:, :, c:c + (c1 - c0)], in_=wview(self.w_in, c0, c1))
            S.sig(ins, lw, ("tm", c0), dma=True)
            c += c1 - c0
        for t in range(NQ):
            sl = t % 2
            st = self.sem("st%d" % sl)
            cs = slice(t * 512, (t + 1) * 512)
            S.wait(e, self.sem("ev"), ("fm", t, 6))
            ins = e.dma_start(out=self.Qs[:, cs].rearrange("(b p) n -> p b n", p=128), in_=b.fm[:, sl, 0:3, :])
            S.sig(ins, st, ("q", t), dma=True)
            ins = e.dma_start(out=self.EK[:, cs].rearrange("(b p) n -> p b n", p=128), in_=b.fm[:, sl, 3:6, :])
            S.sig(ins, st, ("k", t), dma=True)
            ins = e.dma_start(out=self.ECK[:, cs], in_=b.fm[:, sl, 6, :])
            S.sig(ins, st, ("ck", t), dma=True)
            S.wait(e, self.sem("ev"), ("tm", t, 3))
            for hh in range(8):
                ch = 6 + hh if hh < 6 else 14 + (hh - 6)
                dst = self.E[ch, :].rearrange("(p t d) -> p t d", p=128, t=32)
                ins = e.dma_start(out=dst[:, 4 * t:4 * t + 4, :], in_=b.vt[:, sl, :, hh * 64:(hh + 1) * 64])
                S.sig(ins, st, ("cv" if hh == 7 else ("v", hh), t), dma=True)
        S.wait(e, self.sem("st0"), ("cv", NQ - 2))
        S.wait(e, self.sem("st1"), ("cv", NQ - 1))

    def dve(self, e):
        S, b = self.S, self.b
        S.wait(e, self.sem("ld"), "g")
        for t in range(NQ):
            self.nt.dve(e, t, b.xt, b.gbc, b.hb, b.junk, b.ss, b.rstd,
                        wait_x=lambda: S.wait(e, self.sem("ldx"), t),
                        wait_hb_free=lambda: (self.nt.wait_tr_done(e, t - 1) if t >= 1 else None))

    def pe(self, e):
        S, b = self.S, self.b
        S.wait(e, self.sem("ld"), "g")
        S.wait(e, self.sem("lw"), ("tm", self.TM[-1][0]))
        n = 0
        for t in range(NQ):
            self.nt.pe(e, t, b.hb, b.pst, b.identb)
            self.nt.wait_hT(e, t)
            for blk in range(7):
                if n >= 4:
                    S.wait(e, self.sem("ev"), self.evkeys[n - 4])
                for kc in range(8):
                    ins = e.matmul(b.pm[:, n % 4, :], lhsT=b.wfm[:, kc, blk * 128:(blk + 1) * 128], rhs=b.hT[:, kc, :],
                                   start=(kc == 0), stop=(kc == 7))
                S.sig(ins, self.sem("mm"), ("fm", t, blk))
                self.evkeys.append(("fm", t, blk))
                n += 1
            for s in range(4):
                if n >= 4:
                    S.wait(e, self.sem("ev"), self.evkeys[n - 4])
                for kc in range(8):
                    ins = e.matmul(b.pm[:, n % 4, :], lhsT=b.hT[:, kc, s * 128:(s + 1) * 128], rhs=b.wtm[:, kc, :],
                                   start=(kc == 0), stop=(kc == 7))
                S.sig(ins, self.sem("mm"), ("tm", t, s))
                self.evkeys.append(("tm", t, s))
                n += 1

    def reset(self):
        self.evkeys = []

    def act(self, e):
        S, b = self.S, self.b
        n = 0
        for t in range(NQ):
            sl = t % 2
            self.nt.act(e, t, b.pst, b.hT,
                        wait_hT_free=lambda: (S.wait(e, self.sem("mm"), ("tm", t - 1, 3)) if t >= 1 else None),
                        rstd=b.rstd)
            for blk in range(7):
                S.wait(e, self.sem("mm"), ("fm", t, blk))
                if t >= 2 and blk == 0:
                    S.wait(e, self.sem("st%d" % sl), ("cv", t - 2))
                ins = e.activation(out=b.fm[:, sl, blk, :], in_=b.pm[:, n % 4, :], func=AF.Copy)
                S.sig(ins, self.sem("ev"), ("fm", t, blk))
                n += 1
            for s in range(4):
                S.wait(e, self.sem("mm"), ("tm", t, s))
                if t >= 2 and s == 0:
                    S.wait(e, self.sem("st%d" % sl), ("cv", t - 2))
                ins = e.activation(out=b.vt[:, sl, s, :], in_=b.pm[:, n % 4, :], func=AF.Copy)
                S.sig(ins, self.sem("ev"), ("tm", t, s))
                n += 1


class AttnPipe(Phase):
    NSR = NSR

    def reset(self):
        self.gu = {"pe": 0, "act": 0, "dve": 0}
        self.prev_tp = {"pe": None}

    def pe_group(self, e, g):
        S, b = self.S, self.b
        U = len(g["units"])
        g0 = self.gu["pe"]
        for t in range(U + LA):
            if t < U:
                u = g["units"][t]
                gu = g0 + t
                if gu >= self.NSR:
                    S.wait(e, self.sem("p"), gu - self.NSR)
                if t == 0 and g.get("wait_pe") is not None:
                    g["wait_pe"](e)
                ins = e.matmul(b.sring[:, gu % self.NSR, :], lhsT=u["lhsT"], rhs=u["rhs"], start=True, stop=True)
                S.sig(ins, self.sem("qk"), gu)
            if t >= LA:
                v = t - LA
                u = g["units"][v]
                gv = g0 + v
                S.wait(e, self.sem("p"), gv)
                if u["first"] and g["acc_prev"] is not None:
                    S.wait(e, self.sem("ev"), (g["acc_prev"], u["acc"]))
                ins = e.matmul(b.oacc[:, g["par"] * 2 + u["acc"], :], lhsT=u["vl"], rhs=b.pring[:, gv % NPR, :],
                               start=u["first"], stop=u["last"])
                S.sig(ins, self.sem("pv"), gv)
        self.gu["pe"] = g0 + U

    def pe_group_pairs(self, e, g, after_prologue=None):
        S, b = self.S, self.b
        LAP = 2
        U = len(g["units"])
        NP_ = U // 2
        g0 = self.gu["pe"]
        for t in range(NP_ + LAP):
            if t == min(LAP, NP_) and after_prologue is not None:
                after_prologue()
                after_prologue = None
            if t >= LAP:
                S.wait(e, self.sem("p"), g0 + 2 * (t - LAP) + 1)
            elif g0 + 2 * (t - LAP) + 1 >= 0:
                S.wait(e, self.sem("p"), g0 + 2 * (t - LAP) + 1)
            if t == 0 and g.get("wait_pe") is not None:
                g["wait_pe"](e)
            if t < NP_:
                for c in (0, 1):
                    u = g["units"][2 * t + c]
                    gu = g0 + 2 * t + c
                    ins = e.matmul(b.sring[:, gu % self.NSR, :], lhsT=u["lhsT"], rhs=u["rhs"], start=True, stop=True)
                    S.sig(ins, self.sem("qk"), gu)
            if t >= LAP:
                for c in (0, 1):
                    v = 2 * (t - LAP) + c
                    u = g["units"][v]
                    gv = g0 + v
                    if u["first"] and g["acc_prev"] is not None:
                        S.wait(e, self.sem("ev"), (g["acc_prev"], u["acc"]))
                    ins = e.matmul(b.oacc[:, g["par"] * 2 + u["acc"], :], lhsT=u["vl"], rhs=b.pring[:, gv % NPR, :],
                                   start=u["first"], stop=u["last"])
                    S.sig(ins, self.sem("pv"), gv)
        if after_prologue is not None:
            after_prologue()
        self.gu["pe"] = g0 + U

    def pe_post(self, e, g):
        S, b = self.S, self.b
        for c in (0, 1):
            S.wait(e, self.sem("ev"), (g["gid"], c))
            if self.prev_tp["pe"] is not None:
                S.wait(e, self.sem("tpc"), self.prev_tp["pe"])
            for s in range(4):
                ins = e.transpose(b.tps[:, s, 0:65], b.oT[0:65, c, s * 128:(s + 1) * 128], b.identf[0:65, 0:65])
            S.sig(ins, self.sem("tp"), (g["gid"], c))
            self.prev_tp["pe"] = (g["gid"], c)

    def act_group(self, e, g):
        S, b = self.S, self.b
        g0 = self.gu["act"]
        for t, u in enumerate(g["units"]):
            gu = g0 + t
            if u["kind"] == "G":
                S.wait(e, self.sem("gb"), gu)
            else:
                S.wait(e, self.sem("qk"), gu)
            if gu >= NPR:
                S.wait(e, self.sem("pv"), gu - NPR)
            ins = e.activation(out=b.pring[:, gu % NPR, :], in_=b.sring[:, gu % self.NSR, :], func=AF.Exp,
                               bias=u["bias"], scale=u["scale"])
            S.sig(ins, self.sem("p"), gu)
        self.gu["act"] = g0 + len(g["units"])

    def dve_group(self, e, g, consume):
        S, b = self.S, self.b
        g0 = self.gu["dve"]
        last = {}
        for t, u in enumerate(g["units"]):
            gu = g0 + t
            last[u["acc"]] = gu
            if u["kind"] == "G":
                S.wait(e, self.sem("qk"), gu)
                ins = e.scalar_tensor_tensor(out=b.sring[:, gu % self.NSR, :], in0=u["dD"], scalar=u["dcoef"],
                                             in1=b.sring[:, gu % self.NSR, :], op0=ALU.mult, op1=ALU.add)
                S.sig(ins, self.sem("gb"), gu)
        self.gu["dve"] = g0 + len(g["units"])
        for c in (0, 1):
            S.wait(e, self.sem("pv"), last[c])
            if g["gid"] >= 1:
                S.wait(e, self.sem("tp"), (g["gid"] - 1, c))
            ins = e.tensor_copy(out=b.oT[0:65, c, :], in_=b.oacc[0:65, g["par"] * 2 + c, :])
            S.sig(ins, self.sem("ev"), (g["gid"], c))
        for c in (0, 1):
            S.wait(e, self.sem("tp"), (g["gid"], c))
            consume(e, g, c)


class P2a(AttnPipe):
    NSR = 4

    AG_ORDER = [0, 6, 1, 7, 2, 8, 3, 9, 4, 10, 5, 11, 12, 13, 14, 15]

    def __init__(self, prog, tag, G, Qs, ctab, dn, kaug, qaug, identf, lamv, lamc, gsub, ya_d, E=None):
        super().__init__(prog, tag)
        self.E = E
        self.G, self.Qs, self.ctab_d, self.dn_d = G, Qs, ctab, dn
        self.kaug, self.qaug, self.identf_d, self.lamv, self.lamc_d, self.gsub_d, self.ya_d = kaug, qaug, identf, lamv, lamc, gsub, ya_d
        self.ncol = cidx_map()[1]

    def alloc(self, nc, es):
        b = Bufs()
        b.KT = sb(nc, es, "a_KT", [100, 2, SEQ], BF16)
        b.V = sb(nc, es, "a_V", [128, 2, 129, 65], BF16)
        b.QA = sb(nc, es, "a_QA", [100, 2, NTOK], BF16)
        b.QB = sb(nc, es, "a_QB", [100, 2, NTOK], BF16)
        b.ctab = sb(nc, es, "a_ctab", [128, self.ncol], F32)
        b.dn = sb(nc, es, "a_dn", [128, 16, 512], F32)
        b.identf = sb(nc, es, "a_idf", [128, 128], F32)
        b.pring = sb(nc, es, "a_pr", [128, NPR, 512], BF16)
        b.oT = sb(nc, es, "a_oT", [65, 2, 512], F32)
        b.lv = sb(nc, es, "a_lv", [128, 4, 32], F32)
        b.lt = sb(nc, es, "a_lt", [128, 8], F32)
        b.lamc = sb(nc, es, "a_lamc", [128, 2], F32)
        b.gsub = sb(nc, es, "a_gsub", [128, 64], F32)
        b.r = sb(nc, es, "a_r", [128, 8], F32)
        b.t0 = sb(nc, es, "a_t0", [128, 4, 64], F32)
        b.y = sb(nc, es, "a_y", [128, 4, 64], F32)
        b.sq = sb(nc, es, "a_sq", [128, 4, 64], F32)
        b.sst = sb(nc, es, "a_sst", [128, 32, 6], F32)
        b.ya = sb(nc, es, "a_ya", [128, 32, 384], BF16)
        b.sring = ps(nc, es, "a_sr", [128, 4, 512], F32)
        b.oacc = ps(nc, es, "a_oacc", [128, 2, 512], F32)
        b.tps = ps(nc, es, "a_tps", [128, 4, 128], F32)
        return b

    def groups(self):
        b = self.b
        cm, _ = cidx_map()
        out = []
        gid = 0
        for h in range(6):
            sl = h % 2
            sp = float(SL_DIFF[h] / SC_A)
            for i in range(NQ):
                units = []
                tl = band_tiles(h, i)
                qs = slice(512 * i, 512 * i + 512)
                for n, j in enumerate(tl):
                    r, lt = j2idx(j)
                    ks = slice(r * 4096 + lt * 128, r * 4096 + lt * 128 + 128)
                    for comp in (0, 1):
                        base = 64 * comp
                        vi = (r * 32 + lt) * 65
                        u = {"acc": comp, "first": n == 0, "last": n == len(tl) - 1, "scale": SC_A,
                             "vl": b.V[:, sl, :, :].rearrange("p t d -> p (t d)")[:, vi:vi + 128]}
                        if j < 16 * i or j >= 16 * (i + 1):
                            Q = b.QA if j < 16 * i else b.QB
                            u["kind"] = "F"
                            u["lhsT"] = b.KT[base:base + 36, sl, ks]
                            u["rhs"] = Q[base:base + 36, sl, qs]
                            u["bias"] = b.ctab[:, cm[(h, i, j)]:cm[(h, i, j)] + 1]
                        else:
                            u["kind"] = "G"
                            u["lhsT"] = b.KT[base:base + 32, sl, ks]
                            u["rhs"] = b.QA[base:base + 32, sl, qs]
                            u["bias"] = b.ctab[:, 0:1]
                            u["dD"] = b.dn[:, j - 16 * i, :]
                            u["dcoef"] = -sp
                        units.append(u)
                out.append({"gid": gid, "par": 0, "units": units, "h": h, "i": i,
                            "acc_prev": gid - 1 if gid >= 1 else None})
                gid += 1
        return out

    def sp(self, e):
        S, b = self.S, self.b
        ld = self.sem("ldc")
        lst = [(b.ctab[:, :], self.ctab_d[:, :]), (b.dn[:, :, :], self.dn_d[:, :, :]),
               (b.identf[:, :], self.identf_d[:, :]), (b.lamc[:, :], self.lamc_d[:, :]),
               (b.gsub[:, :], self.gsub_d.partition_broadcast(128))]
        for n, (dst, src) in enumerate(lst):
            ins = e.dma_start(out=dst, in_=src)
            S.sig(ins, ld, ("c", n), dma=True)
        for k in range(4):
            ins = e.dma_start(out=b.lv[:, k, :], in_=self.lamv[k, :].partition_broadcast(128))
            S.sig(ins, ld, ("lv", k), dma=True)
        for h in range(6):
            sl = h % 2
            sem = self.sem("ld%d" % sl)
            if h >= 2:
                S.wait(e, self.sem("hd"), h - 2)
            if self.E is not None:
                S.wait(e, "cc", 6 + h)
            for comp in (0, 1):
                rows = slice(h * 64 + comp * 32, h * 64 + comp * 32 + 32)
                base = 64 * comp
                ins = e.dma_start(out=b.KT[base:base + 32, sl, :].rearrange("p (r n) -> p r n", r=4),
                                  in_=self.G[h, :, :].rearrange("r (a n) -> a r n", n=NTOK)[comp * 32:(comp + 1) * 32])
                S.sig(ins, sem, ("k", h, comp), dma=True)
                ins = e.dma_start(out=b.KT[base + 32:base + 36, sl, :], in_=self.kaug[h, :, :])
                S.sig(ins, sem, ("ka", h, comp), dma=True)
                for var, Q in ((0, b.QA), (1, b.QB)):
                    ins = e.dma_start(out=Q[base:base + 32, sl, :], in_=self.Qs[rows, :])
                    S.sig(ins, sem, ("q", h, comp, var), dma=True)
                    ins = e.dma_start(out=Q[base + 32:base + 36, sl, :], in_=self.qaug[h, var, :, :])
                    S.sig(ins, sem, ("qa", h, comp, var), dma=True)
            for r in range(4):
                ins = e.dma_start(out=b.V[:, sl, r * 32:(r + 1) * 32, 0:64],
                                  in_=self.G[6 + h, r, :].rearrange("(p t d) -> p t d", p=128, t=32))
                S.sig(ins, sem, ("v", h, r), dma=True)

    def pe(self, e):
        S, b = self.S, self.b
        S.wait(e, self.sem("ldc"), ("lv", 3))
        S.wait(e, self.sem("ones"), 0)
        prev = None
        for g in self.groups():
            if g["i"] == 0:
                h = g["h"]
                g["wait_pe"] = lambda e, h=h: S.wait(e, self.sem("ld%d" % (h % 2)), ("v", h, 3))
            self.pe_group_pairs(e, g, after_prologue=(lambda p=prev: self.pe_post(e, p)) if prev is not None else None)
            prev = g
        self.pe_post(e, prev)

    def act(self, e):
        S, b = self.S, self.b
        S.wait(e, self.sem("ldc"), ("lv", 3))
        S.wait(e, self.sem("lm"), "dots")
        ins = e.activation(out=b.lt[:, 2:4], in_=b.lt[:, 0:2], func=AF.Exp)
        S.sig(ins, self.sem("lma"), "exp")
        for g in self.groups():
            self.act_group(e, g)
        S.wait(e, self.sem("fin"), "v")
        ins = e.activation(out=b.sst[:, 0:4 * NQ, :], in_=b.sst[:, 0:4 * NQ, :], func=AF.Sqrt)
        S.sig(ins, self.sem("lma"), "sqrt")

    def consume(self, e, g, c):
        S, b = self.S, self.b
        f = self.sem("f")
        h, i = g["h"], g["i"]
        ins = e.reciprocal(out=b.r[:, 4 * c:4 * c + 4], in_=b.tps[:, :, 64])
        S.fence(e, ins, f)
        if c == 0:
            for s in range(4):
                ins = e.tensor_scalar(out=b.t0[:, s, :], in0=b.tps[:, s, 0:64], scalar1=b.r[:, s:s + 1], scalar2=None,
                                      op0=ALU.mult)
            S.sig(ins, self.sem("tpc"), (g["gid"], 0))
        else:
            ins = e.tensor_scalar(out=b.r[:, 4:8], in0=b.r[:, 4:8], scalar1=b.lt[:, 4:5], scalar2=None, op0=ALU.mult)
            S.fence(e, ins, f)
            for s in range(4):
                ins = e.scalar_tensor_tensor(out=b.y[:, s, :], in0=b.tps[:, s, 0:64], scalar=b.r[:, 4 + s:5 + s],
                                             in1=b.t0[:, s, :], op0=ALU.mult, op1=ALU.add)
            S.sig(ins, self.sem("tpc"), (g["gid"], 1))
            S.wait(e, self.sem("tpc"), (g["gid"], 1))
            ins = e.tensor_tensor(out=b.sq[:, :, :], in0=b.y[:, :, :], in1=b.y[:, :, :], op=ALU.mult)
            S.fence(e, ins, f)
            ins = e.tensor_reduce(out=b.sst[:, 4 * i:4 * i + 4, h], in_=b.sq[:, :, :], axis=AX.X, op=ALU.add)
            ins = e.tensor_copy(out=b.ya[:, 4 * i:4 * i + 4, h * 64:(h + 1) * 64], in_=b.y[:, :, :])
            S.fence(e, ins, f)
            if i == NQ - 1:
                S.sig(e.tensor_copy(out=b.lt[:, 7:8], in_=b.lt[:, 4:5]), self.sem("hd"), h)

    def dve(self, e):
        S, b = self.S, self.b
        f = self.sem("f")
        for sl in (0, 1):
            ins = e.memset(b.V[:, sl, 128, :], 0.0)
            ins = e.memset(b.V[:, sl, 0:128, 64:65], 1.0)
        S.sig(ins, self.sem("ones"), 0)
        S.wait(e, self.sem("ldc"), ("lv", 3))
        ins = e.tensor_tensor(out=b.lv[:, 0, :], in0=b.lv[:, 0, :], in1=b.lv[:, 1, :], op=ALU.mult)
        ins = e.tensor_tensor(out=b.lv[:, 2, :], in0=b.lv[:, 2, :], in1=b.lv[:, 3, :], op=ALU.mult)
        S.fence(e, ins, f)
        ins = e.tensor_reduce(out=b.lt[:, 0:1], in_=b.lv[:, 0, :], axis=AX.X, op=ALU.add)
        ins = e.tensor_reduce(out=b.lt[:, 1:2], in_=b.lv[:, 2, :], axis=AX.X, op=ALU.add)
        S.sig(ins, self.sem("lm"), "dots")
        S.wait(e, self.sem("lma"), "exp")
        ins = e.tensor_tensor(out=b.lt[:, 5:6], in0=b.lt[:, 3:4], in1=b.lt[:, 2:3], op=ALU.subtract)
        S.fence(e, ins, f)
        ins = e.tensor_tensor(out=b.lt[:, 4:5], in0=b.lt[:, 5:6], in1=b.lamc[:, 0:1], op=ALU.subtract)
        ins = e.tensor_scalar(out=b.gsub[:, :], in0=b.gsub[:, :], scalar1=b.lamc[:, 1:2], scalar2=None, op0=ALU.mult)
        S.fence(e, ins, f)
        for g in self.groups():
            self.dve_group(e, g, self.consume)
        ins = e.tensor_scalar(out=b.sst[:, 0:4 * NQ, :], in0=b.sst[:, 0:4 * NQ, :], scalar1=1.0 / 64, scalar2=EPS,
                              op0=ALU.mult, op1=ALU.add)
        S.sig(ins, self.sem("fin"), "v")
        S.wait(e, self.sem("lma"), "sqrt")
        ins = e.reciprocal(out=b.sst[:, 0:4 * NQ, :], in_=b.sst[:, 0:4 * NQ, :])
        S.fence(e, ins, f)
        for lt in range(4 * NQ):
            for h in range(6):
                ins = e.scalar_tensor_tensor(out=b.ya[:, lt, h * 64:(h + 1) * 64], in0=b.ya[:, lt, h * 64:(h + 1) * 64],
                                             scalar=b.sst[:, lt, h:h + 1], in1=b.gsub[:, :], op0=ALU.mult, op1=ALU.mult)
        S.sig(ins, self.sem("fin"), "ya")

    def pool(self, e):
        S, b = self.S, self.b
        if self.E is not None:
            for k in self.AG_ORDER:
                ins = e.collective_compute("AllGather", ALU.bypass, replica_groups=[[0, 1, 2, 3], [4, 5, 6, 7]],
                                           ins=[self.E[k, :].rearrange("(p c) -> p c", c=2048)],
                                           outs=[self.G[k, :, :].rearrange("r (p c) -> (r p) c", c=2048)])
                S.sig(ins, "cc", k)
            S.wait(e, "cc", 15)
        S.wait(e, self.sem("fin"), "ya")
        ins = e.dma_start(out=self.ya_d[:, 0:4 * NQ, :], in_=b.ya[:, 0:4 * NQ, :])
        S.sig(ins, self.sem("st"), 0, dma=True)
        S.wait(e, self.sem("st"), 0)


class P2b(AttnPipe):
    NSR = 4

    def __init__(self, prog, tag, x, w_in, g_pre, G, ya_d, ln_g, ln_b, w_sp, b_sp, sinks, w_out, g_post,
                 dw, identb, identf, x1):
        super().__init__(prog, tag)
        self.x, self.w_in, self.g_pre, self.G, self.ya_d = x, w_in, g_pre, G, ya_d
        self.ln_g, self.ln_b, self.w_sp, self.b_sp, self.sinks, self.w_out, self.g_post = ln_g, ln_b, w_sp, b_sp, sinks, w_out, g_post
        self.dw_d, self.identb_d, self.identf_d, self.x1 = dw, identb, identf, x1
        self.nt = NormT(self, "nt")

    def reset(self):
        super().reset()
        self.m = {"pe": 0, "act": 0, "dve": 0}

    def alloc(self, nc, es):
        b = Bufs()
        b.wuv = sb(nc, es, "b_wuv", [128, 8, 512], BF16)
        b.wcq = sb(nc, es, "b_wcq", [128, 8, 3, 128], BF16)
        b.wout = sb(nc, es, "b_wout", [128, 8, D], BF16)
        b.dw = sb(nc, es, "b_dw", [128, 18, 512], F32)
        b.gbc = sb(nc, es, "b_gbc", [128, D], F32)
        b.gpost = sb(nc, es, "b_gpost", [128, D], F32)
        b.lng = sb(nc, es, "b_lng", [128, 256], F32)
        b.lnb = sb(nc, es, "b_lnb", [128, 256], F32)
        b.identb = sb(nc, es, "b_idb", [128, 128], BF16)
        b.identf = sb(nc, es, "b_idf", [128, 128], F32)
        b.wsp = sb(nc, es, "b_wsp", [128, 4, 128], F32)
        b.wsT = sb(nc, es, "b_wsT", [128, 4, 128], BF16)
        b.bsp = sb(nc, es, "b_bsp", [4, 128], F32)
        b.bsT = sb(nc, es, "b_bsT", [128, 4], F32)
        b.esink = sb(nc, es, "b_esink", [128, 6], F32)
        b.zero = sb(nc, es, "b_zero", [128, 1], F32)
        b.xt = sb(nc, es, "b_xt", [128, 2, 4, D], F32)
        b.hb = sb(nc, es, "b_hb", [128, 4, D], BF16)
        b.junk = sb(nc, es, "b_junk", [128, D], F32)
        b.ss = sb(nc, es, "b_ss", [128, 4], F32)
        b.rstd = sb(nc, es, "b_rstd", [128, 4], F32)
        b.hT = sb(nc, es, "b_hT", [128, 8, 512], BF16)
        b.cq = sb(nc, es, "b_cq", [128, 3, 512], BF16)
        b.ckx = sb(nc, es, "b_ckx", [128, 18 * 128], BF16)
        b.cv = sb(nc, es, "b_cv", [128, 19, 2, 65], BF16)
        b.pring = sb(nc, es, "b_pr", [128, NPR, 512], BF16)
        b.oT = sb(nc, es, "b_oT", [65, 2, 512], F32)
        b.r = sb(nc, es, "b_r", [128, 8], F32)
        b.y = sb(nc, es, "b_y", [128, 4, D], BF16)
        b.u = sb(nc, es, "b_u", [128, 4, 256], F32)
        b.vt = sb(nc, es, "b_vt", [128, 4, 256], F32)
        b.vnb = sb(nc, es, "b_vnb", [128, 4, 256], BF16)
        b.st = sb(nc, es, "b_st", [128, 16], F32)
        b.osb = sb(nc, es, "b_osb", [128, 4, D], F32)
        b.os2 = sb(nc, es, "b_os2", [128, 8], F32)
        b.pst = ps(nc, es, "b_pst", [128, 2, 1024], BF16)
        b.sring = ps(nc, es, "b_sr", [128, 4, 512], F32)
        b.oacc = ps(nc, es, "b_oacc", [128, 2, 512], F32)
        b.tps = b.pst[:, 0, :].bitcast(F32).rearrange("p (s d) -> p s d", d=128)
        return b

    def kts(self, i):
        return [kt for kt in range(18) if not (kt == 0 and i == 0) and not (kt == 17 and i == 7)]

    def groups(self, i):
        b = self.b
        out = []
        kts = self.kts(i)
        for g in range(3):
            units = []
            for n, kt in enumerate(kts):
                for kv in (0, 1):
                    head = kv * 3 + g
                    units.append({"acc": kv, "first": n == 0, "last": n == len(kts) - 1, "scale": SC_W, "kind": "G",
                                  "lhsT": b.ckx[kv * 64:(kv + 1) * 64, kt * 128:(kt + 1) * 128],
                                  "rhs": b.cq[kv * 64:(kv + 1) * 64, g, :], "bias": b.zero[:, 0:1],
                                  "dD": b.dw[:, kt, :], "dcoef": -float(SL_WIN[head] / SC_W),
                                  "vl": b.cv[:, :, :, :].rearrange("p t k d -> p (t k d)")[:, (kt * 2 + kv) * 65:(kt * 2 + kv) * 65 + 128]})
            gid = 3 * i + g
            out.append({"gid": gid, "par": 0, "units": units, "g": g, "i": i, "acc_prev": gid - 1 if gid >= 1 else None})
        return out

    MSEM = {"cq": "mA", "uv": "mD", "sp": "mD", "op": "mA", "ws": "mA"}

    def misc_begin(self, e, kind):
        S = self.S
        m = self.m["pe"]
        self.mk.append(kind)
        if m >= self.NSR:
            S.wait(e, self.sem(self.MSEM[self.mk[m - self.NSR]]), ("rel", m - self.NSR))
        self.m["pe"] = m + 1
        return self.b.sring[:, m % self.NSR, :], m

    def sp(self, e):
        S, b = self.S, self.b
        ld = self.sem("ldc")
        lst = [(b.dw[:, :, :], self.dw_d[:, :, :]), (b.identb[:, :], self.identb_d[:, :]), (b.identf[:, :], self.identf_d[:, :]),
               (b.gbc[:, :], self.g_pre.partition_broadcast(128)), (b.gpost[:, :], self.g_post.partition_broadcast(128)),
               (b.lng[:, :], self.ln_g.partition_broadcast(128)), (b.lnb[:, :], self.ln_b.partition_broadcast(128)),
               (b.esink[:, :], self.sinks.partition_broadcast(128)),
               (b.wsp[:, :, :], self.w_sp.rearrange("g t s -> t g s")), (b.bsp[:, :], self.b_sp[:, :])]
        for n, (dst, src) in enumerate(lst):
            ins = e.dma_start(out=dst, in_=src)
            S.sig(ins, ld, ("c", n), dma=True)
        for i in range(NQ):
            sl = i % 2
            if i >= 2:
                S.wait(e, self.sem("stx%d" % sl), i - 2)
            ins = e.dma_start(out=b.xt[:, sl, :, :], in_=self.x[i * 512:(i + 1) * 512, :].rearrange("(s p) d -> p s d", p=128))
            S.sig(ins, self.sem("ldx%d" % sl), i, dma=True)
            if i >= 1:
                S.wait(e, self.sem("tpc"), (3 * (i - 1) + 2, 1))
            lk = self.sem("ldk")
            for kt in self.kts(i):
                if 1 <= kt <= 16 and (kt - 1) % 4 != 0:
                    continue
                if kt == 0:
                    r, lt, n = j2idx(16 * i - 1) + (1,)
                elif kt == 17:
                    r, lt, n = j2idx(16 * (i + 1)) + (1,)
                else:
                    r, lt, n = (kt - 1) // 4, 4 * i, 4
                for kv in (0, 1):
                    ins = e.dma_start(out=b.ckx[kv * 64:(kv + 1) * 64, kt * 128:(kt + n) * 128],
                                      in_=self.G[12 + kv, r, :].rearrange("(a n) -> a n", n=NTOK)[:, lt * 128:(lt + n) * 128])
                    S.sig(ins, lk, ("k", i, kt, kv), dma=True)
                for kv in (0, 1):
                    ins = e.dma_start(out=b.cv[:, kt:kt + n, kv, 0:64],
                                      in_=self.G[14 + kv, r, :].rearrange("(p t d) -> p t d", p=128, t=32)[:, lt:lt + n, :])
                    S.sig(ins, lk, ("v", i, kt) if kv == 1 else ("v0", i, kt), dma=True)
            if i >= 1:
                S.wait(e, self.sem("ytr"), i - 1)
            ins = e.dma_start(out=b.y[:, :, 0:384], in_=self.ya_d[:, 4 * i:4 * i + 4, :])
            S.sig(ins, self.sem("ldy"), i, dma=True)

    def pool(self, e):
        S, b = self.S, self.b
        lw = self.sem("lw")
        ins = e.dma_start(out=b.wuv[:, :, :], in_=wview(self.w_in, 1152, 1664))
        S.sig(ins, lw, "uv", dma=True)
        for g in range(3):
            for kv in (0, 1):
                c0 = 1664 + (kv * 3 + g) * 64
                ins = e.dma_start(out=b.wcq[:, :, g, kv * 64:(kv + 1) * 64], in_=wview(self.w_in, c0, c0 + 64))
                S.sig(ins, lw, ("cq", g, kv), dma=True)
        ins = e.dma_start(out=b.wout[:, :, :], in_=wview(self.w_out, 0, D))
        S.sig(ins, lw, "out", dma=True)
        for i in range(NQ):
            sl = i % 2
            S.wait(e, self.sem("res"), i)
            ins = e.dma_start(out=self.x1[i * 512:(i + 1) * 512, :].rearrange("(s p) d -> p s d", p=128), in_=b.xt[:, sl, :, :])
            S.sig(ins, self.sem("stx%d" % sl), i, dma=True)
        for i in (NQ - 2, NQ - 1):
            if i >= 0:
                S.wait(e, self.sem("stx%d" % (i % 2)), i)

    def pe(self, e):
        S, b = self.S, self.b
        self.mk = []
        S.wait(e, self.sem("ldc"), ("c", 9))
        S.wait(e, self.sem("lw"), "out")
        for g4 in range(4):
            if g4 >= 1:
                S.wait(e, self.sem("mA"), ("ws", g4 - 1))
            ins = e.transpose(b.tps[:, 0, :], b.wsp[:, g4, :], b.identf[:, :])
            S.sig(ins, self.sem("wst"), g4)
        S.wait(e, self.sem("mA"), ("ws", 3))
        ins = e.transpose(b.tps[:, 1, 0:4], b.bsp[0:4, :], b.identf[0:4, 0:4])
        S.sig(ins, self.sem("wst"), 4)
        S.wait(e, self.sem("ones"), 0)
        for i in range(NQ):
            self.nt.pe(e, i, b.hb, b.pst, b.identb)
            self.nt.wait_hT(e, i)
            for g in range(3):
                bank, m = self.misc_begin(e, "cq")
                for kc in range(8):
                    ins = e.matmul(bank, lhsT=b.wcq[:, kc, g, :], rhs=b.hT[:, kc, :], start=(kc == 0), stop=(kc == 7))
                S.sig(ins, self.sem("mm"), m)
            for s in range(4):
                bank, m = self.misc_begin(e, "uv")
                for kc in range(8):
                    ins = e.matmul(bank, lhsT=b.hT[:, kc, s * 128:(s + 1) * 128], rhs=b.wuv[:, kc, :], start=(kc == 0), stop=(kc == 7))
                S.sig(ins, self.sem("mm"), m)
            for s in range(4):
                bank, m = self.misc_begin(e, "sp")
                S.wait(e, self.sem("vn"), (i, s))
                for g4 in range(4):
                    ins = e.matmul(bank[:, g4 * 64:(g4 + 1) * 64], lhsT=b.wsT[:, g4, :], rhs=b.vnb[:, s, g4 * 64:(g4 + 1) * 64],
                                   start=True, stop=True)
                S.sig(ins, self.sem("mm"), m)
            mlast = self.m["pe"] - 1
            prevg = None
            for gi, g in enumerate(self.groups(i)):
                if gi == 0:
                    def w0(e, mlast=mlast, i=i):
                        for mm_ in range(max(0, mlast - self.NSR + 1), mlast + 1):
                            S.wait(e, self.sem(self.MSEM[self.mk[mm_]]), ("rel", mm_))
                        S.wait(e, self.sem("ldk"), ("v", i, self.kts(i)[-1] if self.kts(i)[-1] == 17 else 13))
                        S.wait(e, self.sem("mA"), ("rel", mlast - 8))
                    g["wait_pe"] = w0
                self.pe_group_pairs(e, g, after_prologue=(lambda p=prevg: self.pe_post(e, p)) if prevg is not None else None)
                prevg = g
            self.pe_post(e, prevg)
            S.wait(e, self.sem("ldy"), i)
            S.wait(e, self.sem("gate"), (i, 3))
            S.wait(e, self.sem("tpc"), (3 * i + 2, 1))
            for kc in range(8):
                gq = (NQ + i) * 8 + kc
                S.wait(e, self.nt.n + "_ev", gq - 2 if kc >= 2 else i * 8 + 6 + kc)
                for s in range(4):
                    ins = e.transpose(b.pst[:, gq % 2, s * 128:(s + 1) * 128], b.y[:, s, kc * 128:(kc + 1) * 128], b.identb[:, :])
                S.sig(ins, self.nt.n + "_tr", gq)
            S.wait(e, self.nt.n + "_ev", (NQ + i) * 8 + 7)
            S.wait(e, self.sem("p"), self.gu["pe"] - 1)
            for s in range(4):
                for half in range(2):
                    bank, m = self.misc_begin(e, "op")
                    for kc in range(8):
                        ins = e.matmul(bank, lhsT=b.hT[:, kc, s * 128:(s + 1) * 128], rhs=b.wout[:, kc, half * 512:(half + 1) * 512],
                                       start=(kc == 0), stop=(kc == 7))
                    S.sig(ins, self.sem("mm"), m)
            S.sig(e.transpose(b.pst[:, 0, 0:128], b.identb[:, :], b.identb[:, :]), self.sem("ytr"), i)

    def act(self, e):
        S, b = self.S, self.b
        S.wait(e, self.sem("ldc"), ("c", 9))
        ins = e.activation(out=b.esink[:, :], in_=b.esink[:, :], func=AF.Exp)
        S.sig(ins, self.sem("es"), 0)
        for g4 in range(4):
            S.wait(e, self.sem("wst"), g4)
            ins = e.activation(out=b.wsT[:, g4, :], in_=b.tps[:, 0, :], func=AF.Copy)
            S.sig(ins, self.sem("mA"), ("ws", g4))
        S.wait(e, self.sem("wst"), 4)
        ins = e.activation(out=b.bsT[:, :], in_=b.tps[:, 1, 0:4], func=AF.Copy)
        S.sig(ins, self.sem("es"), 1)
        m = 0
        for i in range(NQ):
            self.nt.act(e, i, b.pst, b.hT,
                        wait_hT_free=lambda: (S.wait(e, self.sem("mm"), self.m_last_op) if i >= 1 else None), rstd=b.rstd)
            for g in range(3):
                S.wait(e, self.sem("mm"), m)
                if i >= 1 and g == 0:
                    S.wait(e, self.sem("pv"), self.gu["act"] - 1)
                ins = e.activation(out=b.cq[:, g, :], in_=b.sring[:, m % self.NSR, :], func=AF.Copy)
                S.sig(ins, self.sem("mA"), ("rel", m))
                m += 1
            m += 8
            S.wait(e, self.sem("lnv"), i)
            ins = e.activation(out=b.st[:, 8:12], in_=b.st[:, 8:12], func=AF.Sqrt)
            S.sig(ins, self.sem("lnq"), i)
            for g in self.groups(i):
                self.act_group(e, g)
            for kc in range(8):
                gq = (NQ + i) * 8 + kc
                S.wait(e, self.nt.n + "_tr", gq)
                ins = e.activation(out=b.hT[:, kc, :], in_=b.pst[:, gq % 2, 0:512], func=AF.Copy)
                S.sig(ins, self.nt.n + "_ev", gq)
            for s in range(4):
                for half in range(2):
                    S.wait(e, self.sem("mm"), m)
                    if s == 0 and half == 0 and i >= 1:
                        S.wait(e, self.sem("res"), i - 1)
                    ins = e.activation(out=b.osb[:, s, half * 512:(half + 1) * 512], in_=b.sring[:, m % self.NSR, :], func=AF.Copy)
                    S.sig(ins, self.sem("mA"), ("rel", m))
                    self.m_last_op = m
                    m += 1
            S.wait(e, self.sem("onv"), i)
            ins = e.activation(out=b.os2[:, 4:8], in_=b.os2[:, 4:8], func=AF.Sqrt)
            S.sig(ins, self.sem("onq"), i)

    def consume(self, e, g, c):
        S, b = self.S, self.b
        f = self.sem("f")
        head = c * 3 + g["g"]
        ins = e.tensor_scalar(out=b.r[:, 0:4], in0=b.tps[:, :, 64], scalar1=b.esink[:, head:head + 1], scalar2=None, op0=ALU.add)
        S.fence(e, ins, f)
        ins = e.reciprocal(out=b.r[:, 0:4], in_=b.r[:, 0:4])
        S.fence(e, ins, f)
        for s in range(4):
            ins = e.tensor_scalar(out=b.y[:, s, 640 + head * 64:640 + (head + 1) * 64], in0=b.tps[:, s, 0:64],
                                  scalar1=b.r[:, s:s + 1], scalar2=None, op0=ALU.mult)
        S.sig(ins, self.sem("tpc"), (g["gid"], c))

    def dve(self, e):
        S, b = self.S, self.b
        f = self.sem("f")
        ins = e.memset(b.cv[:, 18, :, :], 0.0)
        ins = e.memset(b.cv[:, 0:18, :, 64:65], 1.0)
        ins = e.memset(b.zero[:, :], 0.0)
        S.sig(ins, self.sem("ones"), 0)
        S.wait(e, self.sem("ldc"), ("c", 9))
        S.wait(e, self.sem("es"), 1)
        m = 0
        for i in range(NQ):
            sl = i % 2
            xt = b.xt[:, sl, :, :]
            self.nt.dve(e, i, xt, b.gbc, b.hb, b.junk, b.ss, b.rstd,
                        wait_x=lambda: S.wait(e, self.sem("ldx%d" % sl), i),
                        wait_hb_free=lambda: (self.nt.wait_tr_done(e, i - 1) if i >= 1 else None))
            m += 3
            for s in range(4):
                S.wait(e, self.sem("mm"), m)
                if i >= 1 and s == 0:
                    S.wait(e, self.sem("gate"), (i - 1, 3))
                ins = e.tensor_copy(out=b.u[:, s, :], in_=b.sring[:, m % self.NSR, 0:256])
                ins = e.tensor_copy(out=b.vt[:, s, :], in_=b.sring[:, m % self.NSR, 256:512])
                S.sig(ins, self.sem("mD"), ("rel", m))
                S.wait(e, self.sem("mD"), ("rel", m))
                ins = e.tensor_reduce(out=b.st[:, s:s + 1], in_=b.vt[:, s, :], axis=AX.X, op=ALU.add)
                ins = e.tensor_tensor(out=b.junk[:, 0:256], in0=b.vt[:, s, :], in1=b.vt[:, s, :], op=ALU.mult)
                S.fence(e, ins, f)
                ins = e.tensor_reduce(out=b.st[:, 4 + s:5 + s], in_=b.junk[:, 0:256], axis=AX.X, op=ALU.add)
                S.fence(e, ins, f)
                m += 1
            ins = e.tensor_scalar(out=b.st[:, 0:4], in0=b.st[:, 0:4], scalar1=1.0 / 256, scalar2=None, op0=ALU.mult)
            S.fence(e, ins, f)
            ins = e.tensor_tensor(out=b.st[:, 12:16], in0=b.st[:, 0:4], in1=b.st[:, 0:4], op=ALU.mult)
            S.fence(e, ins, f)
            ins = e.scalar_tensor_tensor(out=b.st[:, 8:12], in0=b.st[:, 4:8], scalar=1.0 / 256, in1=b.st[:, 12:16],
                                         op0=ALU.mult, op1=ALU.subtract)
            S.fence(e, ins, f)
            ins = e.tensor_scalar(out=b.st[:, 8:12], in0=b.st[:, 8:12], scalar1=EPS, scalar2=None, op0=ALU.add)
            S.sig(ins, self.sem("lnv"), i)
            S.wait(e, self.sem("lnq"), i)
            ins = e.reciprocal(out=b.st[:, 8:12], in_=b.st[:, 8:12])
            S.fence(e, ins, f)
            for s in range(4):
                ins = e.tensor_scalar(out=b.vt[:, s, :], in0=b.vt[:, s, :], scalar1=b.st[:, s:s + 1], scalar2=b.st[:, 8 + s:9 + s],
                                      op0=ALU.subtract, op1=ALU.mult)
                S.fence(e, ins, f)
                ins = e.tensor_tensor(out=b.vt[:, s, :], in0=b.vt[:, s, :], in1=b.lng[:, :], op=ALU.mult)
                S.fence(e, ins, f)
                ins = e.tensor_tensor(out=b.vnb[:, s, :], in0=b.vt[:, s, :], in1=b.lnb[:, :], op=ALU.add)
                S.sig(ins, self.sem("vn"), (i, s))
            for s in range(4):
                S.wait(e, self.sem("mm"), m)
                if i >= 1 and s == 0:
                    S.wait(e, self.sem("ytr"), i - 1)
                for g4 in range(4):
                    ins = e.scalar_tensor_tensor(out=b.y[:, s, 384 + g4 * 64:384 + (g4 + 1) * 64],
                                                 in0=b.sring[:, m % self.NSR, g4 * 64:(g4 + 1) * 64], scalar=b.bsT[:, g4:g4 + 1],
                                                 in1=b.u[:, s, g4 * 64:(g4 + 1) * 64], op0=ALU.add, op1=ALU.mult)
                S.sig(ins, self.sem("mD"), ("rel", m))
                S.wait(e, self.sem("mD"), ("rel", m))
                S.sig(e.tensor_copy(out=b.r[:, 4:5], in_=b.zero[:, 0:1]), self.sem("gate"), (i, s))
                m += 1
            for g in self.groups(i):
                self.dve_group(e, g, self.consume)
            for s in range(4):
                S.wait(e, self.sem("mA"), ("rel", m + 1))
                ins = e.tensor_tensor(out=b.junk[:, :], in0=b.osb[:, s, :], in1=b.osb[:, s, :], op=ALU.mult)
                S.fence(e, ins, f)
                ins = e.tensor_reduce(out=b.os2[:, s:s + 1], in_=b.junk[:, :], axis=AX.X, op=ALU.add)
                S.fence(e, ins, f)
                m += 2
            ins = e.tensor_scalar(out=b.os2[:, 4:8], in0=b.os2[:, 0:4], scalar1=1.0 / D, scalar2=EPS, op0=ALU.mult, op1=ALU.add)
            S.sig(ins, self.sem("onv"), i)
            S.wait(e, self.sem("onq"), i)
            ins = e.reciprocal(out=b.os2[:, 4:8], in_=b.os2[:, 4:8])
            S.fence(e, ins, f)
            for s in range(4):
                ins = e.tensor_tensor(out=b.osb[:, s, :], in0=b.osb[:, s, :], in1=b.gpost[:, :], op=ALU.mult)
                S.fence(e, ins, f)
                ins = e.scalar_tensor_tensor(out=xt[:, s, :], in0=b.osb[:, s, :], scalar=b.os2[:, 4 + s:5 + s], in1=xt[:, s, :],
                                             op0=ALU.mult, op1=ALU.add)
            S.sig(ins, self.sem("res"), i)
            S.wait(e, self.sem("res"), i)


NMR = 6


class TokPhase(Phase):
    def reset(self):
        self.m = 0
        self.mk = []

    def ring(self, e, rel_sem):
        S = self.S
        m = self.m
        self.mk.append(rel_sem)
        if m >= NMR:
            S.wait(e, self.sem(self.mk[m - NMR]), ("rel", m - NMR))
        self.m = m + 1
        return self.b.ring[:, m % NMR, :], m

    def dve_postnorm(self, e, i, s, xt):
        S, b = self.S, self.b
        f = self.sem("f")
        o = b.osb[:, s % 2, :]
        ins = e.tensor_tensor(out=b.junk[:, :], in0=o, in1=o, op=ALU.mult)
        S.fence(e, ins, f)
        ins = e.tensor_reduce(out=b.os2[:, 0:1], in_=b.junk[:, :], axis=AX.X, op=ALU.add)
        S.fence(e, ins, f)
        ins = e.tensor_scalar(out=b.os2[:, 1:2], in0=b.os2[:, 0:1], scalar1=1.0 / D, scalar2=EPS, op0=ALU.mult, op1=ALU.add)
        S.sig(ins, self.sem("onv"), (i, s))
        S.wait(e, self.sem("onq"), (i, s))
        ins = e.reciprocal(out=b.os2[:, 1:2], in_=b.os2[:, 1:2])
        ins2 = e.tensor_tensor(out=o, in0=o, in1=b.gpost[:, :], op=ALU.mult)
        S.fence(e, ins2, f)
        ins = e.scalar_tensor_tensor(out=xt[:, s, :], in0=o, scalar=b.os2[:, 1:2], in1=xt[:, s, :], op0=ALU.mult, op1=ALU.add)
        S.sig(ins, self.sem("res"), (i, s))
        S.wait(e, self.sem("res"), (i, s))

    def act_postnorm(self, e, i, s):
        S, b = self.S, self.b
        S.wait(e, self.sem("onv"), (i, s))
        ins = e.activation(out=b.os2[:, 1:2], in_=b.os2[:, 1:2], func=AF.Sqrt)
        S.sig(ins, self.sem("onq"), (i, s))

    def pool_store(self, e, xo):
        S, b = self.S, self.b
        for i in range(NQ):
            S.wait(e, self.sem("res"), (i, 3))
            ins = e.dma_start(out=xo[i * 512:(i + 1) * 512, :].rearrange("(s p) d -> p s d", p=128), in_=b.xt[:, :, :])
            S.sig(ins, self.sem("stx"), i, dma=True)
        S.wait(e, self.sem("stx"), NQ - 1)

    def sp_loadx(self, e, i, xin):
        S, b = self.S, self.b
        if i >= 1:
            S.wait(e, self.sem("stx"), i - 1)
        ins = e.dma_start(out=b.xt[:, :, :], in_=xin[i * 512:(i + 1) * 512, :].rearrange("(s p) d -> p s d", p=128))
        S.sig(ins, self.sem("ldx"), i, dma=True)


class P3a(TokPhase):
    NBLK = DFF // 128

    def __init__(self, prog, tag, x1, w_fi, w_fo, g_pre, g_post, identb, x2):
        super().__init__(prog, tag)
        self.x1, self.w_fi, self.w_fo, self.g_pre, self.g_post, self.identb_d, self.x2 = x1, w_fi, w_fo, g_pre, g_post, identb, x2
        self.nt = NormT(self, "nt")

    def alloc(self, nc, es):
        b = Bufs()
        b.wfi = sb(nc, es, "f_wfi", [128, 8, 2 * DFF], BF16)
        b.wfo = sb(nc, es, "f_wfo", [128, self.NBLK, D], BF16)
        b.gbc = sb(nc, es, "f_gbc", [128, D], F32)
        b.gpost = sb(nc, es, "f_gpost", [128, D], F32)
        b.identb = sb(nc, es, "f_idb", [128, 128], BF16)
        b.xt = sb(nc, es, "f_xt", [128, 4, D], F32)
        b.hb = sb(nc, es, "f_hb", [128, 4, D], BF16)
        b.ss = sb(nc, es, "f_ss", [128, 4], F32)
        b.rstd = sb(nc, es, "f_rstd", [128, 4], F32)
        b.hT = sb(nc, es, "f_hT", [128, 8, 512], BF16)
        b.actT = sb(nc, es, "f_actT", [128, self.NBLK, 512], BF16)
        b.sg = sb(nc, es, "f_sg", [128, 2, 512], F32)
        b.junk = b.sg[:, :, :].rearrange("p a b -> p (a b)")
        b.osb = sb(nc, es, "f_osb", [128, 2, D], F32)
        b.os2 = sb(nc, es, "f_os2", [128, 4], F32)
        b.pst = ps(nc, es, "f_pst", [128, 2, 1024], BF16)
        b.ring = ps(nc, es, "f_ring", [128, NMR, 512], F32)
        return b

    def sp(self, e):
        S, b = self.S, self.b
        ld = self.sem("ldc")
        for n, (dst, src) in enumerate([(b.identb[:, :], self.identb_d[:, :]), (b.gbc[:, :], self.g_pre.partition_broadcast(128)),
                                        (b.gpost[:, :], self.g_post.partition_broadcast(128))]):
            ins = e.dma_start(out=dst, in_=src)
            S.sig(ins, ld, ("c", n), dma=True)
        for i in range(NQ):
            self.sp_loadx(e, i, self.x1)

    def pool(self, e):
        S, b = self.S, self.b
        lw = self.sem("lw")
        nch = 8
        w = 2 * DFF // nch
        for k in range(nch):
            ins = e.dma_start(out=b.wfi[:, :, k * w:(k + 1) * w], in_=wview(self.w_fi, k * w, (k + 1) * w))
            S.sig(ins, lw, ("fi", k), dma=True)
        for k in range(2):
            ins = e.dma_start(out=b.wfo[:, :, k * 512:(k + 1) * 512], in_=wview(self.w_fo, k * 512, (k + 1) * 512))
            S.sig(ins, lw, ("fo", k), dma=True)
        self.pool_store(e, self.x2)

    def pe(self, e):
        S, b = self.S, self.b
        S.wait(e, self.sem("ldc"), ("c", 2))
        S.wait(e, self.sem("lw"), ("fo", 1))
        for i in range(NQ):
            self.nt.pe(e, i, b.hb, b.pst, b.identb)
            self.nt.wait_hT(e, i)
            for blk in range(self.NBLK):
                for part, rel in ((0, "mA"), (1, "mD")):
                    bank, m = self.ring(e, rel)
                    c0 = part * DFF + blk * 128
                    for kc in range(8):
                        ins = e.matmul(bank, lhsT=b.wfi[:, kc, c0:c0 + 128], rhs=b.hT[:, kc, :], start=(kc == 0), stop=(kc == 7))
                    S.sig(ins, self.sem("mm"), m)
            S.wait(e, self.sem("mD"), ("rel", self.m - 1))
            for s in range(4):
                for half in range(2):
                    bank, m = self.ring(e, "mA")
                    for blk in range(self.NBLK):
                        ins = e.matmul(bank, lhsT=b.actT[:, blk, s * 128:(s + 1) * 128], rhs=b.wfo[:, blk, half * 512:(half + 1) * 512],
                                       start=(blk == 0), stop=(blk == self.NBLK - 1))
                    S.sig(ins, self.sem("mm"), m)

    def act(self, e):
        S, b = self.S, self.b
        m = 0
        nb = 0
        for i in range(NQ):
            self.nt.act(e, i, b.pst, b.hT,
                        wait_hT_free=lambda: (S.wait(e, self.sem("mm"), m - 9) if i >= 1 else None), rstd=b.rstd)
            for blk in range(self.NBLK):
                S.wait(e, self.sem("mm"), m)
                if nb >= 2:
                    S.wait(e, self.sem("mD"), ("rel", self.sgrel[nb - 2]))
                ins = e.activation(out=b.sg[:, nb % 2, :], in_=b.ring[:, m % NMR, :], func=AF.Silu)
                S.sig(ins, self.sem("mA"), ("rel", m))
                self.sgrel.append(m + 1)
                nb += 1
                m += 2
            for s in range(4):
                for half in range(2):
                    S.wait(e, self.sem("mm"), m)
                    if half == 0 and (i, s) >= (0, 2):
                        ps_ = (i, s - 2) if s >= 2 else (i - 1, s + 2)
                        S.wait(e, self.sem("res"), ps_)
                    ins = e.activation(out=b.osb[:, s % 2, half * 512:(half + 1) * 512], in_=b.ring[:, m % NMR, :], func=AF.Copy)
                    S.sig(ins, self.sem("mA"), ("rel", m))
                    m += 1
                self.act_postnorm(e, i, s)

    def reset(self):
        super().reset()
        self.sgrel = []

    def dve(self, e):
        S, b = self.S, self.b
        S.wait(e, self.sem("ldc"), ("c", 2))
        m = 0
        nb = 0
        for i in range(NQ):
            self.nt.dve(e, i, b.xt, b.gbc, b.hb, b.junk, b.ss, b.rstd,
                        wait_x=lambda: S.wait(e, self.sem("ldx"), i),
                        wait_hb_free=lambda: (self.nt.wait_tr_done(e, i - 1) if i >= 1 else None))
            for blk in range(self.NBLK):
                S.wait(e, self.sem("mm"), m + 1)
                S.wait(e, self.sem("mA"), ("rel", m))
                if i >= 1 and blk == 0:
                    S.wait(e, self.sem("mm"), m - 1)
                ins = e.tensor_tensor(out=b.actT[:, blk, :], in0=b.sg[:, nb % 2, :], in1=b.ring[:, (m + 1) % NMR, :], op=ALU.mult)
                S.sig(ins, self.sem("mD"), ("rel", m + 1))
                nb += 1
                m += 2
            for s in range(4):
                S.wait(e, self.sem("mA"), ("rel", m + 1))
                self.dve_postnorm(e, i, s, b.xt)
                m += 2


class P3b(TokPhase):
    def __init__(self, prog, tag, x2, p, w_up, w_gate, g_gate, g_post, identb, x3):
        super().__init__(prog, tag)
        self.x2, self.p, self.w_up, self.w_gate, self.g_gate, self.g_post, self.identb_d, self.x3 = x2, p, w_up, w_gate, g_gate, g_post, identb, x3
        self.nt = NormT(self, "nt")

    def alloc(self, nc, es):
        b = Bufs()
        b.wg = sb(nc, es, "e_wg", [128, 8, D], BF16)
        b.wu = sb(nc, es, "e_wu", [128, 2, D], BF16)
        b.gbc = sb(nc, es, "e_gbc", [128, D], F32)
        b.gpost = sb(nc, es, "e_gpost", [128, D], F32)
        b.identb = sb(nc, es, "e_idb", [128, 128], BF16)
        b.xt = sb(nc, es, "e_xt", [128, 4, D], F32)
        b.pt = sb(nc, es, "e_pt", [128, 4, 256], F32)
        b.pb = sb(nc, es, "e_pb", [128, 4, 256], BF16)
        b.pT = sb(nc, es, "e_pT", [128, 2, 512], BF16)
        b.hb = sb(nc, es, "e_hb", [128, 4, D], BF16)
        b.junk = sb(nc, es, "e_junk", [128, D], F32)
        b.ss = sb(nc, es, "e_ss", [128, 4], F32)
        b.rstd = sb(nc, es, "e_rstd", [128, 4], F32)
        b.hT = sb(nc, es, "e_hT", [128, 8, 512], BF16)
        b.sgm = sb(nc, es, "e_sgm", [128, 2, D], F32)
        b.osb = sb(nc, es, "e_osb", [128, 2, D], F32)
        b.os2 = sb(nc, es, "e_os2", [128, 4], F32)
        b.pst = ps(nc, es, "e_pst", [128, 2, 1024], BF16)
        b.ring = ps(nc, es, "e_ring", [128, NMR, 512], F32)
        return b

    def sp(self, e):
        S, b = self.S, self.b
        ld = self.sem("ldc")
        for n, (dst, src) in enumerate([(b.identb[:, :], self.identb_d[:, :]), (b.gbc[:, :], self.g_gate.partition_broadcast(128)),
                                        (b.gpost[:, :], self.g_post.partition_broadcast(128))]):
            ins = e.dma_start(out=dst, in_=src)
            S.sig(ins, ld, ("c", n), dma=True)
        for i in range(NQ):
            self.sp_loadx(e, i, self.x2)
            if i >= 1:
                S.wait(e, self.sem("pb"), i - 1)
            ins = e.dma_start(out=b.pt[:, :, :], in_=self.p[i * 512:(i + 1) * 512, :].rearrange("(s p) d -> p s d", p=128))
            S.sig(ins, self.sem("ldp"), i, dma=True)

    def pool(self, e):
        S, b = self.S, self.b
        lw = self.sem("lw")
        for k in range(2):
            ins = e.dma_start(out=b.wg[:, :, k * 512:(k + 1) * 512], in_=wview(self.w_gate, k * 512, (k + 1) * 512))
            S.sig(ins, lw, ("g", k), dma=True)
        ins = e.dma_start(out=b.wu[:, :, :], in_=wview(self.w_up, 0, D))
        S.sig(ins, lw, "u", dma=True)
        self.pool_store(e, self.x3)

    def pe(self, e):
        S, b = self.S, self.b
        S.wait(e, self.sem("ldc"), ("c", 2))
        S.wait(e, self.sem("lw"), "u")
        for i in range(NQ):
            self.nt.pe(e, i, b.hb, b.pst, b.identb)
            S.wait(e, self.sem("pb"), i)
            for kc in range(2):
                gq = (NQ + i) * 8 + kc
                S.wait(e, self.nt.n + "_ev", i * 8 + 6 + kc)
                for s in range(4):
                    ins = e.transpose(b.pst[:, gq % 2, s * 128:(s + 1) * 128], b.pb[:, s, kc * 128:(kc + 1) * 128], b.identb[:, :])
                S.sig(ins, self.nt.n + "_tr", gq)
            S.wait(e, self.nt.n + "_ev", (NQ + i) * 8 + 1)
            for s in range(4):
                for half in range(2):
                    bank, m = self.ring(e, "mD")
                    for kc in range(2):
                        ins = e.matmul(bank, lhsT=b.pT[:, kc, s * 128:(s + 1) * 128], rhs=b.wu[:, kc, half * 512:(half + 1) * 512],
                                       start=(kc == 0), stop=(kc == 1))
                    S.sig(ins, self.sem("mm"), m)
                for half in range(2):
                    bank, m = self.ring(e, "mA")
                    for kc in range(8):
                        ins = e.matmul(bank, lhsT=b.hT[:, kc, s * 128:(s + 1) * 128], rhs=b.wg[:, kc, half * 512:(half + 1) * 512],
                                       start=(kc == 0), stop=(kc == 7))
                    S.sig(ins, self.sem("mm"), m)

    def act(self, e):
        S, b = self.S, self.b
        m = 0
        for i in range(NQ):
            self.nt.act(e, i, b.pst, b.hT,
                        wait_hT_free=lambda: (S.wait(e, self.sem("mm"), m - 1) if i >= 1 else None), rstd=b.rstd)
            for kc in range(2):
                gq = (NQ + i) * 8 + kc
                S.wait(e, self.nt.n + "_tr", gq)
                if i >= 1 and kc == 0:
                    S.wait(e, self.sem("mm"), m - 3)
                ins = e.activation(out=b.pT[:, kc, :], in_=b.pst[:, gq % 2, 0:512], func=AF.Copy)
                S.sig(ins, self.nt.n + "_ev", gq)
            for s in range(4):
                for half in range(2):
                    mz = m + 2 + half
                    S.wait(e, self.sem("mm"), mz)
                    if half == 0 and (i, s) >= (0, 2):
                        ps_ = (i, s - 2) if s >= 2 else (i - 1, s + 2)
                        S.wait(e, self.sem("eg"), ps_)
                    ins = e.activation(out=b.sgm[:, s % 2, half * 512:(half + 1) * 512], in_=b.ring[:, mz % NMR, :], func=AF.Sigmoid)
                    S.sig(ins, self.sem("mA"), ("rel", mz))
                m += 4
                self.act_postnorm(e, i, s)

    def dve(self, e):
        S, b = self.S, self.b
        S.wait(e, self.sem("ldc"), ("c", 2))
        m = 0
        for i in range(NQ):
            self.nt.dve(e, i, b.xt, b.gbc, b.hb, b.junk, b.ss, b.rstd,
                        wait_x=lambda: S.wait(e, self.sem("ldx"), i),
                        wait_hb_free=lambda: (self.nt.wait_tr_done(e, i - 1) if i >= 1 else None))
            S.wait(e, self.sem("ldp"), i)
            if i >= 1:
                S.wait(e, self.nt.n + "_tr", (NQ + i - 1) * 8 + 1)
            ins = e.tensor_copy(out=b.pb[:, :, :], in_=b.pt[:, :, :])
            S.sig(ins, self.sem("pb"), i)
            for s in range(4):
                for half in range(2):
                    S.wait(e, self.sem("mm"), m + half)
                    S.wait(e, self.sem("mA"), ("rel", m + 2 + half))
                    ins = e.tensor_tensor(out=b.osb[:, s % 2, half * 512:(half + 1) * 512], in0=b.sgm[:, s % 2, half * 512:(half + 1) * 512],
                                          in1=b.ring[:, (m + half) % NMR, :], op=ALU.mult)
                    S.sig(ins, self.sem("mD"), ("rel", m + half))
                S.wait(e, self.sem("mD"), ("rel", m + 1))
                S.sig(e.memset(b.os2[:, 2:3], 0.0), self.sem("eg"), (i, s))
                m += 4
                self.dve_postnorm(e, i, s, b.xt)


class AG(Phase):
    def __init__(self, prog, tag, E, G):
        super().__init__(prog, tag)
        self.E, self.G = E, G

    def pool(self, e):
        S = self.S
        for k in range(NCH):
            ins = e.collective_compute("AllGather", ALU.bypass, replica_groups=[[0, 1, 2, 3], [4, 5, 6, 7]],
                                       ins=[self.E[k, :].rearrange("(p c) -> p c", c=2048)],
                                       outs=[self.G[k, :, :].rearrange("r (p c) -> (r p) c", c=2048)])
            S.sig(ins, "cc", k)
        S.wait(e, "cc", NCH - 1)


def _local_tokens(c):
    ii = np.arange(8)[:, None]
    t = np.arange(512)[None, :]
    return (2048 * ii + 512 * c + t).reshape(-1)


WKEYS = [("g_pre_mix", [D]), ("w_in", [D, 2304]), ("g_diff_sub", [64]), ("gmlp_ln_g", [256]), ("gmlp_ln_b", [256]),
         ("w_spatial", [4, 128, 128]), ("b_spatial", [4, 128]), ("swa_sinks", [6]), ("w_out", [D, D]), ("g_post_mix", [D]),
         ("g_pre_ffn", [D]), ("w_ffn_in", [D, 2 * DFF]), ("w_ffn_out", [DFF, D]), ("g_post_ffn", [D]),
         ("w_ple_up", [256, D]), ("w_ple_gate", [D, D]), ("g_ple_gate", [D]), ("g_ple_post", [D])]


def build_fused(nl=L):
    P = Prog()
    ncol = cidx_map()[1]
    x = P.din("x", [NTOK, D], F32)
    pp = P.din("p", [L, NTOK, 256], F32)
    ctab = P.din("ctab", [128, ncol], F32)
    dn = P.din("dn", [128, 16, 512], F32)
    dw = P.din("dw", [128, 18, 512], F32)
    kaug = P.din("kaug", [6, 4, SEQ], BF16)
    qaug = P.din("qaug", [6, 2, 4, NTOK], BF16)
    identb = P.din("identb", [128, 128], BF16)
    identf = P.din("identf", [128, 128], F32)
    lamv = P.din("lamv", [L, 4, 32], F32)
    lamc = P.din("lamc", [L, 128, 2], F32)
    W = {k: P.din(k, [L] + shp, F32) for k, shp in WKEYS}
    E = P.dint("E", [NCH, CHE], BF16)
    G = P.dint("G", [NCH, 4, CHE], BF16)
    Qs = P.dint("Qs", [384, NTOK], BF16)
    ya_d = P.dint("ya_d", [128, 32, 384], BF16)
    x1 = P.dint("x1", [NTOK, D], F32)
    x2 = P.dint("x2", [NTOK, D], F32)
    x3 = P.dint("x3", [NTOK, D], F32)
    xo = P.dout("xo", [NTOK, D], F32)
    for l in range(nl):
        xin = x if l == 0 else x3
        xout = xo if l == nl - 1 else x3
        P.phases.append(P1(P, "p1", xin, W["w_in"][l], W["g_pre_mix"][l], identb, Qs, E))
        P.phases.append(P2a(P, "a", G, Qs, ctab, dn, kaug, qaug, identf, lamv[l], lamc[l], W["g_diff_sub"][l], ya_d, E=E))
        P.phases.append(P2b(P, "b", xin, W["w_in"][l], W["g_pre_mix"][l], G, ya_d, W["gmlp_ln_g"][l], W["gmlp_ln_b"][l],
                            W["w_spatial"][l], W["b_spatial"][l], W["swa_sinks"][l], W["w_out"][l], W["g_post_mix"][l],
                            dw, identb, identf, x1))
        P.phases.append(P3a(P, "f", x1, W["w_ffn_in"][l], W["w_ffn_out"][l], W["g_pre_ffn"][l], W["g_post_ffn"][l], identb, x2))
        P.phases.append(P3b(P, "e", x2, pp[l], W["w_ple_up"][l], W["w_ple_gate"][l], W["g_ple_gate"][l], W["g_ple_post"][l],
                            identb, xout))
    return P.build()


_PROGS = {}


def kernel(**inputs):
    if "F" not in _PROGS:
        _PROGS["F"] = build_fused()
    consts = host_consts()
    tabs = [host_tables(c) for c in range(4)]
    x = np.asarray(inputs["x"], np.float32)
    p = np.asarray(inputs["p"], np.float32)
    lamv = np.stack([np.stack([np.asarray(inputs[k][l], np.float32) for k in ("lam_q1", "lam_k1", "lam_q2", "lam_k2")])
                     for l in range(L)])
    lamc = np.stack([np.tile(np.array([[lam_init(l), 1.0 - lam_init(l)]], np.float32), (128, 1)) for l in range(L)])
    shared = {k: np.ascontiguousarray(np.asarray(inputs[k], np.float32)) for k, _ in WKEYS}
    shared.update(kaug=consts["kaug"], qaug=consts["qaug"], identb=consts["identb"], identf=consts["identf"],
                  lamv=lamv, lamc=lamc)
    cores = list(range(8))
    in_maps = []
    for core in cores:
        bb, c = core // 4, core % 4
        tok = _local_tokens(c)
        m = dict(shared)
        m["x"] = np.ascontiguousarray(x[bb][tok])
        m["p"] = np.ascontiguousarray(p[:, bb][:, tok])
        m["ctab"], m["dn"], m["dw"] = tabs[c]["ctab"], tabs[c]["dn"], tabs[c]["dw"]
        in_maps.append(m)
    res = run_bass_kernel_spmd(_PROGS["F"], in_maps, core_ids=cores)
    out = np.empty((NB, SEQ, D), np.float32)
    for core in cores:
        out[core // 4][_local_tokens(core % 4)] = np.asarray(res.results[core]["xo"], np.float32)
    return out
```
